# Optimizing a Trainium2 kernel written in Bass

```python
import math
import jax
import jax.numpy as jnp
from jax import lax
import numpy as np

D_MODEL = 1024
BATCH = 8
SEQ = 4096
DEPTH = 4

GRID_W = 64
CTX_LEN = 256
NORM_EPS = 1e-6
N_BRANCH = 3
D_BRANCH = D_MODEL

D_RNN = D_BRANCH
LRU_BLOCKS = 8
LRU_BLOCK_W = D_RNN // LRU_BLOCKS
LRU_C = 8.0
RNN_CONV = 4

D_HYENA = D_BRANCH
HYENA_ORDER = 2
HYENA_SHORT = 3
HYENA_BANDS = 16
HYENA_EMB = 2 * HYENA_BANDS + 1
HYENA_FFN = 64
HYENA_MIN_DECAY = 3.07
HYENA_MAX_DECAY = 15.35

D_SSM = D_BRANCH
SSM_HEAD_DIM = 64
SSM_HEADS = D_SSM // SSM_HEAD_DIM
SSM_GROUPS = 2
SSM_STATE = 128
SSM_CONV = 4
SSM_CHUNK = 128
D_XBC = D_SSM + 2 * SSM_GROUPS * SSM_STATE

D_FF = -(-8 * D_MODEL // (3 * 256)) * 256

IN_SIZES = (D_RNN, D_RNN, (HYENA_ORDER + 1) * D_HYENA, D_SSM, D_XBC + 2 * SSM_HEADS, N_BRANCH * D_MODEL)
D_IN = sum(IN_SIZES)

kernel_name = 'hybrid_rglru_hyena_ssd_prefix_block'


def rmsnorm(x, g):
    xf = x.astype(jnp.float32)
    y = xf * lax.rsqrt(jnp.mean(xf * xf, axis=-1, keepdims=True) + NORM_EPS)
    return (y * g.astype(jnp.float32)).astype(x.dtype)


def split_in(p):
    offs = np.cumsum(IN_SIZES)[:-1].tolist()
    return jnp.split(p, offs, axis=-1)


def short_conv(u, w, b):
    k, ch = w.shape
    left = k // 2
    y = lax.conv_general_dilated(u, w[:, None, :].astype(u.dtype), window_strides=(1,),
                                 padding=[(left, k - 1 - left)],
                                 dimension_numbers=('NWC', 'WIO', 'NWC'),
                                 feature_group_count=ch)
    return y + b.astype(u.dtype)


def flip1(t):
    return jnp.flip(t, axis=1)


def to_col_major(u):
    bsz, n, ch = u.shape
    rows = n // GRID_W
    return u.reshape(bsz, rows, GRID_W, ch).transpose(0, 2, 1, 3).reshape(bsz, n, ch)


def to_row_major(u):
    bsz, n, ch = u.shape
    rows = n // GRID_W
    return u.reshape(bsz, GRID_W, rows, ch).transpose(0, 2, 1, 3).reshape(bsz, n, ch)


def _lin_combine(l, r):
    return (l[0] * r[0], r[0] * l[1] + r[1])


def linear_scan(a, b, h0, reverse):
    a_c, b_c = lax.associative_scan(_lin_combine, (a, b), reverse=reverse, axis=1)
    return b_c + a_c * h0[:, None]


def rglru_coeffs(u, w_a, b_a, w_x, b_x, lam):
    bsz, n, _ = u.shape
    uf = u.astype(jnp.float32)
    ub = uf.reshape(bsz, n, LRU_BLOCKS, LRU_BLOCK_W)
    r = jax.nn.sigmoid(jnp.einsum('blhi,dhij->dblhj', ub, w_a.astype(jnp.float32)).reshape(2, bsz, n, D_RNN)
                       + b_a.astype(jnp.float32)[:, None, None])
    i = jax.nn.sigmoid(jnp.einsum('blhi,dhij->dblhj', ub, w_x.astype(jnp.float32)).reshape(2, bsz, n, D_RNN)
                       + b_x.astype(jnp.float32)[:, None, None])
    log_a = -LRU_C * r * jax.nn.softplus(-lam.astype(jnp.float32))[:, None, None]
    a = jnp.exp(log_a)
    gated = jnp.sqrt(-jnp.expm1(2.0 * log_a)) * (i * uf[None])
    return a, gated


def rglru_mixer(x_c, x_l, gate_c, gate_l, conv_w, conv_b, w_a, b_a, w_x, b_x, lam, ctx_out):
    bsz = x_l.shape[0]
    h0 = jnp.zeros((bsz, D_RNN), jnp.float32)
    a_c, v_c = rglru_coeffs(short_conv(x_c, conv_w, conv_b), w_a, b_a, w_x, b_x, lam)
    hf_c = linear_scan(a_c[0], v_c[0], h0, False)
    hb_c = linear_scan(a_c[1], v_c[1], h0, True)
    a_l, v_l = rglru_coeffs(short_conv(x_l, conv_w, conv_b), w_a, b_a, w_x, b_x, lam)
    hf_l = linear_scan(a_l[0], v_l[0], hf_c[:, -1], False)
    hb_l = linear_scan(a_l[1], v_l[1], hb_c[:, 0], True)
    y_l = (hf_l + hb_l).astype(x_l.dtype) * jax.nn.gelu(gate_l)
    y_c = (hf_c + hb_c).astype(x_c.dtype) * jax.nn.gelu(gate_c) if ctx_out else None
    return y_c, y_l


def hyena_filter_spectrum(n, w1, b1, w2, b2, w3, freq, decay):
    f32 = jnp.float32
    t = jnp.linspace(0.0, 1.0, n, dtype=f32)[:, None]
    bands = jnp.linspace(1e-4, HYENA_BANDS - 1, HYENA_BANDS, dtype=f32)
    ang = (2.0 * math.pi / n) * jnp.arange(n, dtype=f32)[:, None] * bands[None]
    emb = jnp.concatenate([t, jnp.cos(ang), jnp.sin(ang)], axis=-1)
    fr = freq.astype(f32)
    h = jnp.sin(fr * (emb @ w1.astype(f32) + b1.astype(f32)))
    h = jnp.sin(fr * (h @ w2.astype(f32) + b2.astype(f32)))
    h = (h @ w3.astype(f32)) * jnp.exp(-t * decay.astype(f32))
    h = h.reshape(n, 2, HYENA_ORDER, D_HYENA)
    k = jnp.concatenate([h[:, 0], jnp.zeros((1, HYENA_ORDER, D_HYENA), f32), h[:0:-1, 1]], axis=0)
    k = k / jnp.sum(jnp.abs(k), axis=0, keepdims=True)
    return jnp.fft.rfft(k, axis=0)


def hyena_mixer(p, short_w, short_b, w1, b1, w2, b2, w3, freq, decay, bias):
    n = p.shape[1]
    u = short_conv(p, short_w, short_b).astype(jnp.float32)
    parts = jnp.split(u, HYENA_ORDER + 1, axis=-1)
    k_f = hyena_filter_spectrum(n, w1, b1, w2, b2, w3, freq, decay)
    bias = bias.astype(jnp.float32)
    z = parts[0]
    for o in range(HYENA_ORDER):
        z_f = jnp.fft.rfft(z, n=2 * n, axis=1)
        conv = jnp.fft.irfft(z_f * k_f[:, o], n=2 * n, axis=1)[:, :n]
        z = parts[o + 1] * (conv + bias[o] * z)
    return z.astype(p.dtype)


def ssd_scan(xs, dt, a, bm, cm, h0, with_y):
    bsz, n = xs.shape[0], xs.shape[1]
    nc = n // SSM_CHUNK
    e = SSM_HEADS // SSM_GROUPS
    shp = (bsz, nc, SSM_CHUNK, SSM_GROUPS)
    x = xs.reshape(shp + (e, SSM_HEAD_DIM))
    dtc = dt.reshape(shp + (e,))
    bc = bm.reshape(shp + (SSM_STATE,))
    cc = cm.reshape(shp + (SSM_STATE,))
    a_cum = jnp.cumsum(dtc * a.reshape(SSM_GROUPS, e), axis=2)
    xdt = x * dtc[..., None]
    states = jnp.einsum('bclgn,bclge,bclgep->bcgepn', bc, jnp.exp(a_cum[:, :, -1:] - a_cum), xdt)
    chunk_decay = jnp.exp(a_cum[:, :, -1])

    def step(h, inp):
        dec, st = inp
        return dec[..., None, None] * h + st, h

    h_init = h0.reshape(bsz, SSM_GROUPS, e, SSM_HEAD_DIM, SSM_STATE)
    h_last, h_prev = lax.scan(step, h_init, (jnp.moveaxis(chunk_decay, 1, 0), jnp.moveaxis(states, 1, 0)))
    h_last = h_last.reshape(bsz, SSM_HEADS, SSM_HEAD_DIM, SSM_STATE)
    if not with_y:
        return None, h_last
    h_prev = jnp.moveaxis(h_prev, 0, 1)
    a_t = jnp.moveaxis(a_cum, 2, -1)
    seg = a_t[..., :, None] - a_t[..., None, :]
    causal = jnp.tril(jnp.ones((SSM_CHUNK, SSM_CHUNK), dtype=bool))
    decay_in = jnp.exp(jnp.where(causal, seg, -jnp.inf))
    cb = jnp.einsum('bclgn,bcsgn->bcgls', cc, bc)
    y = (jnp.einsum('bcgls,bcgels,bcsgep->bclgep', cb, decay_in, xdt)
         + jnp.einsum('bclgn,bcgepn,bclge->bclgep', cc, h_prev, jnp.exp(a_cum)))
    return y.reshape(bsz, n, SSM_HEADS, SSM_HEAD_DIM), h_last


def ssd_bidir(pp, conv_w, conv_b, a, dt_bias, d_skip, hf0, hb0, with_y):
    bsz, n, _ = pp.shape
    xbc = jax.nn.silu(short_conv(pp[..., :D_XBC], conv_w, conv_b)).astype(jnp.float32)
    n_bc = SSM_GROUPS * SSM_STATE
    xs = xbc[..., :D_SSM].reshape(bsz, n, SSM_HEADS, SSM_HEAD_DIM)
    bm = xbc[..., D_SSM:D_SSM + n_bc].reshape(bsz, n, SSM_GROUPS, SSM_STATE)
    cm = xbc[..., D_SSM + n_bc:].reshape(bsz, n, SSM_GROUPS, SSM_STATE)
    dt = jax.nn.softplus(pp[..., D_XBC:].astype(jnp.float32).reshape(bsz, n, 2, SSM_HEADS)
                         + dt_bias.astype(jnp.float32))
    y_f, s_f = ssd_scan(xs, dt[:, :, 0], a[0], bm, cm, hf0, with_y)
    y_b, s_b = ssd_scan(flip1(xs), flip1(dt[:, :, 1]), a[1], flip1(bm), flip1(cm), hb0, with_y)
    if not with_y:
        return None, s_f, s_b
    y = y_f + flip1(y_b) + d_skip.astype(jnp.float32)[:, None] * xs
    return y.reshape(bsz, n, D_SSM), s_f, s_b


def ssd_gate_norm(y, z, norm_w):
    return rmsnorm(y * jax.nn.silu(z.astype(jnp.float32)), norm_w).astype(z.dtype)


def ssd_mixer(p_c, p_l, z_c, z_l, conv_w, conv_b, a_log, dt_bias, d_skip, norm_w, ctx_out):
    bsz = p_l.shape[0]
    a = -jnp.exp(a_log.astype(jnp.float32))
    h0 = jnp.zeros((bsz, SSM_HEADS, SSM_HEAD_DIM, SSM_STATE), jnp.float32)
    y_c, sf_c, sb_c = ssd_bidir(p_c, conv_w, conv_b, a, dt_bias, d_skip, h0, h0, ctx_out)
    y_l, _, _ = ssd_bidir(to_col_major(p_l), conv_w, conv_b, a, dt_bias, d_skip, sf_c, sb_c, True)
    out_l = ssd_gate_norm(to_row_major(y_l), z_l, norm_w)
    out_c = ssd_gate_norm(y_c, z_c, norm_w) if ctx_out else None
    return out_c, out_l


def merge_branches(gate_logits, y_r, y_h, y_s, w_branch, w_out):
    g = jax.nn.sigmoid(gate_logits.astype(jnp.float32)).astype(y_r.dtype)
    g_r, g_h, g_s = jnp.split(g, N_BRANCH, axis=-1)
    m = g_r * (y_r @ w_branch[0]) + g_h * (y_h @ w_branch[1]) + g_s * (y_s @ w_branch[2])
    return m @ w_out


def swiglu(h, w_up, w_down):
    g, u = jnp.split(h @ w_up, 2, axis=-1)
    return (jax.nn.silu(g) * u) @ w_down


def setup_inputs(seed: int = 0) -> dict:
    key = jax.random.key(seed)
    ks = iter(list(jax.random.split(key, 48)))
    f32 = jnp.float32

    def nrm(shape, scale):
        return jax.random.normal(next(ks), shape, f32) * scale

    x = nrm((BATCH, SEQ, D_MODEL), 1.0)
    c = nrm((BATCH, D_MODEL), 1.0)
    ctx = nrm((BATCH, CTX_LEN, D_MODEL), 1.0)
    c_ctx = nrm((D_MODEL,), 1.0)
    w_mod = nrm((DEPTH, D_MODEL, 6 * D_MODEL), 0.5 * D_MODEL ** -0.5)
    b_mod = nrm((DEPTH, 6 * D_MODEL), 0.02)
    norm_mix = 1.0 + nrm((DEPTH, D_MODEL), 0.02)
    norm_ffn = 1.0 + nrm((DEPTH, D_MODEL), 0.02)
    w_in = nrm((DEPTH, D_MODEL, D_IN), D_MODEL ** -0.5)
    rnn_conv_w = nrm((DEPTH, RNN_CONV, D_RNN), RNN_CONV ** -0.5)
    rnn_conv_b = nrm((DEPTH, D_RNN), 0.02)
    rnn_gate_a_w = nrm((DEPTH, 2, LRU_BLOCKS, LRU_BLOCK_W, LRU_BLOCK_W), LRU_BLOCK_W ** -0.5)
    rnn_gate_a_b = nrm((DEPTH, 2, D_RNN), 0.02)
    rnn_gate_x_w = nrm((DEPTH, 2, LRU_BLOCKS, LRU_BLOCK_W, LRU_BLOCK_W), LRU_BLOCK_W ** -0.5)
    rnn_gate_x_b = nrm((DEPTH, 2, D_RNN), 0.02)
    a0 = jax.random.uniform(next(ks), (DEPTH, 2, D_RNN), f32, 0.9, 0.999) ** (1.0 / LRU_C)
    rnn_lambda = jnp.log(a0) - jnp.log1p(-a0)
    hy_short_w = nrm((DEPTH, HYENA_SHORT, (HYENA_ORDER + 1) * D_HYENA), HYENA_SHORT ** -0.5)
    hy_short_b = nrm((DEPTH, (HYENA_ORDER + 1) * D_HYENA), 0.02)
    hy_w1 = nrm((DEPTH, HYENA_EMB, HYENA_FFN), HYENA_EMB ** -0.5)
    hy_b1 = nrm((DEPTH, HYENA_FFN), 0.1)
    hy_w2 = nrm((DEPTH, HYENA_FFN, HYENA_FFN), HYENA_FFN ** -0.5)
    hy_b2 = nrm((DEPTH, HYENA_FFN), 0.1)
    hy_w3 = nrm((DEPTH, HYENA_FFN, 2 * HYENA_ORDER * D_HYENA), HYENA_FFN ** -0.5)
    hy_freq = 1.0 + nrm((DEPTH, HYENA_FFN), 0.1)
    hy_decay = jax.random.uniform(next(ks), (DEPTH, 2 * HYENA_ORDER * D_HYENA), f32, HYENA_MIN_DECAY, HYENA_MAX_DECAY)
    hy_bias = nrm((DEPTH, HYENA_ORDER, D_HYENA), 1.0)
    ssm_conv_w = nrm((DEPTH, SSM_CONV, D_XBC), SSM_CONV ** -0.5)
    ssm_conv_b = nrm((DEPTH, D_XBC), 0.02)
    ssm_a_log = jnp.log(jax.random.uniform(next(ks), (DEPTH, 2, SSM_HEADS), f32, 1.0, 16.0))
    dt0 = jnp.exp(jax.random.uniform(next(ks), (DEPTH, 2, SSM_HEADS), f32, math.log(1e-3), math.log(1e-1)))
    ssm_dt_bias = dt0 + jnp.log(-jnp.expm1(-dt0))
    ssm_d = 1.0 + nrm((DEPTH, SSM_HEADS), 0.1)
    ssm_norm = 1.0 + nrm((DEPTH, D_SSM), 0.02)
    w_branch = nrm((DEPTH, N_BRANCH, D_BRANCH, D_MODEL), D_BRANCH ** -0.5)
    w_out = nrm((DEPTH, D_MODEL, D_MODEL), D_MODEL ** -0.5)
    w_up = nrm((DEPTH, D_MODEL, 2 * D_FF), D_MODEL ** -0.5)
    w_down = nrm((DEPTH, D_FF, D_MODEL), D_FF ** -0.5)
    final_norm = 1.0 + nrm((D_MODEL,), 0.02)
    return {'x': x, 'c': c, 'ctx': ctx, 'c_ctx': c_ctx, 'w_mod': w_mod, 'b_mod': b_mod,
            'norm_mix': norm_mix, 'norm_ffn': norm_ffn, 'w_in': w_in,
            'rnn_conv_w': rnn_conv_w, 'rnn_conv_b': rnn_conv_b,
            'rnn_gate_a_w': rnn_gate_a_w, 'rnn_gate_a_b': rnn_gate_a_b,
            'rnn_gate_x_w': rnn_gate_x_w, 'rnn_gate_x_b': rnn_gate_x_b, 'rnn_lambda': rnn_lambda,
            'hy_short_w': hy_short_w, 'hy_short_b': hy_short_b, 'hy_w1': hy_w1, 'hy_b1': hy_b1,
            'hy_w2': hy_w2, 'hy_b2': hy_b2, 'hy_w3': hy_w3, 'hy_freq': hy_freq,
            'hy_decay': hy_decay, 'hy_bias': hy_bias,
            'ssm_conv_w': ssm_conv_w, 'ssm_conv_b': ssm_conv_b, 'ssm_a_log': ssm_a_log,
            'ssm_dt_bias': ssm_dt_bias, 'ssm_d': ssm_d, 'ssm_norm': ssm_norm,
            'w_branch': w_branch, 'w_out': w_out, 'w_up': w_up, 'w_down': w_down,
            'final_norm': final_norm}


def reference(x, c, ctx, c_ctx, w_mod, b_mod, norm_mix, norm_ffn, w_in,
              rnn_conv_w, rnn_conv_b, rnn_gate_a_w, rnn_gate_a_b, rnn_gate_x_w, rnn_gate_x_b, rnn_lambda,
              hy_short_w, hy_short_b, hy_w1, hy_b1, hy_w2, hy_b2, hy_w3, hy_freq, hy_decay, hy_bias,
              ssm_conv_w, ssm_conv_b, ssm_a_log, ssm_dt_bias, ssm_d, ssm_norm,
              w_branch, w_out, w_up, w_down, final_norm):
    s = ctx
    for i in range(DEPTH):
        ctx_out = i < DEPTH - 1
        mod_l = (jax.nn.silu(c) @ w_mod[i] + b_mod[i])[:, None, :]
        mod_c = jax.nn.silu(c_ctx) @ w_mod[i] + b_mod[i]
        sh1_l, sc1_l, g1_l, sh2_l, sc2_l, g2_l = jnp.split(mod_l, 6, axis=-1)
        sh1_c, sc1_c, g1_c, sh2_c, sc2_c, g2_c = jnp.split(mod_c, 6, axis=-1)

        h_l = rmsnorm(x, norm_mix[i]) * (1 + sc1_l) + sh1_l
        h_c = rmsnorm(s, norm_mix[i]) * (1 + sc1_c) + sh1_c
        rx_l, rg_l, hy_l, sz_l, sp_l, gt_l = split_in(h_l @ w_in[i])
        rx_c, rg_c, hy_c, sz_c, sp_c, gt_c = split_in(h_c @ w_in[i])

        yr_c, yr_l = rglru_mixer(rx_c, rx_l, rg_c, rg_l, rnn_conv_w[i], rnn_conv_b[i],
                                 rnn_gate_a_w[i], rnn_gate_a_b[i], rnn_gate_x_w[i], rnn_gate_x_b[i],
                                 rnn_lambda[i], ctx_out)
        ys_c, ys_l = ssd_mixer(sp_c, sp_l, sz_c, sz_l, ssm_conv_w[i], ssm_conv_b[i], ssm_a_log[i],
                               ssm_dt_bias[i], ssm_d[i], ssm_norm[i], ctx_out)
        yh_l = hyena_mixer(hy_l, hy_short_w[i], hy_short_b[i], hy_w1[i], hy_b1[i], hy_w2[i], hy_b2[i],
                           hy_w3[i], hy_freq[i], hy_decay[i], hy_bias[i])
        x = x + g1_l * merge_branches(gt_l, yr_l, yh_l, ys_l, w_branch[i], w_out[i])
        f_l = rmsnorm(x, norm_ffn[i]) * (1 + sc2_l) + sh2_l
        x = x + g2_l * swiglu(f_l, w_up[i], w_down[i])

        if ctx_out:
            yh_c = hyena_mixer(hy_c, hy_short_w[i], hy_short_b[i], hy_w1[i], hy_b1[i], hy_w2[i], hy_b2[i],
                               hy_w3[i], hy_freq[i], hy_decay[i], hy_bias[i])
            s = s + g1_c * merge_branches(gt_c, yr_c, yh_c, ys_c, w_branch[i], w_out[i])
            f_c = rmsnorm(s, norm_ffn[i]) * (1 + sc2_c) + sh2_c
            s = s + g2_c * swiglu(f_c, w_up[i], w_down[i])
    return rmsnorm(x, final_norm)
```

```python
import numpy as np
import concourse.bass as bass
import concourse.mybir as mybir
from concourse.bass_utils import run_bass_kernel_spmd

F32 = mybir.dt.float32
BF16 = mybir.dt.bfloat16
ALU = mybir.AluOpType
AF = mybir.ActivationFunctionType
AX = mybir.AxisListType

D = 1024
NCTX = 256
NLAT = 4096
T = NCTX + NLAT
DEPTH = 4
D_IN = 10784
D_FF = 2816
EPS = 1e-6
TT = [(0, 256)] + [(256 + 512 * i, 512) for i in range(8)]


class Prog:
    NDMA = 24

    def __init__(self, nc):
        self.nc = nc
        self.engs = {"pe": nc.tensor, "dve": nc.vector, "act": nc.scalar, "pool": nc.gpsimd, "sp": nc.sync}
        self._ctx = []
        self.sem = {}
        self.cnt = {}
        for e in ("pe", "dve", "act", "pool"):
            self.sem[e] = self._enter(nc.semaphore("s_" + e))
            self.cnt[e] = 0
        self.dsem = [self._enter(nc.semaphore("d%d" % i)) for i in range(self.NDMA)]
        self.dcnt = [0] * self.NDMA
        self.dnext = 0
        self.semobj = {}
        for e in ("pe", "dve", "act", "pool"):
            self.semobj[("c", e)] = self.sem[e]
        for i in range(self.NDMA):
            self.semobj[("d", i)] = self.dsem[i]
        self.waited = {e: {} for e in self.engs}
        self.lastw = {}
        self.reads = {}
        self.ninst = 0
        self.uid = 0

    def _enter(self, cm):
        v = cm.__enter__()
        self._ctx.append(cm)
        return v

    def sb(self, name, shape, dt):
        self.uid += 1
        return self._enter(self.nc.sbuf_tensor("%s_%d" % (name, self.uid), list(shape), dt))

    def ps(self, name, shape, dt=F32):
        return self._enter(self.nc.psum_tensor(name, list(shape), dt))

    def mark(self):
        return len(self._ctx)

    def release(self, mark):
        self.barrier()
        while len(self._ctx) > mark:
            cm = self._ctx.pop()
            cm.__exit__(None, None, None)

    def close(self):
        while self._ctx:
            cm = self._ctx.pop()
            cm.__exit__(None, None, None)

    def barrier(self):
        targets = []
        for e in ("pe", "dve", "act", "pool"):
            if self.cnt[e]:
                targets.append((("c", e), self.cnt[e]))
        for i in range(self.NDMA):
            if self.dcnt[i]:
                targets.append((("d", i), self.dcnt[i] * 16))
        for q in ("pe", "dve", "act", "pool", "sp"):
            e = self.engs[q]
            for sk, val in targets:
                if sk == ("c", q):
                    continue
                if self.waited[q].get(sk, 0) < val:
                    e.wait_ge(self.semobj[sk], val)
                    self.waited[q][sk] = val
        self.lastw = {}
        self.reads = {}

    def _deps(self, eng, R, W):
        deps = []
        for r in R:
            lw = self.lastw.get(r)
            if lw is not None:
                deps.append((lw, "raw"))
        for w in W:
            lw = self.lastw.get(w)
            if lw is not None:
                deps.append((lw, "waw"))
            for rd in self.reads.get(w, ()):
                deps.append((rd, "war"))
        own = ("c", eng)
        e = self.engs[eng]
        wt = self.waited[eng]
        need = {}
        for (sk, val), kind in deps:
            if sk == own:
                if eng == "pe":
                    continue
                if kind != "raw":
                    continue
            if wt.get(sk, 0) >= val:
                continue
            if need.get(sk, 0) < val:
                need[sk] = val
        for sk, val in need.items():
            e.wait_ge(self.semobj[sk], val)
            wt[sk] = val

    def _commit(self, tick, R, W):
        for w in W:
            self.lastw[w] = tick
            self.reads[w] = []
        for r in R:
            if r in W:
                continue
            lst = self.reads.setdefault(r, [])
            lst.append(tick)
            if len(lst) > 48:
                best = {}
                for sk, v in lst:
                    if best.get(sk, 0) < v:
                        best[sk] = v
                self.reads[r] = list(best.items())

    def op(self, eng, R, W, fn):
        self._deps(eng, R, W)
        ins = fn(self.engs[eng])
        self.cnt[eng] += 1
        ins.then_inc(self.sem[eng], 1)
        tick = (("c", eng), self.cnt[eng])
        self._commit(tick, R, W)
        self.ninst += 1
        return tick

    def dma(self, q, R, W, out, in_, **kw):
        i = self.dnext
        self.dnext = (self.dnext + 1) % self.NDMA
        sk = ("d", i)
        e = self.engs[q]
        prev = self.dcnt[i] * 16
        if prev and self.waited[q].get(sk, 0) < prev:
            e.wait_ge(self.dsem[i], prev)
            self.waited[q][sk] = prev
        self._deps(q, R, W)
        ins = e.dma_start(out=out, in_=in_, **kw)
        self.dcnt[i] += 1
        ins.then_inc(self.dsem[i], 16)
        tick = (sk, self.dcnt[i] * 16)
        self._commit(tick, R, W)
        self.ninst += 1
        return tick

    def wait_all(self, eng, keys):
        e = self.engs[eng]
        for k in keys:
            lw = self.lastw.get(k)
            if lw is None:
                continue
            sk, val = lw
            if self.waited[eng].get(sk, 0) < val:
                e.wait_ge(self.semobj[sk], val)
                self.waited[eng][sk] = val

    def mm(self, R, W, out, lhsT, rhs, start=True, stop=True):
        return self.op("pe", R, W, lambda e: e.matmul(out, lhsT, rhs, start=start, stop=stop))

    def tr(self, R, W, out, in_, ident):
        return self.op("pe", R, W, lambda e: e.transpose(out, in_, ident))

    def act(self, R, W, out, in_, func, **kw):
        return self.op("act", R, W, lambda e: e.activation(out=out, in_=in_, func=func, **kw))

    def tt(self, R, W, out, in0, in1, op, eng="dve"):
        return self.op(eng, R, W, lambda e: e.tensor_tensor(out=out, in0=in0, in1=in1, op=op))

    def ts(self, R, W, out, in0, s1, s2, op0, op1=None, eng="dve"):
        if op1 is None:
            return self.op(eng, R, W, lambda e: e.tensor_scalar(out=out, in0=in0, scalar1=s1, scalar2=None, op0=op0))
        return self.op(eng, R, W, lambda e: e.tensor_scalar(out=out, in0=in0, scalar1=s1, scalar2=s2, op0=op0, op1=op1))

    def stt(self, R, W, out, in0, scalar, in1, op0, op1, eng="dve"):
        return self.op(eng, R, W, lambda e: e.scalar_tensor_tensor(out=out, in0=in0, scalar=scalar, in1=in1, op0=op0, op1=op1))

    def cp(self, R, W, out, in_, eng="dve"):
        return self.op(eng, R, W, lambda e: e.tensor_copy(out=out, in_=in_))


class Ring:
    def __init__(self, p, name, shape, dt, n):
        self.tiles = [p.sb("%s%d" % (name, i), shape, dt) for i in range(n)]
        self.keys = ["%s#%d_%d" % (name, p.uid, i) for i in range(n)]
        self.i = 0

    def next(self):
        t, k = self.tiles[self.i], self.keys[self.i]
        self.i = (self.i + 1) % len(self.tiles)
        return t, k


class PsRing:
    def __init__(self, p, n=8):
        self.tiles = [p.ps("psb%d" % i, [128, 512]) for i in range(n)]
        self.keys = ["psb%d" % i for i in range(n)]
        self.i = 0

    def next(self):
        t, k = self.tiles[self.i], self.keys[self.i]
        self.i = (self.i + 1) % len(self.tiles)
        return t, k


def fm(ap2d):
    return ap2d.rearrange("(kc p) t -> p kc t", p=128)


def seg_conv(p, out, in_, wv, bv, ntap, left, segs, Rk, Wk):
    p.ts(Rk, Wk, out[:, :], in_[:, :], wv(left), bv, ALU.mult, ALU.add)
    for j in range(ntap):
        d = j - left
        if d == 0:
            continue
        for (s0, s1) in segs:
            lo = max(s0, s0 - d)
            hi = min(s1, s1 - d)
            p.stt(Rk + Wk, Wk, out[:, lo:hi], in_[:, lo + d:hi + d], wv(j), out[:, lo:hi], ALU.mult, ALU.add)


def build(nlayers=DEPTH, stop_after=None, dbg=()):
    nc = bass.Bass("TRN2", target_bir_lowering=False)

    def din(name, shape, dt=F32):
        return nc.dram_tensor(name, list(shape), dt, kind="ExternalInput").ap()

    def dscr(name, shape, dt=F32):
        kind = "ExternalOutput" if name in dbg else "Internal"
        return nc.dram_tensor(name, list(shape), dt, kind=kind).ap()

    xin = din("xin", [D, T])
    cc = din("cc", [128, 8, 2])
    w_mod = din("w_mod", [DEPTH, D, 6 * D])
    b_modT = din("b_modT", [DEPTH, 128, 48])
    norm_mixT = din("norm_mixT", [DEPTH, 128, 8])
    norm_ffnT = din("norm_ffnT", [DEPTH, 128, 8])
    final_normT = din("final_normT", [128, 8])
    w_in = din("w_in", [DEPTH, D, D_IN])
    rnn_cwT = din("rnn_cwT", [DEPTH, 128, 8, 4])
    rnn_cbT = din("rnn_cbT", [DEPTH, 128, 8])
    rnn_aw = din("rnn_aw", [DEPTH, 2, 8, 128, 128])
    rnn_xw = din("rnn_xw", [DEPTH, 2, 8, 128, 128])
    rnn_abT = din("rnn_abT", [DEPTH, 128, 2, 8])
    rnn_xbT = din("rnn_xbT", [DEPTH, 128, 2, 8])
    rnn_lamT = din("rnn_lamT", [DEPTH, 128, 2, 8])
    ident_d = din("ident", [128, 128])
    masks_d = din("masks", [5, 128, 128])
    ssm_cwT = din("ssm_cwT", [DEPTH, 128, 12, 4])
    ssm_cbT = din("ssm_cbT", [DEPTH, 128, 12])
    ssm_alogT = din("ssm_alogT", [DEPTH, 32, 1])
    ssm_dtbT = din("ssm_dtbT", [DEPTH, 32, 1])
    ssm_d = din("ssm_d", [DEPTH, 16])
    ssm_norm = din("ssm_norm", [DEPTH, 1024])
    hy_cwT = din("hy_cwT", [DEPTH, 128, 24, 3])
    hy_cbT = din("hy_cbT", [DEPTH, 128, 24])
    hy_biasT = din("hy_biasT", [DEPTH, 128, 2, 8])
    hy_w1 = din("hy_w1", [DEPTH, 33, 64])
    hy_w2 = din("hy_w2", [DEPTH, 64, 64])
    hy_w3 = din("hy_w3", [DEPTH, 64, 4096])
    hy_b1T = din("hy_b1T", [DEPTH, 64, 1])
    hy_b2T = din("hy_b2T", [DEPTH, 64, 1])
    hy_freqT = din("hy_freqT", [DEPTH, 64, 1])
    hy_decay = din("hy_decay", [DEPTH, 4096])
    embT_l = din("embT_l", [33, NLAT])
    embT_c = din("embT_c", [33, NCTX])
    tv_l = din("tv_l", [128, NLAT // 128])
    tv_c = din("tv_c", [128, NCTX // 128])
    FW_l = din("FW_l", [64, 128, 32, 128], BF16)
    GW_l = din("GW_l", [8, 128, 64, 512], BF16)
    FW_c = din("FW_c", [4, 128, 2, 128], BF16)
    GW_c = din("GW_c", [1, 128, 4, 256], BF16)
    w_branch = din("w_branch", [DEPTH, 3, D, D])
    w_out = din("w_out", [DEPTH, D, D])
    w_up = din("w_up", [DEPTH, D, 2 * D_FF])
    w_down = din("w_down", [DEPTH, D_FF, D])
    out = nc.dram_tensor("out", [D, NLAT], F32, kind="ExternalOutput").ap()

    XT = dscr("XT", [D, T])
    PROJ = dscr("PROJ", [5120, T])
    SSMP = dscr("SSMP", [2592, T])
    GT = dscr("GT", [3072, T], BF16)
    YR = dscr("YR", [D, T], BF16)
    HTD = dscr("HTD", [D, T], BF16)
    HP = dscr("HP", [2, 34, 128, 1024], BF16)
    YSTOK = dscr("YSTOK", [T, 1024])
    HY = dscr("HY", [3072, T])
    Z2 = dscr("Z2", [D, T])
    YH = dscr("YH", [D, T], BF16)
    KF_l = dscr("KF_l", [2, 32, 128, 2, 1024])
    KF_c = dscr("KF_c", [2, 2, 128, 2, 1024])
    AFF = dscr("AFF", [D_FF, T], BF16)

    p = Prog(nc)
    psr = PsRing(p)

    ident = p.sb("ident", [128, 128], F32)
    onesb = p.sb("onesb", [128, 128], BF16)
    modT = p.sb("modT", [128, 48, 2], F32)
    A1 = p.sb("A1", [128, 8, 2], F32)
    A2 = p.sb("A2", [128, 8, 2], F32)
    p.dma("sp", [], ["ident"], ident[:], ident_d[:, :])
    masks = p.sb("masks", [128, 5, 128], F32)
    p.dma("sp", [], ["masks"], masks[:], masks_d.rearrange("m p l -> p m l"))
    LE, GT_, GE, LT, ONES = 0, 1, 2, 3, 4
    p.op("dve", [], ["onesb"], lambda e: e.memset(onesb[:], 1.0))

    def phase_mod(l):
        m = p.mark()
        cs = p.sb("cs", [128, 8, 2], F32)
        bm = p.sb("bm", [128, 48], F32)
        nm = p.sb("nm", [128, 8], F32)
        nf = p.sb("nf", [128, 8], F32)
        p.dma("sp", [], ["cs"], cs[:], cc[:, :, :])
        p.dma("sp", [], ["bm"], bm[:], b_modT[l])
        p.dma("sp", [], ["nm"], nm[:], norm_mixT[l])
        p.dma("sp", [], ["nf"], nf[:], norm_ffnT[l])
        p.act(["cs"], ["cs"], cs[:], cs[:], AF.Silu)
        wring = Ring(p, "wmod", [128, 8, 512], F32, 2)
        pst, pk = psr.next()
        wv = fm(w_mod[l])
        for cg in range(12):
            wt, wk = wring.next()
            p.dma("sp", [], [wk], wt[:], wv[:, :, cg * 512:(cg + 1) * 512])
            for j4 in range(4):
                j = cg * 4 + j4
                for kc in range(8):
                    p.mm([wk, "cs"], [pk], pst[:, j * 2:(j + 1) * 2], wt[:, kc, j4 * 128:(j4 + 1) * 128], cs[:, kc, :],
                         start=(kc == 0), stop=(kc == 7))
        p.tt([pk, "bm"], ["modT"], modT[:], pst[:, 0:96].rearrange("p (j s) -> p j s", s=2),
             bm[:].unsqueeze(2).to_broadcast([128, 48, 2]), ALU.add)
        for (Aq, key, nrm, j0) in ((A1, "A1", nm, 8), (A2, "A2", nf, 32)):
            p.ts(["modT"], [key], Aq[:], modT[:, j0:j0 + 8, :], 1.0, None, ALU.add)
            p.tt([key, "nm", "nf"], [key], Aq[:], Aq[:], nrm[:].unsqueeze(2).to_broadcast([128, 8, 2]), ALU.mult)
        p.release(m)

    def phase_norm(src, A, Akey, bj0, HT):
        m = p.mark()
        xr = Ring(p, "xn", [128, 8, 512], F32, 2)
        sqr = Ring(p, "sq", [128, 8, 512], BF16, 2)
        rr = Ring(p, "rstd", [128, 512], F32, 2)
        tr_ = Ring(p, "tmpn", [128, 512], F32, 3)
        for ti, (t0, tw) in enumerate(TT):
            s = 0 if ti == 0 else 1
            xt, xk = xr.next()
            p.dma("sp", ["XT"], [xk], xt[:, :, :tw], src[:, :, t0:t0 + tw])
            sq, sqk = sqr.next()
            p.act([xk], [sqk], sq[:, :, :tw], xt[:, :, :tw], AF.Square)
            pst, pk = psr.next()
            for kc in range(8):
                p.mm([sqk, "onesb"], [pk], pst[:, :tw], onesb[:], sq[:, kc, :tw], start=(kc == 0), stop=(kc == 7))
            rs, rk = rr.next()
            p.ts([pk], [rk], rs[:, :tw], pst[:, :tw], 1.0 / D, EPS, ALU.mult, ALU.add)
            p.act([rk], [rk], rs[:, :tw], rs[:, :tw], AF.Sqrt)
            p.op("dve", [rk], [rk], lambda e: e.reciprocal(out=rs[:, :tw], in_=rs[:, :tw]))
            for kc in range(8):
                tm, tk = tr_.next()
                p.tt([xk, rk], [tk], tm[:, :tw], xt[:, kc, :tw], rs[:, :tw], ALU.mult)
                p.act([tk, Akey, "modT"], ["HT"], HT[:, kc, t0:t0 + tw], tm[:, :tw], AF.Identity,
                      scale=A[:, kc, s:s + 1], bias=modT[:, bj0 + kc, s:s + 1])
        p.release(m)

    def ht_rhs(HT, kc, ti, ssd):
        t0, tw = TT[ti]
        if ti == 0 or not ssd:
            return HT[:, kc, t0:t0 + tw]
        i = ti - 1
        return HT[:, kc, NCTX:].rearrange("p (r c) -> p c r", c=64)[:, 8 * i:8 * i + 8, :]

    def ht_lhs(HT, kc, q):
        if q < 2:
            return HT[:, kc, q * 128:(q + 1) * 128]
        c2 = q - 2
        return HT[:, kc, NCTX:].rearrange("p (r c) -> p c r", c=64)[:, 2 * c2:2 * c2 + 2, :]

    def phase_proj(l, HT):
        m = p.mark()
        wring = Ring(p, "win", [128, 8, 512], BF16, 3)
        stg = Ring(p, "pstg", [128, T], F32, 2)
        stgb = Ring(p, "pstgb", [128, T], BF16, 2)
        wv = fm(w_in[l])
        ev = [0]

        def load_w(c0, cw):
            wt, wk = wring.next()
            p.dma("pool", [], [wk], wt[:, :, :cw], wv[:, :, c0:c0 + cw])
            return wt, wk

        def fm_group(c0, dst, dst_row0, dkey, ssd=False, gate=False, ncols=512):
            wt, wk = load_w(c0, ncols)
            for j in range((ncols + 127) // 128):
                cw_ = min(128, ncols - j * 128)
                st, stkey = (stgb if gate else stg).next()
                for ti, (t0, tw) in enumerate(TT):
                    pst, pk = psr.next()
                    for kc in range(8):
                        p.mm([wk, "HT"], [pk], pst[:cw_, :tw], wt[:, kc, j * 128:j * 128 + cw_], ht_rhs(HT, kc, ti, ssd),
                             start=(kc == 0), stop=(kc == 7))
                    if gate:
                        p.act([pk], [stkey], st[:cw_, t0:t0 + tw], pst[:cw_, :tw], AF.Sigmoid)
                    else:
                        ev[0] ^= 1
                        if ev[0]:
                            p.act([pk], [stkey], st[:cw_, t0:t0 + tw], pst[:cw_, :tw], AF.Identity)
                        else:
                            p.cp([pk], [stkey], st[:cw_, t0:t0 + tw], pst[:cw_, :tw])
                r0 = dst_row0 + j * 128
                p.dma("sp", [stkey], [dkey], dst[r0:r0 + cw_, :], st[:cw_, :])

        for g in range(10):
            fm_group(g * 512, PROJ, g * 512, "PROJ")
        for g in range(5):
            fm_group(5120 + g * 512, SSMP, g * 512, "SSMP", ssd=True)
        fm_group(7680, SSMP, 2560, "SSMP", ssd=True, ncols=32)
        for g in range(6):
            fm_group(7712 + g * 512, GT, g * 512, "GT", gate=True)
        p.release(m)

    def phase_rglru(l):
        m = p.mark()
        cw = p.sb("rcw", [128, 8, 4], F32)
        cb = p.sb("rcb", [128, 8], F32)
        ab = p.sb("rab", [128, 2, 8], F32)
        xb = p.sb("rxb", [128, 2, 8], F32)
        cA = p.sb("rcA", [128, 2, 8], F32)
        c2A = p.sb("rc2A", [128, 2, 8], F32)
        p.dma("sp", [], ["rcw"], cw[:], rnn_cwT[l])
        p.dma("sp", [], ["rcb"], cb[:], rnn_cbT[l])
        p.dma("sp", [], ["rab"], ab[:], rnn_abT[l])
        p.dma("sp", [], ["rxb"], xb[:], rnn_xbT[l])
        p.dma("sp", [], ["rcA"], cA[:], rnn_lamT[l])
        p.act(["rcA"], ["rcA"], cA[:], cA[:], AF.Exp, scale=-1.0)
        p.act(["rcA"], ["rcA"], cA[:], cA[:], AF.Ln, bias=1.0)
        p.ts(["rcA"], ["rc2A"], c2A[:], cA[:], -16.0, None, ALU.mult)
        p.ts(["rcA"], ["rcA"], cA[:], cA[:], -8.0, None, ALU.mult)
        T1 = p.sb("rT1", [128, T], F32)
        U = p.sb("rU", [128, T], F32)
        Af = p.sb("rA", [128, T], F32)
        Gf = p.sb("rG", [128, T], F32)
        HS = p.sb("rHS", [128, T], F32)
        Y = p.sb("rY", [128, T], BF16)
        gw = Ring(p, "rgw", [128, 4, 128], F32, 2)
        rr = Ring(p, "rr", [128, 512], F32, 2)
        ir = Ring(p, "ri", [128, 512], F32, 2)
        sr = Ring(p, "rs", [128, 512], F32, 2)
        segs = [(0, NCTX), (NCTX, T)]

        def rev(t, lo, hi):
            a = t[:, lo:hi]
            return bass.AP(a.tensor, a.offset + (hi - lo - 1), [list(a.ap[0]), [-1, hi - lo]])

        for hb in range(8):
            p.dma("sp", ["PROJ"], ["rT1"], T1[:], PROJ[hb * 128:(hb + 1) * 128, :])
            seg_conv(p, U, T1, lambda j: cw[:, hb, j:j + 1], cb[:, hb:hb + 1], 4, 2, segs, ["rT1", "rcw", "rcb"], ["rU"])
            g4, gk = gw.next()
            for d in range(2):
                p.dma("sp", [], [gk], g4[:, d, :], rnn_aw[l, d, hb])
                p.dma("sp", [], [gk], g4[:, 2 + d, :], rnn_xw[l, d, hb])
            for d in range(2):
                for ti, (t0, tw) in enumerate(TT):
                    pa, pak = psr.next()
                    px, pxk = psr.next()
                    p.mm([gk, "rU"], [pak], pa[:, :tw], g4[:, d, :], U[:, t0:t0 + tw])
                    p.mm([gk, "rU"], [pxk], px[:, :tw], g4[:, 2 + d, :], U[:, t0:t0 + tw])
                    r_, rk = rr.next()
                    i_, ik = ir.next()
                    s_, sk_ = sr.next()
                    p.act([pak, "rab"], [rk], r_[:, :tw], pa[:, :tw], AF.Sigmoid, bias=ab[:, d, hb:hb + 1])
                    p.act([pxk, "rxb"], [ik], i_[:, :tw], px[:, :tw], AF.Sigmoid, bias=xb[:, d, hb:hb + 1])
                    p.act([rk, "rc2A"], [sk_], s_[:, :tw], r_[:, :tw], AF.Exp, scale=c2A[:, d, hb:hb + 1])
                    p.act([rk, "rcA"], ["rA"], Af[:, t0:t0 + tw], r_[:, :tw], AF.Exp, scale=cA[:, d, hb:hb + 1])
                    p.act([sk_], [sk_], s_[:, :tw], s_[:, :tw], AF.Sqrt, scale=-1.0, bias=1.0)
                    p.tt([ik, sk_], [ik], i_[:, :tw], i_[:, :tw], s_[:, :tw], ALU.mult)
                    p.tt([ik, "rU"], ["rG"], Gf[:, t0:t0 + tw], i_[:, :tw], U[:, t0:t0 + tw], ALU.mult)
                if d == 0:
                    p.op("dve", ["rA", "rG"], ["rHS"], lambda e: e.tensor_tensor_scan(
                        out=HS[:, :], data0=Af[:, :], data1=Gf[:, :], initial=0.0, op0=ALU.mult, op1=ALU.add))
                else:
                    p.op("dve", ["rA", "rG"], ["rT1"], lambda e: e.tensor_tensor_scan(
                        out=rev(T1, 0, NCTX), data0=rev(Af, 0, NCTX), data1=rev(Gf, 0, NCTX), initial=0.0,
                        op0=ALU.mult, op1=ALU.add))
                    p.op("dve", ["rA", "rG", "rT1"], ["rT1"], lambda e: e.tensor_tensor_scan(
                        out=rev(T1, NCTX, T), data0=rev(Af, NCTX, T), data1=rev(Gf, NCTX, T), initial=T1[:, 0:1],
                        op0=ALU.mult, op1=ALU.add))
                    p.tt(["rHS", "rT1"], ["rHS"], HS[:, :], HS[:, :], T1[:, :], ALU.add)
            p.dma("sp", ["PROJ"], ["rT1"], T1[:], PROJ[1024 + hb * 128:1024 + (hb + 1) * 128, :])
            p.tt(["rT1"], ["rG"], Gf[:, :], T1[:, :], T1[:, :], ALU.mult)
            p.ts(["rG"], ["rG"], Gf[:, :], Gf[:, :], 0.044715, 1.0, ALU.mult, ALU.add)
            p.tt(["rG", "rT1"], ["rG"], Gf[:, :], Gf[:, :], T1[:, :], ALU.mult)
            p.act(["rG"], ["rG"], Gf[:, :], Gf[:, :], AF.Sigmoid, scale=1.5957691216057308)
            p.tt(["rG", "rT1"], ["rG"], Gf[:, :], Gf[:, :], T1[:, :], ALU.mult)
            p.tt(["rG", "rHS"], ["rY"], Y[:, :], Gf[:, :], HS[:, :], ALU.mult)
            p.dma("sp", ["rY"], ["YR"], YR[hb * 128:(hb + 1) * 128, :], Y[:])
        p.release(m)

    def phase_ssd(l):
        m = p.mark()
        psb = psr.tiles
        pkk = psr.keys
        cw = p.sb("scw", [128, 12, 4], F32)
        cb = p.sb("scb", [128, 12], F32)
        p.dma("sp", [], ["scw"], cw[:], ssm_cwT[l])
        p.dma("sp", [], ["scb"], cb[:], ssm_cbT[l])
        XTOK = p.sb("XTOK", [128, 34, 1024], BF16)
        BT = p.sb("BT", [128, 2, T], BF16)
        CT = p.sb("CT", [128, 2, T], BF16)
        DT_tok = p.sb("DT_tok", [128, 34, 32], F32)
        ADT_tok = p.sb("ADT_tok", [128, 34, 32], F32)
        EA = p.sb("EA", [128, 34, 32], F32)
        DTE = p.sb("DTE", [128, 34, 32], F32)
        DEC = p.sb("DEC", [128, 34, 32], F32)
        mark_a = p.mark()
        BTOK = p.sb("BTOK", [128, 34, 256], BF16)
        DTD = p.sb("DTD", [128, 34, 32], F32)
        segs = [(0, NCTX), (NCTX, T)]
        m1 = p.mark()
        T1r = Ring(p, "sT1", [128, T], F32, 2)
        XSr = Ring(p, "sXS", [128, T], F32, 1)
        for blk in range(12):
            T1, t1k = T1r.next()
            XS, xsk = XSr.next()
            p.dma("sp", ["SSMP"], [t1k], T1[:], SSMP[1024 + blk * 128:1024 + (blk + 1) * 128, :])
            seg_conv(p, XS, T1, lambda j: cw[:, blk, j:j + 1], cb[:, blk:blk + 1], 4, 2, segs, [t1k, "scw", "scb"], [xsk])
            p.act([xsk], [xsk], XS[:, :], XS[:, :], AF.Silu)
            if blk >= 8:
                g = (blk - 8) % 2
                dstT, dk = (BT, "BT") if blk < 10 else (CT, "CT")
                p.cp([xsk], [dk], dstT[:, g, :], XS[:, :])
            if blk < 10:
                for q0 in range(0, 34, 4):
                    nq = min(4, 34 - q0)
                    pst, pk = psr.next()
                    for qi in range(nq):
                        q = q0 + qi
                        p.tr([xsk, "ident"], [pk], pst[:, qi * 128:(qi + 1) * 128], XS[:, q * 128:(q + 1) * 128], ident[:])
                    if blk < 8:
                        p.cp([pk], ["XTOK"], XTOK[:, q0:q0 + nq, blk * 128:(blk + 1) * 128],
                             pst[:, :nq * 128].rearrange("p (q c) -> p q c", c=128), eng=("dve" if (q0 // 4) % 2 else "act") if False else "dve")
                    else:
                        g = blk - 8
                        p.cp([pk], ["BTOK"], BTOK[:, q0:q0 + nq, g * 128:(g + 1) * 128],
                             pst[:, :nq * 128].rearrange("p (q c) -> p q c", c=128))
        p.release(m1)
        m2 = p.mark()
        DTF = p.sb("DTF", [32, T], F32)
        ADF = p.sb("ADF", [32, T], F32)
        dtb = p.sb("dtb", [32, 1], F32)
        aneg = p.sb("aneg", [32, 1], F32)
        p.dma("sp", ["SSMP"], ["DTF"], DTF[:], SSMP[2560:2592, :])
        p.dma("sp", [], ["dtb"], dtb[:], ssm_dtbT[l])
        p.dma("sp", [], ["aneg"], aneg[:], ssm_alogT[l])
        p.act(["aneg"], ["aneg"], aneg[:], aneg[:], AF.Exp)
        p.ts(["aneg"], ["aneg"], aneg[:], aneg[:], -1.0, None, ALU.mult)
        p.act(["DTF", "dtb"], ["DTF"], DTF[:, :], DTF[:, :], AF.Exp, bias=dtb[:, 0:1])
        p.act(["DTF"], ["DTF"], DTF[:, :], DTF[:, :], AF.Ln, bias=1.0)
        p.ts(["DTF", "aneg"], ["ADF"], ADF[:, :], DTF[:, :], aneg[:, 0:1], None, ALU.mult)
        for (src, sk_, dst, dk) in ((DTF, "DTF", DT_tok, "DT_tok"), (ADF, "ADF", ADT_tok, "ADT_tok")):
            for q0 in range(0, 34, 16):
                nq = min(16, 34 - q0)
                pst, pk = psr.next()
                for qi in range(nq):
                    q = q0 + qi
                    p.tr([sk_, "ident"], [pk], pst[:, qi * 32:(qi + 1) * 32], src[:, q * 128:(q + 1) * 128], ident[:32, :32])
                p.cp([pk], [dk], dst[:, q0:q0 + nq, :], pst[:, :nq * 32].rearrange("p (q c) -> p q c", c=32))
        for d in range(2):
            for cg in range(2):
                rhs = ADT_tok[:, cg * 17:(cg + 1) * 17, d * 16:(d + 1) * 16]
                for (mk_, dst, dk) in (((LE, GE)[d], EA, "EA"), ((GT_, LT)[d], DTE, "DTE"), (ONES, DEC, "DEC")):
                    pst, pk = psr.next()
                    p.mm(["masks", "ADT_tok"], [pk], pst[:, :272], masks[:, mk_, :], rhs)
                    p.act([pk], [dk], dst[:, cg * 17:(cg + 1) * 17, d * 16:(d + 1) * 16],
                          pst[:, :272].rearrange("p (q c) -> p q c", c=16), AF.Exp)
        p.tt(["DT_tok", "DTE"], ["DTD"], DTD[:], DT_tok[:], DTE[:], ALU.mult)
        p.release(m2)
        H = p.sb("Hst", [128, 1024], F32)
        hbr = Ring(p, "Hb", [128, 1024], BF16, 3)
        xsr = Ring(p, "xsd", [128, 1024], BF16, 3)
        for d in range(2):
            order = list(range(34)) if d == 0 else [1, 0] + list(range(33, 1, -1))
            p.op("dve", [], ["Hst"], lambda e: e.memset(H[:], 0.0))
            for q in order:
                hb_, hbk = hbr.next()
                p.act(["Hst"], [hbk], hb_[:], H[:], AF.Identity)
                p.dma("sp", [hbk], ["HP"], HP[d, q], hb_[:])
                xs, xk = xsr.next()
                p.tt(["XTOK", "DTD"], [xk], xs[:].rearrange("p (h c) -> p h c", c=64),
                     XTOK[:, q, :].rearrange("p (h c) -> p h c", c=64),
                     DTD[:, q, d * 16:(d + 1) * 16].unsqueeze(2).to_broadcast([128, 16, 64]), ALU.mult)
                pss = []
                for g in range(2):
                    pst, pk = psr.next()
                    p.mm(["BTOK", xk], [pk], pst[:, :], BTOK[:, q, g * 128:(g + 1) * 128], xs[:, g * 512:(g + 1) * 512])
                    pss.append((pst, pk))
                p.tt(["Hst", "DEC"], ["Hst"], H[:].rearrange("p (h c) -> p h c", c=64), H[:].rearrange("p (h c) -> p h c", c=64),
                     DEC[:, q, d * 16:(d + 1) * 16].unsqueeze(2).to_broadcast([128, 16, 64]), ALU.mult)
                for g in range(2):
                    p.tt(["Hst", pss[g][1]], ["Hst"], H[:, g * 512:(g + 1) * 512], H[:, g * 512:(g + 1) * 512], pss[g][0][:, :], ALU.add)
        p.release(mark_a)
        dsk = p.sb("dsk", [128, 16], F32)
        nw = p.sb("snw", [128, 1024], F32)
        p.dma("sp", [], ["dsk"], dsk[:], ssm_d[l:l + 1, :].to_broadcast([128, 16]))
        p.dma("sp", [], ["snw"], nw[:], ssm_norm[l:l + 1, :].to_broadcast([128, 1024]))
        hpr = Ring(p, "hp", [128, 2, 1024], BF16, 2)
        cbmr = Ring(p, "cbm", [128, 2, 256], F32, 2)
        rsr = Ring(p, "rseg", [128, 16, 128], F32, 1)
        er = Ring(p, "eseg", [128, 16, 128], F32, 1)
        mr = Ring(p, "mseg", [128, 16, 128], BF16, 2)
        xdr = Ring(p, "xdt", [128, 1024], BF16, 2)
        accr = Ring(p, "acc", [128, 1024], F32, 2)
        tmpr = Ring(p, "stmp", [128, 1024], F32, 1)
        zfr = Ring(p, "zf", [128, 8, 128], F32, 1)
        szr = Ring(p, "sz", [128, 1024], F32, 2)
        ysr = Ring(p, "ys", [128, 1024], F32, 2)
        ssr = Ring(p, "ssq", [128, 2], F32, 2)
        for q in range(34):
            hp, hpk = hpr.next()
            for d in range(2):
                p.dma("sp", ["HP"], [hpk], hp[:, d, :], HP[d, q])
            zf, zfk = zfr.next()
            p.dma("sp", ["SSMP"], [zfk], zf[:], SSMP[0:1024, q * 128:(q + 1) * 128].rearrange("(b c) t -> c b t", c=128))
            for g in range(2):
                p.mm(["BT", "CT"], [pkk[0]], psb[0][:, g * 128:(g + 1) * 128], BT[:, g, q * 128:(q + 1) * 128], CT[:, g, q * 128:(q + 1) * 128])
            cbm, cbk = cbmr.next()
            for d in range(2):
                p.tt([pkk[0], "masks"], [cbk], cbm[:, d, :].rearrange("p (g c) -> p g c", c=128),
                     psb[0][:, 0:256].rearrange("p (g c) -> p g c", c=128),
                     masks[:, (LE, GE)[d], :].unsqueeze(1).to_broadcast([128, 2, 128]), ALU.mult)
            acc, acck = accr.next()
            for d in range(2):
                rs, rsk = rsr.next()
                p.tt(["ADT_tok", "masks"], [rsk], rs[:], ADT_tok[:, q, d * 16:(d + 1) * 16].unsqueeze(2).to_broadcast([128, 16, 128]),
                     masks[:, (LE, GE)[d], :].unsqueeze(1).to_broadcast([128, 16, 128]), ALU.mult)
                es, esk = er.next()
                for i in range(4):
                    p.mm(["masks", rsk], [pkk[1 + i]], psb[1 + i][:, :], masks[:, (GT_, LT)[d], :],
                         rs[:, 4 * i:4 * i + 4, :])
                    p.act([pkk[1 + i]], [esk], es[:, 4 * i:4 * i + 4, :], psb[1 + i][:, :].rearrange("p (h c) -> p h c", c=128), AF.Exp)
                ms, msk = mr.next()
                p.tt([esk, cbk], [msk], ms[:].rearrange("p (g e) c -> p g e c", g=2), es[:].rearrange("p (g e) c -> p g e c", g=2),
                     cbm[:, d, :].rearrange("p (g c) -> p g c", c=128).unsqueeze(2).to_broadcast([128, 2, 8, 128]), ALU.mult)
                xd, xdk = xdr.next()
                p.tt(["XTOK", "DT_tok"], [xdk], xd[:].rearrange("p (h c) -> p h c", c=64),
                     XTOK[:, q, :].rearrange("p (h c) -> p h c", c=64),
                     DT_tok[:, q, d * 16:(d + 1) * 16].unsqueeze(2).to_broadcast([128, 16, 64]), ALU.mult)
                for h in range(16):
                    bk = 5 + h // 8
                    p.mm([msk, xdk], [pkk[bk]], psb[bk][:, (h % 8) * 64:(h % 8 + 1) * 64], ms[:, h, :], xd[:, h * 64:(h + 1) * 64],
                         start=(d == 0 and h % 8 == 0), stop=(d == 1 and h % 8 == 7))
                for g in range(2):
                    bk = 7 if g == 0 else 0
                    p.mm(["CT", hpk], [pkk[bk]], psb[bk][:, :], CT[:, g, q * 128:(q + 1) * 128], hp[:, d, g * 512:(g + 1) * 512])
                    eab = EA[:, q, d * 16 + g * 8:d * 16 + (g + 1) * 8].unsqueeze(2).to_broadcast([128, 8, 64])
                    if d == 0:
                        p.tt([pkk[bk], "EA"], [acck], acc[:, g * 512:(g + 1) * 512].rearrange("p (h c) -> p h c", c=64),
                             psb[bk][:, :].rearrange("p (h c) -> p h c", c=64), eab, ALU.mult)
                    else:
                        tm, tmk = tmpr.next()
                        p.tt([pkk[bk], "EA"], [tmk], tm[:, :512].rearrange("p (h c) -> p h c", c=64),
                             psb[bk][:, :].rearrange("p (h c) -> p h c", c=64), eab, ALU.mult)
                        p.tt([acck, tmk], [acck], acc[:, g * 512:(g + 1) * 512], acc[:, g * 512:(g + 1) * 512], tm[:, :512], ALU.add)
            for g in range(2):
                p.tt([acck, pkk[5 + g]], [acck], acc[:, g * 512:(g + 1) * 512], acc[:, g * 512:(g + 1) * 512], psb[5 + g][:, :], ALU.add)
            tm, tmk = tmpr.next()
            p.tt(["XTOK", "dsk"], [tmk], tm[:].rearrange("p (h c) -> p h c", c=64), XTOK[:, q, :].rearrange("p (h c) -> p h c", c=64),
                 dsk[:].unsqueeze(2).to_broadcast([128, 16, 64]), ALU.mult)
            p.tt([acck, tmk], [acck], acc[:], acc[:], tm[:], ALU.add)
            sz, szk = szr.next()
            for g in range(2):
                for b4 in range(4):
                    p.tr([zfk, "ident"], [pkk[1 + g]], psb[1 + g][:, b4 * 128:(b4 + 1) * 128], zf[:, g * 4 + b4, :], ident[:])
                p.act([pkk[1 + g]], [szk], sz[:, g * 512:(g + 1) * 512], psb[1 + g][:, :], AF.Silu)
            p.tt([acck, szk], [acck], acc[:], acc[:], sz[:], ALU.mult)
            ss, ssk = ssr.next()
            p.act([acck], [szk, ssk], sz[:], acc[:], AF.Square, accum_out=ss[:, 0:1])
            p.ts([ssk], [ssk], ss[:, 1:2], ss[:, 0:1], 1.0 / 1024, EPS, ALU.mult, ALU.add)
            p.act([ssk], [ssk], ss[:, 1:2], ss[:, 1:2], AF.Sqrt)
            p.op("dve", [ssk], [ssk], lambda e: e.reciprocal(out=ss[:, 1:2], in_=ss[:, 1:2]))
            ys, ysk = ysr.next()
            p.stt([acck, ssk, "snw"], [ysk], ys[:], acc[:], ss[:, 1:2], nw[:], ALU.mult, ALU.mult)
            if q < 2:
                p.dma("sp", [ysk], ["YSTOK"], YSTOK[q * 128:(q + 1) * 128, :], ys[:])
            else:
                c2 = q - 2
                yv = YSTOK[NCTX:, :].rearrange("(r c) d -> c r d", c=64)
                for cl in range(2):
                    p.dma("sp", [ysk], ["YSTOK"], yv[2 * c2 + cl], ys[cl * 64:(cl + 1) * 64, :])
        p.release(m)

    class Rot:
        def __init__(self, idxs):
            self.idxs = idxs
            self.i = 0

        def next(self):
            k = self.idxs[self.i]
            self.i = (self.i + 1) % len(self.idxs)
            return psr.tiles[k], psr.keys[k]

    PI = float(np.pi)

    def hy_filter(l, n, embT_d, tv_d, FW, KF):
        m = p.mark()
        nt = n // 128
        psb, pkk = psr.tiles, psr.keys
        w1 = p.sb("hw1", [33, 64], F32)
        w2 = p.sb("hw2", [64, 64], F32)
        w3 = p.sb("hw3", [64, 4096], F32)
        fr = p.sb("hfr", [64, 1], F32)
        fb1 = p.sb("hfb1", [64, 1], F32)
        fb2 = p.sb("hfb2", [64, 1], F32)
        emb = p.sb("hemb", [33, n], F32)
        h1 = p.sb("hh1", [64, n], F32)
        h2 = p.sb("hh2", [64, n], F32)
        negpi = p.sb("hnegpi", [128, 1], F32)
        nz0 = p.sb("hnz0", [128, 1], F32)
        dec = p.sb("hdec", [128, 4096], F32)
        negt = p.sb("hnegt", [128, nt], F32)
        p.dma("sp", [], ["hw1"], w1[:], hy_w1[l])
        p.dma("sp", [], ["hw2"], w2[:], hy_w2[l])
        p.dma("sp", [], ["hw3"], w3[:], hy_w3[l])
        p.dma("sp", [], ["hfr"], fr[:], hy_freqT[l])
        p.dma("sp", [], ["hfb1"], fb1[:], hy_b1T[l])
        p.dma("sp", [], ["hfb2"], fb2[:], hy_b2T[l])
        p.dma("sp", [], ["hemb"], emb[:], embT_d[:, :])
        p.dma("sp", [], ["hdec"], dec[:], hy_decay[l:l + 1, :].to_broadcast([128, 4096]))
        p.dma("sp", [], ["hnegt"], negt[:], tv_d[:, :])
        p.ts(["hnegt"], ["hnegt"], negt[:], negt[:], -1.0, None, ALU.mult)
        p.op("dve", [], ["hnegpi"], lambda e: e.memset(negpi[:], -PI))
        p.op("dve", [], ["hnz0"], lambda e: e.memset(nz0[:], 1.0))
        p.op("dve", ["hnz0"], ["hnz0"], lambda e: e.memset(nz0[0:1, :], 0.0))
        p.ts(["hfb1", "hfr"], ["hfb1"], fb1[:], fb1[:], fr[:, 0:1], None, ALU.mult)
        p.ts(["hfb2", "hfr"], ["hfb2"], fb2[:], fb2[:], fr[:, 0:1], None, ALU.mult)
        rot = Rot([0, 1, 2, 3, 4, 5, 6])
        sinr = Ring(p, "hsin", [64, 512], F32, 4)
        for (src, sk_, wgt, wk, fb, fbk, dst, dk) in ((emb, "hemb", w1, "hw1", fb1, "hfb1", h1, "hh1"),
                                                       (h1, "hh1", w2, "hw2", fb2, "hfb2", h2, "hh2")):
            for c0 in range(0, n, 512):
                cwid = min(512, n - c0)
                pst, pk = rot.next()
                p.mm([sk_, wk], [pk], pst[:64, :cwid], wgt[:, :], src[:, c0:c0 + cwid])
                p.ts([pk, "hfr", fbk], [dk], dst[:, c0:c0 + cwid], pst[:64, :cwid], fr[:, 0:1], fb[:, 0:1], ALU.mult, ALU.add)
                sa, sak = sinr.next()
                sb_, sbk = sinr.next()
                dv = dst[:, c0:c0 + cwid]
                p.act([dk], [sak], sa[:, :cwid], dv, AF.Sin, scale=0.25)
                p.act([dk], [sbk], sb_[:, :cwid], dv, AF.Sin, scale=0.125)
                p.tt([sbk], [sbk], sb_[:, :cwid], sb_[:, :cwid], sb_[:, :cwid], ALU.mult)
                p.ts([sbk], [sbk], sb_[:, :cwid], sb_[:, :cwid], -2.0, 1.0, ALU.mult, ALU.add)
                p.tt([sak, sbk], [sbk], sb_[:, :cwid], sa[:, :cwid], sb_[:, :cwid], ALU.mult)
                p.tt([sak], [sak], sa[:, :cwid], sa[:, :cwid], sa[:, :cwid], ALU.mult)
                p.ts([sak], [sak], sa[:, :cwid], sa[:, :cwid], -2.0, 1.0, ALU.mult, ALU.add)
                p.stt([sak, sbk], [dk], dv, sb_[:, :cwid], 4.0, sa[:, :cwid], ALU.mult, ALU.mult)
        UP = p.sb("hUP", [128, nt, 512], BF16)
        UM = p.sb("hUM", [128, nt, 512], BF16)
        rinv = p.sb("hrinv", [128, 512], F32)
        er = Ring(p, "hE", [128, 512], F32, 2)
        hfr_ = Ring(p, "hhf", [128, 512], F32, 2)
        hbr_ = Ring(p, "hhb", [128, 512], F32, 2)
        abr = Ring(p, "hab", [128, 512], F32, 2)
        fring = Ring(p, "hF", [128, nt, 128], BF16, 3)
        kr = Ring(p, "hkt", [128, 512], F32, 3)
        for o in range(2):
            for cg in range(2):
                colf = o * 1024 + cg * 512
                colb = 2048 + colf
                for tc in range(nt):
                    hh = []
                    for dirn, col in ((0, colf), (1, colb)):
                        pst, pk = rot.next()
                        p.mm(["hh2", "hw3"], [pk], pst[:, :], h2[:, tc * 128:(tc + 1) * 128], w3[:, col:col + 512])
                        E, ek = er.next()
                        p.act(["hdec", "hnegt"], [ek], E[:], dec[:, col:col + 512], AF.Exp, scale=negt[:, tc:tc + 1])
                        ht_, hk = (hfr_ if dirn == 0 else hbr_).next()
                        p.tt([pk, ek], [hk], ht_[:], pst[:, :], E[:], ALU.mult)
                        if dirn == 1 and tc == 0:
                            p.ts([hk, "hnz0"], [hk], ht_[:], ht_[:], nz0[:, 0:1], None, ALU.mult)
                        ab, abk = abr.next()
                        p.act([hk], [abk], ab[:], ht_[:], AF.Abs)
                        p.mm(["masks", abk], [pkk[7]], psb[7][:, :], masks[:, ONES, :], ab[:],
                             start=(tc == 0 and dirn == 0), stop=(tc == nt - 1 and dirn == 1))
                        hh.append((ht_, hk))
                    p.tt([hh[0][1], hh[1][1]], ["hUP"], UP[:, tc, :], hh[0][0][:], hh[1][0][:], ALU.add)
                    p.tt([hh[0][1], hh[1][1]], ["hUM"], UM[:, tc, :], hh[0][0][:], hh[1][0][:], ALU.subtract)
                p.op("dve", [pkk[7]], ["hrinv"], lambda e: e.reciprocal(out=rinv[:], in_=psb[7][:, :]))
                for j in range(nt):
                    for (pq, U, uk) in ((0, UP, "hUP"), (1, UM, "hUM")):
                        Ft, fk = fring.next()
                        p.dma("sp", [], [fk], Ft[:], FW[pq * nt + j])
                        pst, pk = rot.next()
                        for tc in range(nt):
                            p.mm([fk, uk], [pk], pst[:, :], Ft[:, tc, :], U[:, tc, :], start=(tc == 0), stop=(tc == nt - 1))
                        kt, kk = kr.next()
                        p.tt([pk, "hrinv"], [kk], kt[:], pst[:, :], rinv[:], ALU.mult)
                        p.dma("sp", [kk], ["KF"], KF[o, j, :, pq, cg * 512:(cg + 1) * 512], kt[:])
        p.release(m)

    def hy_data(l, n, toff, FW, GW, KF, hbias):
        m = p.mark()
        nt = n // 128
        psb, pkk = psr.tiles, psr.keys
        tiles = [(i * 512, 512) for i in range(n // 512)] if n >= 512 else [(0, n)]
        ZT = p.sb("hZT", [128, nt, 512], BF16)
        YSs = p.sb("hYS", [128, nt, 2, 512], BF16)
        zfr = Ring(p, "hzf", [128, n], F32, 1)
        fring = Ring(p, "hF2", [128, nt, 128], BF16, 3)
        kr = Ring(p, "hkt2", [128, 2, 512], F32, 2)
        tr_ = Ring(p, "htm", [128, 512], F32, 4)
        gring = Ring(p, "hG", [128, 8, 512], BF16, 3)
        zpr = Ring(p, "hzp", [128, 512], F32, 2)
        xgr = Ring(p, "hxg", [128, 512], F32, 2)
        znr = Ring(p, "hzn", [128, 512], F32, 2)
        ynr = Ring(p, "hyn", [128, 512], BF16, 2)
        rot = Rot([4, 5, 6, 7])
        for cg in range(2):
            for o in range(2):
                src = HY if o == 0 else Z2
                skey = "HY" if o == 0 else "Z2"
                for b in range(4):
                    zf, zk = zfr.next()
                    r0 = cg * 512 + b * 128
                    p.dma("sp", [skey], [zk], zf[:], src[r0:r0 + 128, toff:toff + n])
                    for tc0 in range(0, nt, 4):
                        nq = min(4, nt - tc0)
                        pst, pk = rot.next()
                        for qi in range(nq):
                            p.tr([zk, "ident"], [pk], pst[:, qi * 128:(qi + 1) * 128], zf[:, (tc0 + qi) * 128:(tc0 + qi + 1) * 128], ident[:])
                        p.cp([pk], ["hZT"], ZT[:, tc0:tc0 + nq, b * 128:(b + 1) * 128], pst[:, :nq * 128].rearrange("p (q c) -> p q c", c=128))
                for j in range(nt):
                    Fc, fck = fring.next()
                    p.dma("sp", [], [fck], Fc[:], FW[j])
                    Fs, fsk = fring.next()
                    p.dma("sp", [], [fsk], Fs[:], FW[nt + j])
                    kt, kk = kr.next()
                    p.dma("sp", ["KF"], [kk], kt[:], KF[o, j, :, :, cg * 512:(cg + 1) * 512])
                    pA, pAk = rot.next()
                    pB, pBk = rot.next()
                    for tc in range(nt):
                        p.mm([fck, "hZT"], [pAk], pA[:, :], Fc[:, tc, :], ZT[:, tc, :], start=(tc == 0), stop=(tc == nt - 1))
                    for tc in range(nt):
                        p.mm([fsk, "hZT"], [pBk], pB[:, :], Fs[:, tc, :], ZT[:, tc, :], start=(tc == 0), stop=(tc == nt - 1))
                    t1, k1 = tr_.next()
                    t2, k2 = tr_.next()
                    p.tt([pAk, kk], [k1], t1[:], pA[:, :], kt[:, 0, :], ALU.mult)
                    p.tt([pBk, kk], [k2], t2[:], pB[:, :], kt[:, 1, :], ALU.mult)
                    p.tt([k1, k2], ["hYS"], YSs[:, j, 0, :], t1[:], t2[:], ALU.subtract)
                    t3, k3 = tr_.next()
                    t4, k4 = tr_.next()
                    p.tt([pAk, kk], [k3], t3[:], pA[:, :], kt[:, 1, :], ALU.mult)
                    p.tt([pBk, kk], [k4], t4[:], pB[:, :], kt[:, 0, :], ALU.mult)
                    p.tt([k3, k4], ["hYS"], YSs[:, j, 1, :], t3[:], t4[:], ALU.add)
                for ti, (t0, tw) in enumerate(tiles):
                    Gt, gk = None, None
                    for j2 in range(2 * nt):
                        if j2 % 8 == 0:
                            ng = min(8, 2 * nt - j2)
                            Gt, gk = gring.next()
                            p.dma("sp", [], [gk], Gt[:, :ng, :tw], GW[ti, :, j2:j2 + ng, :])
                        part, j = j2 // nt, j2 % nt
                        for b in range(4):
                            p.mm(["hYS", gk], [pkk[b]], psb[b][:, :tw], YSs[:, j, part, b * 128:(b + 1) * 128], Gt[:, j2 % 8, :tw],
                                 start=(j2 == 0), stop=(j2 == 2 * nt - 1))
                    for b in range(4):
                        cb_ = cg * 4 + b
                        zp, zpk = zpr.next()
                        p.dma("sp", [skey], [zpk], zp[:, :tw], src[cb_ * 128:(cb_ + 1) * 128, toff + t0:toff + t0 + tw])
                        xg, xgk = xgr.next()
                        xr0 = (1 + o) * 1024 + cb_ * 128
                        p.dma("sp", ["HY"], [xgk], xg[:, :tw], HY[xr0:xr0 + 128, toff + t0:toff + t0 + tw])
                        tm, tmk = tr_.next()
                        p.stt([zpk, "hbias", pkk[b]], [tmk], tm[:, :tw], zp[:, :tw], hbias[:, o, cb_:cb_ + 1], psb[b][:, :tw], ALU.mult, ALU.add)
                        if o == 0:
                            zn, znk = znr.next()
                            p.tt([tmk, xgk], [znk], zn[:, :tw], tm[:, :tw], xg[:, :tw], ALU.mult)
                            p.dma("sp", [znk], ["Z2"], Z2[cb_ * 128:(cb_ + 1) * 128, toff + t0:toff + t0 + tw], zn[:, :tw])
                        else:
                            yn, ynk = ynr.next()
                            p.tt([tmk, xgk], [ynk], yn[:, :tw], tm[:, :tw], xg[:, :tw], ALU.mult)
                            p.dma("sp", [ynk], ["YH"], YH[cb_ * 128:(cb_ + 1) * 128, toff + t0:toff + t0 + tw], yn[:, :tw])
        p.release(m)

    def phase_hyena(l):
        m = p.mark()
        cw = p.sb("hcw", [128, 24, 3], F32)
        cb = p.sb("hcb", [128, 24], F32)
        hbias = p.sb("hbias", [128, 2, 8], F32)
        p.dma("sp", [], ["hcw"], cw[:], hy_cwT[l])
        p.dma("sp", [], ["hcb"], cb[:], hy_cbT[l])
        p.dma("sp", [], ["hbias"], hbias[:], hy_biasT[l])
        segs = [(0, NCTX), (NCTX, T)]
        m1 = p.mark()
        T1r = Ring(p, "hT1", [128, T], F32, 2)
        XSr = Ring(p, "hXS", [128, T], F32, 2)
        for blk in range(24):
            T1, k1 = T1r.next()
            XS, k2 = XSr.next()
            p.dma("sp", ["PROJ"], [k1], T1[:], PROJ[2048 + blk * 128:2048 + (blk + 1) * 128, :])
            seg_conv(p, XS, T1, lambda j: cw[:, blk, j:j + 1], cb[:, blk:blk + 1], 3, 1, segs, [k1, "hcw", "hcb"], [k2])
            p.dma("sp", [k2], ["HY"], HY[blk * 128:(blk + 1) * 128, :], XS[:])
        p.release(m1)
        for (n, toff, embT_d, tv_d, FW, GW, KF) in ((NCTX, 0, embT_c, tv_c, FW_c, GW_c, KF_c),
                                                    (NLAT, NCTX, embT_l, tv_l, FW_l, GW_l, KF_l)):
            hy_filter(l, n, embT_d, tv_d, FW, KF)
            hy_data(l, n, toff, FW, GW, KF, hbias)
        p.release(m)

    def phase_merge(l, xsrc):
        m = p.mark()
        wb = p.sb("wb", [128, 3, 8, D], BF16)
        wo = p.sb("wo", [128, 8, D], BF16)
        for br in range(3):
            p.dma("pool", [], ["wb"], wb[:, br, :, :], fm(w_branch[l, br]))
        p.dma("pool", [], ["wo"], wo[:], fm(w_out[l]))
        ytr = Ring(p, "mytok", [128, 4, D], F32, 1)
        ysr = Ring(p, "mys", [128, 8, 512], BF16, 1)
        yrr = Ring(p, "myr", [128, 8, 512], BF16, 1)
        yhr = Ring(p, "myh", [128, 8, 512], BF16, 1)
        gr = Ring(p, "mg", [128, 24, 512], BF16, 1)
        mtr = Ring(p, "mmt", [128, 8, 512], BF16, 1)
        xr = Ring(p, "mx", [128, 8, 512], F32, 1)
        xnr = Ring(p, "mxn", [128, 8, 512], F32, 1)
        tr_ = Ring(p, "mtm", [128, 512], F32, 4)
        for ti, (t0, tw) in enumerate(TT):
            s_ = 0 if ti == 0 else 1
            nsub = tw // 128
            yt, ytk = ytr.next()
            p.dma("sp", ["YSTOK"], [ytk], yt[:, :nsub, :], YSTOK[t0:t0 + tw, :].rearrange("(a p) d -> p a d", p=128))
            ys, ysk = ysr.next()
            for kc in range(8):
                pst, pk = psr.next()
                for a in range(nsub):
                    p.tr([ytk, "ident"], [pk], pst[:, a * 128:(a + 1) * 128], yt[:, a, kc * 128:(kc + 1) * 128], ident[:])
                p.cp([pk], [ysk], ys[:, kc, :tw], pst[:, :tw], eng=("dve" if kc % 2 else "act")) if False else (
                    p.act([pk], [ysk], ys[:, kc, :tw], pst[:, :tw], AF.Identity) if kc % 2 == 0 else p.cp([pk], [ysk], ys[:, kc, :tw], pst[:, :tw]))
            yr, yrk = yrr.next()
            p.dma("sp", ["YR"], [yrk], yr[:, :, :tw], fm(YR)[:, :, t0:t0 + tw])
            yh, yhk = yhr.next()
            p.dma("sp", ["YH"], [yhk], yh[:, :, :tw], fm(YH)[:, :, t0:t0 + tw])
            g, gk = gr.next()
            p.dma("sp", ["GT"], [gk], g[:, :, :tw], fm(GT)[:, :, t0:t0 + tw])
            xt, xk = xr.next()
            p.dma("sp", ["XT"], [xk], xt[:, :, :tw], xsrc[:, :, t0:t0 + tw])
            mt, mtk = mtr.next()
            for cb_ in range(8):
                pbs = []
                for br, (yb, ybk) in enumerate(((yr, yrk), (yh, yhk), (ys, ysk))):
                    pst, pk = psr.next()
                    for kc in range(8):
                        p.mm(["wb", ybk], [pk], pst[:, :tw], wb[:, br, kc, cb_ * 128:(cb_ + 1) * 128], yb[:, kc, :tw],
                             start=(kc == 0), stop=(kc == 7))
                    pbs.append((pst, pk))
                t1, k1 = tr_.next()
                t2, k2 = tr_.next()
                p.tt([pbs[0][1], gk], [k1], t1[:, :tw], pbs[0][0][:, :tw], g[:, cb_, :tw], ALU.mult)
                p.tt([pbs[1][1], gk], [k2], t2[:, :tw], pbs[1][0][:, :tw], g[:, 8 + cb_, :tw], ALU.mult)
                p.tt([k1, k2], [k1], t1[:, :tw], t1[:, :tw], t2[:, :tw], ALU.add)
                t3, k3 = tr_.next()
                p.tt([pbs[2][1], gk], [k3], t3[:, :tw], pbs[2][0][:, :tw], g[:, 16 + cb_, :tw], ALU.mult)
                p.tt([k1, k3], [mtk], mt[:, cb_, :tw], t1[:, :tw], t3[:, :tw], ALU.add)
            xn, xnk = xnr.next()
            for co in range(8):
                pst, pk = psr.next()
                for cb_ in range(8):
                    p.mm(["wo", mtk], [pk], pst[:, :tw], wo[:, cb_, co * 128:(co + 1) * 128], mt[:, cb_, :tw],
                         start=(cb_ == 0), stop=(cb_ == 7))
                p.stt([pk, "modT", xk], [xnk], xn[:, co, :tw], pst[:, :tw], modT[:, 16 + co, s_:s_ + 1], xt[:, co, :tw], ALU.mult, ALU.add)
            p.dma("sp", [xnk], ["XT"], fm(XT)[:, :, t0:t0 + tw], xn[:, :, :tw])
        p.release(m)

    def phase_ffn(l, HT):
        m = p.mark()
        wv = fm(w_up[l])
        wr = Ring(p, "fwu", [128, 2, 8, 128], BF16, 3)
        str_ = Ring(p, "fst", [128, T], BF16, 2)
        sgr = Ring(p, "fsg", [128, 512], F32, 3)
        for j in range(22):
            wt, wk = wr.next()
            p.dma("pool", [], [wk], wt[:, 0, :, :], wv[:, :, j * 128:(j + 1) * 128])
            p.dma("pool", [], [wk], wt[:, 1, :, :], wv[:, :, D_FF + j * 128:D_FF + (j + 1) * 128])
            st, stk = str_.next()
            for ti, (t0, tw) in enumerate(TT):
                pg, pgk = psr.next()
                pu, puk = psr.next()
                for kc in range(8):
                    p.mm([wk, "HT"], [pgk], pg[:, :tw], wt[:, 0, kc, :], HT[:, kc, t0:t0 + tw], start=(kc == 0), stop=(kc == 7))
                for kc in range(8):
                    p.mm([wk, "HT"], [puk], pu[:, :tw], wt[:, 1, kc, :], HT[:, kc, t0:t0 + tw], start=(kc == 0), stop=(kc == 7))
                sg, sgk = sgr.next()
                p.act([pgk], [sgk], sg[:, :tw], pg[:, :tw], AF.Silu)
                p.tt([sgk, puk], [stk], st[:, t0:t0 + tw], sg[:, :tw], pu[:, :tw], ALU.mult)
            p.dma("sp", [stk], ["AFF"], AFF[j * 128:(j + 1) * 128, :], st[:])
        p.release(m)

    def phase_ffn2(l):
        m = p.mark()
        wd = p.sb("fwd", [128, 22, D], BF16)
        p.dma("pool", [], ["fwd"], wd[:], w_down[l].rearrange("(j p) c -> p j c", p=128))
        ar = Ring(p, "fa", [128, 22, 512], BF16, 2)
        xr = Ring(p, "fx", [128, 8, 512], F32, 2)
        xnr = Ring(p, "fxn", [128, 8, 512], F32, 2)
        av = AFF.rearrange("(j p) t -> p j t", p=128)
        for ti, (t0, tw) in enumerate(TT):
            s_ = 0 if ti == 0 else 1
            at, ak = ar.next()
            p.dma("sp", ["AFF"], [ak], at[:, :, :tw], av[:, :, t0:t0 + tw])
            xt, xk = xr.next()
            p.dma("sp", ["XT"], [xk], xt[:, :, :tw], fm(XT)[:, :, t0:t0 + tw])
            xn, xnk = xnr.next()
            for co in range(8):
                pst, pk = psr.next()
                for j in range(22):
                    p.mm(["fwd", ak], [pk], pst[:, :tw], wd[:, j, co * 128:(co + 1) * 128], at[:, j, :tw], start=(j == 0), stop=(j == 21))
                p.stt([pk, "modT", xk], [xnk], xn[:, co, :tw], pst[:, :tw], modT[:, 40 + co, s_:s_ + 1], xt[:, co, :tw], ALU.mult, ALU.add)
            p.dma("sp", [xnk], ["XT"], fm(XT)[:, :, t0:t0 + tw], xn[:, :, :tw])
        p.release(m)

    def phase_final():
        m = p.mark()
        fn = p.sb("fnw", [128, 8], F32)
        p.dma("sp", [], ["fnw"], fn[:], final_normT[:, :])
        xr = Ring(p, "ox", [128, 8, 512], F32, 2)
        sqr = Ring(p, "osq", [128, 8, 512], BF16, 2)
        rr = Ring(p, "orstd", [128, 512], F32, 2)
        outr = Ring(p, "oo", [128, 8, 512], F32, 2)
        src = fm(XT)
        for ti, (t0, tw) in enumerate(TT):
            if ti == 0:
                continue
            xt, xk = xr.next()
            p.dma("sp", ["XT"], [xk], xt[:], src[:, :, t0:t0 + tw])
            sq, sqk = sqr.next()
            p.act([xk], [sqk], sq[:], xt[:], AF.Square)
            pst, pk = psr.next()
            for kc in range(8):
                p.mm([sqk, "onesb"], [pk], pst[:, :], onesb[:], sq[:, kc, :], start=(kc == 0), stop=(kc == 7))
            rs, rk = rr.next()
            p.ts([pk], [rk], rs[:], pst[:, :], 1.0 / D, EPS, ALU.mult, ALU.add)
            p.act([rk], [rk], rs[:], rs[:], AF.Sqrt)
            p.op("dve", [rk], [rk], lambda e: e.reciprocal(out=rs[:], in_=rs[:]))
            ot, ok_ = outr.next()
            for kc in range(8):
                p.stt([xk, "fnw", rk], [ok_], ot[:, kc, :], xt[:, kc, :], fn[:, kc:kc + 1], rs[:], ALU.mult, ALU.mult)
            p.dma("sp", [ok_], ["OUT"], fm(out)[:, :, t0 - NCTX:t0 - NCTX + tw], ot[:])
        p.wait_all("sp", ["OUT"])
        p.release(m)

    for l in range(nlayers):
        phase_mod(l)
        mk = p.mark()
        HT = p.sb("HT", [128, 8, T], BF16)
        phase_norm(fm(xin if l == 0 else XT), A1, "A1", 0, HT)
        if "HTD" in dbg:
            p.dma("sp", ["HT"], ["HTD"], fm(HTD), HT[:])
        if stop_after == "norm":
            p.release(mk)
            break
        phase_proj(l, HT)
        p.release(mk)
        if stop_after == "proj":
            break
        if stop_after not in ("ssd", "hyena"):
            phase_rglru(l)
        if stop_after == "rglru":
            break
        if stop_after != "hyena":
            phase_ssd(l)
        if stop_after == "ssd":
            break
        phase_hyena(l)
        if stop_after == "hyena":
            break
        phase_merge(l, fm(xin if l == 0 else XT))
        if stop_after == "merge":
            break
        mk = p.mark()
        HT = p.sb("HT", [128, 8, T], BF16)
        phase_norm(fm(XT), A2, "A2", 24, HT)
        phase_ffn(l, HT)
        p.release(mk)
        phase_ffn2(l)
    if stop_after is None:
        phase_final()
    p.barrier()
    p.close()
    print("instructions:", p.ninst)
    return nc


def fmT(v, nchunk):
    return np.ascontiguousarray(np.swapaxes(v.reshape(v.shape[:-1] + (nchunk, 128)), -1, -2))

def hyena_consts(n):
    import ml_dtypes
    f = np.float32
    t = np.linspace(0.0, 1.0, n, dtype=f)
    bands = np.linspace(1e-4, 15.0, 16, dtype=f)
    ang = (f(2.0 * np.pi / n) * np.arange(n, dtype=f)[:, None]) * bands[None]
    emb = np.concatenate([t[:, None], np.cos(ang), np.sin(ang)], axis=-1).astype(f)
    embT = np.ascontiguousarray(emb.T)
    nt = n // 128
    tv = np.ascontiguousarray(t.reshape(nt, 128).T)
    N = 2 * n
    tt = np.arange(n, dtype=np.int64)
    ff = np.arange(n, dtype=np.int64)
    ph = ((2 * ff[None, :] + 1) * tt[:, None]) % (2 * N)
    angm = np.pi * ph.astype(np.float64) / N
    C = np.cos(angm); S = np.sin(angm)
    def tile_f(M):
        return M.reshape(nt, 128, nt, 128).transpose(2, 1, 0, 3)
    FW = np.concatenate([tile_f(C), tile_f(S)], axis=0).astype(ml_dtypes.bfloat16)
    tw = 512 if n >= 512 else n
    def tile_g(M):
        return (M.T * (2.0 / N)).reshape(nt, 128, n // tw, tw).transpose(2, 1, 0, 3)
    GW = np.concatenate([tile_g(C), tile_g(S)], axis=2).astype(ml_dtypes.bfloat16)
    return embT, tv, np.ascontiguousarray(FW), np.ascontiguousarray(GW)

def prep_shared(inp):
    f = np.float32
    sh = {}
    sh["w_mod"] = inp["w_mod"]
    sh["b_modT"] = fmT(inp["b_mod"], 48)
    sh["norm_mixT"] = fmT(inp["norm_mix"], 8)
    sh["norm_ffnT"] = fmT(inp["norm_ffn"], 8)
    sh["final_normT"] = fmT(inp["final_norm"], 8)
    sh["w_in"] = inp["w_in"]
    sh["rnn_cwT"] = np.ascontiguousarray(inp["rnn_conv_w"].reshape(4, 4, 8, 128).transpose(0, 3, 2, 1))
    sh["rnn_cbT"] = fmT(inp["rnn_conv_b"], 8)
    sh["rnn_aw"] = inp["rnn_gate_a_w"]
    sh["rnn_xw"] = inp["rnn_gate_x_w"]
    sh["rnn_abT"] = np.ascontiguousarray(fmT(inp["rnn_gate_a_b"], 8).transpose(0, 2, 1, 3))
    sh["rnn_xbT"] = np.ascontiguousarray(fmT(inp["rnn_gate_x_b"], 8).transpose(0, 2, 1, 3))
    sh["rnn_lamT"] = np.ascontiguousarray(fmT(inp["rnn_lambda"], 8).transpose(0, 2, 1, 3))
    sh["ident"] = np.eye(128, dtype=f)
    j = np.arange(128)[:, None]; ll = np.arange(128)[None, :]
    sh["masks"] = np.stack([(j <= ll), (j > ll), (j >= ll), (j < ll), np.ones((128, 128), bool)]).astype(f)
    sh["ssm_cwT"] = np.ascontiguousarray(inp["ssm_conv_w"].reshape(4, 4, 12, 128).transpose(0, 3, 2, 1))
    sh["ssm_cbT"] = fmT(inp["ssm_conv_b"], 12)
    sh["ssm_alogT"] = np.ascontiguousarray(inp["ssm_a_log"].reshape(4, 32, 1))
    sh["ssm_dtbT"] = np.ascontiguousarray(inp["ssm_dt_bias"].reshape(4, 32, 1))
    sh["ssm_d"] = inp["ssm_d"]
    sh["ssm_norm"] = inp["ssm_norm"]
    sh["hy_cwT"] = np.ascontiguousarray(inp["hy_short_w"].reshape(4, 3, 24, 128).transpose(0, 3, 2, 1))
    sh["hy_cbT"] = fmT(inp["hy_short_b"], 24)
    sh["hy_biasT"] = np.ascontiguousarray(fmT(inp["hy_bias"], 8).transpose(0, 2, 1, 3))
    sh["hy_w1"] = inp["hy_w1"]; sh["hy_w2"] = inp["hy_w2"]; sh["hy_w3"] = inp["hy_w3"]
    sh["hy_b1T"] = np.ascontiguousarray(inp["hy_b1"].reshape(4, 64, 1))
    sh["hy_b2T"] = np.ascontiguousarray(inp["hy_b2"].reshape(4, 64, 1))
    sh["hy_freqT"] = np.ascontiguousarray(inp["hy_freq"].reshape(4, 64, 1))
    sh["hy_decay"] = inp["hy_decay"]
    for tag, n in (("l", 4096), ("c", 256)):
        emb, tv, FW, GW = hyena_consts(n)
        sh["embT_" + tag] = emb; sh["tv_" + tag] = tv; sh["FW_" + tag] = FW; sh["GW_" + tag] = GW
    sh["w_branch"] = inp["w_branch"]; sh["w_out"] = inp["w_out"]; sh["w_up"] = inp["w_up"]; sh["w_down"] = inp["w_down"]
    return sh

def prep_core(inp, b):
    xin = np.ascontiguousarray(np.concatenate([inp["ctx"][b].T, inp["x"][b].T], axis=1))
    cc = np.stack([inp["c_ctx"], inp["c"][b]], axis=-1)
    cc = np.ascontiguousarray(cc.reshape(8, 128, 2).transpose(1, 0, 2))
    return {"xin": xin, "cc": cc}


def kernel(**inputs):
    inp = {k: np.asarray(v) for k, v in inputs.items()}
    sh = prep_shared(inp)
    nc = build()
    in_maps = []
    for b in range(8):
        im = dict(sh)
        im.update(prep_core(inp, b))
        in_maps.append(im)
    res = run_bass_kernel_spmd(nc, in_maps, core_ids=list(range(8)))
    out = np.stack([np.ascontiguousarray(np.asarray(r["out"]).T) for r in res.results], axis=0)
    return out.astype(np.float32)
```

```python
import numpy as np
import concourse.bass as bass
import concourse.mybir as mybir
from concourse.bass_utils import run_bass_kernel_spmd

F32 = mybir.dt.float32
BF16 = mybir.dt.bfloat16
ALU = mybir.AluOpType
AF = mybir.ActivationFunctionType
AX = mybir.AxisListType

D = 1024
NCTX = 256
NLAT = 4096
T = NCTX + NLAT
DEPTH = 4
D_IN = 10784
D_FF = 2816
EPS = 1e-6
HY_DENSE = False
HY4_STOP = None
TT = [(0, 256)] + [(256 + 512 * i, 512) for i in range(8)]


class Prog:
    NDMA = 24

    def __init__(self, nc):
        self.nc = nc
        self.engs = {"pe": nc.tensor, "dve": nc.vector, "act": nc.scalar, "pool": nc.gpsimd, "sp": nc.sync}
        self._ctx = []
        self.sem = {}
        self.cnt = {}
        for e in ("pe", "dve", "act", "pool"):
            self.sem[e] = self._enter(nc.semaphore("s_" + e))
            self.cnt[e] = 0
        self.dsem = [self._enter(nc.semaphore("d%d" % i)) for i in range(self.NDMA)]
        self.dcnt = [0] * self.NDMA
        self.dnext = 0
        self.semobj = {}
        for e in ("pe", "dve", "act", "pool"):
            self.semobj[("c", e)] = self.sem[e]
        for i in range(self.NDMA):
            self.semobj[("d", i)] = self.dsem[i]
        self.waited = {e: {} for e in self.engs}
        self.lastw = {}
        self.reads = {}
        self.ninst = 0
        self.uid = 0

    def _enter(self, cm):
        v = cm.__enter__()
        self._ctx.append(cm)
        return v

    def sb(self, name, shape, dt):
        self.uid += 1
        return self._enter(self.nc.sbuf_tensor("%s_%d" % (name, self.uid), list(shape), dt))

    def ps(self, name, shape, dt=F32):
        return self._enter(self.nc.psum_tensor(name, list(shape), dt))

    def mark(self):
        return len(self._ctx)

    def release(self, mark):
        self.barrier()
        while len(self._ctx) > mark:
            cm = self._ctx.pop()
            cm.__exit__(None, None, None)

    def close(self):
        while self._ctx:
            cm = self._ctx.pop()
            cm.__exit__(None, None, None)

    def barrier(self):
        targets = []
        for e in ("pe", "dve", "act", "pool"):
            if self.cnt[e]:
                targets.append((("c", e), self.cnt[e]))
        for i in range(self.NDMA):
            if self.dcnt[i]:
                targets.append((("d", i), self.dcnt[i] * 16))
        for q in ("pe", "dve", "act", "pool", "sp"):
            e = self.engs[q]
            for sk, val in targets:
                if sk == ("c", q):
                    continue
                if self.waited[q].get(sk, 0) < val:
                    e.wait_ge(self.semobj[sk], val)
                    self.waited[q][sk] = val
        self.lastw = {}
        self.reads = {}

    def _deps(self, eng, R, W):
        deps = []
        for r in R:
            lw = self.lastw.get(r)
            if lw is not None:
                deps.append((lw, "raw"))
        for w in W:
            lw = self.lastw.get(w)
            if lw is not None:
                deps.append((lw, "waw"))
            for rd in self.reads.get(w, ()):
                deps.append((rd, "war"))
        own = ("c", eng)
        e = self.engs[eng]
        wt = self.waited[eng]
        need = {}
        for (sk, val), kind in deps:
            if sk == own:
                if eng == "pe":
                    continue
                if kind != "raw":
                    continue
            if wt.get(sk, 0) >= val:
                continue
            if need.get(sk, 0) < val:
                need[sk] = val
        for sk, val in need.items():
            e.wait_ge(self.semobj[sk], val)
            wt[sk] = val

    def _commit(self, tick, R, W):
        for w in W:
            self.lastw[w] = tick
            self.reads[w] = []
        for r in R:
            if r in W:
                continue
            lst = self.reads.setdefault(r, [])
            lst.append(tick)
            if len(lst) > 48:
                best = {}
                for sk, v in lst:
                    if best.get(sk, 0) < v:
                        best[sk] = v
                self.reads[r] = list(best.items())

    def op(self, eng, R, W, fn):
        self._deps(eng, R, W)
        ins = fn(self.engs[eng])
        self.cnt[eng] += 1
        ins.then_inc(self.sem[eng], 1)
        tick = (("c", eng), self.cnt[eng])
        self._commit(tick, R, W)
        self.ninst += 1
        return tick

    def dma(self, q, R, W, out, in_, **kw):
        i = self.dnext
        self.dnext = (self.dnext + 1) % self.NDMA
        sk = ("d", i)
        e = self.engs[q]
        prev = self.dcnt[i] * 16
        if prev and self.waited[q].get(sk, 0) < prev:
            e.wait_ge(self.dsem[i], prev)
            self.waited[q][sk] = prev
        self._deps(q, R, W)
        ins = e.dma_start(out=out, in_=in_, **kw)
        self.dcnt[i] += 1
        ins.then_inc(self.dsem[i], 16)
        tick = (sk, self.dcnt[i] * 16)
        self._commit(tick, R, W)
        self.ninst += 1
        return tick

    def wait_all(self, eng, keys):
        e = self.engs[eng]
        for k in keys:
            lw = self.lastw.get(k)
            if lw is None:
                continue
            sk, val = lw
            if self.waited[eng].get(sk, 0) < val:
                e.wait_ge(self.semobj[sk], val)
                self.waited[eng][sk] = val

    def mm(self, R, W, out, lhsT, rhs, start=True, stop=True):
        return self.op("pe", R, W, lambda e: e.matmul(out, lhsT, rhs, start=start, stop=stop))

    def tr(self, R, W, out, in_, ident):
        return self.op("pe", R, W, lambda e: e.transpose(out, in_, ident))

    def act(self, R, W, out, in_, func, **kw):
        return self.op("act", R, W, lambda e: e.activation(out=out, in_=in_, func=func, **kw))

    def tt(self, R, W, out, in0, in1, op, eng="dve"):
        return self.op(eng, R, W, lambda e: e.tensor_tensor(out=out, in0=in0, in1=in1, op=op))

    def ts(self, R, W, out, in0, s1, s2, op0, op1=None, eng="dve"):
        if op1 is None:
            return self.op(eng, R, W, lambda e: e.tensor_scalar(out=out, in0=in0, scalar1=s1, scalar2=None, op0=op0))
        return self.op(eng, R, W, lambda e: e.tensor_scalar(out=out, in0=in0, scalar1=s1, scalar2=s2, op0=op0, op1=op1))

    def stt(self, R, W, out, in0, scalar, in1, op0, op1, eng="dve"):
        return self.op(eng, R, W, lambda e: e.scalar_tensor_tensor(out=out, in0=in0, scalar=scalar, in1=in1, op0=op0, op1=op1))

    def cp(self, R, W, out, in_, eng="dve"):
        return self.op(eng, R, W, lambda e: e.tensor_copy(out=out, in_=in_))


class Ring:
    def __init__(self, p, name, shape, dt, n):
        self.tiles = [p.sb("%s%d" % (name, i), shape, dt) for i in range(n)]
        self.keys = ["%s#%d_%d" % (name, p.uid, i) for i in range(n)]
        self.i = 0

    def next(self):
        t, k = self.tiles[self.i], self.keys[self.i]
        self.i = (self.i + 1) % len(self.tiles)
        return t, k


class PsRing:
    def __init__(self, p, n=8):
        self.tiles = [p.ps("psb%d" % i, [128, 512]) for i in range(n)]
        self.keys = ["psb%d" % i for i in range(n)]
        self.i = 0

    def next(self):
        t, k = self.tiles[self.i], self.keys[self.i]
        self.i = (self.i + 1) % len(self.tiles)
        return t, k


def fm(ap2d):
    return ap2d.rearrange("(kc p) t -> p kc t", p=128)


def seg_conv(p, out, in_, wv, bv, ntap, left, segs, Rk, Wk):
    p.ts(Rk, Wk, out[:, :], in_[:, :], wv(left), bv, ALU.mult, ALU.add)
    for j in range(ntap):
        d = j - left
        if d == 0:
            continue
        for (s0, s1) in segs:
            lo = max(s0, s0 - d)
            hi = min(s1, s1 - d)
            p.stt(Rk + Wk, Wk, out[:, lo:hi], in_[:, lo + d:hi + d], wv(j), out[:, lo:hi], ALU.mult, ALU.add)


def build(nlayers=DEPTH, stop_after=None, dbg=()):
    nc = bass.Bass("TRN2", target_bir_lowering=False)

    def din(name, shape, dt=F32):
        return nc.dram_tensor(name, list(shape), dt, kind="ExternalInput").ap()

    def dscr(name, shape, dt=F32):
        kind = "ExternalOutput" if name in dbg else "Internal"
        return nc.dram_tensor(name, list(shape), dt, kind=kind).ap()

    xin = din("xin", [D, T])
    cc = din("cc", [128, 8, 2])
    w_mod = din("w_mod", [DEPTH, D, 6 * D])
    b_modT = din("b_modT", [DEPTH, 128, 48])
    norm_mixT = din("norm_mixT", [DEPTH, 128, 8])
    norm_ffnT = din("norm_ffnT", [DEPTH, 128, 8])
    final_normT = din("final_normT", [128, 8])
    w_in = din("w_in", [DEPTH, D, D_IN])
    rnn_cwT = din("rnn_cwT", [DEPTH, 128, 8, 4])
    rnn_cbT = din("rnn_cbT", [DEPTH, 128, 8])
    rnn_aw = din("rnn_aw", [DEPTH, 2, 8, 128, 128])
    rnn_xw = din("rnn_xw", [DEPTH, 2, 8, 128, 128])
    rnn_abT = din("rnn_abT", [DEPTH, 128, 2, 8])
    rnn_xbT = din("rnn_xbT", [DEPTH, 128, 2, 8])
    rnn_lamT = din("rnn_lamT", [DEPTH, 128, 2, 8])
    ident_d = din("ident", [128, 128])
    masks_d = din("masks", [5, 128, 128])
    ssm_cwT = din("ssm_cwT", [DEPTH, 128, 12, 4])
    ssm_cbT = din("ssm_cbT", [DEPTH, 128, 12])
    ssm_alogT = din("ssm_alogT", [DEPTH, 32, 1])
    ssm_dtbT = din("ssm_dtbT", [DEPTH, 32, 1])
    ssm_d = din("ssm_d", [DEPTH, 16])
    ssm_norm = din("ssm_norm", [DEPTH, 1024])
    hy_cwT = din("hy_cwT", [DEPTH, 128, 24, 3])
    hy_cbT = din("hy_cbT", [DEPTH, 128, 24])
    hy_biasT = din("hy_biasT", [DEPTH, 128, 2, 8])
    hy_w1 = din("hy_w1", [DEPTH, 33, 64])
    hy_w2 = din("hy_w2", [DEPTH, 64, 64])
    hy_w3 = din("hy_w3", [DEPTH, 64, 4096])
    hy_b1T = din("hy_b1T", [DEPTH, 64, 1])
    hy_b2T = din("hy_b2T", [DEPTH, 64, 1])
    hy_freqT = din("hy_freqT", [DEPTH, 64, 1])
    hy_decay = din("hy_decay", [DEPTH, 4096])
    embT_l = din("embT_l", [33, NLAT])
    embT_c = din("embT_c", [33, NCTX])
    tv_l = din("tv_l", [128, NLAT // 128]) if HY_DENSE else None
    tv_c = din("tv_c", [128, NCTX // 128])
    FW_l = din("FW_l", [64, 128, 32, 128], BF16) if HY_DENSE else None
    GW_l = din("GW_l", [8, 128, 64, 512], BF16) if HY_DENSE else None
    FW_c = din("FW_c", [4, 128, 2, 128], BF16)
    GW_c = din("GW_c", [1, 128, 4, 256], BF16)
    S1_d = din("S1", [32, 128], BF16)
    S2_d = din("S2", [128, 32], BF16)
    WF_d = din("WF", [128, 64, 3, 64], BF16)
    WI_d = din("WI", [64, 64, 3, 128], BF16)
    tvec = din("tvec", [1, NLAT])
    hy_ndecT = din("hy_ndecT", [DEPTH, 128, 32])
    w_branch = din("w_branch", [DEPTH, 3, D, D])
    w_out = din("w_out", [DEPTH, D, D])
    w_up = din("w_up", [DEPTH, D, 2 * D_FF])
    w_down = din("w_down", [DEPTH, D_FF, D])
    out = nc.dram_tensor("out", [D, NLAT], F32, kind="ExternalOutput").ap()

    XT = dscr("XT", [D, T])
    PROJ = dscr("PROJ", [5120, T])
    SSMP = dscr("SSMP", [2592, T])
    GT = dscr("GT", [3072, T], BF16)
    YR = dscr("YR", [D, T], BF16)
    HTD = dscr("HTD", [D, T], BF16)
    HP = dscr("HP", [2, 34, 128, 1024], BF16)
    YSTOK = dscr("YSTOK", [T, 1024])
    HY = dscr("HY", [3072, T])
    Z2 = dscr("Z2", [D, T])
    YH = dscr("YH", [D, T], BF16)
    KF_l = dscr("KF_l", [2, 32, 128, 2, 1024]) if HY_DENSE else None
    KF_c = dscr("KF_c", [2, 2, 128, 2, 1024])
    AFF = dscr("AFF", [D_FF, T], BF16)
    HFB = dscr("HFB", [2, 2048, NLAT], BF16)
    HYB = dscr("HYB", [D, T], BF16)
    Z2B = dscr("Z2B", [D, T], BF16)
    KF2 = dscr("KF2", [2, 8, 64, 2, 64, 128])
    CONV = dscr("CONV", [D, NLAT])

    p = Prog(nc)
    psr = PsRing(p)

    ident = p.sb("ident", [128, 128], F32)
    onesb = p.sb("onesb", [128, 128], BF16)
    modT = p.sb("modT", [128, 48, 2], F32)
    A1 = p.sb("A1", [128, 8, 2], F32)
    A2 = p.sb("A2", [128, 8, 2], F32)
    p.dma("sp", [], ["ident"], ident[:], ident_d[:, :])
    masks = p.sb("masks", [128, 5, 128], F32)
    p.dma("sp", [], ["masks"], masks[:], masks_d.rearrange("m p l -> p m l"))
    LE, GT_, GE, LT, ONES = 0, 1, 2, 3, 4
    p.op("dve", [], ["onesb"], lambda e: e.memset(onesb[:], 1.0))

    def phase_mod(l):
        m = p.mark()
        cs = p.sb("cs", [128, 8, 2], F32)
        bm = p.sb("bm", [128, 48], F32)
        nm = p.sb("nm", [128, 8], F32)
        nf = p.sb("nf", [128, 8], F32)
        p.dma("sp", [], ["cs"], cs[:], cc[:, :, :])
        p.dma("sp", [], ["bm"], bm[:], b_modT[l])
        p.dma("sp", [], ["nm"], nm[:], norm_mixT[l])
        p.dma("sp", [], ["nf"], nf[:], norm_ffnT[l])
        p.act(["cs"], ["cs"], cs[:], cs[:], AF.Silu)
        wring = Ring(p, "wmod", [128, 8, 512], F32, 2)
        pst, pk = psr.next()
        wv = fm(w_mod[l])
        for cg in range(12):
            wt, wk = wring.next()
            p.dma("sp", [], [wk], wt[:], wv[:, :, cg * 512:(cg + 1) * 512])
            for j4 in range(4):
                j = cg * 4 + j4
                for kc in range(8):
                    p.mm([wk, "cs"], [pk], pst[:, j * 2:(j + 1) * 2], wt[:, kc, j4 * 128:(j4 + 1) * 128], cs[:, kc, :],
                         start=(kc == 0), stop=(kc == 7))
        p.tt([pk, "bm"], ["modT"], modT[:], pst[:, 0:96].rearrange("p (j s) -> p j s", s=2),
             bm[:].unsqueeze(2).to_broadcast([128, 48, 2]), ALU.add)
        for (Aq, key, nrm, j0) in ((A1, "A1", nm, 8), (A2, "A2", nf, 32)):
            p.ts(["modT"], [key], Aq[:], modT[:, j0:j0 + 8, :], 1.0, None, ALU.add)
            p.tt([key, "nm", "nf"], [key], Aq[:], Aq[:], nrm[:].unsqueeze(2).to_broadcast([128, 8, 2]), ALU.mult)
        p.release(m)

    def phase_norm(src, A, Akey, bj0, HT):
        m = p.mark()
        xr = Ring(p, "xn", [128, 8, 512], F32, 2)
        sqr = Ring(p, "sq", [128, 8, 512], BF16, 2)
        rr = Ring(p, "rstd", [128, 512], F32, 2)
        tr_ = Ring(p, "tmpn", [128, 512], F32, 3)
        for ti, (t0, tw) in enumerate(TT):
            s = 0 if ti == 0 else 1
            xt, xk = xr.next()
            p.dma("sp", ["XT"], [xk], xt[:, :, :tw], src[:, :, t0:t0 + tw])
            sq, sqk = sqr.next()
            p.act([xk], [sqk], sq[:, :, :tw], xt[:, :, :tw], AF.Square)
            pst, pk = psr.next()
            for kc in range(8):
                p.mm([sqk, "onesb"], [pk], pst[:, :tw], onesb[:], sq[:, kc, :tw], start=(kc == 0), stop=(kc == 7))
            rs, rk = rr.next()
            p.ts([pk], [rk], rs[:, :tw], pst[:, :tw], 1.0 / D, EPS, ALU.mult, ALU.add)
            p.act([rk], [rk], rs[:, :tw], rs[:, :tw], AF.Sqrt)
            p.op("dve", [rk], [rk], lambda e: e.reciprocal(out=rs[:, :tw], in_=rs[:, :tw]))
            for kc in range(8):
                tm, tk = tr_.next()
                p.tt([xk, rk], [tk], tm[:, :tw], xt[:, kc, :tw], rs[:, :tw], ALU.mult)
                p.act([tk, Akey, "modT"], ["HT"], HT[:, kc, t0:t0 + tw], tm[:, :tw], AF.Identity,
                      scale=A[:, kc, s:s + 1], bias=modT[:, bj0 + kc, s:s + 1])
        p.release(m)

    def ht_rhs(HT, kc, ti, ssd):
        t0, tw = TT[ti]
        if ti == 0 or not ssd:
            return HT[:, kc, t0:t0 + tw]
        i = ti - 1
        return HT[:, kc, NCTX:].rearrange("p (r c) -> p c r", c=64)[:, 8 * i:8 * i + 8, :]

    def ht_lhs(HT, kc, q):
        if q < 2:
            return HT[:, kc, q * 128:(q + 1) * 128]
        c2 = q - 2
        return HT[:, kc, NCTX:].rearrange("p (r c) -> p c r", c=64)[:, 2 * c2:2 * c2 + 2, :]

    def phase_proj(l, HT):
        m = p.mark()
        wring = Ring(p, "win", [128, 8, 512], BF16, 3)
        stg = Ring(p, "pstg", [128, T], F32, 2)
        stgb = Ring(p, "pstgb", [128, T], BF16, 2)
        wv = fm(w_in[l])
        ev = [0]

        def load_w(c0, cw):
            wt, wk = wring.next()
            p.dma("pool", [], [wk], wt[:, :, :cw], wv[:, :, c0:c0 + cw])
            return wt, wk

        def fm_group(c0, dst, dst_row0, dkey, ssd=False, gate=False, ncols=512):
            wt, wk = load_w(c0, ncols)
            for j in range((ncols + 127) // 128):
                cw_ = min(128, ncols - j * 128)
                st, stkey = (stgb if gate else stg).next()
                for ti, (t0, tw) in enumerate(TT):
                    pst, pk = psr.next()
                    for kc in range(8):
                        p.mm([wk, "HT"], [pk], pst[:cw_, :tw], wt[:, kc, j * 128:j * 128 + cw_], ht_rhs(HT, kc, ti, ssd),
                             start=(kc == 0), stop=(kc == 7))
                    if gate:
                        p.act([pk], [stkey], st[:cw_, t0:t0 + tw], pst[:cw_, :tw], AF.Sigmoid)
                    else:
                        ev[0] ^= 1
                        if ev[0]:
                            p.act([pk], [stkey], st[:cw_, t0:t0 + tw], pst[:cw_, :tw], AF.Identity)
                        else:
                            p.cp([pk], [stkey], st[:cw_, t0:t0 + tw], pst[:cw_, :tw])
                r0 = dst_row0 + j * 128
                p.dma("sp", [stkey], [dkey], dst[r0:r0 + cw_, :], st[:cw_, :])

        for g in range(10):
            fm_group(g * 512, PROJ, g * 512, "PROJ")
        for g in range(5):
            fm_group(5120 + g * 512, SSMP, g * 512, "SSMP", ssd=True)
        fm_group(7680, SSMP, 2560, "SSMP", ssd=True, ncols=32)
        for g in range(6):
            fm_group(7712 + g * 512, GT, g * 512, "GT", gate=True)
        p.release(m)

    def phase_rglru(l):
        m = p.mark()
        cw = p.sb("rcw", [128, 8, 4], F32)
        cb = p.sb("rcb", [128, 8], F32)
        ab = p.sb("rab", [128, 2, 8], F32)
        xb = p.sb("rxb", [128, 2, 8], F32)
        cA = p.sb("rcA", [128, 2, 8], F32)
        c2A = p.sb("rc2A", [128, 2, 8], F32)
        p.dma("sp", [], ["rcw"], cw[:], rnn_cwT[l])
        p.dma("sp", [], ["rcb"], cb[:], rnn_cbT[l])
        p.dma("sp", [], ["rab"], ab[:], rnn_abT[l])
        p.dma("sp", [], ["rxb"], xb[:], rnn_xbT[l])
        p.dma("sp", [], ["rcA"], cA[:], rnn_lamT[l])
        p.act(["rcA"], ["rcA"], cA[:], cA[:], AF.Exp, scale=-1.0)
        p.act(["rcA"], ["rcA"], cA[:], cA[:], AF.Ln, bias=1.0)
        p.ts(["rcA"], ["rc2A"], c2A[:], cA[:], -16.0, None, ALU.mult)
        p.ts(["rcA"], ["rcA"], cA[:], cA[:], -8.0, None, ALU.mult)
        T1 = p.sb("rT1", [128, T], F32)
        U = p.sb("rU", [128, T], F32)
        Af = p.sb("rA", [128, T], F32)
        Gf = p.sb("rG", [128, T], F32)
        HS = p.sb("rHS", [128, T], F32)
        Y = p.sb("rY", [128, T], BF16)
        gw = Ring(p, "rgw", [128, 4, 128], F32, 2)
        rr = Ring(p, "rr", [128, 512], F32, 2)
        ir = Ring(p, "ri", [128, 512], F32, 2)
        sr = Ring(p, "rs", [128, 512], F32, 2)
        segs = [(0, NCTX), (NCTX, T)]

        def rev(t, lo, hi):
            a = t[:, lo:hi]
            return bass.AP(a.tensor, a.offset + (hi - lo - 1), [list(a.ap[0]), [-1, hi - lo]])

        for hb in range(8):
            p.dma("sp", ["PROJ"], ["rT1"], T1[:], PROJ[hb * 128:(hb + 1) * 128, :])
            seg_conv(p, U, T1, lambda j: cw[:, hb, j:j + 1], cb[:, hb:hb + 1], 4, 2, segs, ["rT1", "rcw", "rcb"], ["rU"])
            g4, gk = gw.next()
            for d in range(2):
                p.dma("sp", [], [gk], g4[:, d, :], rnn_aw[l, d, hb])
                p.dma("sp", [], [gk], g4[:, 2 + d, :], rnn_xw[l, d, hb])
            for d in range(2):
                for ti, (t0, tw) in enumerate(TT):
                    pa, pak = psr.next()
                    px, pxk = psr.next()
                    p.mm([gk, "rU"], [pak], pa[:, :tw], g4[:, d, :], U[:, t0:t0 + tw])
                    p.mm([gk, "rU"], [pxk], px[:, :tw], g4[:, 2 + d, :], U[:, t0:t0 + tw])
                    r_, rk = rr.next()
                    i_, ik = ir.next()
                    s_, sk_ = sr.next()
                    p.act([pak, "rab"], [rk], r_[:, :tw], pa[:, :tw], AF.Sigmoid, bias=ab[:, d, hb:hb + 1])
                    p.act([pxk, "rxb"], [ik], i_[:, :tw], px[:, :tw], AF.Sigmoid, bias=xb[:, d, hb:hb + 1])
                    p.act([rk, "rc2A"], [sk_], s_[:, :tw], r_[:, :tw], AF.Exp, scale=c2A[:, d, hb:hb + 1])
                    p.act([rk, "rcA"], ["rA"], Af[:, t0:t0 + tw], r_[:, :tw], AF.Exp, scale=cA[:, d, hb:hb + 1])
                    p.act([sk_], [sk_], s_[:, :tw], s_[:, :tw], AF.Sqrt, scale=-1.0, bias=1.0)
                    p.tt([ik, sk_], [ik], i_[:, :tw], i_[:, :tw], s_[:, :tw], ALU.mult)
                    p.tt([ik, "rU"], ["rG"], Gf[:, t0:t0 + tw], i_[:, :tw], U[:, t0:t0 + tw], ALU.mult)
                if d == 0:
                    p.op("dve", ["rA", "rG"], ["rHS"], lambda e: e.tensor_tensor_scan(
                        out=HS[:, :], data0=Af[:, :], data1=Gf[:, :], initial=0.0, op0=ALU.mult, op1=ALU.add))
                else:
                    p.op("dve", ["rA", "rG"], ["rT1"], lambda e: e.tensor_tensor_scan(
                        out=rev(T1, 0, NCTX), data0=rev(Af, 0, NCTX), data1=rev(Gf, 0, NCTX), initial=0.0,
                        op0=ALU.mult, op1=ALU.add))
                    p.op("dve", ["rA", "rG", "rT1"], ["rT1"], lambda e: e.tensor_tensor_scan(
                        out=rev(T1, NCTX, T), data0=rev(Af, NCTX, T), data1=rev(Gf, NCTX, T), initial=T1[:, 0:1],
                        op0=ALU.mult, op1=ALU.add))
                    p.tt(["rHS", "rT1"], ["rHS"], HS[:, :], HS[:, :], T1[:, :], ALU.add)
            p.dma("sp", ["PROJ"], ["rT1"], T1[:], PROJ[1024 + hb * 128:1024 + (hb + 1) * 128, :])
            p.tt(["rT1"], ["rG"], Gf[:, :], T1[:, :], T1[:, :], ALU.mult)
            p.ts(["rG"], ["rG"], Gf[:, :], Gf[:, :], 0.044715, 1.0, ALU.mult, ALU.add)
            p.tt(["rG", "rT1"], ["rG"], Gf[:, :], Gf[:, :], T1[:, :], ALU.mult)
            p.act(["rG"], ["rG"], Gf[:, :], Gf[:, :], AF.Sigmoid, scale=1.5957691216057308)
            p.tt(["rG", "rT1"], ["rG"], Gf[:, :], Gf[:, :], T1[:, :], ALU.mult)
            p.tt(["rG", "rHS"], ["rY"], Y[:, :], Gf[:, :], HS[:, :], ALU.mult)
            p.dma("sp", ["rY"], ["YR"], YR[hb * 128:(hb + 1) * 128, :], Y[:])
        p.release(m)

    def phase_ssd(l):
        m = p.mark()
        psb = psr.tiles
        pkk = psr.keys
        cw = p.sb("scw", [128, 12, 4], F32)
        cb = p.sb("scb", [128, 12], F32)
        p.dma("sp", [], ["scw"], cw[:], ssm_cwT[l])
        p.dma("sp", [], ["scb"], cb[:], ssm_cbT[l])
        XTOK = p.sb("XTOK", [128, 34, 1024], BF16)
        BT = p.sb("BT", [128, 2, T], BF16)
        CT = p.sb("CT", [128, 2, T], BF16)
        DT_tok = p.sb("DT_tok", [128, 34, 32], F32)
        ADT_tok = p.sb("ADT_tok", [128, 34, 32], F32)
        EA = p.sb("EA", [128, 34, 32], F32)
        DTE = p.sb("DTE", [128, 34, 32], F32)
        DEC = p.sb("DEC", [128, 34, 32], F32)
        mark_a = p.mark()
        BTOK = p.sb("BTOK", [128, 34, 256], BF16)
        DTD = p.sb("DTD", [128, 34, 32], F32)
        segs = [(0, NCTX), (NCTX, T)]
        m1 = p.mark()
        T1r = Ring(p, "sT1", [128, T], F32, 2)
        XSr = Ring(p, "sXS", [128, T], F32, 1)
        for blk in range(12):
            T1, t1k = T1r.next()
            XS, xsk = XSr.next()
            p.dma("sp", ["SSMP"], [t1k], T1[:], SSMP[1024 + blk * 128:1024 + (blk + 1) * 128, :])
            seg_conv(p, XS, T1, lambda j: cw[:, blk, j:j + 1], cb[:, blk:blk + 1], 4, 2, segs, [t1k, "scw", "scb"], [xsk])
            p.act([xsk], [xsk], XS[:, :], XS[:, :], AF.Silu)
            if blk >= 8:
                g = (blk - 8) % 2
                dstT, dk = (BT, "BT") if blk < 10 else (CT, "CT")
                p.cp([xsk], [dk], dstT[:, g, :], XS[:, :])
            if blk < 10:
                for q0 in range(0, 34, 4):
                    nq = min(4, 34 - q0)
                    pst, pk = psr.next()
                    for qi in range(nq):
                        q = q0 + qi
                        p.tr([xsk, "ident"], [pk], pst[:, qi * 128:(qi + 1) * 128], XS[:, q * 128:(q + 1) * 128], ident[:])
                    if blk < 8:
                        p.cp([pk], ["XTOK"], XTOK[:, q0:q0 + nq, blk * 128:(blk + 1) * 128],
                             pst[:, :nq * 128].rearrange("p (q c) -> p q c", c=128), eng=("dve" if (q0 // 4) % 2 else "act") if False else "dve")
                    else:
                        g = blk - 8
                        p.cp([pk], ["BTOK"], BTOK[:, q0:q0 + nq, g * 128:(g + 1) * 128],
                             pst[:, :nq * 128].rearrange("p (q c) -> p q c", c=128))
        p.release(m1)
        m2 = p.mark()
        DTF = p.sb("DTF", [32, T], F32)
        ADF = p.sb("ADF", [32, T], F32)
        dtb = p.sb("dtb", [32, 1], F32)
        aneg = p.sb("aneg", [32, 1], F32)
        p.dma("sp", ["SSMP"], ["DTF"], DTF[:], SSMP[2560:2592, :])
        p.dma("sp", [], ["dtb"], dtb[:], ssm_dtbT[l])
        p.dma("sp", [], ["aneg"], aneg[:], ssm_alogT[l])
        p.act(["aneg"], ["aneg"], aneg[:], aneg[:], AF.Exp)
        p.ts(["aneg"], ["aneg"], aneg[:], aneg[:], -1.0, None, ALU.mult)
        p.act(["DTF", "dtb"], ["DTF"], DTF[:, :], DTF[:, :], AF.Exp, bias=dtb[:, 0:1])
        p.act(["DTF"], ["DTF"], DTF[:, :], DTF[:, :], AF.Ln, bias=1.0)
        p.ts(["DTF", "aneg"], ["ADF"], ADF[:, :], DTF[:, :], aneg[:, 0:1], None, ALU.mult)
        for (src, sk_, dst, dk) in ((DTF, "DTF", DT_tok, "DT_tok"), (ADF, "ADF", ADT_tok, "ADT_tok")):
            for q0 in range(0, 34, 16):
                nq = min(16, 34 - q0)
                pst, pk = psr.next()
                for qi in range(nq):
                    q = q0 + qi
                    p.tr([sk_, "ident"], [pk], pst[:, qi * 32:(qi + 1) * 32], src[:, q * 128:(q + 1) * 128], ident[:32, :32])
                p.cp([pk], [dk], dst[:, q0:q0 + nq, :], pst[:, :nq * 32].rearrange("p (q c) -> p q c", c=32))
        for d in range(2):
            for cg in range(2):
                rhs = ADT_tok[:, cg * 17:(cg + 1) * 17, d * 16:(d + 1) * 16]
                for (mk_, dst, dk) in (((LE, GE)[d], EA, "EA"), ((GT_, LT)[d], DTE, "DTE"), (ONES, DEC, "DEC")):
                    pst, pk = psr.next()
                    p.mm(["masks", "ADT_tok"], [pk], pst[:, :272], masks[:, mk_, :], rhs)
                    p.act([pk], [dk], dst[:, cg * 17:(cg + 1) * 17, d * 16:(d + 1) * 16],
                          pst[:, :272].rearrange("p (q c) -> p q c", c=16), AF.Exp)
        p.tt(["DT_tok", "DTE"], ["DTD"], DTD[:], DT_tok[:], DTE[:], ALU.mult)
        p.release(m2)
        H = p.sb("Hst", [128, 1024], F32)
        hbr = Ring(p, "Hb", [128, 1024], BF16, 3)
        xsr = Ring(p, "xsd", [128, 1024], BF16, 3)
        for d in range(2):
            order = list(range(34)) if d == 0 else [1, 0] + list(range(33, 1, -1))
            p.op("dve", [], ["Hst"], lambda e: e.memset(H[:], 0.0))
            for q in order:
                hb_, hbk = hbr.next()
                p.act(["Hst"], [hbk], hb_[:], H[:], AF.Identity)
                p.dma("sp", [hbk], ["HP"], HP[d, q], hb_[:])
                xs, xk = xsr.next()
                p.tt(["XTOK", "DTD"], [xk], xs[:].rearrange("p (h c) -> p h c", c=64),
                     XTOK[:, q, :].rearrange("p (h c) -> p h c", c=64),
                     DTD[:, q, d * 16:(d + 1) * 16].unsqueeze(2).to_broadcast([128, 16, 64]), ALU.mult)
                pss = []
                for g in range(2):
                    pst, pk = psr.next()
                    p.mm(["BTOK", xk], [pk], pst[:, :], BTOK[:, q, g * 128:(g + 1) * 128], xs[:, g * 512:(g + 1) * 512])
                    pss.append((pst, pk))
                p.tt(["Hst", "DEC"], ["Hst"], H[:].rearrange("p (h c) -> p h c", c=64), H[:].rearrange("p (h c) -> p h c", c=64),
                     DEC[:, q, d * 16:(d + 1) * 16].unsqueeze(2).to_broadcast([128, 16, 64]), ALU.mult)
                for g in range(2):
                    p.tt(["Hst", pss[g][1]], ["Hst"], H[:, g * 512:(g + 1) * 512], H[:, g * 512:(g + 1) * 512], pss[g][0][:, :], ALU.add)
        p.release(mark_a)
        dsk = p.sb("dsk", [128, 16], F32)
        nw = p.sb("snw", [128, 1024], F32)
        p.dma("sp", [], ["dsk"], dsk[:], ssm_d[l:l + 1, :].to_broadcast([128, 16]))
        p.dma("sp", [], ["snw"], nw[:], ssm_norm[l:l + 1, :].to_broadcast([128, 1024]))
        hpr = Ring(p, "hp", [128, 2, 1024], BF16, 2)
        cbmr = Ring(p, "cbm", [128, 2, 256], F32, 2)
        rsr = Ring(p, "rseg", [128, 16, 128], F32, 1)
        er = Ring(p, "eseg", [128, 16, 128], F32, 1)
        mr = Ring(p, "mseg", [128, 16, 128], BF16, 2)
        xdr = Ring(p, "xdt", [128, 1024], BF16, 2)
        accr = Ring(p, "acc", [128, 1024], F32, 2)
        tmpr = Ring(p, "stmp", [128, 1024], F32, 1)
        zfr = Ring(p, "zf", [128, 8, 128], F32, 1)
        szr = Ring(p, "sz", [128, 1024], F32, 2)
        ysr = Ring(p, "ys", [128, 1024], F32, 2)
        ssr = Ring(p, "ssq", [128, 2], F32, 2)
        for q in range(34):
            hp, hpk = hpr.next()
            for d in range(2):
                p.dma("sp", ["HP"], [hpk], hp[:, d, :], HP[d, q])
            zf, zfk = zfr.next()
            p.dma("sp", ["SSMP"], [zfk], zf[:], SSMP[0:1024, q * 128:(q + 1) * 128].rearrange("(b c) t -> c b t", c=128))
            for g in range(2):
                p.mm(["BT", "CT"], [pkk[0]], psb[0][:, g * 128:(g + 1) * 128], BT[:, g, q * 128:(q + 1) * 128], CT[:, g, q * 128:(q + 1) * 128])
            cbm, cbk = cbmr.next()
            for d in range(2):
                p.tt([pkk[0], "masks"], [cbk], cbm[:, d, :].rearrange("p (g c) -> p g c", c=128),
                     psb[0][:, 0:256].rearrange("p (g c) -> p g c", c=128),
                     masks[:, (LE, GE)[d], :].unsqueeze(1).to_broadcast([128, 2, 128]), ALU.mult)
            acc, acck = accr.next()
            for d in range(2):
                rs, rsk = rsr.next()
                p.tt(["ADT_tok", "masks"], [rsk], rs[:], ADT_tok[:, q, d * 16:(d + 1) * 16].unsqueeze(2).to_broadcast([128, 16, 128]),
                     masks[:, (LE, GE)[d], :].unsqueeze(1).to_broadcast([128, 16, 128]), ALU.mult)
                es, esk = er.next()
                for i in range(4):
                    p.mm(["masks", rsk], [pkk[1 + i]], psb[1 + i][:, :], masks[:, (GT_, LT)[d], :],
                         rs[:, 4 * i:4 * i + 4, :])
                    p.act([pkk[1 + i]], [esk], es[:, 4 * i:4 * i + 4, :], psb[1 + i][:, :].rearrange("p (h c) -> p h c", c=128), AF.Exp)
                ms, msk = mr.next()
                p.tt([esk, cbk], [msk], ms[:].rearrange("p (g e) c -> p g e c", g=2), es[:].rearrange("p (g e) c -> p g e c", g=2),
                     cbm[:, d, :].rearrange("p (g c) -> p g c", c=128).unsqueeze(2).to_broadcast([128, 2, 8, 128]), ALU.mult)
                xd, xdk = xdr.next()
                p.tt(["XTOK", "DT_tok"], [xdk], xd[:].rearrange("p (h c) -> p h c", c=64),
                     XTOK[:, q, :].rearrange("p (h c) -> p h c", c=64),
                     DT_tok[:, q, d * 16:(d + 1) * 16].unsqueeze(2).to_broadcast([128, 16, 64]), ALU.mult)
                for h in range(16):
                    bk = 5 + h // 8
                    p.mm([msk, xdk], [pkk[bk]], psb[bk][:, (h % 8) * 64:(h % 8 + 1) * 64], ms[:, h, :], xd[:, h * 64:(h + 1) * 64],
                         start=(d == 0 and h % 8 == 0), stop=(d == 1 and h % 8 == 7))
                for g in range(2):
                    bk = 7 if g == 0 else 0
                    p.mm(["CT", hpk], [pkk[bk]], psb[bk][:, :], CT[:, g, q * 128:(q + 1) * 128], hp[:, d, g * 512:(g + 1) * 512])
                    eab = EA[:, q, d * 16 + g * 8:d * 16 + (g + 1) * 8].unsqueeze(2).to_broadcast([128, 8, 64])
                    if d == 0:
                        p.tt([pkk[bk], "EA"], [acck], acc[:, g * 512:(g + 1) * 512].rearrange("p (h c) -> p h c", c=64),
                             psb[bk][:, :].rearrange("p (h c) -> p h c", c=64), eab, ALU.mult)
                    else:
                        tm, tmk = tmpr.next()
                        p.tt([pkk[bk], "EA"], [tmk], tm[:, :512].rearrange("p (h c) -> p h c", c=64),
                             psb[bk][:, :].rearrange("p (h c) -> p h c", c=64), eab, ALU.mult)
                        p.tt([acck, tmk], [acck], acc[:, g * 512:(g + 1) * 512], acc[:, g * 512:(g + 1) * 512], tm[:, :512], ALU.add)
            for g in range(2):
                p.tt([acck, pkk[5 + g]], [acck], acc[:, g * 512:(g + 1) * 512], acc[:, g * 512:(g + 1) * 512], psb[5 + g][:, :], ALU.add)
            tm, tmk = tmpr.next()
            p.tt(["XTOK", "dsk"], [tmk], tm[:].rearrange("p (h c) -> p h c", c=64), XTOK[:, q, :].rearrange("p (h c) -> p h c", c=64),
                 dsk[:].unsqueeze(2).to_broadcast([128, 16, 64]), ALU.mult)
            p.tt([acck, tmk], [acck], acc[:], acc[:], tm[:], ALU.add)
            sz, szk = szr.next()
            for g in range(2):
                for b4 in range(4):
                    p.tr([zfk, "ident"], [pkk[1 + g]], psb[1 + g][:, b4 * 128:(b4 + 1) * 128], zf[:, g * 4 + b4, :], ident[:])
                p.act([pkk[1 + g]], [szk], sz[:, g * 512:(g + 1) * 512], psb[1 + g][:, :], AF.Silu)
            p.tt([acck, szk], [acck], acc[:], acc[:], sz[:], ALU.mult)
            ss, ssk = ssr.next()
            p.act([acck], [szk, ssk], sz[:], acc[:], AF.Square, accum_out=ss[:, 0:1])
            p.ts([ssk], [ssk], ss[:, 1:2], ss[:, 0:1], 1.0 / 1024, EPS, ALU.mult, ALU.add)
            p.act([ssk], [ssk], ss[:, 1:2], ss[:, 1:2], AF.Sqrt)
            p.op("dve", [ssk], [ssk], lambda e: e.reciprocal(out=ss[:, 1:2], in_=ss[:, 1:2]))
            ys, ysk = ysr.next()
            p.stt([acck, ssk, "snw"], [ysk], ys[:], acc[:], ss[:, 1:2], nw[:], ALU.mult, ALU.mult)
            if q < 2:
                p.dma("sp", [ysk], ["YSTOK"], YSTOK[q * 128:(q + 1) * 128, :], ys[:])
            else:
                c2 = q - 2
                yv = YSTOK[NCTX:, :].rearrange("(r c) d -> c r d", c=64)
                for cl in range(2):
                    p.dma("sp", [ysk], ["YSTOK"], yv[2 * c2 + cl], ys[cl * 64:(cl + 1) * 64, :])
        p.release(m)

    class Rot:
        def __init__(self, idxs):
            self.idxs = idxs
            self.i = 0

        def next(self):
            k = self.idxs[self.i]
            self.i = (self.i + 1) % len(self.idxs)
            return psr.tiles[k], psr.keys[k]

    PI = float(np.pi)

    def hy_filter(l, n, embT_d, tv_d, FW, KF):
        m = p.mark()
        nt = n // 128
        psb, pkk = psr.tiles, psr.keys
        w1 = p.sb("hw1", [33, 64], F32)
        w2 = p.sb("hw2", [64, 64], F32)
        w3 = p.sb("hw3", [64, 4096], F32)
        fr = p.sb("hfr", [64, 1], F32)
        fb1 = p.sb("hfb1", [64, 1], F32)
        fb2 = p.sb("hfb2", [64, 1], F32)
        emb = p.sb("hemb", [33, n], F32)
        h1 = p.sb("hh1", [64, n], F32)
        h2 = p.sb("hh2", [64, n], F32)
        negpi = p.sb("hnegpi", [128, 1], F32)
        nz0 = p.sb("hnz0", [128, 1], F32)
        dec = p.sb("hdec", [128, 4096], F32)
        negt = p.sb("hnegt", [128, nt], F32)
        p.dma("sp", [], ["hw1"], w1[:], hy_w1[l])
        p.dma("sp", [], ["hw2"], w2[:], hy_w2[l])
        p.dma("sp", [], ["hw3"], w3[:], hy_w3[l])
        p.dma("sp", [], ["hfr"], fr[:], hy_freqT[l])
        p.dma("sp", [], ["hfb1"], fb1[:], hy_b1T[l])
        p.dma("sp", [], ["hfb2"], fb2[:], hy_b2T[l])
        p.dma("sp", [], ["hemb"], emb[:], embT_d[:, :])
        p.dma("sp", [], ["hdec"], dec[:], hy_decay[l:l + 1, :].to_broadcast([128, 4096]))
        p.dma("sp", [], ["hnegt"], negt[:], tv_d[:, :])
        p.ts(["hnegt"], ["hnegt"], negt[:], negt[:], -1.0, None, ALU.mult)
        p.op("dve", [], ["hnegpi"], lambda e: e.memset(negpi[:], -PI))
        p.op("dve", [], ["hnz0"], lambda e: e.memset(nz0[:], 1.0))
        p.op("dve", ["hnz0"], ["hnz0"], lambda e: e.memset(nz0[0:1, :], 0.0))
        p.ts(["hfb1", "hfr"], ["hfb1"], fb1[:], fb1[:], fr[:, 0:1], None, ALU.mult)
        p.ts(["hfb2", "hfr"], ["hfb2"], fb2[:], fb2[:], fr[:, 0:1], None, ALU.mult)
        rot = Rot([0, 1, 2, 3, 4, 5, 6])
        sinr = Ring(p, "hsin", [64, 512], F32, 4)
        for (src, sk_, wgt, wk, fb, fbk, dst, dk) in ((emb, "hemb", w1, "hw1", fb1, "hfb1", h1, "hh1"),
                                                       (h1, "hh1", w2, "hw2", fb2, "hfb2", h2, "hh2")):
            for c0 in range(0, n, 512):
                cwid = min(512, n - c0)
                pst, pk = rot.next()
                p.mm([sk_, wk], [pk], pst[:64, :cwid], wgt[:, :], src[:, c0:c0 + cwid])
                p.ts([pk, "hfr", fbk], [dk], dst[:, c0:c0 + cwid], pst[:64, :cwid], fr[:, 0:1], fb[:, 0:1], ALU.mult, ALU.add)
                sa, sak = sinr.next()
                sb_, sbk = sinr.next()
                dv = dst[:, c0:c0 + cwid]
                p.act([dk], [sak], sa[:, :cwid], dv, AF.Sin, scale=0.25)
                p.act([dk], [sbk], sb_[:, :cwid], dv, AF.Sin, scale=0.125)
                p.tt([sbk], [sbk], sb_[:, :cwid], sb_[:, :cwid], sb_[:, :cwid], ALU.mult)
                p.ts([sbk], [sbk], sb_[:, :cwid], sb_[:, :cwid], -2.0, 1.0, ALU.mult, ALU.add)
                p.tt([sak, sbk], [sbk], sb_[:, :cwid], sa[:, :cwid], sb_[:, :cwid], ALU.mult)
                p.tt([sak], [sak], sa[:, :cwid], sa[:, :cwid], sa[:, :cwid], ALU.mult)
                p.ts([sak], [sak], sa[:, :cwid], sa[:, :cwid], -2.0, 1.0, ALU.mult, ALU.add)
                p.stt([sak, sbk], [dk], dv, sb_[:, :cwid], 4.0, sa[:, :cwid], ALU.mult, ALU.mult)
        UP = p.sb("hUP", [128, nt, 512], BF16)
        UM = p.sb("hUM", [128, nt, 512], BF16)
        rinv = p.sb("hrinv", [128, 512], F32)
        er = Ring(p, "hE", [128, 512], F32, 2)
        hfr_ = Ring(p, "hhf", [128, 512], F32, 2)
        hbr_ = Ring(p, "hhb", [128, 512], F32, 2)
        abr = Ring(p, "hab", [128, 512], F32, 2)
        fring = Ring(p, "hF", [128, nt, 128], BF16, 3)
        kr = Ring(p, "hkt", [128, 512], F32, 3)
        for o in range(2):
            for cg in range(2):
                colf = o * 1024 + cg * 512
                colb = 2048 + colf
                for tc in range(nt):
                    hh = []
                    for dirn, col in ((0, colf), (1, colb)):
                        pst, pk = rot.next()
                        p.mm(["hh2", "hw3"], [pk], pst[:, :], h2[:, tc * 128:(tc + 1) * 128], w3[:, col:col + 512])
                        E, ek = er.next()
                        p.act(["hdec", "hnegt"], [ek], E[:], dec[:, col:col + 512], AF.Exp, scale=negt[:, tc:tc + 1])
                        ht_, hk = (hfr_ if dirn == 0 else hbr_).next()
                        p.tt([pk, ek], [hk], ht_[:], pst[:, :], E[:], ALU.mult)
                        if dirn == 1 and tc == 0:
                            p.ts([hk, "hnz0"], [hk], ht_[:], ht_[:], nz0[:, 0:1], None, ALU.mult)
                        ab, abk = abr.next()
                        p.act([hk], [abk], ab[:], ht_[:], AF.Abs)
                        p.mm(["masks", abk], [pkk[7]], psb[7][:, :], masks[:, ONES, :], ab[:],
                             start=(tc == 0 and dirn == 0), stop=(tc == nt - 1 and dirn == 1))
                        hh.append((ht_, hk))
                    p.tt([hh[0][1], hh[1][1]], ["hUP"], UP[:, tc, :], hh[0][0][:], hh[1][0][:], ALU.add)
                    p.tt([hh[0][1], hh[1][1]], ["hUM"], UM[:, tc, :], hh[0][0][:], hh[1][0][:], ALU.subtract)
                p.op("dve", [pkk[7]], ["hrinv"], lambda e: e.reciprocal(out=rinv[:], in_=psb[7][:, :]))
                for j in range(nt):
                    for (pq, U, uk) in ((0, UP, "hUP"), (1, UM, "hUM")):
                        Ft, fk = fring.next()
                        p.dma("sp", [], [fk], Ft[:], FW[pq * nt + j])
                        pst, pk = rot.next()
                        for tc in range(nt):
                            p.mm([fk, uk], [pk], pst[:, :], Ft[:, tc, :], U[:, tc, :], start=(tc == 0), stop=(tc == nt - 1))
                        kt, kk = kr.next()
                        p.tt([pk, "hrinv"], [kk], kt[:], pst[:, :], rinv[:], ALU.mult)
                        p.dma("sp", [kk], ["KF"], KF[o, j, :, pq, cg * 512:(cg + 1) * 512], kt[:])
        p.release(m)

    def hy_data(l, n, toff, FW, GW, KF, hbias):
        m = p.mark()
        nt = n // 128
        psb, pkk = psr.tiles, psr.keys
        tiles = [(i * 512, 512) for i in range(n // 512)] if n >= 512 else [(0, n)]
        ZT = p.sb("hZT", [128, nt, 512], BF16)
        YSs = p.sb("hYS", [128, nt, 2, 512], BF16)
        zfr = Ring(p, "hzf", [128, n], F32, 1)
        fring = Ring(p, "hF2", [128, nt, 128], BF16, 3)
        kr = Ring(p, "hkt2", [128, 2, 512], F32, 2)
        tr_ = Ring(p, "htm", [128, 512], F32, 4)
        gring = Ring(p, "hG", [128, 8, 512], BF16, 3)
        zpr = Ring(p, "hzp", [128, 512], F32, 2)
        xgr = Ring(p, "hxg", [128, 512], F32, 2)
        znr = Ring(p, "hzn", [128, 512], F32, 2)
        ynr = Ring(p, "hyn", [128, 512], BF16, 2)
        rot = Rot([4, 5, 6, 7])
        for cg in range(2):
            for o in range(2):
                src = HY if o == 0 else Z2
                skey = "HY" if o == 0 else "Z2"
                for b in range(4):
                    zf, zk = zfr.next()
                    r0 = cg * 512 + b * 128
                    p.dma("sp", [skey], [zk], zf[:], src[r0:r0 + 128, toff:toff + n])
                    for tc0 in range(0, nt, 4):
                        nq = min(4, nt - tc0)
                        pst, pk = rot.next()
                        for qi in range(nq):
                            p.tr([zk, "ident"], [pk], pst[:, qi * 128:(qi + 1) * 128], zf[:, (tc0 + qi) * 128:(tc0 + qi + 1) * 128], ident[:])
                        p.cp([pk], ["hZT"], ZT[:, tc0:tc0 + nq, b * 128:(b + 1) * 128], pst[:, :nq * 128].rearrange("p (q c) -> p q c", c=128))
                for j in range(nt):
                    Fc, fck = fring.next()
                    p.dma("sp", [], [fck], Fc[:], FW[j])
                    Fs, fsk = fring.next()
                    p.dma("sp", [], [fsk], Fs[:], FW[nt + j])
                    kt, kk = kr.next()
                    p.dma("sp", ["KF"], [kk], kt[:], KF[o, j, :, :, cg * 512:(cg + 1) * 512])
                    pA, pAk = rot.next()
                    pB, pBk = rot.next()
                    for tc in range(nt):
                        p.mm([fck, "hZT"], [pAk], pA[:, :], Fc[:, tc, :], ZT[:, tc, :], start=(tc == 0), stop=(tc == nt - 1))
                    for tc in range(nt):
                        p.mm([fsk, "hZT"], [pBk], pB[:, :], Fs[:, tc, :], ZT[:, tc, :], start=(tc == 0), stop=(tc == nt - 1))
                    t1, k1 = tr_.next()
                    t2, k2 = tr_.next()
                    p.tt([pAk, kk], [k1], t1[:], pA[:, :], kt[:, 0, :], ALU.mult)
                    p.tt([pBk, kk], [k2], t2[:], pB[:, :], kt[:, 1, :], ALU.mult)
                    p.tt([k1, k2], ["hYS"], YSs[:, j, 0, :], t1[:], t2[:], ALU.subtract)
                    t3, k3 = tr_.next()
                    t4, k4 = tr_.next()
                    p.tt([pAk, kk], [k3], t3[:], pA[:, :], kt[:, 1, :], ALU.mult)
                    p.tt([pBk, kk], [k4], t4[:], pB[:, :], kt[:, 0, :], ALU.mult)
                    p.tt([k3, k4], ["hYS"], YSs[:, j, 1, :], t3[:], t4[:], ALU.add)
                for ti, (t0, tw) in enumerate(tiles):
                    Gt, gk = None, None
                    for j2 in range(2 * nt):
                        if j2 % 8 == 0:
                            ng = min(8, 2 * nt - j2)
                            Gt, gk = gring.next()
                            p.dma("sp", [], [gk], Gt[:, :ng, :tw], GW[ti, :, j2:j2 + ng, :])
                        part, j = j2 // nt, j2 % nt
                        for b in range(4):
                            p.mm(["hYS", gk], [pkk[b]], psb[b][:, :tw], YSs[:, j, part, b * 128:(b + 1) * 128], Gt[:, j2 % 8, :tw],
                                 start=(j2 == 0), stop=(j2 == 2 * nt - 1))
                    for b in range(4):
                        cb_ = cg * 4 + b
                        zp, zpk = zpr.next()
                        p.dma("sp", [skey], [zpk], zp[:, :tw], src[cb_ * 128:(cb_ + 1) * 128, toff + t0:toff + t0 + tw])
                        xg, xgk = xgr.next()
                        xr0 = (1 + o) * 1024 + cb_ * 128
                        p.dma("sp", ["HY"], [xgk], xg[:, :tw], HY[xr0:xr0 + 128, toff + t0:toff + t0 + tw])
                        tm, tmk = tr_.next()
                        p.stt([zpk, "hbias", pkk[b]], [tmk], tm[:, :tw], zp[:, :tw], hbias[:, o, cb_:cb_ + 1], psb[b][:, :tw], ALU.mult, ALU.add)
                        if o == 0:
                            zn, znk = znr.next()
                            p.tt([tmk, xgk], [znk], zn[:, :tw], tm[:, :tw], xg[:, :tw], ALU.mult)
                            p.dma("sp", [znk], ["Z2"], Z2[cb_ * 128:(cb_ + 1) * 128, toff + t0:toff + t0 + tw], zn[:, :tw])
                        else:
                            yn, ynk = ynr.next()
                            p.tt([tmk, xgk], [ynk], yn[:, :tw], tm[:, :tw], xg[:, :tw], ALU.mult)
                            p.dma("sp", [ynk], ["YH"], YH[cb_ * 128:(cb_ + 1) * 128, toff + t0:toff + t0 + tw], yn[:, :tw])
        p.release(m)


    def hy4(l, hbias):
        m = p.mark()
        n = NLAT
        toff = NCTX
        psb, pkk = psr.tiles, psr.keys
        identb = p.sb("identb", [128, 128], BF16)
        p.cp(["ident"], ["identb"], identb[:], ident[:])
        S1 = p.sb("S1", [32, 128], BF16)
        S2 = p.sb("S2", [128, 32], BF16)
        p.dma("sp", [], ["S1"], S1[:], S1_d[:, :])
        p.dma("sp", [], ["S2"], S2[:], S2_d[:, :])
        rot = Rot([0, 1, 2, 3, 4, 5, 6, 7])
        evc = [0]

        def evac(R, W, out, in_):
            evc[0] ^= 1
            if evc[0]:
                p.act(R, W, out, in_, AF.Identity)
            else:
                p.cp(R, W, out, in_)

        def v_zin(X):
            return X[:32, :].rearrange("p (c t) -> p c t", t=128)

        def v_A(X):
            return X[:, :].rearrange("p (r f c) -> p r f c", r=2, f=64)

        def v_Y(X):
            return X[:64, :].rearrange("p (r f c) -> p r f c", r=2, f=64)

        def v_B(X):
            return X[:, :].rearrange("p (c k) -> p c k", k=128)

        def load_zin(X, xk, src_rows, skey):
            zv = v_zin(X)
            for c4 in range(4):
                p.dma("sp", [skey], [xk], zv[:, c4 * 32:(c4 + 1) * 32, :],
                      src_rows[c4 * 32:(c4 + 1) * 32, :].rearrange("c (a b) -> a c b", b=128))

        def stage1(Xi, xik, Xo, xok):
            zin = v_zin(Xi)
            Afl = v_A(Xo).rearrange("p r f c -> p (r f) c")
            for c0 in range(0, 128, 4):
                pst, pk = rot.next()
                for q in range(4):
                    p.mm([xik, "S1"], [pk], pst[:, q * 128:(q + 1) * 128], zin[:, c0 + q, :], S1[:, :])
                evac([pk], [xok], Afl[:, :, c0:c0 + 4].rearrange("p a c -> p c a"), pst[:, :].rearrange("p (c a) -> p c a", a=128))

        def stage2(XA, xak, f1g, XA2=None, xak2=None):
            A = v_A(XA)
            A2 = v_A(XA2) if XA2 is not None else A
            k2 = xak2 if XA2 is not None else xak
            zr, zrk = rot.next()
            zi, zik = rot.next()
            for q in range(4):
                f1 = f1g * 4 + q
                o_ = slice(q * 128, (q + 1) * 128)
                p.mm(["WF", xak], [zrk], zr[:64, o_], WF[:, f1, 0, :], A[:, 0, f1, :], start=True, stop=False)
                p.mm(["WF", xak], [zrk], zr[:64, o_], WF[:, f1, 2, :], A[:, 1, f1, :], start=False, stop=True)
            for q in range(4):
                f1 = f1g * 4 + q
                o_ = slice(q * 128, (q + 1) * 128)
                p.mm(["WF", k2], [zik], zi[:64, o_], WF[:, f1, 1, :], A2[:, 0, f1, :], start=True, stop=False)
                p.mm(["WF", k2], [zik], zi[:64, o_], WF[:, f1, 0, :], A2[:, 1, f1, :], start=False, stop=True)
            return zr, zrk, zi, zik

        m0 = p.mark()
        w1 = p.sb("hw1", [33, 64], F32)
        w2 = p.sb("hw2", [64, 64], F32)
        w3 = p.sb("hw3", [64, 4096], F32)
        fr = p.sb("hfr", [64, 1], F32)
        fb1 = p.sb("hfb1", [64, 1], F32)
        fb2 = p.sb("hfb2", [64, 1], F32)
        h2 = p.sb("hh2", [64, n], F32)
        ndec = p.sb("hndec", [128, 32], F32)
        tbc = p.sb("htbc", [128, n], F32)
        p.dma("sp", [], ["hw1"], w1[:], hy_w1[l])
        p.dma("sp", [], ["hw2"], w2[:], hy_w2[l])
        p.dma("sp", [], ["hw3"], w3[:], hy_w3[l])
        p.dma("sp", [], ["hfr"], fr[:], hy_freqT[l])
        p.dma("sp", [], ["hfb1"], fb1[:], hy_b1T[l])
        p.dma("sp", [], ["hfb2"], fb2[:], hy_b2T[l])
        p.dma("sp", [], ["hndec"], ndec[:], hy_ndecT[l])
        p.dma("sp", [], ["htbc"], tbc[:], tvec[0:1, :].to_broadcast([128, n]))
        p.ts(["hfb1", "hfr"], ["hfb1"], fb1[:], fb1[:], fr[:, 0:1], None, ALU.mult)
        p.ts(["hfb2", "hfr"], ["hfb2"], fb2[:], fb2[:], fr[:, 0:1], None, ALU.mult)
        mm_ = p.mark()
        emb = p.sb("hemb", [33, n], F32)
        h1 = p.sb("hh1", [64, n], F32)
        p.dma("sp", [], ["hemb"], emb[:], embT_l[:, :])
        sinr = Ring(p, "hsin", [64, 512], F32, 4)
        for (src, sk_, wgt, wk, fb, fbk, dst, dk) in ((emb, "hemb", w1, "hw1", fb1, "hfb1", h1, "hh1"),
                                                       (h1, "hh1", w2, "hw2", fb2, "hfb2", h2, "hh2")):
            for c0 in range(0, n, 512):
                pst, pk = rot.next()
                p.mm([sk_, wk], [pk], pst[:64, :], wgt[:, :], src[:, c0:c0 + 512])
                dv = dst[:, c0:c0 + 512]
                p.ts([pk, "hfr", fbk], [dk], dv, pst[:64, :], fr[:, 0:1], fb[:, 0:1], ALU.mult, ALU.add)
                sa, sak = sinr.next()
                sb_, sbk = sinr.next()
                p.act([dk], [sak], sa[:, :], dv, AF.Sin, scale=0.25)
                p.act([dk], [sbk], sb_[:, :], dv, AF.Sin, scale=0.125)
                p.tt([sbk], [sbk], sb_[:, :], sb_[:, :], sb_[:, :], ALU.mult)
                p.ts([sbk], [sbk], sb_[:, :], sb_[:, :], -2.0, 1.0, ALU.mult, ALU.add)
                p.tt([sak, sbk], [sbk], sb_[:, :], sa[:, :], sb_[:, :], ALU.mult)
                p.tt([sak], [sak], sa[:, :], sa[:, :], sa[:, :], ALU.mult)
                p.ts([sak], [sak], sa[:, :], sa[:, :], -2.0, 1.0, ALU.mult, ALU.add)
                p.stt([sak, sbk], [dk], dv, sb_[:, :], 4.0, sa[:, :], ALU.mult, ALU.mult)
        p.release(mm_)
        hrow = [p.sb("hrow0", [128, n], F32), p.sb("hrow1", [128, n], F32)]
        hrk = ["hrow0", "hrow1"]
        junk = p.sb("hjunk", [128, n], F32)
        upr = Ring(p, "hup", [128, n], F32, 1)
        ubr = Ring(p, "hub", [128, n], BF16, 2)
        er = Ring(p, "hE", [128, 512], F32, 3)
        ssr = Ring(p, "hss", [128, 4], F32, 2)
        for o in range(2):
            for cb_ in range(8):
                ss, ssk = ssr.next()
                for dirn in range(2):
                    colblk = dirn * 16 + o * 8 + cb_
                    col = colblk * 128
                    for tt_ in range(8):
                        pst, pk = rot.next()
                        p.mm(["hh2", "hw3"], [pk], pst[:, :], w3[:, col:col + 128], h2[:, tt_ * 512:(tt_ + 1) * 512])
                        E, ek = er.next()
                        p.act(["htbc", "hndec"], [ek], E[:], tbc[:, tt_ * 512:(tt_ + 1) * 512], AF.Exp, scale=ndec[:, colblk:colblk + 1])
                        p.tt([pk, ek], [hrk[dirn]], hrow[dirn][:, tt_ * 512:(tt_ + 1) * 512], pst[:, :], E[:], ALU.mult)
                    if dirn == 1:
                        p.op("dve", [hrk[1]], [hrk[1]], lambda e: e.memset(hrow[1][:, 0:1], 0.0))
                    p.act([hrk[dirn]], ["hjunk", ssk], junk[:], hrow[dirn][:], AF.Abs, accum_out=ss[:, dirn:dirn + 1])
                p.tt([ssk], [ssk], ss[:, 2:3], ss[:, 0:1], ss[:, 1:2], ALU.add)
                p.op("dve", [ssk], [ssk], lambda e: e.reciprocal(out=ss[:, 3:4], in_=ss[:, 2:3]))
                for sgn, opx in ((0, ALU.add), (1, ALU.subtract)):
                    up, upk = upr.next()
                    ub, ubk = ubr.next()
                    p.tt([hrk[0], hrk[1]], [upk], up[:], hrow[0][:], hrow[1][:], opx)
                    p.ts([upk, ssk], [ubk], ub[:], up[:], ss[:, 3:4], None, ALU.mult)
                    r0 = o * 1024 + cb_ * 128
                    p.dma("sp", [ubk], ["HFB"], HFB[sgn, r0:r0 + 128, :], ub[:])
        p.release(m0)
        if HY4_STOP == "F0":
            p.release(m)
            return
        WF = p.sb("WF", [128, 64, 3, 64], BF16)
        p.dma("sp", [], ["WF"], WF[:], WF_d[:, :, :, :])
        X1 = p.sb("X1", [128, 16384], BF16)
        X2 = p.sb("X2", [128, 16384], BF16)
        m1 = p.mark()
        X3 = p.sb("X3", [128, 16384], BF16)
        ksr = Ring(p, "hks", [64, 2, 16, 128], F32, 2)
        for o in range(2):
            for cb_ in range(8):
                r0 = o * 1024 + cb_ * 128
                load_zin(X1, "X1", HFB[0, r0:r0 + 128, :], "HFB")
                stage1(X1, "X1", X2, "X2")
                load_zin(X1, "X1", HFB[1, r0:r0 + 128, :], "HFB")
                stage1(X1, "X1", X3, "X3")
                ks, ksk = None, None
                for f1g in range(16):
                    if f1g % 4 == 0:
                        ks, ksk = ksr.next()
                    zr, zrk, zi, zik = stage2(X2, "X2", f1g, X3, "X3")
                    fo = (f1g % 4) * 4
                    evac([zrk], [ksk], ks[:, 0, fo:fo + 4, :], zr[:64, :].rearrange("p (f c) -> p f c", c=128))
                    evac([zik], [ksk], ks[:, 1, fo:fo + 4, :], zi[:64, :].rearrange("p (f c) -> p f c", c=128))
                    if f1g % 4 == 3:
                        f0 = (f1g // 4) * 16
                        p.dma("sp", [ksk], ["KF2"], KF2[o, cb_, :, :, f0:f0 + 16, :], ks[:])
        p.release(m1)
        if HY4_STOP == "F1":
            p.release(m)
            return
        WI = p.sb("WI", [64, 64, 3, 128], BF16)
        for f4 in range(4):
            p.dma("sp", [], ["WI"], WI[:, f4 * 16:(f4 + 1) * 16, :, :], WI_d[:, f4 * 16:(f4 + 1) * 16, :, :])
        ktr = Ring(p, "hkt", [64, 2, 8, 128], F32, 2)
        tr_ = Ring(p, "htm", [64, 512], F32, 4)
        zsr = Ring(p, "hzs", [64, 512], F32, 4)
        ytr = Ring(p, "hyt", [32, 8, 128], F32, 2)
        for o in range(2):
            src = HY if o == 0 else Z2
            srcb = HYB if o == 0 else Z2B
            skey = "HY" if o == 0 else "Z2"
            md = p.mark()
            if HY4_STOP == "D0":
                p.release(m)
                return
            for cb_ in range(8):
                load_zin(X1, "X1", srcb[cb_ * 128:(cb_ + 1) * 128, toff:toff + n], skey)
                stage1(X1, "X1", X2, "X2")
                if HY4_STOP == "D0b":
                    p.release(m)
                    return
                Y = v_Y(X1)
                kt, kk = None, None
                for f1g in range(16):
                    if f1g % 2 == 0:
                        kt, kk = ktr.next()
                        f0 = (f1g // 2) * 8
                        p.dma("sp", ["KF2"], [kk], kt[:], KF2[o, cb_, :, :, f0:f0 + 8, :])
                    zr, zrk, zi, zik = stage2(X2, "X2", f1g)
                    fo = (f1g % 2) * 4
                    kr_ = kt[:, 0, fo:fo + 4, :].rearrange("p f c -> p (f c)")
                    ki_ = kt[:, 1, fo:fo + 4, :].rearrange("p f c -> p (f c)")
                    if HY4_STOP == "D1a":
                        evac([zrk], ["X1"], Y[:, 0, f1g * 4:f1g * 4 + 4, :], zr[:64, :].rearrange("p (f c) -> p f c", c=128))
                        evac([zik], ["X1"], Y[:, 1, f1g * 4:f1g * 4 + 4, :], zi[:64, :].rearrange("p (f c) -> p f c", c=128))
                        continue
                    szr, szrk = zsr.next()
                    szi, szik = zsr.next()
                    p.act([zrk], [szrk], szr[:], zr[:64, :], AF.Identity)
                    p.act([zik], [szik], szi[:], zi[:64, :], AF.Identity)
                    t1, k1 = tr_.next()
                    t2, k2 = tr_.next()
                    p.tt([szrk, kk], [k1], t1[:], szr[:], kr_, ALU.mult)
                    p.tt([szik, kk], [k2], t2[:], szi[:], ki_, ALU.mult)
                    p.tt([k1, k2], ["X1"], Y[:, 0, f1g * 4:f1g * 4 + 4, :], t1[:].rearrange("p (f c) -> p f c", c=128),
                         t2[:].rearrange("p (f c) -> p f c", c=128), ALU.subtract)
                    t3, k3 = tr_.next()
                    t4, k4 = tr_.next()
                    p.tt([szrk, kk], [k3], t3[:], szr[:], ki_, ALU.mult)
                    p.tt([szik, kk], [k4], t4[:], szi[:], kr_, ALU.mult)
                    p.tt([k3, k4], ["X1"], Y[:, 1, f1g * 4:f1g * 4 + 4, :], t3[:].rearrange("p (f c) -> p f c", c=128),
                         t4[:].rearrange("p (f c) -> p f c", c=128), ALU.add)
                if HY4_STOP in ("D1", "D1a"):
                    p.release(m)
                    return
                Bt = v_B(X2)
                for f1g in range(16):
                    br, brk = rot.next()
                    bi, bik = rot.next()
                    for q in range(4):
                        f1 = f1g * 4 + q
                        o_ = slice(q * 128, (q + 1) * 128)
                        p.mm(["WI", "X1"], [brk], br[:, o_], WI[:, f1, 0, :], Y[:, 0, f1, :], start=True, stop=False)
                        p.mm(["WI", "X1"], [brk], br[:, o_], WI[:, f1, 1, :], Y[:, 1, f1, :], start=False, stop=True)
                    for q in range(4):
                        f1 = f1g * 4 + q
                        o_ = slice(q * 128, (q + 1) * 128)
                        p.mm(["WI", "X1"], [bik], bi[:, o_], WI[:, f1, 0, :], Y[:, 1, f1, :], start=True, stop=False)
                        p.mm(["WI", "X1"], [bik], bi[:, o_], WI[:, f1, 2, :], Y[:, 0, f1, :], start=False, stop=True)
                    for ri, (bb, bbk) in enumerate(((br, brk), (bi, bik))):
                        k0 = ri * 64 + f1g * 4
                        evac([bbk], ["X2"], Bt[:, :, k0:k0 + 4].rearrange("p c f -> p f c"), bb[:, :].rearrange("p (f c) -> p f c", c=128))
                if HY4_STOP == "D2":
                    p.release(m)
                    return
                B2 = v_B(X1)
                for c0 in range(0, 128, 8):
                    pst, pk = rot.next()
                    pv = pst[:, :].bitcast(BF16)
                    for q in range(8):
                        p.tr(["X2", "identb"], [pk], pv[:, q * 128:(q + 1) * 128], Bt[:, c0 + q, :], identb[:])
                    evac([pk], ["X1"], B2[:, c0:c0 + 8, :], pv[:, :].rearrange("p (c k) -> p c k", k=128))
                if HY4_STOP == "D3":
                    p.release(m)
                    return
                yt, ytk = None, None
                for c0 in range(0, 128, 4):
                    if c0 % 8 == 0:
                        yt, ytk = ytr.next()
                    pst, pk = rot.next()
                    p.mm(["S2", "X1"], [pk], pst[:32, :], S2[:, :], B2[:, c0:c0 + 4, :])
                    evac([pk], [ytk], yt[:, c0 % 8:c0 % 8 + 4, :], pst[:32, :].rearrange("p (c k) -> p c k", k=128))
                    if c0 % 8 == 4:
                        cr = cb_ * 128 + c0 - 4
                        p.dma("sp", [ytk], ["CONV"], CONV[cr:cr + 8, :].rearrange("c (a b) -> a c b", b=128), yt[:])
            if HY4_STOP == "D4":
                p.release(m)
                return
            GW_ = 512
            cvr = Ring(p, "hcv", [128, GW_], F32, 2)
            zpr = Ring(p, "hzp", [128, GW_], F32, 2)
            xgr = Ring(p, "hxg", [128, GW_], F32, 2)
            ybr = Ring(p, "hyb", [128, GW_], BF16, 2)
            for cb_ in range(8):
                for g0 in range(0, n, GW_):
                    cv, cvk = cvr.next()
                    p.dma("sp", ["CONV"], [cvk], cv[:], CONV[cb_ * 128:(cb_ + 1) * 128, g0:g0 + GW_])
                    zp, zpk = zpr.next()
                    p.dma("sp", [skey], [zpk], zp[:], src[cb_ * 128:(cb_ + 1) * 128, toff + g0:toff + g0 + GW_])
                    xg, xgk = xgr.next()
                    xr0 = (1 + o) * 1024 + cb_ * 128
                    p.dma("sp", ["HY"], [xgk], xg[:], HY[xr0:xr0 + 128, toff + g0:toff + g0 + GW_])
                    p.stt([zpk, "hbias", cvk], [cvk], cv[:], zp[:], hbias[:, o, cb_:cb_ + 1], cv[:], ALU.mult, ALU.add)
                    if o == 0:
                        p.tt([cvk, xgk], [cvk], cv[:], cv[:], xg[:], ALU.mult)
                        p.dma("sp", [cvk], ["Z2"], Z2[cb_ * 128:(cb_ + 1) * 128, toff + g0:toff + g0 + GW_], cv[:])
                        yb, ybk = ybr.next()
                        p.act([cvk], [ybk], yb[:], cv[:], AF.Identity)
                        p.dma("sp", [ybk], ["Z2"], Z2B[cb_ * 128:(cb_ + 1) * 128, toff + g0:toff + g0 + GW_], yb[:])
                    else:
                        yb, ybk = ybr.next()
                        p.tt([cvk, xgk], [ybk], yb[:], cv[:], xg[:], ALU.mult)
                        p.dma("sp", [ybk], ["YH"], YH[cb_ * 128:(cb_ + 1) * 128, toff + g0:toff + g0 + GW_], yb[:])
            p.release(md)
        p.release(m)

    def phase_hyena(l):
        m = p.mark()
        cw = p.sb("hcw", [128, 24, 3], F32)
        cb = p.sb("hcb", [128, 24], F32)
        hbias = p.sb("hbias", [128, 2, 8], F32)
        p.dma("sp", [], ["hcw"], cw[:], hy_cwT[l])
        p.dma("sp", [], ["hcb"], cb[:], hy_cbT[l])
        p.dma("sp", [], ["hbias"], hbias[:], hy_biasT[l])
        segs = [(0, NCTX), (NCTX, T)]
        m1 = p.mark()
        T1r = Ring(p, "hT1", [128, T], F32, 2)
        XSr = Ring(p, "hXS", [128, T], F32, 2)
        XBr = Ring(p, "hXB", [128, T], BF16, 2)
        for blk in range(24):
            T1, k1 = T1r.next()
            XS, k2 = XSr.next()
            p.dma("sp", ["PROJ"], [k1], T1[:], PROJ[2048 + blk * 128:2048 + (blk + 1) * 128, :])
            seg_conv(p, XS, T1, lambda j: cw[:, blk, j:j + 1], cb[:, blk:blk + 1], 3, 1, segs, [k1, "hcw", "hcb"], [k2])
            p.dma("sp", [k2], ["HY"], HY[blk * 128:(blk + 1) * 128, :], XS[:])
            if blk < 8:
                xb_, xbk = XBr.next()
                p.act([k2], [xbk], xb_[:], XS[:, :], AF.Identity)
                p.dma("sp", [xbk], ["HY"], HYB[blk * 128:(blk + 1) * 128, :], xb_[:])
        p.release(m1)
        hy_filter(l, NCTX, embT_c, tv_c, FW_c, KF_c)
        hy_data(l, NCTX, 0, FW_c, GW_c, KF_c, hbias)
        if HY_DENSE:
            hy_filter(l, NLAT, embT_l, tv_l, FW_l, KF_l)
            hy_data(l, NLAT, NCTX, FW_l, GW_l, KF_l, hbias)
        else:
            hy4(l, hbias)
        p.release(m)

    def phase_merge(l, xsrc):
        m = p.mark()
        wb = p.sb("wb", [128, 3, 8, D], BF16)
        wo = p.sb("wo", [128, 8, D], BF16)
        for br in range(3):
            p.dma("pool", [], ["wb"], wb[:, br, :, :], fm(w_branch[l, br]))
        p.dma("pool", [], ["wo"], wo[:], fm(w_out[l]))
        ytr = Ring(p, "mytok", [128, 4, D], F32, 1)
        ysr = Ring(p, "mys", [128, 8, 512], BF16, 1)
        yrr = Ring(p, "myr", [128, 8, 512], BF16, 1)
        yhr = Ring(p, "myh", [128, 8, 512], BF16, 1)
        gr = Ring(p, "mg", [128, 24, 512], BF16, 1)
        mtr = Ring(p, "mmt", [128, 8, 512], BF16, 1)
        xr = Ring(p, "mx", [128, 8, 512], F32, 1)
        xnr = Ring(p, "mxn", [128, 8, 512], F32, 1)
        tr_ = Ring(p, "mtm", [128, 512], F32, 4)
        for ti, (t0, tw) in enumerate(TT):
            s_ = 0 if ti == 0 else 1
            nsub = tw // 128
            yt, ytk = ytr.next()
            p.dma("sp", ["YSTOK"], [ytk], yt[:, :nsub, :], YSTOK[t0:t0 + tw, :].rearrange("(a p) d -> p a d", p=128))
            ys, ysk = ysr.next()
            for kc in range(8):
                pst, pk = psr.next()
                for a in range(nsub):
                    p.tr([ytk, "ident"], [pk], pst[:, a * 128:(a + 1) * 128], yt[:, a, kc * 128:(kc + 1) * 128], ident[:])
                p.cp([pk], [ysk], ys[:, kc, :tw], pst[:, :tw], eng=("dve" if kc % 2 else "act")) if False else (
                    p.act([pk], [ysk], ys[:, kc, :tw], pst[:, :tw], AF.Identity) if kc % 2 == 0 else p.cp([pk], [ysk], ys[:, kc, :tw], pst[:, :tw]))
            yr, yrk = yrr.next()
            p.dma("sp", ["YR"], [yrk], yr[:, :, :tw], fm(YR)[:, :, t0:t0 + tw])
            yh, yhk = yhr.next()
            p.dma("sp", ["YH"], [yhk], yh[:, :, :tw], fm(YH)[:, :, t0:t0 + tw])
            g, gk = gr.next()
            p.dma("sp", ["GT"], [gk], g[:, :, :tw], fm(GT)[:, :, t0:t0 + tw])
            xt, xk = xr.next()
            p.dma("sp", ["XT"], [xk], xt[:, :, :tw], xsrc[:, :, t0:t0 + tw])
            mt, mtk = mtr.next()
            for cb_ in range(8):
                pbs = []
                for br, (yb, ybk) in enumerate(((yr, yrk), (yh, yhk), (ys, ysk))):
                    pst, pk = psr.next()
                    for kc in range(8):
                        p.mm(["wb", ybk], [pk], pst[:, :tw], wb[:, br, kc, cb_ * 128:(cb_ + 1) * 128], yb[:, kc, :tw],
                             start=(kc == 0), stop=(kc == 7))
                    pbs.append((pst, pk))
                t1, k1 = tr_.next()
                t2, k2 = tr_.next()
                p.tt([pbs[0][1], gk], [k1], t1[:, :tw], pbs[0][0][:, :tw], g[:, cb_, :tw], ALU.mult)
                p.tt([pbs[1][1], gk], [k2], t2[:, :tw], pbs[1][0][:, :tw], g[:, 8 + cb_, :tw], ALU.mult)
                p.tt([k1, k2], [k1], t1[:, :tw], t1[:, :tw], t2[:, :tw], ALU.add)
                t3, k3 = tr_.next()
                p.tt([pbs[2][1], gk], [k3], t3[:, :tw], pbs[2][0][:, :tw], g[:, 16 + cb_, :tw], ALU.mult)
                p.tt([k1, k3], [mtk], mt[:, cb_, :tw], t1[:, :tw], t3[:, :tw], ALU.add)
            xn, xnk = xnr.next()
            for co in range(8):
                pst, pk = psr.next()
                for cb_ in range(8):
                    p.mm(["wo", mtk], [pk], pst[:, :tw], wo[:, cb_, co * 128:(co + 1) * 128], mt[:, cb_, :tw],
                         start=(cb_ == 0), stop=(cb_ == 7))
                p.stt([pk, "modT", xk], [xnk], xn[:, co, :tw], pst[:, :tw], modT[:, 16 + co, s_:s_ + 1], xt[:, co, :tw], ALU.mult, ALU.add)
            p.dma("sp", [xnk], ["XT"], fm(XT)[:, :, t0:t0 + tw], xn[:, :, :tw])
        p.release(m)

    def phase_ffn(l, HT):
        m = p.mark()
        wv = fm(w_up[l])
        wr = Ring(p, "fwu", [128, 2, 8, 128], BF16, 3)
        str_ = Ring(p, "fst", [128, T], BF16, 2)
        sgr = Ring(p, "fsg", [128, 512], F32, 3)
        for j in range(22):
            wt, wk = wr.next()
            p.dma("pool", [], [wk], wt[:, 0, :, :], wv[:, :, j * 128:(j + 1) * 128])
            p.dma("pool", [], [wk], wt[:, 1, :, :], wv[:, :, D_FF + j * 128:D_FF + (j + 1) * 128])
            st, stk = str_.next()
            for ti, (t0, tw) in enumerate(TT):
                pg, pgk = psr.next()
                pu, puk = psr.next()
                for kc in range(8):
                    p.mm([wk, "HT"], [pgk], pg[:, :tw], wt[:, 0, kc, :], HT[:, kc, t0:t0 + tw], start=(kc == 0), stop=(kc == 7))
                for kc in range(8):
                    p.mm([wk, "HT"], [puk], pu[:, :tw], wt[:, 1, kc, :], HT[:, kc, t0:t0 + tw], start=(kc == 0), stop=(kc == 7))
                sg, sgk = sgr.next()
                p.act([pgk], [sgk], sg[:, :tw], pg[:, :tw], AF.Silu)
                p.tt([sgk, puk], [stk], st[:, t0:t0 + tw], sg[:, :tw], pu[:, :tw], ALU.mult)
            p.dma("sp", [stk], ["AFF"], AFF[j * 128:(j + 1) * 128, :], st[:])
        p.release(m)

    def phase_ffn2(l):
        m = p.mark()
        wd = p.sb("fwd", [128, 22, D], BF16)
        p.dma("pool", [], ["fwd"], wd[:], w_down[l].rearrange("(j p) c -> p j c", p=128))
        ar = Ring(p, "fa", [128, 22, 512], BF16, 2)
        xr = Ring(p, "fx", [128, 8, 512], F32, 2)
        xnr = Ring(p, "fxn", [128, 8, 512], F32, 2)
        av = AFF.rearrange("(j p) t -> p j t", p=128)
        for ti, (t0, tw) in enumerate(TT):
            s_ = 0 if ti == 0 else 1
            at, ak = ar.next()
            p.dma("sp", ["AFF"], [ak], at[:, :, :tw], av[:, :, t0:t0 + tw])
            xt, xk = xr.next()
            p.dma("sp", ["XT"], [xk], xt[:, :, :tw], fm(XT)[:, :, t0:t0 + tw])
            xn, xnk = xnr.next()
            for co in range(8):
                pst, pk = psr.next()
                for j in range(22):
                    p.mm(["fwd", ak], [pk], pst[:, :tw], wd[:, j, co * 128:(co + 1) * 128], at[:, j, :tw], start=(j == 0), stop=(j == 21))
                p.stt([pk, "modT", xk], [xnk], xn[:, co, :tw], pst[:, :tw], modT[:, 40 + co, s_:s_ + 1], xt[:, co, :tw], ALU.mult, ALU.add)
            p.dma("sp", [xnk], ["XT"], fm(XT)[:, :, t0:t0 + tw], xn[:, :, :tw])
        p.release(m)

    def phase_final():
        m = p.mark()
        fn = p.sb("fnw", [128, 8], F32)
        p.dma("sp", [], ["fnw"], fn[:], final_normT[:, :])
        xr = Ring(p, "ox", [128, 8, 512], F32, 2)
        sqr = Ring(p, "osq", [128, 8, 512], BF16, 2)
        rr = Ring(p, "orstd", [128, 512], F32, 2)
        outr = Ring(p, "oo", [128, 8, 512], F32, 2)
        src = fm(XT)
        for ti, (t0, tw) in enumerate(TT):
            if ti == 0:
                continue
            xt, xk = xr.next()
            p.dma("sp", ["XT"], [xk], xt[:], src[:, :, t0:t0 + tw])
            sq, sqk = sqr.next()
            p.act([xk], [sqk], sq[:], xt[:], AF.Square)
            pst, pk = psr.next()
            for kc in range(8):
                p.mm([sqk, "onesb"], [pk], pst[:, :], onesb[:], sq[:, kc, :], start=(kc == 0), stop=(kc == 7))
            rs, rk = rr.next()
            p.ts([pk], [rk], rs[:], pst[:, :], 1.0 / D, EPS, ALU.mult, ALU.add)
            p.act([rk], [rk], rs[:], rs[:], AF.Sqrt)
            p.op("dve", [rk], [rk], lambda e: e.reciprocal(out=rs[:], in_=rs[:]))
            ot, ok_ = outr.next()
            for kc in range(8):
                p.stt([xk, "fnw", rk], [ok_], ot[:, kc, :], xt[:, kc, :], fn[:, kc:kc + 1], rs[:], ALU.mult, ALU.mult)
            p.dma("sp", [ok_], ["OUT"], fm(out)[:, :, t0 - NCTX:t0 - NCTX + tw], ot[:])
        p.wait_all("sp", ["OUT"])
        p.release(m)

    for l in range(nlayers):
        phase_mod(l)
        mk = p.mark()
        HT = p.sb("HT", [128, 8, T], BF16)
        phase_norm(fm(xin if l == 0 else XT), A1, "A1", 0, HT)
        if "HTD" in dbg:
            p.dma("sp", ["HT"], ["HTD"], fm(HTD), HT[:])
        if stop_after == "norm":
            p.release(mk)
            break
        phase_proj(l, HT)
        p.release(mk)
        if stop_after == "proj":
            break
        if stop_after not in ("ssd", "hyena"):
            phase_rglru(l)
        if stop_after == "rglru":
            break
        if stop_after != "hyena":
            phase_ssd(l)
        if stop_after == "ssd":
            break
        phase_hyena(l)
        if stop_after == "hyena":
            break
        phase_merge(l, fm(xin if l == 0 else XT))
        if stop_after == "merge":
            break
        mk = p.mark()
        HT = p.sb("HT", [128, 8, T], BF16)
        phase_norm(fm(XT), A2, "A2", 24, HT)
        phase_ffn(l, HT)
        p.release(mk)
        phase_ffn2(l)
    if stop_after is None:
        phase_final()
    p.barrier()
    p.close()
    print("instructions:", p.ninst)
    return nc


def fmT(v, nchunk):
    return np.ascontiguousarray(np.swapaxes(v.reshape(v.shape[:-1] + (nchunk, 128)), -1, -2))

def hyena_emb(n):
    f = np.float32
    t = np.linspace(0.0, 1.0, n, dtype=f)
    bands = np.linspace(1e-4, 15.0, 16, dtype=f)
    ang = (f(2.0 * np.pi / n) * np.arange(n, dtype=f)[:, None]) * bands[None]
    emb = np.concatenate([t[:, None], np.cos(ang), np.sin(ang)], axis=-1).astype(f)
    return np.ascontiguousarray(emb.T)

def hyena_consts(n):
    import ml_dtypes
    f = np.float32
    t = np.linspace(0.0, 1.0, n, dtype=f)
    bands = np.linspace(1e-4, 15.0, 16, dtype=f)
    ang = (f(2.0 * np.pi / n) * np.arange(n, dtype=f)[:, None]) * bands[None]
    emb = np.concatenate([t[:, None], np.cos(ang), np.sin(ang)], axis=-1).astype(f)
    embT = np.ascontiguousarray(emb.T)
    nt = n // 128
    tv = np.ascontiguousarray(t.reshape(nt, 128).T)
    N = 2 * n
    tt = np.arange(n, dtype=np.int64)
    ff = np.arange(n, dtype=np.int64)
    ph = ((2 * ff[None, :] + 1) * tt[:, None]) % (2 * N)
    angm = np.pi * ph.astype(np.float64) / N
    C = np.cos(angm); S = np.sin(angm)
    def tile_f(M):
        return M.reshape(nt, 128, nt, 128).transpose(2, 1, 0, 3)
    FW = np.concatenate([tile_f(C), tile_f(S)], axis=0).astype(ml_dtypes.bfloat16)
    tw = 512 if n >= 512 else n
    def tile_g(M):
        return (M.T * (2.0 / N)).reshape(nt, 128, n // tw, tw).transpose(2, 1, 0, 3)
    GW = np.concatenate([tile_g(C), tile_g(S)], axis=2).astype(ml_dtypes.bfloat16)
    return embT, tv, np.ascontiguousarray(FW), np.ascontiguousarray(GW)

def hyena_consts4():
    import ml_dtypes
    bf = ml_dtypes.bfloat16
    n, N = 4096, 8192
    t1 = np.arange(32, dtype=np.int64)[:, None]
    f1 = np.arange(64, dtype=np.int64)[None, :]
    g = 2.0 * np.pi * (((2 * f1 + 1) * t1) % 128).astype(np.float64) / 128.0
    S1 = np.concatenate([np.cos(g), -np.sin(g)], axis=1).astype(bf)
    S2 = (np.concatenate([np.cos(g).T, -np.sin(g).T], axis=0) * (2.0 / N)).astype(bf)
    f1v = np.arange(64, dtype=np.int64)[:, None, None]
    t2v = np.arange(128, dtype=np.int64)[None, :, None]
    f2v = np.arange(64, dtype=np.int64)[None, None, :]
    ph = ((2 * f1v + 1) * t2v + 128 * f2v * t2v) % 16384
    phi = 2.0 * np.pi * ph.astype(np.float64) / 16384.0
    Wr = np.cos(phi); Wi = -np.sin(phi)
    WF = np.stack([Wr, Wi, -Wi], axis=0).transpose(2, 1, 0, 3)
    WI = np.stack([Wr, Wi, -Wi], axis=0).transpose(3, 1, 0, 2)
    tvec = np.linspace(0.0, 1.0, n, dtype=np.float32)[None, :]
    return S1, S2, np.ascontiguousarray(WF.astype(bf)), np.ascontiguousarray(WI.astype(bf)), np.ascontiguousarray(tvec)

def prep_shared(inp):
    f = np.float32
    sh = {}
    sh["w_mod"] = inp["w_mod"]
    sh["b_modT"] = fmT(inp["b_mod"], 48)
    sh["norm_mixT"] = fmT(inp["norm_mix"], 8)
    sh["norm_ffnT"] = fmT(inp["norm_ffn"], 8)
    sh["final_normT"] = fmT(inp["final_norm"], 8)
    sh["w_in"] = inp["w_in"]
    sh["rnn_cwT"] = np.ascontiguousarray(inp["rnn_conv_w"].reshape(4, 4, 8, 128).transpose(0, 3, 2, 1))
    sh["rnn_cbT"] = fmT(inp["rnn_conv_b"], 8)
    sh["rnn_aw"] = inp["rnn_gate_a_w"]
    sh["rnn_xw"] = inp["rnn_gate_x_w"]
    sh["rnn_abT"] = np.ascontiguousarray(fmT(inp["rnn_gate_a_b"], 8).transpose(0, 2, 1, 3))
    sh["rnn_xbT"] = np.ascontiguousarray(fmT(inp["rnn_gate_x_b"], 8).transpose(0, 2, 1, 3))
    sh["rnn_lamT"] = np.ascontiguousarray(fmT(inp["rnn_lambda"], 8).transpose(0, 2, 1, 3))
    sh["ident"] = np.eye(128, dtype=f)
    j = np.arange(128)[:, None]; ll = np.arange(128)[None, :]
    sh["masks"] = np.stack([(j <= ll), (j > ll), (j >= ll), (j < ll), np.ones((128, 128), bool)]).astype(f)
    sh["ssm_cwT"] = np.ascontiguousarray(inp["ssm_conv_w"].reshape(4, 4, 12, 128).transpose(0, 3, 2, 1))
    sh["ssm_cbT"] = fmT(inp["ssm_conv_b"], 12)
    sh["ssm_alogT"] = np.ascontiguousarray(inp["ssm_a_log"].reshape(4, 32, 1))
    sh["ssm_dtbT"] = np.ascontiguousarray(inp["ssm_dt_bias"].reshape(4, 32, 1))
    sh["ssm_d"] = inp["ssm_d"]
    sh["ssm_norm"] = inp["ssm_norm"]
    sh["hy_cwT"] = np.ascontiguousarray(inp["hy_short_w"].reshape(4, 3, 24, 128).transpose(0, 3, 2, 1))
    sh["hy_cbT"] = fmT(inp["hy_short_b"], 24)
    sh["hy_biasT"] = np.ascontiguousarray(fmT(inp["hy_bias"], 8).transpose(0, 2, 1, 3))
    sh["hy_w1"] = inp["hy_w1"]; sh["hy_w2"] = inp["hy_w2"]; sh["hy_w3"] = inp["hy_w3"]
    sh["hy_b1T"] = np.ascontiguousarray(inp["hy_b1"].reshape(4, 64, 1))
    sh["hy_b2T"] = np.ascontiguousarray(inp["hy_b2"].reshape(4, 64, 1))
    sh["hy_freqT"] = np.ascontiguousarray(inp["hy_freq"].reshape(4, 64, 1))
    sh["hy_decay"] = inp["hy_decay"]
    emb, tv, FW, GW = hyena_consts(256)
    sh["embT_c"] = emb; sh["tv_c"] = tv; sh["FW_c"] = FW; sh["GW_c"] = GW
    sh["embT_l"] = hyena_emb(4096)
    S1, S2, WF, WI, tvec = hyena_consts4()
    sh["S1"] = S1; sh["S2"] = S2; sh["WF"] = WF; sh["WI"] = WI; sh["tvec"] = tvec
    sh["hy_ndecT"] = np.ascontiguousarray(-fmT(inp["hy_decay"], 32))
    sh["w_branch"] = inp["w_branch"]; sh["w_out"] = inp["w_out"]; sh["w_up"] = inp["w_up"]; sh["w_down"] = inp["w_down"]
    return sh

def prep_core(inp, b):
    xin = np.ascontiguousarray(np.concatenate([inp["ctx"][b].T, inp["x"][b].T], axis=1))
    cc = np.stack([inp["c_ctx"], inp["c"][b]], axis=-1)
    cc = np.ascontiguousarray(cc.reshape(8, 128, 2).transpose(1, 0, 2))
    return {"xin": xin, "cc": cc}


def kernel(**inputs):
    inp = {k: np.asarray(v) for k, v in inputs.items()}
    sh = prep_shared(inp)
    nc = build()
    in_maps = []
    for b in range(8):
        im = dict(sh)
        im.update(prep_core(inp, b))
        in_maps.append(im)
    res = run_bass_kernel_spmd(nc, in_maps, core_ids=list(range(8)))
    out = np.stack([np.ascontiguousarray(np.asarray(r["out"]).T) for r in res.results], axis=0)
    return out.astype(np.float32)
```

```python
import numpy as np
import concourse.bass as bass
import concourse.mybir as mybir
from concourse.bass_utils import run_bass_kernel_spmd

F32 = mybir.dt.float32
BF16 = mybir.dt.bfloat16
ALU = mybir.AluOpType
AF = mybir.ActivationFunctionType
AX = mybir.AxisListType

D = 1024
NCTX = 256
NLAT = 4096
T = NCTX + NLAT
DEPTH = 4
D_IN = 10784
D_FF = 2816
EPS = 1e-6
HY_DENSE = False
HY4_STOP = None
TT = [(0, 256)] + [(256 + 512 * i, 512) for i in range(8)]


class Prog:
    NDMA = 24

    def __init__(self, nc):
        self.nc = nc
        self.engs = {"pe": nc.tensor, "dve": nc.vector, "act": nc.scalar, "pool": nc.gpsimd, "sp": nc.sync}
        self._ctx = []
        self.sem = {}
        self.cnt = {}
        for e in ("pe", "dve", "act", "pool"):
            self.sem[e] = self._enter(nc.semaphore("s_" + e))
            self.cnt[e] = 0
        self.dsem = [self._enter(nc.semaphore("d%d" % i)) for i in range(self.NDMA)]
        self.dcnt = [0] * self.NDMA
        self.dnext = 0
        self.semobj = {}
        for e in ("pe", "dve", "act", "pool"):
            self.semobj[("c", e)] = self.sem[e]
        for i in range(self.NDMA):
            self.semobj[("d", i)] = self.dsem[i]
        self.waited = {e: {} for e in self.engs}
        self.lastw = {}
        self.reads = {}
        self.ninst = 0
        self.uid = 0

    def _enter(self, cm):
        v = cm.__enter__()
        self._ctx.append(cm)
        return v

    def sb(self, name, shape, dt):
        self.uid += 1
        return self._enter(self.nc.sbuf_tensor("%s_%d" % (name, self.uid), list(shape), dt))

    def ps(self, name, shape, dt=F32):
        return self._enter(self.nc.psum_tensor(name, list(shape), dt))

    def mark(self):
        return len(self._ctx)

    def release(self, mark):
        self.barrier()
        while len(self._ctx) > mark:
            cm = self._ctx.pop()
            cm.__exit__(None, None, None)

    def close(self):
        while self._ctx:
            cm = self._ctx.pop()
            cm.__exit__(None, None, None)

    def barrier(self):
        targets = []
        for e in ("pe", "dve", "act", "pool"):
            if self.cnt[e]:
                targets.append((("c", e), self.cnt[e]))
        for i in range(self.NDMA):
            if self.dcnt[i]:
                targets.append((("d", i), self.dcnt[i] * 16))
        for q in ("pe", "dve", "act", "pool", "sp"):
            e = self.engs[q]
            for sk, val in targets:
                if sk == ("c", q):
                    continue
                if self.waited[q].get(sk, 0) < val:
                    e.wait_ge(self.semobj[sk], val)
                    self.waited[q][sk] = val
        self.lastw = {}
        self.reads = {}

    def _deps(self, eng, R, W):
        deps = []
        for r in R:
            lw = self.lastw.get(r)
            if lw is not None:
                deps.append((lw, "raw"))
        for w in W:
            lw = self.lastw.get(w)
            if lw is not None:
                deps.append((lw, "waw"))
            for rd in self.reads.get(w, ()):
                deps.append((rd, "war"))
        own = ("c", eng)
        e = self.engs[eng]
        wt = self.waited[eng]
        need = {}
        for (sk, val), kind in deps:
            if sk == own:
                if eng == "pe":
                    continue
                if kind != "raw":
                    continue
            if wt.get(sk, 0) >= val:
                continue
            if need.get(sk, 0) < val:
                need[sk] = val
        for sk, val in need.items():
            e.wait_ge(self.semobj[sk], val)
            wt[sk] = val

    def _commit(self, tick, R, W):
        for w in W:
            self.lastw[w] = tick
            self.reads[w] = []
        for r in R:
            if r in W:
                continue
            lst = self.reads.setdefault(r, [])
            lst.append(tick)
            if len(lst) > 48:
                best = {}
                for sk, v in lst:
                    if best.get(sk, 0) < v:
                        best[sk] = v
                self.reads[r] = list(best.items())

    def op(self, eng, R, W, fn):
        self._deps(eng, R, W)
        ins = fn(self.engs[eng])
        self.cnt[eng] += 1
        ins.then_inc(self.sem[eng], 1)
        tick = (("c", eng), self.cnt[eng])
        self._commit(tick, R, W)
        self.ninst += 1
        return tick

    def dma(self, q, R, W, out, in_, **kw):
        i = self.dnext
        self.dnext = (self.dnext + 1) % self.NDMA
        sk = ("d", i)
        e = self.engs[q]
        prev = self.dcnt[i] * 16
        if prev and self.waited[q].get(sk, 0) < prev:
            e.wait_ge(self.dsem[i], prev)
            self.waited[q][sk] = prev
        self._deps(q, R, W)
        ins = e.dma_start(out=out, in_=in_, **kw)
        self.dcnt[i] += 1
        ins.then_inc(self.dsem[i], 16)
        tick = (sk, self.dcnt[i] * 16)
        self._commit(tick, R, W)
        self.ninst += 1
        return tick

    def wait_all(self, eng, keys):
        e = self.engs[eng]
        for k in keys:
            lw = self.lastw.get(k)
            if lw is None:
                continue
            sk, val = lw
            if self.waited[eng].get(sk, 0) < val:
                e.wait_ge(self.semobj[sk], val)
                self.waited[eng][sk] = val

    def mm(self, R, W, out, lhsT, rhs, start=True, stop=True):
        return self.op("pe", R, W, lambda e: e.matmul(out, lhsT, rhs, start=start, stop=stop))

    def tr(self, R, W, out, in_, ident):
        return self.op("pe", R, W, lambda e: e.transpose(out, in_, ident))

    def act(self, R, W, out, in_, func, **kw):
        return self.op("act", R, W, lambda e: e.activation(out=out, in_=in_, func=func, **kw))

    def tt(self, R, W, out, in0, in1, op, eng="dve"):
        return self.op(eng, R, W, lambda e: e.tensor_tensor(out=out, in0=in0, in1=in1, op=op))

    def ts(self, R, W, out, in0, s1, s2, op0, op1=None, eng="dve"):
        if op1 is None:
            return self.op(eng, R, W, lambda e: e.tensor_scalar(out=out, in0=in0, scalar1=s1, scalar2=None, op0=op0))
        return self.op(eng, R, W, lambda e: e.tensor_scalar(out=out, in0=in0, scalar1=s1, scalar2=s2, op0=op0, op1=op1))

    def stt(self, R, W, out, in0, scalar, in1, op0, op1, eng="dve"):
        return self.op(eng, R, W, lambda e: e.scalar_tensor_tensor(out=out, in0=in0, scalar=scalar, in1=in1, op0=op0, op1=op1))

    def cp(self, R, W, out, in_, eng="dve"):
        return self.op(eng, R, W, lambda e: e.tensor_copy(out=out, in_=in_))


class Ring:
    def __init__(self, p, name, shape, dt, n):
        self.tiles = [p.sb("%s%d" % (name, i), shape, dt) for i in range(n)]
        self.keys = ["%s#%d_%d" % (name, p.uid, i) for i in range(n)]
        self.i = 0

    def next(self):
        t, k = self.tiles[self.i], self.keys[self.i]
        self.i = (self.i + 1) % len(self.tiles)
        return t, k


class PsRing:
    def __init__(self, p, n=8):
        self.tiles = [p.ps("psb%d" % i, [128, 512]) for i in range(n)]
        self.keys = ["psb%d" % i for i in range(n)]
        self.i = 0

    def next(self):
        t, k = self.tiles[self.i], self.keys[self.i]
        self.i = (self.i + 1) % len(self.tiles)
        return t, k


def fm(ap2d):
    return ap2d.rearrange("(kc p) t -> p kc t", p=128)


def seg_conv(p, out, in_, wv, bv, ntap, left, segs, Rk, Wk):
    p.ts(Rk, Wk, out[:, :], in_[:, :], wv(left), bv, ALU.mult, ALU.add)
    for j in range(ntap):
        d = j - left
        if d == 0:
            continue
        for (s0, s1) in segs:
            lo = max(s0, s0 - d)
            hi = min(s1, s1 - d)
            p.stt(Rk + Wk, Wk, out[:, lo:hi], in_[:, lo + d:hi + d], wv(j), out[:, lo:hi], ALU.mult, ALU.add)


def build(nlayers=DEPTH, stop_after=None, dbg=()):
    nc = bass.Bass("TRN2", target_bir_lowering=False)

    def din(name, shape, dt=F32):
        return nc.dram_tensor(name, list(shape), dt, kind="ExternalInput").ap()

    def dscr(name, shape, dt=F32):
        kind = "ExternalOutput" if name in dbg else "Internal"
        return nc.dram_tensor(name, list(shape), dt, kind=kind).ap()

    xin = din("xin", [D, T])
    cc = din("cc", [128, 8, 2])
    w_mod = din("w_mod", [DEPTH, D, 6 * D])
    b_modT = din("b_modT", [DEPTH, 128, 48])
    norm_mixT = din("norm_mixT", [DEPTH, 128, 8])
    norm_ffnT = din("norm_ffnT", [DEPTH, 128, 8])
    final_normT = din("final_normT", [128, 8])
    w_in = din("w_in", [DEPTH, D, D_IN])
    rnn_cwT = din("rnn_cwT", [DEPTH, 128, 8, 4])
    rnn_cbT = din("rnn_cbT", [DEPTH, 128, 8])
    rnn_aw = din("rnn_aw", [DEPTH, 2, 8, 128, 128])
    rnn_xw = din("rnn_xw", [DEPTH, 2, 8, 128, 128])
    rnn_abT = din("rnn_abT", [DEPTH, 128, 2, 8])
    rnn_xbT = din("rnn_xbT", [DEPTH, 128, 2, 8])
    rnn_lamT = din("rnn_lamT", [DEPTH, 128, 2, 8])
    ident_d = din("ident", [128, 128])
    masks_d = din("masks", [5, 128, 128])
    ssm_cwT = din("ssm_cwT", [DEPTH, 128, 12, 4])
    ssm_cbT = din("ssm_cbT", [DEPTH, 128, 12])
    ssm_alogT = din("ssm_alogT", [DEPTH, 32, 1])
    ssm_dtbT = din("ssm_dtbT", [DEPTH, 32, 1])
    ssm_d = din("ssm_d", [DEPTH, 16])
    ssm_norm = din("ssm_norm", [DEPTH, 1024])
    hy_cwT = din("hy_cwT", [DEPTH, 128, 24, 3])
    hy_cbT = din("hy_cbT", [DEPTH, 128, 24])
    hy_biasT = din("hy_biasT", [DEPTH, 128, 2, 8])
    hy_w1 = din("hy_w1", [DEPTH, 33, 64])
    hy_w2 = din("hy_w2", [DEPTH, 64, 64])
    hy_w3 = din("hy_w3", [DEPTH, 64, 4096])
    hy_b1T = din("hy_b1T", [DEPTH, 64, 1])
    hy_b2T = din("hy_b2T", [DEPTH, 64, 1])
    hy_freqT = din("hy_freqT", [DEPTH, 64, 1])
    hy_decay = din("hy_decay", [DEPTH, 4096])
    embT_l = din("embT_l", [33, NLAT])
    embT_c = din("embT_c", [33, NCTX])
    tv_l = din("tv_l", [128, NLAT // 128]) if HY_DENSE else None
    tv_c = din("tv_c", [128, NCTX // 128])
    FW_l = din("FW_l", [64, 128, 32, 128], BF16) if HY_DENSE else None
    GW_l = din("GW_l", [8, 128, 64, 512], BF16) if HY_DENSE else None
    FW_c = din("FW_c", [4, 128, 2, 128], BF16)
    GW_c = din("GW_c", [1, 128, 4, 256], BF16)
    S1_d = din("S1", [32, 128], BF16)
    S2_d = din("S2", [128, 32], BF16)
    WF_d = din("WF", [128, 64, 3, 64], BF16)
    WI_d = din("WI", [64, 64, 3, 128], BF16)
    tvec = din("tvec", [1, NLAT])
    hy_ndecT = din("hy_ndecT", [DEPTH, 128, 32])
    w_branch = din("w_branch", [DEPTH, 3, D, D])
    w_out = din("w_out", [DEPTH, D, D])
    w_up = din("w_up", [DEPTH, D, 2 * D_FF])
    w_down = din("w_down", [DEPTH, D_FF, D])
    out = nc.dram_tensor("out", [D, NLAT], F32, kind="ExternalOutput").ap()

    XT = dscr("XT", [D, T])
    PROJ = dscr("PROJ", [5120, T])
    SSMP = dscr("SSMP", [2592, T])
    GT = dscr("GT", [3072, T], BF16)
    YR = dscr("YR", [D, T], BF16)
    HTD = dscr("HTD", [D, T], BF16)
    HP = dscr("HP", [2, 34, 128, 1024], BF16)
    YSTOK = dscr("YSTOK", [T, 1024])
    HY = dscr("HY", [3072, T])
    Z2 = dscr("Z2", [D, T])
    YH = dscr("YH", [D, T], BF16)
    KF_l = dscr("KF_l", [2, 32, 128, 2, 1024]) if HY_DENSE else None
    KF_c = dscr("KF_c", [2, 2, 128, 2, 1024])
    AFF = dscr("AFF", [D_FF, T], BF16)
    HFB = dscr("HFB", [2, 2048, NLAT], BF16)
    HYB = dscr("HYB", [D, T], BF16)
    Z2B = dscr("Z2B", [D, T], BF16)
    KF2 = dscr("KF2", [2, 16, 64, 2, 64, 64])
    CONV = dscr("CONV", [D, NLAT])

    p = Prog(nc)
    psr = PsRing(p)

    ident = p.sb("ident", [128, 128], F32)
    onesb = p.sb("onesb", [128, 128], BF16)
    modT = p.sb("modT", [128, 48, 2], F32)
    A1 = p.sb("A1", [128, 8, 2], F32)
    A2 = p.sb("A2", [128, 8, 2], F32)
    p.dma("sp", [], ["ident"], ident[:], ident_d[:, :])
    masks = p.sb("masks", [128, 5, 128], F32)
    p.dma("sp", [], ["masks"], masks[:], masks_d.rearrange("m p l -> p m l"))
    LE, GT_, GE, LT, ONES = 0, 1, 2, 3, 4
    p.op("dve", [], ["onesb"], lambda e: e.memset(onesb[:], 1.0))

    def phase_mod(l):
        m = p.mark()
        cs = p.sb("cs", [128, 8, 2], F32)
        bm = p.sb("bm", [128, 48], F32)
        nm = p.sb("nm", [128, 8], F32)
        nf = p.sb("nf", [128, 8], F32)
        p.dma("sp", [], ["cs"], cs[:], cc[:, :, :])
        p.dma("sp", [], ["bm"], bm[:], b_modT[l])
        p.dma("sp", [], ["nm"], nm[:], norm_mixT[l])
        p.dma("sp", [], ["nf"], nf[:], norm_ffnT[l])
        p.act(["cs"], ["cs"], cs[:], cs[:], AF.Silu)
        wring = Ring(p, "wmod", [128, 8, 512], F32, 2)
        pst, pk = psr.next()
        wv = fm(w_mod[l])
        for cg in range(12):
            wt, wk = wring.next()
            p.dma("sp", [], [wk], wt[:], wv[:, :, cg * 512:(cg + 1) * 512])
            for j4 in range(4):
                j = cg * 4 + j4
                for kc in range(8):
                    p.mm([wk, "cs"], [pk], pst[:, j * 2:(j + 1) * 2], wt[:, kc, j4 * 128:(j4 + 1) * 128], cs[:, kc, :],
                         start=(kc == 0), stop=(kc == 7))
        p.tt([pk, "bm"], ["modT"], modT[:], pst[:, 0:96].rearrange("p (j s) -> p j s", s=2),
             bm[:].unsqueeze(2).to_broadcast([128, 48, 2]), ALU.add)
        for (Aq, key, nrm, j0) in ((A1, "A1", nm, 8), (A2, "A2", nf, 32)):
            p.ts(["modT"], [key], Aq[:], modT[:, j0:j0 + 8, :], 1.0, None, ALU.add)
            p.tt([key, "nm", "nf"], [key], Aq[:], Aq[:], nrm[:].unsqueeze(2).to_broadcast([128, 8, 2]), ALU.mult)
        p.release(m)

    def phase_norm(src, A, Akey, bj0, HT):
        m = p.mark()
        xr = Ring(p, "xn", [128, 8, 512], F32, 2)
        sqr = Ring(p, "sq", [128, 8, 512], BF16, 2)
        rr = Ring(p, "rstd", [128, 512], F32, 2)
        tr_ = Ring(p, "tmpn", [128, 512], F32, 3)
        for ti, (t0, tw) in enumerate(TT):
            s = 0 if ti == 0 else 1
            xt, xk = xr.next()
            p.dma("sp", ["XT"], [xk], xt[:, :, :tw], src[:, :, t0:t0 + tw])
            sq, sqk = sqr.next()
            p.act([xk], [sqk], sq[:, :, :tw], xt[:, :, :tw], AF.Square)
            pst, pk = psr.next()
            for kc in range(8):
                p.mm([sqk, "onesb"], [pk], pst[:, :tw], onesb[:], sq[:, kc, :tw], start=(kc == 0), stop=(kc == 7))
            rs, rk = rr.next()
            p.ts([pk], [rk], rs[:, :tw], pst[:, :tw], 1.0 / D, EPS, ALU.mult, ALU.add)
            p.act([rk], [rk], rs[:, :tw], rs[:, :tw], AF.Sqrt)
            p.op("dve", [rk], [rk], lambda e: e.reciprocal(out=rs[:, :tw], in_=rs[:, :tw]))
            for kc in range(8):
                tm, tk = tr_.next()
                p.tt([xk, rk], [tk], tm[:, :tw], xt[:, kc, :tw], rs[:, :tw], ALU.mult)
                p.act([tk, Akey, "modT"], ["HT"], HT[:, kc, t0:t0 + tw], tm[:, :tw], AF.Identity,
                      scale=A[:, kc, s:s + 1], bias=modT[:, bj0 + kc, s:s + 1])
        p.release(m)

    def ht_rhs(HT, kc, ti, ssd):
        t0, tw = TT[ti]
        if ti == 0 or not ssd:
            return HT[:, kc, t0:t0 + tw]
        i = ti - 1
        return HT[:, kc, NCTX:].rearrange("p (r c) -> p c r", c=64)[:, 8 * i:8 * i + 8, :]

    def ht_lhs(HT, kc, q):
        if q < 2:
            return HT[:, kc, q * 128:(q + 1) * 128]
        c2 = q - 2
        return HT[:, kc, NCTX:].rearrange("p (r c) -> p c r", c=64)[:, 2 * c2:2 * c2 + 2, :]

    def phase_proj(l, HT):
        m = p.mark()
        wring = Ring(p, "win", [128, 8, 512], BF16, 3)
        stg = Ring(p, "pstg", [128, T], F32, 2)
        stgb = Ring(p, "pstgb", [128, T], BF16, 2)
        wv = fm(w_in[l])
        ev = [0]

        def load_w(c0, cw):
            wt, wk = wring.next()
            p.dma("pool", [], [wk], wt[:, :, :cw], wv[:, :, c0:c0 + cw])
            return wt, wk

        def fm_group(c0, dst, dst_row0, dkey, ssd=False, gate=False, ncols=512):
            wt, wk = load_w(c0, ncols)
            for j in range((ncols + 127) // 128):
                cw_ = min(128, ncols - j * 128)
                st, stkey = (stgb if gate else stg).next()
                for ti, (t0, tw) in enumerate(TT):
                    pst, pk = psr.next()
                    for kc in range(8):
                        p.mm([wk, "HT"], [pk], pst[:cw_, :tw], wt[:, kc, j * 128:j * 128 + cw_], ht_rhs(HT, kc, ti, ssd),
                             start=(kc == 0), stop=(kc == 7))
                    if gate:
                        p.act([pk], [stkey], st[:cw_, t0:t0 + tw], pst[:cw_, :tw], AF.Sigmoid)
                    else:
                        ev[0] ^= 1
                        if ev[0]:
                            p.act([pk], [stkey], st[:cw_, t0:t0 + tw], pst[:cw_, :tw], AF.Identity)
                        else:
                            p.cp([pk], [stkey], st[:cw_, t0:t0 + tw], pst[:cw_, :tw])
                r0 = dst_row0 + j * 128
                p.dma("sp", [stkey], [dkey], dst[r0:r0 + cw_, :], st[:cw_, :])

        for g in range(10):
            fm_group(g * 512, PROJ, g * 512, "PROJ")
        for g in range(5):
            fm_group(5120 + g * 512, SSMP, g * 512, "SSMP", ssd=True)
        fm_group(7680, SSMP, 2560, "SSMP", ssd=True, ncols=32)
        for g in range(6):
            fm_group(7712 + g * 512, GT, g * 512, "GT", gate=True)
        p.release(m)

    def phase_rglru(l):
        m = p.mark()
        cw = p.sb("rcw", [128, 8, 4], F32)
        cb = p.sb("rcb", [128, 8], F32)
        ab = p.sb("rab", [128, 2, 8], F32)
        xb = p.sb("rxb", [128, 2, 8], F32)
        cA = p.sb("rcA", [128, 2, 8], F32)
        c2A = p.sb("rc2A", [128, 2, 8], F32)
        p.dma("sp", [], ["rcw"], cw[:], rnn_cwT[l])
        p.dma("sp", [], ["rcb"], cb[:], rnn_cbT[l])
        p.dma("sp", [], ["rab"], ab[:], rnn_abT[l])
        p.dma("sp", [], ["rxb"], xb[:], rnn_xbT[l])
        p.dma("sp", [], ["rcA"], cA[:], rnn_lamT[l])
        p.act(["rcA"], ["rcA"], cA[:], cA[:], AF.Exp, scale=-1.0)
        p.act(["rcA"], ["rcA"], cA[:], cA[:], AF.Ln, bias=1.0)
        p.ts(["rcA"], ["rc2A"], c2A[:], cA[:], -16.0, None, ALU.mult)
        p.ts(["rcA"], ["rcA"], cA[:], cA[:], -8.0, None, ALU.mult)
        T1 = p.sb("rT1", [128, T], F32)
        U = p.sb("rU", [128, T], F32)
        Af = p.sb("rA", [128, T], F32)
        Gf = p.sb("rG", [128, T], F32)
        HS = p.sb("rHS", [128, T], F32)
        Y = p.sb("rY", [128, T], BF16)
        gw = Ring(p, "rgw", [128, 4, 128], F32, 2)
        rr = Ring(p, "rr", [128, 512], F32, 2)
        ir = Ring(p, "ri", [128, 512], F32, 2)
        sr = Ring(p, "rs", [128, 512], F32, 2)
        segs = [(0, NCTX), (NCTX, T)]

        def rev(t, lo, hi):
            a = t[:, lo:hi]
            return bass.AP(a.tensor, a.offset + (hi - lo - 1), [list(a.ap[0]), [-1, hi - lo]])

        for hb in range(8):
            p.dma("sp", ["PROJ"], ["rT1"], T1[:], PROJ[hb * 128:(hb + 1) * 128, :])
            seg_conv(p, U, T1, lambda j: cw[:, hb, j:j + 1], cb[:, hb:hb + 1], 4, 2, segs, ["rT1", "rcw", "rcb"], ["rU"])
            g4, gk = gw.next()
            for d in range(2):
                p.dma("sp", [], [gk], g4[:, d, :], rnn_aw[l, d, hb])
                p.dma("sp", [], [gk], g4[:, 2 + d, :], rnn_xw[l, d, hb])
            for d in range(2):
                for ti, (t0, tw) in enumerate(TT):
                    pa, pak = psr.next()
                    px, pxk = psr.next()
                    p.mm([gk, "rU"], [pak], pa[:, :tw], g4[:, d, :], U[:, t0:t0 + tw])
                    p.mm([gk, "rU"], [pxk], px[:, :tw], g4[:, 2 + d, :], U[:, t0:t0 + tw])
                    r_, rk = rr.next()
                    i_, ik = ir.next()
                    s_, sk_ = sr.next()
                    p.act([pak, "rab"], [rk], r_[:, :tw], pa[:, :tw], AF.Sigmoid, bias=ab[:, d, hb:hb + 1])
                    p.act([pxk, "rxb"], [ik], i_[:, :tw], px[:, :tw], AF.Sigmoid, bias=xb[:, d, hb:hb + 1])
                    p.act([rk, "rc2A"], [sk_], s_[:, :tw], r_[:, :tw], AF.Exp, scale=c2A[:, d, hb:hb + 1])
                    p.act([rk, "rcA"], ["rA"], Af[:, t0:t0 + tw], r_[:, :tw], AF.Exp, scale=cA[:, d, hb:hb + 1])
                    p.act([sk_], [sk_], s_[:, :tw], s_[:, :tw], AF.Sqrt, scale=-1.0, bias=1.0)
                    p.tt([ik, sk_], [ik], i_[:, :tw], i_[:, :tw], s_[:, :tw], ALU.mult)
                    p.tt([ik, "rU"], ["rG"], Gf[:, t0:t0 + tw], i_[:, :tw], U[:, t0:t0 + tw], ALU.mult)
                if d == 0:
                    p.op("dve", ["rA", "rG"], ["rHS"], lambda e: e.tensor_tensor_scan(
                        out=HS[:, :], data0=Af[:, :], data1=Gf[:, :], initial=0.0, op0=ALU.mult, op1=ALU.add))
                else:
                    p.op("dve", ["rA", "rG"], ["rT1"], lambda e: e.tensor_tensor_scan(
                        out=rev(T1, 0, NCTX), data0=rev(Af, 0, NCTX), data1=rev(Gf, 0, NCTX), initial=0.0,
                        op0=ALU.mult, op1=ALU.add))
                    p.op("dve", ["rA", "rG", "rT1"], ["rT1"], lambda e: e.tensor_tensor_scan(
                        out=rev(T1, NCTX, T), data0=rev(Af, NCTX, T), data1=rev(Gf, NCTX, T), initial=T1[:, 0:1],
                        op0=ALU.mult, op1=ALU.add))
                    p.tt(["rHS", "rT1"], ["rHS"], HS[:, :], HS[:, :], T1[:, :], ALU.add)
            p.dma("sp", ["PROJ"], ["rT1"], T1[:], PROJ[1024 + hb * 128:1024 + (hb + 1) * 128, :])
            p.tt(["rT1"], ["rG"], Gf[:, :], T1[:, :], T1[:, :], ALU.mult)
            p.ts(["rG"], ["rG"], Gf[:, :], Gf[:, :], 0.044715, 1.0, ALU.mult, ALU.add)
            p.tt(["rG", "rT1"], ["rG"], Gf[:, :], Gf[:, :], T1[:, :], ALU.mult)
            p.act(["rG"], ["rG"], Gf[:, :], Gf[:, :], AF.Sigmoid, scale=1.5957691216057308)
            p.tt(["rG", "rT1"], ["rG"], Gf[:, :], Gf[:, :], T1[:, :], ALU.mult)
            p.tt(["rG", "rHS"], ["rY"], Y[:, :], Gf[:, :], HS[:, :], ALU.mult)
            p.dma("sp", ["rY"], ["YR"], YR[hb * 128:(hb + 1) * 128, :], Y[:])
        p.release(m)

    def phase_ssd(l):
        m = p.mark()
        psb = psr.tiles
        pkk = psr.keys
        cw = p.sb("scw", [128, 12, 4], F32)
        cb = p.sb("scb", [128, 12], F32)
        p.dma("sp", [], ["scw"], cw[:], ssm_cwT[l])
        p.dma("sp", [], ["scb"], cb[:], ssm_cbT[l])
        XTOK = p.sb("XTOK", [128, 34, 1024], BF16)
        BT = p.sb("BT", [128, 2, T], BF16)
        CT = p.sb("CT", [128, 2, T], BF16)
        DT_tok = p.sb("DT_tok", [128, 34, 32], F32)
        ADT_tok = p.sb("ADT_tok", [128, 34, 32], F32)
        EA = p.sb("EA", [128, 34, 32], F32)
        DTE = p.sb("DTE", [128, 34, 32], F32)
        DEC = p.sb("DEC", [128, 34, 32], F32)
        mark_a = p.mark()
        BTOK = p.sb("BTOK", [128, 34, 256], BF16)
        DTD = p.sb("DTD", [128, 34, 32], F32)
        segs = [(0, NCTX), (NCTX, T)]
        m1 = p.mark()
        T1r = Ring(p, "sT1", [128, T], F32, 2)
        XSr = Ring(p, "sXS", [128, T], F32, 1)
        for blk in range(12):
            T1, t1k = T1r.next()
            XS, xsk = XSr.next()
            p.dma("sp", ["SSMP"], [t1k], T1[:], SSMP[1024 + blk * 128:1024 + (blk + 1) * 128, :])
            seg_conv(p, XS, T1, lambda j: cw[:, blk, j:j + 1], cb[:, blk:blk + 1], 4, 2, segs, [t1k, "scw", "scb"], [xsk])
            p.act([xsk], [xsk], XS[:, :], XS[:, :], AF.Silu)
            if blk >= 8:
                g = (blk - 8) % 2
                dstT, dk = (BT, "BT") if blk < 10 else (CT, "CT")
                p.cp([xsk], [dk], dstT[:, g, :], XS[:, :])
            if blk < 10:
                for q0 in range(0, 34, 4):
                    nq = min(4, 34 - q0)
                    pst, pk = psr.next()
                    for qi in range(nq):
                        q = q0 + qi
                        p.tr([xsk, "ident"], [pk], pst[:, qi * 128:(qi + 1) * 128], XS[:, q * 128:(q + 1) * 128], ident[:])
                    if blk < 8:
                        p.cp([pk], ["XTOK"], XTOK[:, q0:q0 + nq, blk * 128:(blk + 1) * 128],
                             pst[:, :nq * 128].rearrange("p (q c) -> p q c", c=128), eng=("dve" if (q0 // 4) % 2 else "act") if False else "dve")
                    else:
                        g = blk - 8
                        p.cp([pk], ["BTOK"], BTOK[:, q0:q0 + nq, g * 128:(g + 1) * 128],
                             pst[:, :nq * 128].rearrange("p (q c) -> p q c", c=128))
        p.release(m1)
        m2 = p.mark()
        DTF = p.sb("DTF", [32, T], F32)
        ADF = p.sb("ADF", [32, T], F32)
        dtb = p.sb("dtb", [32, 1], F32)
        aneg = p.sb("aneg", [32, 1], F32)
        p.dma("sp", ["SSMP"], ["DTF"], DTF[:], SSMP[2560:2592, :])
        p.dma("sp", [], ["dtb"], dtb[:], ssm_dtbT[l])
        p.dma("sp", [], ["aneg"], aneg[:], ssm_alogT[l])
        p.act(["aneg"], ["aneg"], aneg[:], aneg[:], AF.Exp)
        p.ts(["aneg"], ["aneg"], aneg[:], aneg[:], -1.0, None, ALU.mult)
        p.act(["DTF", "dtb"], ["DTF"], DTF[:, :], DTF[:, :], AF.Exp, bias=dtb[:, 0:1])
        p.act(["DTF"], ["DTF"], DTF[:, :], DTF[:, :], AF.Ln, bias=1.0)
        p.ts(["DTF", "aneg"], ["ADF"], ADF[:, :], DTF[:, :], aneg[:, 0:1], None, ALU.mult)
        for (src, sk_, dst, dk) in ((DTF, "DTF", DT_tok, "DT_tok"), (ADF, "ADF", ADT_tok, "ADT_tok")):
            for q0 in range(0, 34, 16):
                nq = min(16, 34 - q0)
                pst, pk = psr.next()
                for qi in range(nq):
                    q = q0 + qi
                    p.tr([sk_, "ident"], [pk], pst[:, qi * 32:(qi + 1) * 32], src[:, q * 128:(q + 1) * 128], ident[:32, :32])
                p.cp([pk], [dk], dst[:, q0:q0 + nq, :], pst[:, :nq * 32].rearrange("p (q c) -> p q c", c=32))
        for d in range(2):
            for cg in range(2):
                rhs = ADT_tok[:, cg * 17:(cg + 1) * 17, d * 16:(d + 1) * 16]
                for (mk_, dst, dk) in (((LE, GE)[d], EA, "EA"), ((GT_, LT)[d], DTE, "DTE"), (ONES, DEC, "DEC")):
                    pst, pk = psr.next()
                    p.mm(["masks", "ADT_tok"], [pk], pst[:, :272], masks[:, mk_, :], rhs)
                    p.act([pk], [dk], dst[:, cg * 17:(cg + 1) * 17, d * 16:(d + 1) * 16],
                          pst[:, :272].rearrange("p (q c) -> p q c", c=16), AF.Exp)
        p.tt(["DT_tok", "DTE"], ["DTD"], DTD[:], DT_tok[:], DTE[:], ALU.mult)
        p.release(m2)
        H = p.sb("Hst", [128, 1024], F32)
        hbr = Ring(p, "Hb", [128, 1024], BF16, 3)
        xsr = Ring(p, "xsd", [128, 1024], BF16, 3)
        for d in range(2):
            order = list(range(34)) if d == 0 else [1, 0] + list(range(33, 1, -1))
            p.op("dve", [], ["Hst"], lambda e: e.memset(H[:], 0.0))
            for q in order:
                hb_, hbk = hbr.next()
                p.act(["Hst"], [hbk], hb_[:], H[:], AF.Identity)
                p.dma("sp", [hbk], ["HP"], HP[d, q], hb_[:])
                xs, xk = xsr.next()
                p.tt(["XTOK", "DTD"], [xk], xs[:].rearrange("p (h c) -> p h c", c=64),
                     XTOK[:, q, :].rearrange("p (h c) -> p h c", c=64),
                     DTD[:, q, d * 16:(d + 1) * 16].unsqueeze(2).to_broadcast([128, 16, 64]), ALU.mult)
                pss = []
                for g in range(2):
                    pst, pk = psr.next()
                    p.mm(["BTOK", xk], [pk], pst[:, :], BTOK[:, q, g * 128:(g + 1) * 128], xs[:, g * 512:(g + 1) * 512])
                    pss.append((pst, pk))
                p.tt(["Hst", "DEC"], ["Hst"], H[:].rearrange("p (h c) -> p h c", c=64), H[:].rearrange("p (h c) -> p h c", c=64),
                     DEC[:, q, d * 16:(d + 1) * 16].unsqueeze(2).to_broadcast([128, 16, 64]), ALU.mult)
                for g in range(2):
                    p.tt(["Hst", pss[g][1]], ["Hst"], H[:, g * 512:(g + 1) * 512], H[:, g * 512:(g + 1) * 512], pss[g][0][:, :], ALU.add)
        p.release(mark_a)
        dsk = p.sb("dsk", [128, 16], F32)
        nw = p.sb("snw", [128, 1024], F32)
        p.dma("sp", [], ["dsk"], dsk[:], ssm_d[l:l + 1, :].to_broadcast([128, 16]))
        p.dma("sp", [], ["snw"], nw[:], ssm_norm[l:l + 1, :].to_broadcast([128, 1024]))
        hpr = Ring(p, "hp", [128, 2, 1024], BF16, 2)
        cbmr = Ring(p, "cbm", [128, 2, 256], F32, 2)
        rsr = Ring(p, "rseg", [128, 16, 128], F32, 1)
        er = Ring(p, "eseg", [128, 16, 128], F32, 1)
        mr = Ring(p, "mseg", [128, 16, 128], BF16, 2)
        xdr = Ring(p, "xdt", [128, 1024], BF16, 2)
        accr = Ring(p, "acc", [128, 1024], F32, 2)
        tmpr = Ring(p, "stmp", [128, 1024], F32, 1)
        zfr = Ring(p, "zf", [128, 8, 128], F32, 1)
        szr = Ring(p, "sz", [128, 1024], F32, 2)
        ysr = Ring(p, "ys", [128, 1024], F32, 2)
        ssr = Ring(p, "ssq", [128, 2], F32, 2)
        for q in range(34):
            hp, hpk = hpr.next()
            for d in range(2):
                p.dma("sp", ["HP"], [hpk], hp[:, d, :], HP[d, q])
            zf, zfk = zfr.next()
            p.dma("sp", ["SSMP"], [zfk], zf[:], SSMP[0:1024, q * 128:(q + 1) * 128].rearrange("(b c) t -> c b t", c=128))
            for g in range(2):
                p.mm(["BT", "CT"], [pkk[0]], psb[0][:, g * 128:(g + 1) * 128], BT[:, g, q * 128:(q + 1) * 128], CT[:, g, q * 128:(q + 1) * 128])
            cbm, cbk = cbmr.next()
            for d in range(2):
                p.tt([pkk[0], "masks"], [cbk], cbm[:, d, :].rearrange("p (g c) -> p g c", c=128),
                     psb[0][:, 0:256].rearrange("p (g c) -> p g c", c=128),
                     masks[:, (LE, GE)[d], :].unsqueeze(1).to_broadcast([128, 2, 128]), ALU.mult)
            acc, acck = accr.next()
            for d in range(2):
                rs, rsk = rsr.next()
                p.tt(["ADT_tok", "masks"], [rsk], rs[:], ADT_tok[:, q, d * 16:(d + 1) * 16].unsqueeze(2).to_broadcast([128, 16, 128]),
                     masks[:, (LE, GE)[d], :].unsqueeze(1).to_broadcast([128, 16, 128]), ALU.mult)
                es, esk = er.next()
                for i in range(4):
                    p.mm(["masks", rsk], [pkk[1 + i]], psb[1 + i][:, :], masks[:, (GT_, LT)[d], :],
                         rs[:, 4 * i:4 * i + 4, :])
                    p.act([pkk[1 + i]], [esk], es[:, 4 * i:4 * i + 4, :], psb[1 + i][:, :].rearrange("p (h c) -> p h c", c=128), AF.Exp)
                ms, msk = mr.next()
                p.tt([esk, cbk], [msk], ms[:].rearrange("p (g e) c -> p g e c", g=2), es[:].rearrange("p (g e) c -> p g e c", g=2),
                     cbm[:, d, :].rearrange("p (g c) -> p g c", c=128).unsqueeze(2).to_broadcast([128, 2, 8, 128]), ALU.mult)
                xd, xdk = xdr.next()
                p.tt(["XTOK", "DT_tok"], [xdk], xd[:].rearrange("p (h c) -> p h c", c=64),
                     XTOK[:, q, :].rearrange("p (h c) -> p h c", c=64),
                     DT_tok[:, q, d * 16:(d + 1) * 16].unsqueeze(2).to_broadcast([128, 16, 64]), ALU.mult)
                for h in range(16):
                    bk = 5 + h // 8
                    p.mm([msk, xdk], [pkk[bk]], psb[bk][:, (h % 8) * 64:(h % 8 + 1) * 64], ms[:, h, :], xd[:, h * 64:(h + 1) * 64],
                         start=(d == 0 and h % 8 == 0), stop=(d == 1 and h % 8 == 7))
                for g in range(2):
                    bk = 7 if g == 0 else 0
                    p.mm(["CT", hpk], [pkk[bk]], psb[bk][:, :], CT[:, g, q * 128:(q + 1) * 128], hp[:, d, g * 512:(g + 1) * 512])
                    eab = EA[:, q, d * 16 + g * 8:d * 16 + (g + 1) * 8].unsqueeze(2).to_broadcast([128, 8, 64])
                    if d == 0:
                        p.tt([pkk[bk], "EA"], [acck], acc[:, g * 512:(g + 1) * 512].rearrange("p (h c) -> p h c", c=64),
                             psb[bk][:, :].rearrange("p (h c) -> p h c", c=64), eab, ALU.mult)
                    else:
                        tm, tmk = tmpr.next()
                        p.tt([pkk[bk], "EA"], [tmk], tm[:, :512].rearrange("p (h c) -> p h c", c=64),
                             psb[bk][:, :].rearrange("p (h c) -> p h c", c=64), eab, ALU.mult)
                        p.tt([acck, tmk], [acck], acc[:, g * 512:(g + 1) * 512], acc[:, g * 512:(g + 1) * 512], tm[:, :512], ALU.add)
            for g in range(2):
                p.tt([acck, pkk[5 + g]], [acck], acc[:, g * 512:(g + 1) * 512], acc[:, g * 512:(g + 1) * 512], psb[5 + g][:, :], ALU.add)
            tm, tmk = tmpr.next()
            p.tt(["XTOK", "dsk"], [tmk], tm[:].rearrange("p (h c) -> p h c", c=64), XTOK[:, q, :].rearrange("p (h c) -> p h c", c=64),
                 dsk[:].unsqueeze(2).to_broadcast([128, 16, 64]), ALU.mult)
            p.tt([acck, tmk], [acck], acc[:], acc[:], tm[:], ALU.add)
            sz, szk = szr.next()
            for g in range(2):
                for b4 in range(4):
                    p.tr([zfk, "ident"], [pkk[1 + g]], psb[1 + g][:, b4 * 128:(b4 + 1) * 128], zf[:, g * 4 + b4, :], ident[:])
                p.act([pkk[1 + g]], [szk], sz[:, g * 512:(g + 1) * 512], psb[1 + g][:, :], AF.Silu)
            p.tt([acck, szk], [acck], acc[:], acc[:], sz[:], ALU.mult)
            ss, ssk = ssr.next()
            p.act([acck], [szk, ssk], sz[:], acc[:], AF.Square, accum_out=ss[:, 0:1])
            p.ts([ssk], [ssk], ss[:, 1:2], ss[:, 0:1], 1.0 / 1024, EPS, ALU.mult, ALU.add)
            p.act([ssk], [ssk], ss[:, 1:2], ss[:, 1:2], AF.Sqrt)
            p.op("dve", [ssk], [ssk], lambda e: e.reciprocal(out=ss[:, 1:2], in_=ss[:, 1:2]))
            ys, ysk = ysr.next()
            p.stt([acck, ssk, "snw"], [ysk], ys[:], acc[:], ss[:, 1:2], nw[:], ALU.mult, ALU.mult)
            if q < 2:
                p.dma("sp", [ysk], ["YSTOK"], YSTOK[q * 128:(q + 1) * 128, :], ys[:])
            else:
                c2 = q - 2
                yv = YSTOK[NCTX:, :].rearrange("(r c) d -> c r d", c=64)
                for cl in range(2):
                    p.dma("sp", [ysk], ["YSTOK"], yv[2 * c2 + cl], ys[cl * 64:(cl + 1) * 64, :])
        p.release(m)

    class Rot:
        def __init__(self, idxs):
            self.idxs = idxs
            self.i = 0

        def next(self):
            k = self.idxs[self.i]
            self.i = (self.i + 1) % len(self.idxs)
            return psr.tiles[k], psr.keys[k]

    PI = float(np.pi)

    def hy_filter(l, n, embT_d, tv_d, FW, KF):
        m = p.mark()
        nt = n // 128
        psb, pkk = psr.tiles, psr.keys
        w1 = p.sb("hw1", [33, 64], F32)
        w2 = p.sb("hw2", [64, 64], F32)
        w3 = p.sb("hw3", [64, 4096], F32)
        fr = p.sb("hfr", [64, 1], F32)
        fb1 = p.sb("hfb1", [64, 1], F32)
        fb2 = p.sb("hfb2", [64, 1], F32)
        emb = p.sb("hemb", [33, n], F32)
        h1 = p.sb("hh1", [64, n], F32)
        h2 = p.sb("hh2", [64, n], F32)
        negpi = p.sb("hnegpi", [128, 1], F32)
        nz0 = p.sb("hnz0", [128, 1], F32)
        dec = p.sb("hdec", [128, 4096], F32)
        negt = p.sb("hnegt", [128, nt], F32)
        p.dma("sp", [], ["hw1"], w1[:], hy_w1[l])
        p.dma("sp", [], ["hw2"], w2[:], hy_w2[l])
        p.dma("sp", [], ["hw3"], w3[:], hy_w3[l])
        p.dma("sp", [], ["hfr"], fr[:], hy_freqT[l])
        p.dma("sp", [], ["hfb1"], fb1[:], hy_b1T[l])
        p.dma("sp", [], ["hfb2"], fb2[:], hy_b2T[l])
        p.dma("sp", [], ["hemb"], emb[:], embT_d[:, :])
        p.dma("sp", [], ["hdec"], dec[:], hy_decay[l:l + 1, :].to_broadcast([128, 4096]))
        p.dma("sp", [], ["hnegt"], negt[:], tv_d[:, :])
        p.ts(["hnegt"], ["hnegt"], negt[:], negt[:], -1.0, None, ALU.mult)
        p.op("dve", [], ["hnegpi"], lambda e: e.memset(negpi[:], -PI))
        p.op("dve", [], ["hnz0"], lambda e: e.memset(nz0[:], 1.0))
        p.op("dve", ["hnz0"], ["hnz0"], lambda e: e.memset(nz0[0:1, :], 0.0))
        p.ts(["hfb1", "hfr"], ["hfb1"], fb1[:], fb1[:], fr[:, 0:1], None, ALU.mult)
        p.ts(["hfb2", "hfr"], ["hfb2"], fb2[:], fb2[:], fr[:, 0:1], None, ALU.mult)
        rot = Rot([0, 1, 2, 3, 4, 5, 6])
        sinr = Ring(p, "hsin", [64, 512], F32, 4)
        for (src, sk_, wgt, wk, fb, fbk, dst, dk) in ((emb, "hemb", w1, "hw1", fb1, "hfb1", h1, "hh1"),
                                                       (h1, "hh1", w2, "hw2", fb2, "hfb2", h2, "hh2")):
            for c0 in range(0, n, 512):
                cwid = min(512, n - c0)
                pst, pk = rot.next()
                p.mm([sk_, wk], [pk], pst[:64, :cwid], wgt[:, :], src[:, c0:c0 + cwid])
                p.ts([pk, "hfr", fbk], [dk], dst[:, c0:c0 + cwid], pst[:64, :cwid], fr[:, 0:1], fb[:, 0:1], ALU.mult, ALU.add)
                sa, sak = sinr.next()
                sb_, sbk = sinr.next()
                dv = dst[:, c0:c0 + cwid]
                p.act([dk], [sak], sa[:, :cwid], dv, AF.Sin, scale=0.25)
                p.act([dk], [sbk], sb_[:, :cwid], dv, AF.Sin, scale=0.125)
                p.tt([sbk], [sbk], sb_[:, :cwid], sb_[:, :cwid], sb_[:, :cwid], ALU.mult)
                p.ts([sbk], [sbk], sb_[:, :cwid], sb_[:, :cwid], -2.0, 1.0, ALU.mult, ALU.add)
                p.tt([sak, sbk], [sbk], sb_[:, :cwid], sa[:, :cwid], sb_[:, :cwid], ALU.mult)
                p.tt([sak], [sak], sa[:, :cwid], sa[:, :cwid], sa[:, :cwid], ALU.mult)
                p.ts([sak], [sak], sa[:, :cwid], sa[:, :cwid], -2.0, 1.0, ALU.mult, ALU.add)
                p.stt([sak, sbk], [dk], dv, sb_[:, :cwid], 4.0, sa[:, :cwid], ALU.mult, ALU.mult)
        UP = p.sb("hUP", [128, nt, 512], BF16)
        UM = p.sb("hUM", [128, nt, 512], BF16)
        rinv = p.sb("hrinv", [128, 512], F32)
        er = Ring(p, "hE", [128, 512], F32, 2)
        hfr_ = Ring(p, "hhf", [128, 512], F32, 2)
        hbr_ = Ring(p, "hhb", [128, 512], F32, 2)
        abr = Ring(p, "hab", [128, 512], F32, 2)
        fring = Ring(p, "hF", [128, nt, 128], BF16, 3)
        kr = Ring(p, "hkt", [128, 512], F32, 3)
        for o in range(2):
            for cg in range(2):
                colf = o * 1024 + cg * 512
                colb = 2048 + colf
                for tc in range(nt):
                    hh = []
                    for dirn, col in ((0, colf), (1, colb)):
                        pst, pk = rot.next()
                        p.mm(["hh2", "hw3"], [pk], pst[:, :], h2[:, tc * 128:(tc + 1) * 128], w3[:, col:col + 512])
                        E, ek = er.next()
                        p.act(["hdec", "hnegt"], [ek], E[:], dec[:, col:col + 512], AF.Exp, scale=negt[:, tc:tc + 1])
                        ht_, hk = (hfr_ if dirn == 0 else hbr_).next()
                        p.tt([pk, ek], [hk], ht_[:], pst[:, :], E[:], ALU.mult)
                        if dirn == 1 and tc == 0:
                            p.ts([hk, "hnz0"], [hk], ht_[:], ht_[:], nz0[:, 0:1], None, ALU.mult)
                        ab, abk = abr.next()
                        p.act([hk], [abk], ab[:], ht_[:], AF.Abs)
                        p.mm(["masks", abk], [pkk[7]], psb[7][:, :], masks[:, ONES, :], ab[:],
                             start=(tc == 0 and dirn == 0), stop=(tc == nt - 1 and dirn == 1))
                        hh.append((ht_, hk))
                    p.tt([hh[0][1], hh[1][1]], ["hUP"], UP[:, tc, :], hh[0][0][:], hh[1][0][:], ALU.add)
                    p.tt([hh[0][1], hh[1][1]], ["hUM"], UM[:, tc, :], hh[0][0][:], hh[1][0][:], ALU.subtract)
                p.op("dve", [pkk[7]], ["hrinv"], lambda e: e.reciprocal(out=rinv[:], in_=psb[7][:, :]))
                for j in range(nt):
                    for (pq, U, uk) in ((0, UP, "hUP"), (1, UM, "hUM")):
                        Ft, fk = fring.next()
                        p.dma("sp", [], [fk], Ft[:], FW[pq * nt + j])
                        pst, pk = rot.next()
                        for tc in range(nt):
                            p.mm([fk, uk], [pk], pst[:, :], Ft[:, tc, :], U[:, tc, :], start=(tc == 0), stop=(tc == nt - 1))
                        kt, kk = kr.next()
                        p.tt([pk, "hrinv"], [kk], kt[:], pst[:, :], rinv[:], ALU.mult)
                        p.dma("sp", [kk], ["KF"], KF[o, j, :, pq, cg * 512:(cg + 1) * 512], kt[:])
        p.release(m)

    def hy_data(l, n, toff, FW, GW, KF, hbias):
        m = p.mark()
        nt = n // 128
        psb, pkk = psr.tiles, psr.keys
        tiles = [(i * 512, 512) for i in range(n // 512)] if n >= 512 else [(0, n)]
        ZT = p.sb("hZT", [128, nt, 512], BF16)
        YSs = p.sb("hYS", [128, nt, 2, 512], BF16)
        zfr = Ring(p, "hzf", [128, n], F32, 1)
        fring = Ring(p, "hF2", [128, nt, 128], BF16, 3)
        kr = Ring(p, "hkt2", [128, 2, 512], F32, 2)
        tr_ = Ring(p, "htm", [128, 512], F32, 4)
        gring = Ring(p, "hG", [128, 8, 512], BF16, 3)
        zpr = Ring(p, "hzp", [128, 512], F32, 2)
        xgr = Ring(p, "hxg", [128, 512], F32, 2)
        znr = Ring(p, "hzn", [128, 512], F32, 2)
        ynr = Ring(p, "hyn", [128, 512], BF16, 2)
        rot = Rot([4, 5, 6, 7])
        for cg in range(2):
            for o in range(2):
                src = HY if o == 0 else Z2
                skey = "HY" if o == 0 else "Z2"
                for b in range(4):
                    zf, zk = zfr.next()
                    r0 = cg * 512 + b * 128
                    p.dma("sp", [skey], [zk], zf[:], src[r0:r0 + 128, toff:toff + n])
                    for tc0 in range(0, nt, 4):
                        nq = min(4, nt - tc0)
                        pst, pk = rot.next()
                        for qi in range(nq):
                            p.tr([zk, "ident"], [pk], pst[:, qi * 128:(qi + 1) * 128], zf[:, (tc0 + qi) * 128:(tc0 + qi + 1) * 128], ident[:])
                        p.cp([pk], ["hZT"], ZT[:, tc0:tc0 + nq, b * 128:(b + 1) * 128], pst[:, :nq * 128].rearrange("p (q c) -> p q c", c=128))
                for j in range(nt):
                    Fc, fck = fring.next()
                    p.dma("sp", [], [fck], Fc[:], FW[j])
                    Fs, fsk = fring.next()
                    p.dma("sp", [], [fsk], Fs[:], FW[nt + j])
                    kt, kk = kr.next()
                    p.dma("sp", ["KF"], [kk], kt[:], KF[o, j, :, :, cg * 512:(cg + 1) * 512])
                    pA, pAk = rot.next()
                    pB, pBk = rot.next()
                    for tc in range(nt):
                        p.mm([fck, "hZT"], [pAk], pA[:, :], Fc[:, tc, :], ZT[:, tc, :], start=(tc == 0), stop=(tc == nt - 1))
                    for tc in range(nt):
                        p.mm([fsk, "hZT"], [pBk], pB[:, :], Fs[:, tc, :], ZT[:, tc, :], start=(tc == 0), stop=(tc == nt - 1))
                    t1, k1 = tr_.next()
                    t2, k2 = tr_.next()
                    p.tt([pAk, kk], [k1], t1[:], pA[:, :], kt[:, 0, :], ALU.mult)
                    p.tt([pBk, kk], [k2], t2[:], pB[:, :], kt[:, 1, :], ALU.mult)
                    p.tt([k1, k2], ["hYS"], YSs[:, j, 0, :], t1[:], t2[:], ALU.subtract)
                    t3, k3 = tr_.next()
                    t4, k4 = tr_.next()
                    p.tt([pAk, kk], [k3], t3[:], pA[:, :], kt[:, 1, :], ALU.mult)
                    p.tt([pBk, kk], [k4], t4[:], pB[:, :], kt[:, 0, :], ALU.mult)
                    p.tt([k3, k4], ["hYS"], YSs[:, j, 1, :], t3[:], t4[:], ALU.add)
                for ti, (t0, tw) in enumerate(tiles):
                    Gt, gk = None, None
                    for j2 in range(2 * nt):
                        if j2 % 8 == 0:
                            ng = min(8, 2 * nt - j2)
                            Gt, gk = gring.next()
                            p.dma("sp", [], [gk], Gt[:, :ng, :tw], GW[ti, :, j2:j2 + ng, :])
                        part, j = j2 // nt, j2 % nt
                        for b in range(4):
                            p.mm(["hYS", gk], [pkk[b]], psb[b][:, :tw], YSs[:, j, part, b * 128:(b + 1) * 128], Gt[:, j2 % 8, :tw],
                                 start=(j2 == 0), stop=(j2 == 2 * nt - 1))
                    for b in range(4):
                        cb_ = cg * 4 + b
                        zp, zpk = zpr.next()
                        p.dma("sp", [skey], [zpk], zp[:, :tw], src[cb_ * 128:(cb_ + 1) * 128, toff + t0:toff + t0 + tw])
                        xg, xgk = xgr.next()
                        xr0 = (1 + o) * 1024 + cb_ * 128
                        p.dma("sp", ["HY"], [xgk], xg[:, :tw], HY[xr0:xr0 + 128, toff + t0:toff + t0 + tw])
                        tm, tmk = tr_.next()
                        p.stt([zpk, "hbias", pkk[b]], [tmk], tm[:, :tw], zp[:, :tw], hbias[:, o, cb_:cb_ + 1], psb[b][:, :tw], ALU.mult, ALU.add)
                        if o == 0:
                            zn, znk = znr.next()
                            p.tt([tmk, xgk], [znk], zn[:, :tw], tm[:, :tw], xg[:, :tw], ALU.mult)
                            p.dma("sp", [znk], ["Z2"], Z2[cb_ * 128:(cb_ + 1) * 128, toff + t0:toff + t0 + tw], zn[:, :tw])
                        else:
                            yn, ynk = ynr.next()
                            p.tt([tmk, xgk], [ynk], yn[:, :tw], tm[:, :tw], xg[:, :tw], ALU.mult)
                            p.dma("sp", [ynk], ["YH"], YH[cb_ * 128:(cb_ + 1) * 128, toff + t0:toff + t0 + tw], yn[:, :tw])
        p.release(m)


    def hy4(l, hbias):
        m = p.mark()
        n = NLAT
        toff = NCTX
        psb, pkk = psr.tiles, psr.keys
        identb = p.sb("identb", [128, 128], BF16)
        p.cp(["ident"], ["identb"], identb[:], ident[:])
        S1 = p.sb("S1", [32, 128], BF16)
        S2 = p.sb("S2", [128, 32], BF16)
        p.dma("sp", [], ["S1"], S1[:], S1_d[:, :])
        p.dma("sp", [], ["S2"], S2[:], S2_d[:, :])
        rot = Rot([0, 1, 2, 3, 4, 5, 6, 7])
        evc = [0]

        def evac(R, W, out, in_):
            evc[0] ^= 1
            if evc[0]:
                p.act(R, W, out, in_, AF.Identity)
            else:
                p.cp(R, W, out, in_)

        CB = 64

        def v_zin(X):
            return X[:32, :].rearrange("p (c t) -> p c t", t=128)

        def v_A(X):
            return X[:, :].rearrange("p (r f c) -> p r f c", r=2, f=64)

        def v_Y(X):
            return X[:64, :].rearrange("p (r f c) -> p r f c", r=2, f=64)

        def v_B(X):
            return X[:, :].rearrange("p (c k) -> p c k", k=128)

        def load_zin(X, xk, src_rows, skey):
            zv = v_zin(X)
            for c4 in range(CB // 32):
                p.dma("sp", [skey], [xk], zv[:, c4 * 32:(c4 + 1) * 32, :],
                      src_rows[c4 * 32:(c4 + 1) * 32, :].rearrange("c (a b) -> a c b", b=128))

        def stage1(Xi, xik, Xo, xok):
            zin = v_zin(Xi)
            Afl = v_A(Xo).rearrange("p r f c -> p (r f) c")
            for c0 in range(0, CB, 4):
                pst, pk = rot.next()
                for q in range(4):
                    p.mm([xik, "S1"], [pk], pst[:, q * 128:(q + 1) * 128], zin[:, c0 + q, :], S1[:, :])
                evac([pk], [xok], Afl[:, :, c0:c0 + 4].rearrange("p a c -> p c a"), pst[:, :].rearrange("p (c a) -> p c a", a=128))

        def stage2(XA_, xak, g8, XA2=None, xak2=None):
            A = v_A(XA_)
            A2 = v_A(XA2) if XA2 is not None else A
            k2 = xak2 if XA2 is not None else xak
            zr, zrk = rot.next()
            zi, zik = rot.next()
            for q in range(8):
                f1 = g8 * 8 + q
                o_ = slice(q * CB, (q + 1) * CB)
                p.mm(["WF", xak], [zrk], zr[:64, o_], WF[:, f1, 0, :], A[:, 0, f1, :], start=True, stop=False)
                p.mm(["WF", xak], [zrk], zr[:64, o_], WF[:, f1, 2, :], A[:, 1, f1, :], start=False, stop=True)
            for q in range(8):
                f1 = g8 * 8 + q
                o_ = slice(q * CB, (q + 1) * CB)
                p.mm(["WF", k2], [zik], zi[:64, o_], WF[:, f1, 1, :], A2[:, 0, f1, :], start=True, stop=False)
                p.mm(["WF", k2], [zik], zi[:64, o_], WF[:, f1, 0, :], A2[:, 1, f1, :], start=False, stop=True)
            return zr, zrk, zi, zik

        m0 = p.mark()
        w1 = p.sb("hw1", [33, 64], F32)
        w2 = p.sb("hw2", [64, 64], F32)
        w3 = p.sb("hw3", [64, 4096], F32)
        fr = p.sb("hfr", [64, 1], F32)
        fb1 = p.sb("hfb1", [64, 1], F32)
        fb2 = p.sb("hfb2", [64, 1], F32)
        h2 = p.sb("hh2", [64, n], F32)
        ndec = p.sb("hndec", [128, 32], F32)
        tbc = p.sb("htbc", [128, n], F32)
        p.dma("sp", [], ["hw1"], w1[:], hy_w1[l])
        p.dma("sp", [], ["hw2"], w2[:], hy_w2[l])
        p.dma("sp", [], ["hw3"], w3[:], hy_w3[l])
        p.dma("sp", [], ["hfr"], fr[:], hy_freqT[l])
        p.dma("sp", [], ["hfb1"], fb1[:], hy_b1T[l])
        p.dma("sp", [], ["hfb2"], fb2[:], hy_b2T[l])
        p.dma("sp", [], ["hndec"], ndec[:], hy_ndecT[l])
        p.dma("sp", [], ["htbc"], tbc[:], tvec[0:1, :].to_broadcast([128, n]))
        p.ts(["hfb1", "hfr"], ["hfb1"], fb1[:], fb1[:], fr[:, 0:1], None, ALU.mult)
        p.ts(["hfb2", "hfr"], ["hfb2"], fb2[:], fb2[:], fr[:, 0:1], None, ALU.mult)
        mm_ = p.mark()
        emb = p.sb("hemb", [33, n], F32)
        h1 = p.sb("hh1", [64, n], F32)
        p.dma("sp", [], ["hemb"], emb[:], embT_l[:, :])
        sinr = Ring(p, "hsin", [64, 512], F32, 4)
        for (src, sk_, wgt, wk, fb, fbk, dst, dk) in ((emb, "hemb", w1, "hw1", fb1, "hfb1", h1, "hh1"),
                                                       (h1, "hh1", w2, "hw2", fb2, "hfb2", h2, "hh2")):
            for c0 in range(0, n, 512):
                pst, pk = rot.next()
                p.mm([sk_, wk], [pk], pst[:64, :], wgt[:, :], src[:, c0:c0 + 512])
                dv = dst[:, c0:c0 + 512]
                p.ts([pk, "hfr", fbk], [dk], dv, pst[:64, :], fr[:, 0:1], fb[:, 0:1], ALU.mult, ALU.add)
                sa, sak = sinr.next()
                sb_, sbk = sinr.next()
                p.act([dk], [sak], sa[:, :], dv, AF.Sin, scale=0.25)
                p.act([dk], [sbk], sb_[:, :], dv, AF.Sin, scale=0.125)
                p.tt([sbk], [sbk], sb_[:, :], sb_[:, :], sb_[:, :], ALU.mult)
                p.ts([sbk], [sbk], sb_[:, :], sb_[:, :], -2.0, 1.0, ALU.mult, ALU.add)
                p.tt([sak, sbk], [sbk], sb_[:, :], sa[:, :], sb_[:, :], ALU.mult)
                p.tt([sak], [sak], sa[:, :], sa[:, :], sa[:, :], ALU.mult)
                p.ts([sak], [sak], sa[:, :], sa[:, :], -2.0, 1.0, ALU.mult, ALU.add)
                p.stt([sak, sbk], [dk], dv, sb_[:, :], 4.0, sa[:, :], ALU.mult, ALU.mult)
        p.release(mm_)
        hrow = [p.sb("hrow0", [128, n], F32), p.sb("hrow1", [128, n], F32)]
        hrk = ["hrow0", "hrow1"]
        junk = p.sb("hjunk", [128, n], F32)
        upr = Ring(p, "hup", [128, n], F32, 1)
        ubr = Ring(p, "hub", [128, n], BF16, 2)
        er = Ring(p, "hE", [128, 512], F32, 3)
        ssr = Ring(p, "hss", [128, 4], F32, 2)
        for o in range(2):
            for cb_ in range(8):
                ss, ssk = ssr.next()
                for dirn in range(2):
                    colblk = dirn * 16 + o * 8 + cb_
                    col = colblk * 128
                    for tt_ in range(8):
                        pst, pk = rot.next()
                        p.mm(["hh2", "hw3"], [pk], pst[:, :], w3[:, col:col + 128], h2[:, tt_ * 512:(tt_ + 1) * 512])
                        E, ek = er.next()
                        p.act(["htbc", "hndec"], [ek], E[:], tbc[:, tt_ * 512:(tt_ + 1) * 512], AF.Exp, scale=ndec[:, colblk:colblk + 1])
                        p.tt([pk, ek], [hrk[dirn]], hrow[dirn][:, tt_ * 512:(tt_ + 1) * 512], pst[:, :], E[:], ALU.mult)
                    if dirn == 1:
                        p.op("dve", [hrk[1]], [hrk[1]], lambda e: e.memset(hrow[1][:, 0:1], 0.0))
                    p.act([hrk[dirn]], ["hjunk", ssk], junk[:], hrow[dirn][:], AF.Abs, accum_out=ss[:, dirn:dirn + 1])
                p.tt([ssk], [ssk], ss[:, 2:3], ss[:, 0:1], ss[:, 1:2], ALU.add)
                p.op("dve", [ssk], [ssk], lambda e: e.reciprocal(out=ss[:, 3:4], in_=ss[:, 2:3]))
                for sgn, opx in ((0, ALU.add), (1, ALU.subtract)):
                    up, upk = upr.next()
                    ub, ubk = ubr.next()
                    p.tt([hrk[0], hrk[1]], [upk], up[:], hrow[0][:], hrow[1][:], opx)
                    p.ts([upk, ssk], [ubk], ub[:], up[:], ss[:, 3:4], None, ALU.mult)
                    r0 = o * 1024 + cb_ * 128
                    p.dma("sp", [ubk], ["HFB"], HFB[sgn, r0:r0 + 128, :], ub[:])
        p.release(m0)
        if HY4_STOP == "F0":
            p.release(m)
            return
        WF = p.sb("WF", [128, 64, 3, 64], BF16)
        p.dma("sp", [], ["WF"], WF[:], WF_d[:, :, :, :])
        NBLK = D // CB
        m1 = p.mark()
        XE = CB * 128
        fb = {}
        for nm in ("Zp", "Zm", "Ap", "Am"):
            for i in range(2):
                fb[(nm, i)] = (p.sb("X%s%d" % (nm, i), [128, XE], BF16), "X%s%d" % (nm, i))
        ksr = Ring(p, "hks", [64, 2, 16, CB], F32, 2)

        def f1_load(b):
            o, blk = divmod(b, NBLK)
            r0 = o * 1024 + blk * CB
            i = b % 2
            load_zin(fb[("Zp", i)][0], fb[("Zp", i)][1], HFB[0, r0:r0 + CB, :], "HFB")
            load_zin(fb[("Zm", i)][0], fb[("Zm", i)][1], HFB[1, r0:r0 + CB, :], "HFB")

        f1_load(0)
        deferred = []
        for b in range(2 * NBLK):
            o, blk = divmod(b, NBLK)
            i = b % 2
            if b + 1 < 2 * NBLK:
                f1_load(b + 1)
            for fn in deferred:
                fn()
            deferred = []
            stage1(fb[("Zp", i)][0], fb[("Zp", i)][1], fb[("Ap", i)][0], fb[("Ap", i)][1])
            stage1(fb[("Zm", i)][0], fb[("Zm", i)][1], fb[("Am", i)][0], fb[("Am", i)][1])
            ks, ksk = None, None
            for g8 in range(8):
                if g8 % 2 == 0:
                    ks, ksk = ksr.next()
                zr, zrk, zi, zik = stage2(fb[("Ap", i)][0], fb[("Ap", i)][1], g8, fb[("Am", i)][0], fb[("Am", i)][1])
                fo = (g8 % 2) * 8
                evac([zrk], [ksk], ks[:, 0, fo:fo + 8, :], zr[:64, :].rearrange("p (f c) -> p f c", c=CB))
                evac([zik], [ksk], ks[:, 1, fo:fo + 8, :], zi[:64, :].rearrange("p (f c) -> p f c", c=CB))
                if g8 % 2 == 1:
                    f0 = (g8 // 2) * 16
                    p.dma("sp", [ksk], ["KF2"], KF2[o, blk, :, :, f0:f0 + 16, :], ks[:])
        for fn in deferred:
            fn()
        deferred = []
        p.release(m1)
        if HY4_STOP == "F1":
            p.release(m)
            return
        WI = p.sb("WI", [64, 64, 3, 128], BF16)
        for f4 in range(4):
            p.dma("sp", [], ["WI"], WI[:, f4 * 16:(f4 + 1) * 16, :, :], WI_d[:, f4 * 16:(f4 + 1) * 16, :, :])
        XA = [(p.sb("XAa", [128, XE], BF16), "XAa"), (p.sb("XAb", [128, XE], BF16), "XAb")]
        XB = [(p.sb("XBa", [128, XE], BF16), "XBa"), (p.sb("XBb", [128, XE], BF16), "XBb")]
        ktr = Ring(p, "hkt", [64, 2, 16, CB], F32, 3)
        tr_ = Ring(p, "htm", [64, 512], F32, 4)
        zsr = Ring(p, "hzs", [64, 512], F32, 4)
        ytr = Ring(p, "hyt", [32, 8, 128], F32, 2)
        for o in range(2):
            src = HY if o == 0 else Z2
            srcb = HYB if o == 0 else Z2B
            skey = "HY" if o == 0 else "Z2"
            md = p.mark()

            def d_load(b):
                load_zin(XA[b % 2][0], XA[b % 2][1], srcb[b * CB:(b + 1) * CB, toff:toff + n], skey)

            def k_load(b, gg):
                kt, kk = ktr.next()
                p.dma("sp", ["KF2"], [kk], kt[:], KF2[o, b, :, :, gg * 16:(gg + 1) * 16, :])
                return kt, kk

            d_load(0)
            deferred = []
            for b in range(NBLK):
                i = b % 2
                X1, x1k = XA[i]
                X2, x2k = XB[i]
                if b + 1 < NBLK:
                    d_load(b + 1)
                for fn in deferred:
                    fn()
                deferred = []
                stage1(X1, x1k, X2, x2k)
                Y = v_Y(X1)
                knext = k_load(b, 0)
                for g8 in range(8):
                    if g8 % 2 == 0:
                        kt, kk = knext
                        if g8 + 2 < 8:
                            knext = k_load(b, g8 // 2 + 1)
                    zr, zrk, zi, zik = stage2(X2, x2k, g8)
                    fo = (g8 % 2) * 8
                    kr_ = kt[:, 0, fo:fo + 8, :].rearrange("p f c -> p (f c)")
                    ki_ = kt[:, 1, fo:fo + 8, :].rearrange("p f c -> p (f c)")
                    szr, szrk = zsr.next()
                    szi, szik = zsr.next()
                    p.act([zrk], [szrk], szr[:], zr[:64, :], AF.Identity)
                    p.act([zik], [szik], szi[:], zi[:64, :], AF.Identity)
                    t1, k1 = tr_.next()
                    t2, k2 = tr_.next()
                    p.tt([szrk, kk], [k1], t1[:], szr[:], kr_, ALU.mult)
                    p.tt([szik, kk], [k2], t2[:], szi[:], ki_, ALU.mult)
                    p.tt([k1, k2], [x1k], Y[:, 0, g8 * 8:g8 * 8 + 8, :], t1[:].rearrange("p (f c) -> p f c", c=CB),
                         t2[:].rearrange("p (f c) -> p f c", c=CB), ALU.subtract)
                    t3, k3 = tr_.next()
                    t4, k4 = tr_.next()
                    p.tt([szrk, kk], [k3], t3[:], szr[:], ki_, ALU.mult, eng="pool")
                    p.tt([szik, kk], [k4], t4[:], szi[:], kr_, ALU.mult, eng="pool")
                    p.tt([k3, k4], [x1k], Y[:, 1, g8 * 8:g8 * 8 + 8, :], t3[:].rearrange("p (f c) -> p f c", c=CB),
                         t4[:].rearrange("p (f c) -> p f c", c=CB), ALU.add, eng="pool")
                Bt = v_B(X2)
                for g8 in range(8):
                    br, brk = rot.next()
                    bi, bik = rot.next()
                    for q in range(8):
                        f1 = g8 * 8 + q
                        o_ = slice(q * CB, (q + 1) * CB)
                        p.mm(["WI", x1k], [brk], br[:, o_], WI[:, f1, 0, :], Y[:, 0, f1, :], start=True, stop=False)
                        p.mm(["WI", x1k], [brk], br[:, o_], WI[:, f1, 1, :], Y[:, 1, f1, :], start=False, stop=True)
                    for q in range(8):
                        f1 = g8 * 8 + q
                        o_ = slice(q * CB, (q + 1) * CB)
                        p.mm(["WI", x1k], [bik], bi[:, o_], WI[:, f1, 0, :], Y[:, 1, f1, :], start=True, stop=False)
                        p.mm(["WI", x1k], [bik], bi[:, o_], WI[:, f1, 2, :], Y[:, 0, f1, :], start=False, stop=True)
                    for ri, (bb, bbk) in enumerate(((br, brk), (bi, bik))):
                        k0 = ri * 64 + g8 * 8
                        evac([bbk], [x2k], Bt[:, :, k0:k0 + 8].rearrange("p c f -> p f c"), bb[:, :].rearrange("p (f c) -> p f c", c=CB))
                B2 = v_B(X1)
                for c0 in range(0, CB, 8):
                    pst, pk = rot.next()
                    pv = pst[:, :].bitcast(BF16)
                    for q in range(8):
                        p.tr([x2k, "identb"], [pk], pv[:, q * 128:(q + 1) * 128], Bt[:, c0 + q, :], identb[:])
                    evac([pk], [x1k], B2[:, c0:c0 + 8, :], pv[:, :].rearrange("p (c k) -> p c k", k=128))
                yt, ytk = None, None
                for c0 in range(0, CB, 4):
                    if c0 % 8 == 0:
                        yt, ytk = ytr.next()
                    pst, pk = rot.next()
                    p.mm(["S2", x1k], [pk], pst[:32, :], S2[:, :], B2[:, c0:c0 + 4, :])
                    evac([pk], [ytk], yt[:, c0 % 8:c0 % 8 + 4, :], pst[:32, :].rearrange("p (c k) -> p c k", k=128))
                    if c0 % 8 == 4:
                        cr = b * CB + c0 - 4
                        p.dma("sp", [ytk], ["CONV"], CONV[cr:cr + 8, :].rearrange("c (a b) -> a c b", b=128), yt[:])
                if b % 2 == 1 or True:
                    for fn in deferred:
                        fn()
                    deferred = []
            if HY4_STOP == "D4":
                p.release(m)
                return
            GW_ = 512
            cvr = Ring(p, "hcv", [128, GW_], F32, 2)
            zpr = Ring(p, "hzp", [128, GW_], F32, 2)
            xgr = Ring(p, "hxg", [128, GW_], F32, 2)
            ybr = Ring(p, "hyb", [128, GW_], BF16, 2)
            for cb_ in range(8):
                for g0 in range(0, n, GW_):
                    cv, cvk = cvr.next()
                    p.dma("sp", ["CONV"], [cvk], cv[:], CONV[cb_ * 128:(cb_ + 1) * 128, g0:g0 + GW_])
                    zp, zpk = zpr.next()
                    p.dma("sp", [skey], [zpk], zp[:], src[cb_ * 128:(cb_ + 1) * 128, toff + g0:toff + g0 + GW_])
                    xg, xgk = xgr.next()
                    xr0 = (1 + o) * 1024 + cb_ * 128
                    p.dma("sp", ["HY"], [xgk], xg[:], HY[xr0:xr0 + 128, toff + g0:toff + g0 + GW_])
                    p.stt([zpk, "hbias", cvk], [cvk], cv[:], zp[:], hbias[:, o, cb_:cb_ + 1], cv[:], ALU.mult, ALU.add)
                    if o == 0:
                        p.tt([cvk, xgk], [cvk], cv[:], cv[:], xg[:], ALU.mult)
                        p.dma("sp", [cvk], ["Z2"], Z2[cb_ * 128:(cb_ + 1) * 128, toff + g0:toff + g0 + GW_], cv[:])
                        yb, ybk = ybr.next()
                        p.act([cvk], [ybk], yb[:], cv[:], AF.Identity)
                        p.dma("sp", [ybk], ["Z2"], Z2B[cb_ * 128:(cb_ + 1) * 128, toff + g0:toff + g0 + GW_], yb[:])
                    else:
                        yb, ybk = ybr.next()
                        p.tt([cvk, xgk], [ybk], yb[:], cv[:], xg[:], ALU.mult)
                        p.dma("sp", [ybk], ["YH"], YH[cb_ * 128:(cb_ + 1) * 128, toff + g0:toff + g0 + GW_], yb[:])
            p.release(md)
        p.release(m)

    def phase_hyena(l):
        m = p.mark()
        cw = p.sb("hcw", [128, 24, 3], F32)
        cb = p.sb("hcb", [128, 24], F32)
        hbias = p.sb("hbias", [128, 2, 8], F32)
        p.dma("sp", [], ["hcw"], cw[:], hy_cwT[l])
        p.dma("sp", [], ["hcb"], cb[:], hy_cbT[l])
        p.dma("sp", [], ["hbias"], hbias[:], hy_biasT[l])
        segs = [(0, NCTX), (NCTX, T)]
        m1 = p.mark()
        T1r = Ring(p, "hT1", [128, T], F32, 2)
        XSr = Ring(p, "hXS", [128, T], F32, 2)
        XBr = Ring(p, "hXB", [128, T], BF16, 2)
        for blk in range(24):
            T1, k1 = T1r.next()
            XS, k2 = XSr.next()
            p.dma("sp", ["PROJ"], [k1], T1[:], PROJ[2048 + blk * 128:2048 + (blk + 1) * 128, :])
            seg_conv(p, XS, T1, lambda j: cw[:, blk, j:j + 1], cb[:, blk:blk + 1], 3, 1, segs, [k1, "hcw", "hcb"], [k2])
            p.dma("sp", [k2], ["HY"], HY[blk * 128:(blk + 1) * 128, :], XS[:])
            if blk < 8:
                xb_, xbk = XBr.next()
                p.act([k2], [xbk], xb_[:], XS[:, :], AF.Identity)
                p.dma("sp", [xbk], ["HY"], HYB[blk * 128:(blk + 1) * 128, :], xb_[:])
        p.release(m1)
        hy_filter(l, NCTX, embT_c, tv_c, FW_c, KF_c)
        hy_data(l, NCTX, 0, FW_c, GW_c, KF_c, hbias)
        if HY_DENSE:
            hy_filter(l, NLAT, embT_l, tv_l, FW_l, KF_l)
            hy_data(l, NLAT, NCTX, FW_l, GW_l, KF_l, hbias)
        else:
            hy4(l, hbias)
        p.release(m)

    def phase_merge(l, xsrc):
        m = p.mark()
        wb = p.sb("wb", [128, 3, 8, D], BF16)
        wo = p.sb("wo", [128, 8, D], BF16)
        for br in range(3):
            p.dma("pool", [], ["wb"], wb[:, br, :, :], fm(w_branch[l, br]))
        p.dma("pool", [], ["wo"], wo[:], fm(w_out[l]))
        ytr = Ring(p, "mytok", [128, 4, D], F32, 1)
        ysr = Ring(p, "mys", [128, 8, 512], BF16, 1)
        yrr = Ring(p, "myr", [128, 8, 512], BF16, 1)
        yhr = Ring(p, "myh", [128, 8, 512], BF16, 1)
        gr = Ring(p, "mg", [128, 24, 512], BF16, 1)
        mtr = Ring(p, "mmt", [128, 8, 512], BF16, 1)
        xr = Ring(p, "mx", [128, 8, 512], F32, 1)
        xnr = Ring(p, "mxn", [128, 8, 512], F32, 1)
        tr_ = Ring(p, "mtm", [128, 512], F32, 4)
        for ti, (t0, tw) in enumerate(TT):
            s_ = 0 if ti == 0 else 1
            nsub = tw // 128
            yt, ytk = ytr.next()
            p.dma("sp", ["YSTOK"], [ytk], yt[:, :nsub, :], YSTOK[t0:t0 + tw, :].rearrange("(a p) d -> p a d", p=128))
            ys, ysk = ysr.next()
            for kc in range(8):
                pst, pk = psr.next()
                for a in range(nsub):
                    p.tr([ytk, "ident"], [pk], pst[:, a * 128:(a + 1) * 128], yt[:, a, kc * 128:(kc + 1) * 128], ident[:])
                p.cp([pk], [ysk], ys[:, kc, :tw], pst[:, :tw], eng=("dve" if kc % 2 else "act")) if False else (
                    p.act([pk], [ysk], ys[:, kc, :tw], pst[:, :tw], AF.Identity) if kc % 2 == 0 else p.cp([pk], [ysk], ys[:, kc, :tw], pst[:, :tw]))
            yr, yrk = yrr.next()
            p.dma("sp", ["YR"], [yrk], yr[:, :, :tw], fm(YR)[:, :, t0:t0 + tw])
            yh, yhk = yhr.next()
            p.dma("sp", ["YH"], [yhk], yh[:, :, :tw], fm(YH)[:, :, t0:t0 + tw])
            g, gk = gr.next()
            p.dma("sp", ["GT"], [gk], g[:, :, :tw], fm(GT)[:, :, t0:t0 + tw])
            xt, xk = xr.next()
            p.dma("sp", ["XT"], [xk], xt[:, :, :tw], xsrc[:, :, t0:t0 + tw])
            mt, mtk = mtr.next()
            for cb_ in range(8):
                pbs = []
                for br, (yb, ybk) in enumerate(((yr, yrk), (yh, yhk), (ys, ysk))):
                    pst, pk = psr.next()
                    for kc in range(8):
                        p.mm(["wb", ybk], [pk], pst[:, :tw], wb[:, br, kc, cb_ * 128:(cb_ + 1) * 128], yb[:, kc, :tw],
                             start=(kc == 0), stop=(kc == 7))
                    pbs.append((pst, pk))
                t1, k1 = tr_.next()
                t2, k2 = tr_.next()
                p.tt([pbs[0][1], gk], [k1], t1[:, :tw], pbs[0][0][:, :tw], g[:, cb_, :tw], ALU.mult)
                p.tt([pbs[1][1], gk], [k2], t2[:, :tw], pbs[1][0][:, :tw], g[:, 8 + cb_, :tw], ALU.mult)
                p.tt([k1, k2], [k1], t1[:, :tw], t1[:, :tw], t2[:, :tw], ALU.add)
                t3, k3 = tr_.next()
                p.tt([pbs[2][1], gk], [k3], t3[:, :tw], pbs[2][0][:, :tw], g[:, 16 + cb_, :tw], ALU.mult)
                p.tt([k1, k3], [mtk], mt[:, cb_, :tw], t1[:, :tw], t3[:, :tw], ALU.add)
            xn, xnk = xnr.next()
            for co in range(8):
                pst, pk = psr.next()
                for cb_ in range(8):
                    p.mm(["wo", mtk], [pk], pst[:, :tw], wo[:, cb_, co * 128:(co + 1) * 128], mt[:, cb_, :tw],
                         start=(cb_ == 0), stop=(cb_ == 7))
                p.stt([pk, "modT", xk], [xnk], xn[:, co, :tw], pst[:, :tw], modT[:, 16 + co, s_:s_ + 1], xt[:, co, :tw], ALU.mult, ALU.add)
            p.dma("sp", [xnk], ["XT"], fm(XT)[:, :, t0:t0 + tw], xn[:, :, :tw])
        p.release(m)

    def phase_ffn(l, HT):
        m = p.mark()
        wv = fm(w_up[l])
        wr = Ring(p, "fwu", [128, 2, 8, 128], BF16, 3)
        str_ = Ring(p, "fst", [128, T], BF16, 2)
        sgr = Ring(p, "fsg", [128, 512], F32, 3)
        for j in range(22):
            wt, wk = wr.next()
            p.dma("pool", [], [wk], wt[:, 0, :, :], wv[:, :, j * 128:(j + 1) * 128])
            p.dma("pool", [], [wk], wt[:, 1, :, :], wv[:, :, D_FF + j * 128:D_FF + (j + 1) * 128])
            st, stk = str_.next()
            for ti, (t0, tw) in enumerate(TT):
                pg, pgk = psr.next()
                pu, puk = psr.next()
                for kc in range(8):
                    p.mm([wk, "HT"], [pgk], pg[:, :tw], wt[:, 0, kc, :], HT[:, kc, t0:t0 + tw], start=(kc == 0), stop=(kc == 7))
                for kc in range(8):
                    p.mm([wk, "HT"], [puk], pu[:, :tw], wt[:, 1, kc, :], HT[:, kc, t0:t0 + tw], start=(kc == 0), stop=(kc == 7))
                sg, sgk = sgr.next()
                p.act([pgk], [sgk], sg[:, :tw], pg[:, :tw], AF.Silu)
                p.tt([sgk, puk], [stk], st[:, t0:t0 + tw], sg[:, :tw], pu[:, :tw], ALU.mult)
            p.dma("sp", [stk], ["AFF"], AFF[j * 128:(j + 1) * 128, :], st[:])
        p.release(m)

    def phase_ffn2(l):
        m = p.mark()
        wd = p.sb("fwd", [128, 22, D], BF16)
        p.dma("pool", [], ["fwd"], wd[:], w_down[l].rearrange("(j p) c -> p j c", p=128))
        ar = Ring(p, "fa", [128, 22, 512], BF16, 2)
        xr = Ring(p, "fx", [128, 8, 512], F32, 2)
        xnr = Ring(p, "fxn", [128, 8, 512], F32, 2)
        av = AFF.rearrange("(j p) t -> p j t", p=128)
        for ti, (t0, tw) in enumerate(TT):
            s_ = 0 if ti == 0 else 1
            at, ak = ar.next()
            p.dma("sp", ["AFF"], [ak], at[:, :, :tw], av[:, :, t0:t0 + tw])
            xt, xk = xr.next()
            p.dma("sp", ["XT"], [xk], xt[:, :, :tw], fm(XT)[:, :, t0:t0 + tw])
            xn, xnk = xnr.next()
            for co in range(8):
                pst, pk = psr.next()
                for j in range(22):
                    p.mm(["fwd", ak], [pk], pst[:, :tw], wd[:, j, co * 128:(co + 1) * 128], at[:, j, :tw], start=(j == 0), stop=(j == 21))
                p.stt([pk, "modT", xk], [xnk], xn[:, co, :tw], pst[:, :tw], modT[:, 40 + co, s_:s_ + 1], xt[:, co, :tw], ALU.mult, ALU.add)
            p.dma("sp", [xnk], ["XT"], fm(XT)[:, :, t0:t0 + tw], xn[:, :, :tw])
        p.release(m)

    def phase_final():
        m = p.mark()
        fn = p.sb("fnw", [128, 8], F32)
        p.dma("sp", [], ["fnw"], fn[:], final_normT[:, :])
        xr = Ring(p, "ox", [128, 8, 512], F32, 2)
        sqr = Ring(p, "osq", [128, 8, 512], BF16, 2)
        rr = Ring(p, "orstd", [128, 512], F32, 2)
        outr = Ring(p, "oo", [128, 8, 512], F32, 2)
        src = fm(XT)
        for ti, (t0, tw) in enumerate(TT):
            if ti == 0:
                continue
            xt, xk = xr.next()
            p.dma("sp", ["XT"], [xk], xt[:], src[:, :, t0:t0 + tw])
            sq, sqk = sqr.next()
            p.act([xk], [sqk], sq[:], xt[:], AF.Square)
            pst, pk = psr.next()
            for kc in range(8):
                p.mm([sqk, "onesb"], [pk], pst[:, :], onesb[:], sq[:, kc, :], start=(kc == 0), stop=(kc == 7))
            rs, rk = rr.next()
            p.ts([pk], [rk], rs[:], pst[:, :], 1.0 / D, EPS, ALU.mult, ALU.add)
            p.act([rk], [rk], rs[:], rs[:], AF.Sqrt)
            p.op("dve", [rk], [rk], lambda e: e.reciprocal(out=rs[:], in_=rs[:]))
            ot, ok_ = outr.next()
            for kc in range(8):
                p.stt([xk, "fnw", rk], [ok_], ot[:, kc, :], xt[:, kc, :], fn[:, kc:kc + 1], rs[:], ALU.mult, ALU.mult)
            p.dma("sp", [ok_], ["OUT"], fm(out)[:, :, t0 - NCTX:t0 - NCTX + tw], ot[:])
        p.wait_all("sp", ["OUT"])
        p.release(m)

    for l in range(nlayers):
        phase_mod(l)
        mk = p.mark()
        HT = p.sb("HT", [128, 8, T], BF16)
        phase_norm(fm(xin if l == 0 else XT), A1, "A1", 0, HT)
        if "HTD" in dbg:
            p.dma("sp", ["HT"], ["HTD"], fm(HTD), HT[:])
        if stop_after == "norm":
            p.release(mk)
            break
        phase_proj(l, HT)
        p.release(mk)
        if stop_after == "proj":
            break
        if stop_after not in ("ssd", "hyena"):
            phase_rglru(l)
        if stop_after == "rglru":
            break
        if stop_after != "hyena":
            phase_ssd(l)
        if stop_after == "ssd":
            break
        phase_hyena(l)
        if stop_after == "hyena":
            break
        phase_merge(l, fm(xin if l == 0 else XT))
        if stop_after == "merge":
            break
        mk = p.mark()
        HT = p.sb("HT", [128, 8, T], BF16)
        phase_norm(fm(XT), A2, "A2", 24, HT)
        phase_ffn(l, HT)
        p.release(mk)
        phase_ffn2(l)
    if stop_after is None:
        phase_final()
    p.barrier()
    p.close()
    print("instructions:", p.ninst)
    return nc


def fmT(v, nchunk):
    return np.ascontiguousarray(np.swapaxes(v.reshape(v.shape[:-1] + (nchunk, 128)), -1, -2))

def hyena_emb(n):
    f = np.float32
    t = np.linspace(0.0, 1.0, n, dtype=f)
    bands = np.linspace(1e-4, 15.0, 16, dtype=f)
    ang = (f(2.0 * np.pi / n) * np.arange(n, dtype=f)[:, None]) * bands[None]
    emb = np.concatenate([t[:, None], np.cos(ang), np.sin(ang)], axis=-1).astype(f)
    return np.ascontiguousarray(emb.T)

def hyena_consts(n):
    import ml_dtypes
    f = np.float32
    t = np.linspace(0.0, 1.0, n, dtype=f)
    bands = np.linspace(1e-4, 15.0, 16, dtype=f)
    ang = (f(2.0 * np.pi / n) * np.arange(n, dtype=f)[:, None]) * bands[None]
    emb = np.concatenate([t[:, None], np.cos(ang), np.sin(ang)], axis=-1).astype(f)
    embT = np.ascontiguousarray(emb.T)
    nt = n // 128
    tv = np.ascontiguousarray(t.reshape(nt, 128).T)
    N = 2 * n
    tt = np.arange(n, dtype=np.int64)
    ff = np.arange(n, dtype=np.int64)
    ph = ((2 * ff[None, :] + 1) * tt[:, None]) % (2 * N)
    angm = np.pi * ph.astype(np.float64) / N
    C = np.cos(angm); S = np.sin(angm)
    def tile_f(M):
        return M.reshape(nt, 128, nt, 128).transpose(2, 1, 0, 3)
    FW = np.concatenate([tile_f(C), tile_f(S)], axis=0).astype(ml_dtypes.bfloat16)
    tw = 512 if n >= 512 else n
    def tile_g(M):
        return (M.T * (2.0 / N)).reshape(nt, 128, n // tw, tw).transpose(2, 1, 0, 3)
    GW = np.concatenate([tile_g(C), tile_g(S)], axis=2).astype(ml_dtypes.bfloat16)
    return embT, tv, np.ascontiguousarray(FW), np.ascontiguousarray(GW)

def hyena_consts4():
    import ml_dtypes
    bf = ml_dtypes.bfloat16
    n, N = 4096, 8192
    t1 = np.arange(32, dtype=np.int64)[:, None]
    f1 = np.arange(64, dtype=np.int64)[None, :]
    g = 2.0 * np.pi * (((2 * f1 + 1) * t1) % 128).astype(np.float64) / 128.0
    S1 = np.concatenate([np.cos(g), -np.sin(g)], axis=1).astype(bf)
    S2 = (np.concatenate([np.cos(g).T, -np.sin(g).T], axis=0) * (2.0 / N)).astype(bf)
    f1v = np.arange(64, dtype=np.int64)[:, None, None]
    t2v = np.arange(128, dtype=np.int64)[None, :, None]
    f2v = np.arange(64, dtype=np.int64)[None, None, :]
    ph = ((2 * f1v + 1) * t2v + 128 * f2v * t2v) % 16384
    phi = 2.0 * np.pi * ph.astype(np.float64) / 16384.0
    Wr = np.cos(phi); Wi = -np.sin(phi)
    WF = np.stack([Wr, Wi, -Wi], axis=0).transpose(2, 1, 0, 3)
    WI = np.stack([Wr, Wi, -Wi], axis=0).transpose(3, 1, 0, 2)
    tvec = np.linspace(0.0, 1.0, n, dtype=np.float32)[None, :]
    return S1, S2, np.ascontiguousarray(WF.astype(bf)), np.ascontiguousarray(WI.astype(bf)), np.ascontiguousarray(tvec)

def prep_shared(inp):
    f = np.float32
    sh = {}
    sh["w_mod"] = inp["w_mod"]
    sh["b_modT"] = fmT(inp["b_mod"], 48)
    sh["norm_mixT"] = fmT(inp["norm_mix"], 8)
    sh["norm_ffnT"] = fmT(inp["norm_ffn"], 8)
    sh["final_normT"] = fmT(inp["final_norm"], 8)
    sh["w_in"] = inp["w_in"]
    sh["rnn_cwT"] = np.ascontiguousarray(inp["rnn_conv_w"].reshape(4, 4, 8, 128).transpose(0, 3, 2, 1))
    sh["rnn_cbT"] = fmT(inp["rnn_conv_b"], 8)
    sh["rnn_aw"] = inp["rnn_gate_a_w"]
    sh["rnn_xw"] = inp["rnn_gate_x_w"]
    sh["rnn_abT"] = np.ascontiguousarray(fmT(inp["rnn_gate_a_b"], 8).transpose(0, 2, 1, 3))
    sh["rnn_xbT"] = np.ascontiguousarray(fmT(inp["rnn_gate_x_b"], 8).transpose(0, 2, 1, 3))
    sh["rnn_lamT"] = np.ascontiguousarray(fmT(inp["rnn_lambda"], 8).transpose(0, 2, 1, 3))
    sh["ident"] = np.eye(128, dtype=f)
    j = np.arange(128)[:, None]; ll = np.arange(128)[None, :]
    sh["masks"] = np.stack([(j <= ll), (j > ll), (j >= ll), (j < ll), np.ones((128, 128), bool)]).astype(f)
    sh["ssm_cwT"] = np.ascontiguousarray(inp["ssm_conv_w"].reshape(4, 4, 12, 128).transpose(0, 3, 2, 1))
    sh["ssm_cbT"] = fmT(inp["ssm_conv_b"], 12)
    sh["ssm_alogT"] = np.ascontiguousarray(inp["ssm_a_log"].reshape(4, 32, 1))
    sh["ssm_dtbT"] = np.ascontiguousarray(inp["ssm_dt_bias"].reshape(4, 32, 1))
    sh["ssm_d"] = inp["ssm_d"]
    sh["ssm_norm"] = inp["ssm_norm"]
    sh["hy_cwT"] = np.ascontiguousarray(inp["hy_short_w"].reshape(4, 3, 24, 128).transpose(0, 3, 2, 1))
    sh["hy_cbT"] = fmT(inp["hy_short_b"], 24)
    sh["hy_biasT"] = np.ascontiguousarray(fmT(inp["hy_bias"], 8).transpose(0, 2, 1, 3))
    sh["hy_w1"] = inp["hy_w1"]; sh["hy_w2"] = inp["hy_w2"]; sh["hy_w3"] = inp["hy_w3"]
    sh["hy_b1T"] = np.ascontiguousarray(inp["hy_b1"].reshape(4, 64, 1))
    sh["hy_b2T"] = np.ascontiguousarray(inp["hy_b2"].reshape(4, 64, 1))
    sh["hy_freqT"] = np.ascontiguousarray(inp["hy_freq"].reshape(4, 64, 1))
    sh["hy_decay"] = inp["hy_decay"]
    emb, tv, FW, GW = hyena_consts(256)
    sh["embT_c"] = emb; sh["tv_c"] = tv; sh["FW_c"] = FW; sh["GW_c"] = GW
    sh["embT_l"] = hyena_emb(4096)
    S1, S2, WF, WI, tvec = hyena_consts4()
    sh["S1"] = S1; sh["S2"] = S2; sh["WF"] = WF; sh["WI"] = WI; sh["tvec"] = tvec
    sh["hy_ndecT"] = np.ascontiguousarray(-fmT(inp["hy_decay"], 32))
    sh["w_branch"] = inp["w_branch"]; sh["w_out"] = inp["w_out"]; sh["w_up"] = inp["w_up"]; sh["w_down"] = inp["w_down"]
    return sh

def prep_core(inp, b):
    xin = np.ascontiguousarray(np.concatenate([inp["ctx"][b].T, inp["x"][b].T], axis=1))
    cc = np.stack([inp["c_ctx"], inp["c"][b]], axis=-1)
    cc = np.ascontiguousarray(cc.reshape(8, 128, 2).transpose(1, 0, 2))
    return {"xin": xin, "cc": cc}


def kernel(**inputs):
    inp = {k: np.asarray(v) for k, v in inputs.items()}
    sh = prep_shared(inp)
    nc = build()
    in_maps = []
    for b in range(8):
        im = dict(sh)
        im.update(prep_core(inp, b))
        in_maps.append(im)
    res = run_bass_kernel_spmd(nc, in_maps, core_ids=list(range(8)))
    out = np.stack([np.ascontiguousarray(np.asarray(r["out"]).T) for r in res.results], axis=0)
    return out.astype(np.float32)
```

```python
import numpy as np
import concourse.bass as bass
import concourse.mybir as mybir
from concourse.bass_utils import run_bass_kernel_spmd

F32 = mybir.dt.float32
BF16 = mybir.dt.bfloat16
ALU = mybir.AluOpType
AF = mybir.ActivationFunctionType
AX = mybir.AxisListType

D = 1024
NCTX = 256
NLAT = 4096
T = NCTX + NLAT
DEPTH = 4
D_IN = 10784
D_FF = 2816
EPS = 1e-6
HY_DENSE = False
HY4_STOP = None
TT = [(0, 256)] + [(256 + 512 * i, 512) for i in range(8)]


class Prog:
    NDMA = 24

    def __init__(self, nc):
        self.nc = nc
        self.engs = {"pe": nc.tensor, "dve": nc.vector, "act": nc.scalar, "pool": nc.gpsimd, "sp": nc.sync}
        self._ctx = []
        self.sem = {}
        self.cnt = {}
        for e in ("pe", "dve", "act", "pool"):
            self.sem[e] = self._enter(nc.semaphore("s_" + e))
            self.cnt[e] = 0
        self.dsem = [self._enter(nc.semaphore("d%d" % i)) for i in range(self.NDMA)]
        self.dcnt = [0] * self.NDMA
        self.dnext = 0
        self.semobj = {}
        for e in ("pe", "dve", "act", "pool"):
            self.semobj[("c", e)] = self.sem[e]
        for i in range(self.NDMA):
            self.semobj[("d", i)] = self.dsem[i]
        self.waited = {e: {} for e in self.engs}
        self.lastw = {}
        self.reads = {}
        self.ninst = 0
        self.uid = 0

    def _enter(self, cm):
        v = cm.__enter__()
        self._ctx.append(cm)
        return v

    def sb(self, name, shape, dt):
        self.uid += 1
        return self._enter(self.nc.sbuf_tensor("%s_%d" % (name, self.uid), list(shape), dt))

    def ps(self, name, shape, dt=F32):
        return self._enter(self.nc.psum_tensor(name, list(shape), dt))

    def mark(self):
        return len(self._ctx)

    def release(self, mark):
        self.barrier()
        while len(self._ctx) > mark:
            cm = self._ctx.pop()
            cm.__exit__(None, None, None)

    def close(self):
        while self._ctx:
            cm = self._ctx.pop()
            cm.__exit__(None, None, None)

    def barrier(self):
        targets = []
        for e in ("pe", "dve", "act", "pool"):
            if self.cnt[e]:
                targets.append((("c", e), self.cnt[e]))
        for i in range(self.NDMA):
            if self.dcnt[i]:
                targets.append((("d", i), self.dcnt[i] * 16))
        for q in ("pe", "dve", "act", "pool", "sp"):
            e = self.engs[q]
            for sk, val in targets:
                if sk == ("c", q):
                    continue
                if self.waited[q].get(sk, 0) < val:
                    e.wait_ge(self.semobj[sk], val)
                    self.waited[q][sk] = val
        self.lastw = {}
        self.reads = {}

    def _deps(self, eng, R, W):
        deps = []
        for r in R:
            lw = self.lastw.get(r)
            if lw is not None:
                deps.append((lw, "raw"))
        for w in W:
            lw = self.lastw.get(w)
            if lw is not None:
                deps.append((lw, "waw"))
            for rd in self.reads.get(w, ()):
                deps.append((rd, "war"))
        own = ("c", eng)
        e = self.engs[eng]
        wt = self.waited[eng]
        need = {}
        for (sk, val), kind in deps:
            if sk == own:
                if eng == "pe":
                    continue
                if kind != "raw":
                    continue
            if wt.get(sk, 0) >= val:
                continue
            if need.get(sk, 0) < val:
                need[sk] = val
        for sk, val in need.items():
            e.wait_ge(self.semobj[sk], val)
            wt[sk] = val

    def _commit(self, tick, R, W):
        for w in W:
            self.lastw[w] = tick
            self.reads[w] = []
        for r in R:
            if r in W:
                continue
            lst = self.reads.setdefault(r, [])
            lst.append(tick)
            if len(lst) > 48:
                best = {}
                for sk, v in lst:
                    if best.get(sk, 0) < v:
                        best[sk] = v
                self.reads[r] = list(best.items())

    def op(self, eng, R, W, fn):
        self._deps(eng, R, W)
        ins = fn(self.engs[eng])
        self.cnt[eng] += 1
        ins.then_inc(self.sem[eng], 1)
        tick = (("c", eng), self.cnt[eng])
        self._commit(tick, R, W)
        self.ninst += 1
        return tick

    def dma(self, q, R, W, out, in_, **kw):
        i = self.dnext
        self.dnext = (self.dnext + 1) % self.NDMA
        sk = ("d", i)
        e = self.engs[q]
        prev = self.dcnt[i] * 16
        if prev and self.waited[q].get(sk, 0) < prev:
            e.wait_ge(self.dsem[i], prev)
            self.waited[q][sk] = prev
        self._deps(q, R, W)
        ins = e.dma_start(out=out, in_=in_, **kw)
        self.dcnt[i] += 1
        ins.then_inc(self.dsem[i], 16)
        tick = (sk, self.dcnt[i] * 16)
        self._commit(tick, R, W)
        self.ninst += 1
        return tick

    def wait_all(self, eng, keys):
        e = self.engs[eng]
        for k in keys:
            lw = self.lastw.get(k)
            if lw is None:
                continue
            sk, val = lw
            if self.waited[eng].get(sk, 0) < val:
                e.wait_ge(self.semobj[sk], val)
                self.waited[eng][sk] = val

    def mm(self, R, W, out, lhsT, rhs, start=True, stop=True):
        return self.op("pe", R, W, lambda e: e.matmul(out, lhsT, rhs, start=start, stop=stop))

    def tr(self, R, W, out, in_, ident):
        return self.op("pe", R, W, lambda e: e.transpose(out, in_, ident))

    def act(self, R, W, out, in_, func, **kw):
        return self.op("act", R, W, lambda e: e.activation(out=out, in_=in_, func=func, **kw))

    def tt(self, R, W, out, in0, in1, op, eng="dve"):
        return self.op(eng, R, W, lambda e: e.tensor_tensor(out=out, in0=in0, in1=in1, op=op))

    def ts(self, R, W, out, in0, s1, s2, op0, op1=None, eng="dve"):
        if op1 is None:
            return self.op(eng, R, W, lambda e: e.tensor_scalar(out=out, in0=in0, scalar1=s1, scalar2=None, op0=op0))
        return self.op(eng, R, W, lambda e: e.tensor_scalar(out=out, in0=in0, scalar1=s1, scalar2=s2, op0=op0, op1=op1))

    def stt(self, R, W, out, in0, scalar, in1, op0, op1, eng="dve"):
        return self.op(eng, R, W, lambda e: e.scalar_tensor_tensor(out=out, in0=in0, scalar=scalar, in1=in1, op0=op0, op1=op1))

    def cp(self, R, W, out, in_, eng="dve"):
        return self.op(eng, R, W, lambda e: e.tensor_copy(out=out, in_=in_))


class Ring:
    def __init__(self, p, name, shape, dt, n):
        self.tiles = [p.sb("%s%d" % (name, i), shape, dt) for i in range(n)]
        self.keys = ["%s#%d_%d" % (name, p.uid, i) for i in range(n)]
        self.i = 0

    def next(self):
        t, k = self.tiles[self.i], self.keys[self.i]
        self.i = (self.i + 1) % len(self.tiles)
        return t, k


class PsRing:
    def __init__(self, p, n=8):
        self.tiles = [p.ps("psb%d" % i, [128, 512]) for i in range(n)]
        self.keys = ["psb%d" % i for i in range(n)]
        self.i = 0

    def next(self):
        t, k = self.tiles[self.i], self.keys[self.i]
        self.i = (self.i + 1) % len(self.tiles)
        return t, k


def fm(ap2d):
    return ap2d.rearrange("(kc p) t -> p kc t", p=128)


def seg_conv(p, out, in_, wv, bv, ntap, left, segs, Rk, Wk):
    p.ts(Rk, Wk, out[:, :], in_[:, :], wv(left), bv, ALU.mult, ALU.add)
    for j in range(ntap):
        d = j - left
        if d == 0:
            continue
        for (s0, s1) in segs:
            lo = max(s0, s0 - d)
            hi = min(s1, s1 - d)
            p.stt(Rk + Wk, Wk, out[:, lo:hi], in_[:, lo + d:hi + d], wv(j), out[:, lo:hi], ALU.mult, ALU.add)


def build(nlayers=DEPTH, stop_after=None, dbg=()):
    nc = bass.Bass("TRN2", target_bir_lowering=False)

    def din(name, shape, dt=F32):
        return nc.dram_tensor(name, list(shape), dt, kind="ExternalInput").ap()

    def dscr(name, shape, dt=F32):
        kind = "ExternalOutput" if name in dbg else "Internal"
        return nc.dram_tensor(name, list(shape), dt, kind=kind).ap()

    xin = din("xin", [D, T])
    cc = din("cc", [128, 8, 2])
    w_mod = din("w_mod", [DEPTH, D, 6 * D])
    b_modT = din("b_modT", [DEPTH, 128, 48])
    norm_mixT = din("norm_mixT", [DEPTH, 128, 8])
    norm_ffnT = din("norm_ffnT", [DEPTH, 128, 8])
    final_normT = din("final_normT", [128, 8])
    w_in = din("w_in", [DEPTH, D, D_IN])
    rnn_cwT = din("rnn_cwT", [DEPTH, 128, 8, 4])
    rnn_cbT = din("rnn_cbT", [DEPTH, 128, 8])
    rnn_aw = din("rnn_aw", [DEPTH, 2, 8, 128, 128])
    rnn_xw = din("rnn_xw", [DEPTH, 2, 8, 128, 128])
    rnn_abT = din("rnn_abT", [DEPTH, 128, 2, 8])
    rnn_xbT = din("rnn_xbT", [DEPTH, 128, 2, 8])
    rnn_lamT = din("rnn_lamT", [DEPTH, 128, 2, 8])
    ident_d = din("ident", [128, 128])
    masks_d = din("masks", [5, 128, 128])
    ssm_cwT = din("ssm_cwT", [DEPTH, 128, 12, 4])
    ssm_cbT = din("ssm_cbT", [DEPTH, 128, 12])
    ssm_alogT = din("ssm_alogT", [DEPTH, 32, 1])
    ssm_dtbT = din("ssm_dtbT", [DEPTH, 32, 1])
    ssm_d = din("ssm_d", [DEPTH, 16])
    ssm_norm = din("ssm_norm", [DEPTH, 1024])
    hy_cwT = din("hy_cwT", [DEPTH, 128, 24, 3])
    hy_cbT = din("hy_cbT", [DEPTH, 128, 24])
    hy_biasT = din("hy_biasT", [DEPTH, 128, 2, 8])
    hy_w1 = din("hy_w1", [DEPTH, 33, 64])
    hy_w2 = din("hy_w2", [DEPTH, 64, 64])
    hy_w3 = din("hy_w3", [DEPTH, 64, 4096])
    hy_b1T = din("hy_b1T", [DEPTH, 64, 1])
    hy_b2T = din("hy_b2T", [DEPTH, 64, 1])
    hy_freqT = din("hy_freqT", [DEPTH, 64, 1])
    hy_decay = din("hy_decay", [DEPTH, 4096])
    embT_l = din("embT_l", [33, NLAT])
    embT_c = din("embT_c", [33, NCTX])
    tv_l = din("tv_l", [128, NLAT // 128]) if HY_DENSE else None
    tv_c = din("tv_c", [128, NCTX // 128])
    FW_l = din("FW_l", [64, 128, 32, 128], BF16) if HY_DENSE else None
    GW_l = din("GW_l", [8, 128, 64, 512], BF16) if HY_DENSE else None
    FW_c = din("FW_c", [4, 128, 2, 128], BF16)
    GW_c = din("GW_c", [1, 128, 4, 256], BF16)
    S1_d = din("S1", [32, 128], BF16)
    S2_d = din("S2", [128, 32], BF16)
    WF_d = din("WF", [128, 64, 3, 64], BF16)
    WI_d = din("WI", [64, 64, 3, 128], BF16)
    tvec = din("tvec", [1, NLAT])
    hy_ndecT = din("hy_ndecT", [DEPTH, 128, 32])
    w_branch = din("w_branch", [DEPTH, 3, D, D])
    w_out = din("w_out", [DEPTH, D, D])
    w_up = din("w_up", [DEPTH, D, 2 * D_FF])
    w_down = din("w_down", [DEPTH, D_FF, D])
    out = nc.dram_tensor("out", [D, NLAT], F32, kind="ExternalOutput").ap()

    XT = dscr("XT", [D, T])
    PROJ = dscr("PROJ", [5120, T])
    SSMP = dscr("SSMP", [2592, T])
    GT = dscr("GT", [3072, T], BF16)
    YR = dscr("YR", [D, T], BF16)
    HTD = dscr("HTD", [D, T], BF16)
    HP = dscr("HP", [2, 34, 128, 1024], BF16)
    YSTOK = dscr("YSTOK", [T, 1024])
    HY = dscr("HY", [3072, T])
    Z2 = dscr("Z2", [D, T])
    YH = dscr("YH", [D, T], BF16)
    KF_l = dscr("KF_l", [2, 32, 128, 2, 1024]) if HY_DENSE else None
    KF_c = dscr("KF_c", [2, 2, 128, 2, 1024])
    AFF = dscr("AFF", [D_FF, T], BF16)
    HFB = dscr("HFB", [2, 2048, NLAT], BF16)
    HYB = dscr("HYB", [D, T], BF16)
    Z2B = dscr("Z2B", [D, T], BF16)
    KF2 = dscr("KF2", [2, 16, 64, 2, 64, 64])
    CONV = dscr("CONV", [D, NLAT])

    p = Prog(nc)
    psr = PsRing(p)

    ident = p.sb("ident", [128, 128], F32)
    onesb = p.sb("onesb", [128, 128], BF16)
    modT = p.sb("modT", [128, 48, 2], F32)
    A1 = p.sb("A1", [128, 8, 2], F32)
    A2 = p.sb("A2", [128, 8, 2], F32)
    p.dma("sp", [], ["ident"], ident[:], ident_d[:, :])
    masks = p.sb("masks", [128, 5, 128], F32)
    p.dma("sp", [], ["masks"], masks[:], masks_d.rearrange("m p l -> p m l"))
    LE, GT_, GE, LT, ONES = 0, 1, 2, 3, 4
    p.op("dve", [], ["onesb"], lambda e: e.memset(onesb[:], 1.0))

    def phase_mod(l):
        m = p.mark()
        cs = p.sb("cs", [128, 8, 2], F32)
        bm = p.sb("bm", [128, 48], F32)
        nm = p.sb("nm", [128, 8], F32)
        nf = p.sb("nf", [128, 8], F32)
        p.dma("sp", [], ["cs"], cs[:], cc[:, :, :])
        p.dma("sp", [], ["bm"], bm[:], b_modT[l])
        p.dma("sp", [], ["nm"], nm[:], norm_mixT[l])
        p.dma("sp", [], ["nf"], nf[:], norm_ffnT[l])
        p.act(["cs"], ["cs"], cs[:], cs[:], AF.Silu)
        wring = Ring(p, "wmod", [128, 8, 512], F32, 2)
        pst, pk = psr.next()
        wv = fm(w_mod[l])
        for cg in range(12):
            wt, wk = wring.next()
            p.dma("sp", [], [wk], wt[:], wv[:, :, cg * 512:(cg + 1) * 512])
            for j4 in range(4):
                j = cg * 4 + j4
                for kc in range(8):
                    p.mm([wk, "cs"], [pk], pst[:, j * 2:(j + 1) * 2], wt[:, kc, j4 * 128:(j4 + 1) * 128], cs[:, kc, :],
                         start=(kc == 0), stop=(kc == 7))
        p.tt([pk, "bm"], ["modT"], modT[:], pst[:, 0:96].rearrange("p (j s) -> p j s", s=2),
             bm[:].unsqueeze(2).to_broadcast([128, 48, 2]), ALU.add)
        for (Aq, key, nrm, j0) in ((A1, "A1", nm, 8), (A2, "A2", nf, 32)):
            p.ts(["modT"], [key], Aq[:], modT[:, j0:j0 + 8, :], 1.0, None, ALU.add)
            p.tt([key, "nm", "nf"], [key], Aq[:], Aq[:], nrm[:].unsqueeze(2).to_broadcast([128, 8, 2]), ALU.mult)
        p.release(m)

    def phase_norm(src, A, Akey, bj0, HT):
        m = p.mark()
        xr = Ring(p, "xn", [128, 8, 512], F32, 2)
        sqr = Ring(p, "sq", [128, 8, 512], BF16, 2)
        rr = Ring(p, "rstd", [128, 512], F32, 2)
        tr_ = Ring(p, "tmpn", [128, 512], F32, 3)
        for ti, (t0, tw) in enumerate(TT):
            s = 0 if ti == 0 else 1
            xt, xk = xr.next()
            p.dma("sp", ["XT"], [xk], xt[:, :, :tw], src[:, :, t0:t0 + tw])
            sq, sqk = sqr.next()
            p.act([xk], [sqk], sq[:, :, :tw], xt[:, :, :tw], AF.Square)
            pst, pk = psr.next()
            for kc in range(8):
                p.mm([sqk, "onesb"], [pk], pst[:, :tw], onesb[:], sq[:, kc, :tw], start=(kc == 0), stop=(kc == 7))
            rs, rk = rr.next()
            p.ts([pk], [rk], rs[:, :tw], pst[:, :tw], 1.0 / D, EPS, ALU.mult, ALU.add)
            p.act([rk], [rk], rs[:, :tw], rs[:, :tw], AF.Sqrt)
            p.op("dve", [rk], [rk], lambda e: e.reciprocal(out=rs[:, :tw], in_=rs[:, :tw]))
            for kc in range(8):
                tm, tk = tr_.next()
                p.tt([xk, rk], [tk], tm[:, :tw], xt[:, kc, :tw], rs[:, :tw], ALU.mult)
                p.act([tk, Akey, "modT"], ["HT"], HT[:, kc, t0:t0 + tw], tm[:, :tw], AF.Identity,
                      scale=A[:, kc, s:s + 1], bias=modT[:, bj0 + kc, s:s + 1])
        p.release(m)

    def ht_rhs(HT, kc, ti, ssd):
        t0, tw = TT[ti]
        if ti == 0 or not ssd:
            return HT[:, kc, t0:t0 + tw]
        i = ti - 1
        return HT[:, kc, NCTX:].rearrange("p (r c) -> p c r", c=64)[:, 8 * i:8 * i + 8, :]

    def ht_lhs(HT, kc, q):
        if q < 2:
            return HT[:, kc, q * 128:(q + 1) * 128]
        c2 = q - 2
        return HT[:, kc, NCTX:].rearrange("p (r c) -> p c r", c=64)[:, 2 * c2:2 * c2 + 2, :]

    def phase_proj(l, HT):
        m = p.mark()
        wring = Ring(p, "win", [128, 8, 512], BF16, 3)
        stg = Ring(p, "pstg", [128, T], F32, 2)
        stgb = Ring(p, "pstgb", [128, T], BF16, 2)
        wv = fm(w_in[l])
        ev = [0]

        def load_w(c0, cw):
            wt, wk = wring.next()
            p.dma("pool", [], [wk], wt[:, :, :cw], wv[:, :, c0:c0 + cw])
            return wt, wk

        def fm_group(c0, dst, dst_row0, dkey, ssd=False, gate=False, ncols=512):
            wt, wk = load_w(c0, ncols)
            for j in range((ncols + 127) // 128):
                cw_ = min(128, ncols - j * 128)
                st, stkey = (stgb if gate else stg).next()
                for ti, (t0, tw) in enumerate(TT):
                    pst, pk = psr.next()
                    for kc in range(8):
                        p.mm([wk, "HT"], [pk], pst[:cw_, :tw], wt[:, kc, j * 128:j * 128 + cw_], ht_rhs(HT, kc, ti, ssd),
                             start=(kc == 0), stop=(kc == 7))
                    if gate:
                        p.act([pk], [stkey], st[:cw_, t0:t0 + tw], pst[:cw_, :tw], AF.Sigmoid)
                    else:
                        ev[0] ^= 1
                        if ev[0]:
                            p.act([pk], [stkey], st[:cw_, t0:t0 + tw], pst[:cw_, :tw], AF.Identity)
                        else:
                            p.cp([pk], [stkey], st[:cw_, t0:t0 + tw], pst[:cw_, :tw])
                r0 = dst_row0 + j * 128
                p.dma("sp", [stkey], [dkey], dst[r0:r0 + cw_, :], st[:cw_, :])

        for g in range(10):
            fm_group(g * 512, PROJ, g * 512, "PROJ")
        for g in range(5):
            fm_group(5120 + g * 512, SSMP, g * 512, "SSMP", ssd=True)
        fm_group(7680, SSMP, 2560, "SSMP", ssd=True, ncols=32)
        for g in range(6):
            fm_group(7712 + g * 512, GT, g * 512, "GT", gate=True)
        p.release(m)

    def phase_rglru(l):
        m = p.mark()
        cw = p.sb("rcw", [128, 8, 4], F32)
        cb = p.sb("rcb", [128, 8], F32)
        ab = p.sb("rab", [128, 2, 8], F32)
        xb = p.sb("rxb", [128, 2, 8], F32)
        cA = p.sb("rcA", [128, 2, 8], F32)
        c2A = p.sb("rc2A", [128, 2, 8], F32)
        p.dma("sp", [], ["rcw"], cw[:], rnn_cwT[l])
        p.dma("sp", [], ["rcb"], cb[:], rnn_cbT[l])
        p.dma("sp", [], ["rab"], ab[:], rnn_abT[l])
        p.dma("sp", [], ["rxb"], xb[:], rnn_xbT[l])
        p.dma("sp", [], ["rcA"], cA[:], rnn_lamT[l])
        p.act(["rcA"], ["rcA"], cA[:], cA[:], AF.Exp, scale=-1.0)
        p.act(["rcA"], ["rcA"], cA[:], cA[:], AF.Ln, bias=1.0)
        p.ts(["rcA"], ["rc2A"], c2A[:], cA[:], -16.0, None, ALU.mult)
        p.ts(["rcA"], ["rcA"], cA[:], cA[:], -8.0, None, ALU.mult)
        T1 = p.sb("rT1", [128, T], F32)
        U = p.sb("rU", [128, T], F32)
        Af = p.sb("rA", [128, T], F32)
        Gf = p.sb("rG", [128, T], F32)
        HS = p.sb("rHS", [128, T], F32)
        Y = p.sb("rY", [128, T], BF16)
        gw = Ring(p, "rgw", [128, 4, 128], F32, 2)
        rr = Ring(p, "rr", [128, 512], F32, 2)
        ir = Ring(p, "ri", [128, 512], F32, 2)
        sr = Ring(p, "rs", [128, 512], F32, 2)
        segs = [(0, NCTX), (NCTX, T)]

        def rev(t, lo, hi):
            a = t[:, lo:hi]
            return bass.AP(a.tensor, a.offset + (hi - lo - 1), [list(a.ap[0]), [-1, hi - lo]])

        for hb in range(8):
            p.dma("sp", ["PROJ"], ["rT1"], T1[:], PROJ[hb * 128:(hb + 1) * 128, :])
            seg_conv(p, U, T1, lambda j: cw[:, hb, j:j + 1], cb[:, hb:hb + 1], 4, 2, segs, ["rT1", "rcw", "rcb"], ["rU"])
            g4, gk = gw.next()
            for d in range(2):
                p.dma("sp", [], [gk], g4[:, d, :], rnn_aw[l, d, hb])
                p.dma("sp", [], [gk], g4[:, 2 + d, :], rnn_xw[l, d, hb])
            for d in range(2):
                for ti, (t0, tw) in enumerate(TT):
                    pa, pak = psr.next()
                    px, pxk = psr.next()
                    p.mm([gk, "rU"], [pak], pa[:, :tw], g4[:, d, :], U[:, t0:t0 + tw])
                    p.mm([gk, "rU"], [pxk], px[:, :tw], g4[:, 2 + d, :], U[:, t0:t0 + tw])
                    r_, rk = rr.next()
                    i_, ik = ir.next()
                    s_, sk_ = sr.next()
                    p.act([pak, "rab"], [rk], r_[:, :tw], pa[:, :tw], AF.Sigmoid, bias=ab[:, d, hb:hb + 1])
                    p.act([pxk, "rxb"], [ik], i_[:, :tw], px[:, :tw], AF.Sigmoid, bias=xb[:, d, hb:hb + 1])
                    p.act([rk, "rc2A"], [sk_], s_[:, :tw], r_[:, :tw], AF.Exp, scale=c2A[:, d, hb:hb + 1])
                    p.act([rk, "rcA"], ["rA"], Af[:, t0:t0 + tw], r_[:, :tw], AF.Exp, scale=cA[:, d, hb:hb + 1])
                    p.act([sk_], [sk_], s_[:, :tw], s_[:, :tw], AF.Sqrt, scale=-1.0, bias=1.0)
                    p.tt([ik, sk_], [ik], i_[:, :tw], i_[:, :tw], s_[:, :tw], ALU.mult)
                    p.tt([ik, "rU"], ["rG"], Gf[:, t0:t0 + tw], i_[:, :tw], U[:, t0:t0 + tw], ALU.mult)
                if d == 0:
                    p.op("dve", ["rA", "rG"], ["rHS"], lambda e: e.tensor_tensor_scan(
                        out=HS[:, :], data0=Af[:, :], data1=Gf[:, :], initial=0.0, op0=ALU.mult, op1=ALU.add))
                else:
                    p.op("dve", ["rA", "rG"], ["rT1"], lambda e: e.tensor_tensor_scan(
                        out=rev(T1, 0, NCTX), data0=rev(Af, 0, NCTX), data1=rev(Gf, 0, NCTX), initial=0.0,
                        op0=ALU.mult, op1=ALU.add))
                    p.op("dve", ["rA", "rG", "rT1"], ["rT1"], lambda e: e.tensor_tensor_scan(
                        out=rev(T1, NCTX, T), data0=rev(Af, NCTX, T), data1=rev(Gf, NCTX, T), initial=T1[:, 0:1],
                        op0=ALU.mult, op1=ALU.add))
                    p.tt(["rHS", "rT1"], ["rHS"], HS[:, :], HS[:, :], T1[:, :], ALU.add)
            p.dma("sp", ["PROJ"], ["rT1"], T1[:], PROJ[1024 + hb * 128:1024 + (hb + 1) * 128, :])
            p.tt(["rT1"], ["rG"], Gf[:, :], T1[:, :], T1[:, :], ALU.mult)
            p.ts(["rG"], ["rG"], Gf[:, :], Gf[:, :], 0.044715, 1.0, ALU.mult, ALU.add)
            p.tt(["rG", "rT1"], ["rG"], Gf[:, :], Gf[:, :], T1[:, :], ALU.mult)
            p.act(["rG"], ["rG"], Gf[:, :], Gf[:, :], AF.Sigmoid, scale=1.5957691216057308)
            p.tt(["rG", "rT1"], ["rG"], Gf[:, :], Gf[:, :], T1[:, :], ALU.mult)
            p.tt(["rG", "rHS"], ["rY"], Y[:, :], Gf[:, :], HS[:, :], ALU.mult)
            p.dma("sp", ["rY"], ["YR"], YR[hb * 128:(hb + 1) * 128, :], Y[:])
        p.release(m)

    def phase_ssd(l):
        m = p.mark()
        psb = psr.tiles
        pkk = psr.keys
        cw = p.sb("scw", [128, 12, 4], F32)
        cb = p.sb("scb", [128, 12], F32)
        p.dma("sp", [], ["scw"], cw[:], ssm_cwT[l])
        p.dma("sp", [], ["scb"], cb[:], ssm_cbT[l])
        XTOK = p.sb("XTOK", [128, 34, 1024], BF16)
        BT = p.sb("BT", [128, 2, T], BF16)
        CT = p.sb("CT", [128, 2, T], BF16)
        DT_tok = p.sb("DT_tok", [128, 34, 32], F32)
        ADT_tok = p.sb("ADT_tok", [128, 34, 32], F32)
        EA = p.sb("EA", [128, 34, 32], F32)
        DTE = p.sb("DTE", [128, 34, 32], F32)
        DEC = p.sb("DEC", [128, 34, 32], F32)
        mark_a = p.mark()
        BTOK = p.sb("BTOK", [128, 34, 256], BF16)
        DTD = p.sb("DTD", [128, 34, 32], F32)
        segs = [(0, NCTX), (NCTX, T)]
        m1 = p.mark()
        T1r = Ring(p, "sT1", [128, T], F32, 2)
        XSr = Ring(p, "sXS", [128, T], F32, 1)
        for blk in range(12):
            T1, t1k = T1r.next()
            XS, xsk = XSr.next()
            p.dma("sp", ["SSMP"], [t1k], T1[:], SSMP[1024 + blk * 128:1024 + (blk + 1) * 128, :])
            seg_conv(p, XS, T1, lambda j: cw[:, blk, j:j + 1], cb[:, blk:blk + 1], 4, 2, segs, [t1k, "scw", "scb"], [xsk])
            p.act([xsk], [xsk], XS[:, :], XS[:, :], AF.Silu)
            if blk >= 8:
                g = (blk - 8) % 2
                dstT, dk = (BT, "BT") if blk < 10 else (CT, "CT")
                p.cp([xsk], [dk], dstT[:, g, :], XS[:, :])
            if blk < 10:
                for q0 in range(0, 34, 4):
                    nq = min(4, 34 - q0)
                    pst, pk = psr.next()
                    for qi in range(nq):
                        q = q0 + qi
                        p.tr([xsk, "ident"], [pk], pst[:, qi * 128:(qi + 1) * 128], XS[:, q * 128:(q + 1) * 128], ident[:])
                    if blk < 8:
                        p.cp([pk], ["XTOK"], XTOK[:, q0:q0 + nq, blk * 128:(blk + 1) * 128],
                             pst[:, :nq * 128].rearrange("p (q c) -> p q c", c=128), eng=("dve" if (q0 // 4) % 2 else "act") if False else "dve")
                    else:
                        g = blk - 8
                        p.cp([pk], ["BTOK"], BTOK[:, q0:q0 + nq, g * 128:(g + 1) * 128],
                             pst[:, :nq * 128].rearrange("p (q c) -> p q c", c=128))
        p.release(m1)
        m2 = p.mark()
        DTF = p.sb("DTF", [32, T], F32)
        ADF = p.sb("ADF", [32, T], F32)
        dtb = p.sb("dtb", [32, 1], F32)
        aneg = p.sb("aneg", [32, 1], F32)
        p.dma("sp", ["SSMP"], ["DTF"], DTF[:], SSMP[2560:2592, :])
        p.dma("sp", [], ["dtb"], dtb[:], ssm_dtbT[l])
        p.dma("sp", [], ["aneg"], aneg[:], ssm_alogT[l])
        p.act(["aneg"], ["aneg"], aneg[:], aneg[:], AF.Exp)
        p.ts(["aneg"], ["aneg"], aneg[:], aneg[:], -1.0, None, ALU.mult)
        p.act(["DTF", "dtb"], ["DTF"], DTF[:, :], DTF[:, :], AF.Exp, bias=dtb[:, 0:1])
        p.act(["DTF"], ["DTF"], DTF[:, :], DTF[:, :], AF.Ln, bias=1.0)
        p.ts(["DTF", "aneg"], ["ADF"], ADF[:, :], DTF[:, :], aneg[:, 0:1], None, ALU.mult)
        for (src, sk_, dst, dk) in ((DTF, "DTF", DT_tok, "DT_tok"), (ADF, "ADF", ADT_tok, "ADT_tok")):
            for q0 in range(0, 34, 16):
                nq = min(16, 34 - q0)
                pst, pk = psr.next()
                for qi in range(nq):
                    q = q0 + qi
                    p.tr([sk_, "ident"], [pk], pst[:, qi * 32:(qi + 1) * 32], src[:, q * 128:(q + 1) * 128], ident[:32, :32])
                p.cp([pk], [dk], dst[:, q0:q0 + nq, :], pst[:, :nq * 32].rearrange("p (q c) -> p q c", c=32))
        for d in range(2):
            for cg in range(2):
                rhs = ADT_tok[:, cg * 17:(cg + 1) * 17, d * 16:(d + 1) * 16]
                for (mk_, dst, dk) in (((LE, GE)[d], EA, "EA"), ((GT_, LT)[d], DTE, "DTE"), (ONES, DEC, "DEC")):
                    pst, pk = psr.next()
                    p.mm(["masks", "ADT_tok"], [pk], pst[:, :272], masks[:, mk_, :], rhs)
                    p.act([pk], [dk], dst[:, cg * 17:(cg + 1) * 17, d * 16:(d + 1) * 16],
                          pst[:, :272].rearrange("p (q c) -> p q c", c=16), AF.Exp)
        p.tt(["DT_tok", "DTE"], ["DTD"], DTD[:], DT_tok[:], DTE[:], ALU.mult)
        p.release(m2)
        H = p.sb("Hst", [128, 1024], F32)
        hbr = Ring(p, "Hb", [128, 1024], BF16, 3)
        xsr = Ring(p, "xsd", [128, 1024], BF16, 3)
        for d in range(2):
            order = list(range(34)) if d == 0 else [1, 0] + list(range(33, 1, -1))
            p.op("dve", [], ["Hst"], lambda e: e.memset(H[:], 0.0))
            for q in order:
                hb_, hbk = hbr.next()
                p.act(["Hst"], [hbk], hb_[:], H[:], AF.Identity)
                p.dma("sp", [hbk], ["HP"], HP[d, q], hb_[:])
                xs, xk = xsr.next()
                p.tt(["XTOK", "DTD"], [xk], xs[:].rearrange("p (h c) -> p h c", c=64),
                     XTOK[:, q, :].rearrange("p (h c) -> p h c", c=64),
                     DTD[:, q, d * 16:(d + 1) * 16].unsqueeze(2).to_broadcast([128, 16, 64]), ALU.mult)
                pss = []
                for g in range(2):
                    pst, pk = psr.next()
                    p.mm(["BTOK", xk], [pk], pst[:, :], BTOK[:, q, g * 128:(g + 1) * 128], xs[:, g * 512:(g + 1) * 512])
                    pss.append((pst, pk))
                p.tt(["Hst", "DEC"], ["Hst"], H[:].rearrange("p (h c) -> p h c", c=64), H[:].rearrange("p (h c) -> p h c", c=64),
                     DEC[:, q, d * 16:(d + 1) * 16].unsqueeze(2).to_broadcast([128, 16, 64]), ALU.mult)
                for g in range(2):
                    p.tt(["Hst", pss[g][1]], ["Hst"], H[:, g * 512:(g + 1) * 512], H[:, g * 512:(g + 1) * 512], pss[g][0][:, :], ALU.add)
        p.release(mark_a)
        dsk = p.sb("dsk", [128, 16], F32)
        nw = p.sb("snw", [128, 1024], F32)
        p.dma("sp", [], ["dsk"], dsk[:], ssm_d[l:l + 1, :].to_broadcast([128, 16]))
        p.dma("sp", [], ["snw"], nw[:], ssm_norm[l:l + 1, :].to_broadcast([128, 1024]))
        hpr = Ring(p, "hp", [128, 2, 1024], BF16, 2)
        cbmr = Ring(p, "cbm", [128, 2, 256], F32, 2)
        rsr = Ring(p, "rseg", [128, 16, 128], F32, 1)
        er = Ring(p, "eseg", [128, 16, 128], F32, 1)
        mr = Ring(p, "mseg", [128, 16, 128], BF16, 2)
        xdr = Ring(p, "xdt", [128, 1024], BF16, 2)
        accr = Ring(p, "acc", [128, 1024], F32, 2)
        tmpr = Ring(p, "stmp", [128, 1024], F32, 1)
        zfr = Ring(p, "zf", [128, 8, 128], F32, 1)
        szr = Ring(p, "sz", [128, 1024], F32, 2)
        ysr = Ring(p, "ys", [128, 1024], F32, 2)
        ssr = Ring(p, "ssq", [128, 2], F32, 2)
        for q in range(34):
            hp, hpk = hpr.next()
            for d in range(2):
                p.dma("sp", ["HP"], [hpk], hp[:, d, :], HP[d, q])
            zf, zfk = zfr.next()
            p.dma("sp", ["SSMP"], [zfk], zf[:], SSMP[0:1024, q * 128:(q + 1) * 128].rearrange("(b c) t -> c b t", c=128))
            for g in range(2):
                p.mm(["BT", "CT"], [pkk[0]], psb[0][:, g * 128:(g + 1) * 128], BT[:, g, q * 128:(q + 1) * 128], CT[:, g, q * 128:(q + 1) * 128])
            cbm, cbk = cbmr.next()
            for d in range(2):
                p.tt([pkk[0], "masks"], [cbk], cbm[:, d, :].rearrange("p (g c) -> p g c", c=128),
                     psb[0][:, 0:256].rearrange("p (g c) -> p g c", c=128),
                     masks[:, (LE, GE)[d], :].unsqueeze(1).to_broadcast([128, 2, 128]), ALU.mult)
            acc, acck = accr.next()
            for d in range(2):
                rs, rsk = rsr.next()
                p.tt(["ADT_tok", "masks"], [rsk], rs[:], ADT_tok[:, q, d * 16:(d + 1) * 16].unsqueeze(2).to_broadcast([128, 16, 128]),
                     masks[:, (LE, GE)[d], :].unsqueeze(1).to_broadcast([128, 16, 128]), ALU.mult)
                es, esk = er.next()
                for i in range(4):
                    p.mm(["masks", rsk], [pkk[1 + i]], psb[1 + i][:, :], masks[:, (GT_, LT)[d], :],
                         rs[:, 4 * i:4 * i + 4, :])
                    p.act([pkk[1 + i]], [esk], es[:, 4 * i:4 * i + 4, :], psb[1 + i][:, :].rearrange("p (h c) -> p h c", c=128), AF.Exp)
                ms, msk = mr.next()
                p.tt([esk, cbk], [msk], ms[:].rearrange("p (g e) c -> p g e c", g=2), es[:].rearrange("p (g e) c -> p g e c", g=2),
                     cbm[:, d, :].rearrange("p (g c) -> p g c", c=128).unsqueeze(2).to_broadcast([128, 2, 8, 128]), ALU.mult)
                xd, xdk = xdr.next()
                p.tt(["XTOK", "DT_tok"], [xdk], xd[:].rearrange("p (h c) -> p h c", c=64),
                     XTOK[:, q, :].rearrange("p (h c) -> p h c", c=64),
                     DT_tok[:, q, d * 16:(d + 1) * 16].unsqueeze(2).to_broadcast([128, 16, 64]), ALU.mult)
                for h in range(16):
                    bk = 5 + h // 8
                    p.mm([msk, xdk], [pkk[bk]], psb[bk][:, (h % 8) * 64:(h % 8 + 1) * 64], ms[:, h, :], xd[:, h * 64:(h + 1) * 64],
                         start=(d == 0 and h % 8 == 0), stop=(d == 1 and h % 8 == 7))
                for g in range(2):
                    bk = 7 if g == 0 else 0
                    p.mm(["CT", hpk], [pkk[bk]], psb[bk][:, :], CT[:, g, q * 128:(q + 1) * 128], hp[:, d, g * 512:(g + 1) * 512])
                    eab = EA[:, q, d * 16 + g * 8:d * 16 + (g + 1) * 8].unsqueeze(2).to_broadcast([128, 8, 64])
                    if d == 0:
                        p.tt([pkk[bk], "EA"], [acck], acc[:, g * 512:(g + 1) * 512].rearrange("p (h c) -> p h c", c=64),
                             psb[bk][:, :].rearrange("p (h c) -> p h c", c=64), eab, ALU.mult)
                    else:
                        tm, tmk = tmpr.next()
                        p.tt([pkk[bk], "EA"], [tmk], tm[:, :512].rearrange("p (h c) -> p h c", c=64),
                             psb[bk][:, :].rearrange("p (h c) -> p h c", c=64), eab, ALU.mult)
                        p.tt([acck, tmk], [acck], acc[:, g * 512:(g + 1) * 512], acc[:, g * 512:(g + 1) * 512], tm[:, :512], ALU.add)
            for g in range(2):
                p.tt([acck, pkk[5 + g]], [acck], acc[:, g * 512:(g + 1) * 512], acc[:, g * 512:(g + 1) * 512], psb[5 + g][:, :], ALU.add)
            tm, tmk = tmpr.next()
            p.tt(["XTOK", "dsk"], [tmk], tm[:].rearrange("p (h c) -> p h c", c=64), XTOK[:, q, :].rearrange("p (h c) -> p h c", c=64),
                 dsk[:].unsqueeze(2).to_broadcast([128, 16, 64]), ALU.mult)
            p.tt([acck, tmk], [acck], acc[:], acc[:], tm[:], ALU.add)
            sz, szk = szr.next()
            for g in range(2):
                for b4 in range(4):
                    p.tr([zfk, "ident"], [pkk[1 + g]], psb[1 + g][:, b4 * 128:(b4 + 1) * 128], zf[:, g * 4 + b4, :], ident[:])
                p.act([pkk[1 + g]], [szk], sz[:, g * 512:(g + 1) * 512], psb[1 + g][:, :], AF.Silu)
            p.tt([acck, szk], [acck], acc[:], acc[:], sz[:], ALU.mult)
            ss, ssk = ssr.next()
            p.act([acck], [szk, ssk], sz[:], acc[:], AF.Square, accum_out=ss[:, 0:1])
            p.ts([ssk], [ssk], ss[:, 1:2], ss[:, 0:1], 1.0 / 1024, EPS, ALU.mult, ALU.add)
            p.act([ssk], [ssk], ss[:, 1:2], ss[:, 1:2], AF.Sqrt)
            p.op("dve", [ssk], [ssk], lambda e: e.reciprocal(out=ss[:, 1:2], in_=ss[:, 1:2]))
            ys, ysk = ysr.next()
            p.stt([acck, ssk, "snw"], [ysk], ys[:], acc[:], ss[:, 1:2], nw[:], ALU.mult, ALU.mult)
            if q < 2:
                p.dma("sp", [ysk], ["YSTOK"], YSTOK[q * 128:(q + 1) * 128, :], ys[:])
            else:
                c2 = q - 2
                yv = YSTOK[NCTX:, :].rearrange("(r c) d -> c r d", c=64)
                for cl in range(2):
                    p.dma("sp", [ysk], ["YSTOK"], yv[2 * c2 + cl], ys[cl * 64:(cl + 1) * 64, :])
        p.release(m)

    class Rot:
        def __init__(self, idxs):
            self.idxs = idxs
            self.i = 0

        def next(self):
            k = self.idxs[self.i]
            self.i = (self.i + 1) % len(self.idxs)
            return psr.tiles[k], psr.keys[k]

    PI = float(np.pi)

    def hy_filter(l, n, embT_d, tv_d, FW, KF):
        m = p.mark()
        nt = n // 128
        psb, pkk = psr.tiles, psr.keys
        w1 = p.sb("hw1", [33, 64], F32)
        w2 = p.sb("hw2", [64, 64], F32)
        w3 = p.sb("hw3", [64, 4096], F32)
        fr = p.sb("hfr", [64, 1], F32)
        fb1 = p.sb("hfb1", [64, 1], F32)
        fb2 = p.sb("hfb2", [64, 1], F32)
        emb = p.sb("hemb", [33, n], F32)
        h1 = p.sb("hh1", [64, n], F32)
        h2 = p.sb("hh2", [64, n], F32)
        negpi = p.sb("hnegpi", [128, 1], F32)
        nz0 = p.sb("hnz0", [128, 1], F32)
        dec = p.sb("hdec", [128, 4096], F32)
        negt = p.sb("hnegt", [128, nt], F32)
        p.dma("sp", [], ["hw1"], w1[:], hy_w1[l])
        p.dma("sp", [], ["hw2"], w2[:], hy_w2[l])
        p.dma("sp", [], ["hw3"], w3[:], hy_w3[l])
        p.dma("sp", [], ["hfr"], fr[:], hy_freqT[l])
        p.dma("sp", [], ["hfb1"], fb1[:], hy_b1T[l])
        p.dma("sp", [], ["hfb2"], fb2[:], hy_b2T[l])
        p.dma("sp", [], ["hemb"], emb[:], embT_d[:, :])
        p.dma("sp", [], ["hdec"], dec[:], hy_decay[l:l + 1, :].to_broadcast([128, 4096]))
        p.dma("sp", [], ["hnegt"], negt[:], tv_d[:, :])
        p.ts(["hnegt"], ["hnegt"], negt[:], negt[:], -1.0, None, ALU.mult)
        p.op("dve", [], ["hnegpi"], lambda e: e.memset(negpi[:], -PI))
        p.op("dve", [], ["hnz0"], lambda e: e.memset(nz0[:], 1.0))
        p.op("dve", ["hnz0"], ["hnz0"], lambda e: e.memset(nz0[0:1, :], 0.0))
        p.ts(["hfb1", "hfr"], ["hfb1"], fb1[:], fb1[:], fr[:, 0:1], None, ALU.mult)
        p.ts(["hfb2", "hfr"], ["hfb2"], fb2[:], fb2[:], fr[:, 0:1], None, ALU.mult)
        rot = Rot([0, 1, 2, 3, 4, 5, 6])
        sinr = Ring(p, "hsin", [64, 512], F32, 4)
        for (src, sk_, wgt, wk, fb, fbk, dst, dk) in ((emb, "hemb", w1, "hw1", fb1, "hfb1", h1, "hh1"),
                                                       (h1, "hh1", w2, "hw2", fb2, "hfb2", h2, "hh2")):
            for c0 in range(0, n, 512):
                cwid = min(512, n - c0)
                pst, pk = rot.next()
                p.mm([sk_, wk], [pk], pst[:64, :cwid], wgt[:, :], src[:, c0:c0 + cwid])
                p.ts([pk, "hfr", fbk], [dk], dst[:, c0:c0 + cwid], pst[:64, :cwid], fr[:, 0:1], fb[:, 0:1], ALU.mult, ALU.add)
                sa, sak = sinr.next()
                sb_, sbk = sinr.next()
                dv = dst[:, c0:c0 + cwid]
                p.act([dk], [sak], sa[:, :cwid], dv, AF.Sin, scale=0.25)
                p.act([dk], [sbk], sb_[:, :cwid], dv, AF.Sin, scale=0.125)
                p.tt([sbk], [sbk], sb_[:, :cwid], sb_[:, :cwid], sb_[:, :cwid], ALU.mult)
                p.ts([sbk], [sbk], sb_[:, :cwid], sb_[:, :cwid], -2.0, 1.0, ALU.mult, ALU.add)
                p.tt([sak, sbk], [sbk], sb_[:, :cwid], sa[:, :cwid], sb_[:, :cwid], ALU.mult)
                p.tt([sak], [sak], sa[:, :cwid], sa[:, :cwid], sa[:, :cwid], ALU.mult)
                p.ts([sak], [sak], sa[:, :cwid], sa[:, :cwid], -2.0, 1.0, ALU.mult, ALU.add)
                p.stt([sak, sbk], [dk], dv, sb_[:, :cwid], 4.0, sa[:, :cwid], ALU.mult, ALU.mult)
        UP = p.sb("hUP", [128, nt, 512], BF16)
        UM = p.sb("hUM", [128, nt, 512], BF16)
        rinv = p.sb("hrinv", [128, 512], F32)
        er = Ring(p, "hE", [128, 512], F32, 2)
        hfr_ = Ring(p, "hhf", [128, 512], F32, 2)
        hbr_ = Ring(p, "hhb", [128, 512], F32, 2)
        abr = Ring(p, "hab", [128, 512], F32, 2)
        fring = Ring(p, "hF", [128, nt, 128], BF16, 3)
        kr = Ring(p, "hkt", [128, 512], F32, 3)
        for o in range(2):
            for cg in range(2):
                colf = o * 1024 + cg * 512
                colb = 2048 + colf
                for tc in range(nt):
                    hh = []
                    for dirn, col in ((0, colf), (1, colb)):
                        pst, pk = rot.next()
                        p.mm(["hh2", "hw3"], [pk], pst[:, :], h2[:, tc * 128:(tc + 1) * 128], w3[:, col:col + 512])
                        E, ek = er.next()
                        p.act(["hdec", "hnegt"], [ek], E[:], dec[:, col:col + 512], AF.Exp, scale=negt[:, tc:tc + 1])
                        ht_, hk = (hfr_ if dirn == 0 else hbr_).next()
                        p.tt([pk, ek], [hk], ht_[:], pst[:, :], E[:], ALU.mult)
                        if dirn == 1 and tc == 0:
                            p.ts([hk, "hnz0"], [hk], ht_[:], ht_[:], nz0[:, 0:1], None, ALU.mult)
                        ab, abk = abr.next()
                        p.act([hk], [abk], ab[:], ht_[:], AF.Abs)
                        p.mm(["masks", abk], [pkk[7]], psb[7][:, :], masks[:, ONES, :], ab[:],
                             start=(tc == 0 and dirn == 0), stop=(tc == nt - 1 and dirn == 1))
                        hh.append((ht_, hk))
                    p.tt([hh[0][1], hh[1][1]], ["hUP"], UP[:, tc, :], hh[0][0][:], hh[1][0][:], ALU.add)
                    p.tt([hh[0][1], hh[1][1]], ["hUM"], UM[:, tc, :], hh[0][0][:], hh[1][0][:], ALU.subtract)
                p.op("dve", [pkk[7]], ["hrinv"], lambda e: e.reciprocal(out=rinv[:], in_=psb[7][:, :]))
                for j in range(nt):
                    for (pq, U, uk) in ((0, UP, "hUP"), (1, UM, "hUM")):
                        Ft, fk = fring.next()
                        p.dma("sp", [], [fk], Ft[:], FW[pq * nt + j])
                        pst, pk = rot.next()
                        for tc in range(nt):
                            p.mm([fk, uk], [pk], pst[:, :], Ft[:, tc, :], U[:, tc, :], start=(tc == 0), stop=(tc == nt - 1))
                        kt, kk = kr.next()
                        p.tt([pk, "hrinv"], [kk], kt[:], pst[:, :], rinv[:], ALU.mult)
                        p.dma("sp", [kk], ["KF"], KF[o, j, :, pq, cg * 512:(cg + 1) * 512], kt[:])
        p.release(m)

    def hy_data(l, n, toff, FW, GW, KF, hbias):
        m = p.mark()
        nt = n // 128
        psb, pkk = psr.tiles, psr.keys
        tiles = [(i * 512, 512) for i in range(n // 512)] if n >= 512 else [(0, n)]
        ZT = p.sb("hZT", [128, nt, 512], BF16)
        YSs = p.sb("hYS", [128, nt, 2, 512], BF16)
        zfr = Ring(p, "hzf", [128, n], F32, 1)
        fring = Ring(p, "hF2", [128, nt, 128], BF16, 3)
        kr = Ring(p, "hkt2", [128, 2, 512], F32, 2)
        tr_ = Ring(p, "htm", [128, 512], F32, 4)
        gring = Ring(p, "hG", [128, 8, 512], BF16, 3)
        zpr = Ring(p, "hzp", [128, 512], F32, 2)
        xgr = Ring(p, "hxg", [128, 512], F32, 2)
        znr = Ring(p, "hzn", [128, 512], F32, 2)
        ynr = Ring(p, "hyn", [128, 512], BF16, 2)
        rot = Rot([4, 5, 6, 7])
        for cg in range(2):
            for o in range(2):
                src = HY if o == 0 else Z2
                skey = "HY" if o == 0 else "Z2"
                for b in range(4):
                    zf, zk = zfr.next()
                    r0 = cg * 512 + b * 128
                    p.dma("sp", [skey], [zk], zf[:], src[r0:r0 + 128, toff:toff + n])
                    for tc0 in range(0, nt, 4):
                        nq = min(4, nt - tc0)
                        pst, pk = rot.next()
                        for qi in range(nq):
                            p.tr([zk, "ident"], [pk], pst[:, qi * 128:(qi + 1) * 128], zf[:, (tc0 + qi) * 128:(tc0 + qi + 1) * 128], ident[:])
                        p.cp([pk], ["hZT"], ZT[:, tc0:tc0 + nq, b * 128:(b + 1) * 128], pst[:, :nq * 128].rearrange("p (q c) -> p q c", c=128))
                for j in range(nt):
                    Fc, fck = fring.next()
                    p.dma("sp", [], [fck], Fc[:], FW[j])
                    Fs, fsk = fring.next()
                    p.dma("sp", [], [fsk], Fs[:], FW[nt + j])
                    kt, kk = kr.next()
                    p.dma("sp", ["KF"], [kk], kt[:], KF[o, j, :, :, cg * 512:(cg + 1) * 512])
                    pA, pAk = rot.next()
                    pB, pBk = rot.next()
                    for tc in range(nt):
                        p.mm([fck, "hZT"], [pAk], pA[:, :], Fc[:, tc, :], ZT[:, tc, :], start=(tc == 0), stop=(tc == nt - 1))
                    for tc in range(nt):
                        p.mm([fsk, "hZT"], [pBk], pB[:, :], Fs[:, tc, :], ZT[:, tc, :], start=(tc == 0), stop=(tc == nt - 1))
                    t1, k1 = tr_.next()
                    t2, k2 = tr_.next()
                    p.tt([pAk, kk], [k1], t1[:], pA[:, :], kt[:, 0, :], ALU.mult)
                    p.tt([pBk, kk], [k2], t2[:], pB[:, :], kt[:, 1, :], ALU.mult)
                    p.tt([k1, k2], ["hYS"], YSs[:, j, 0, :], t1[:], t2[:], ALU.subtract)
                    t3, k3 = tr_.next()
                    t4, k4 = tr_.next()
                    p.tt([pAk, kk], [k3], t3[:], pA[:, :], kt[:, 1, :], ALU.mult)
                    p.tt([pBk, kk], [k4], t4[:], pB[:, :], kt[:, 0, :], ALU.mult)
                    p.tt([k3, k4], ["hYS"], YSs[:, j, 1, :], t3[:], t4[:], ALU.add)
                for ti, (t0, tw) in enumerate(tiles):
                    Gt, gk = None, None
                    for j2 in range(2 * nt):
                        if j2 % 8 == 0:
                            ng = min(8, 2 * nt - j2)
                            Gt, gk = gring.next()
                            p.dma("sp", [], [gk], Gt[:, :ng, :tw], GW[ti, :, j2:j2 + ng, :])
                        part, j = j2 // nt, j2 % nt
                        for b in range(4):
                            p.mm(["hYS", gk], [pkk[b]], psb[b][:, :tw], YSs[:, j, part, b * 128:(b + 1) * 128], Gt[:, j2 % 8, :tw],
                                 start=(j2 == 0), stop=(j2 == 2 * nt - 1))
                    for b in range(4):
                        cb_ = cg * 4 + b
                        zp, zpk = zpr.next()
                        p.dma("sp", [skey], [zpk], zp[:, :tw], src[cb_ * 128:(cb_ + 1) * 128, toff + t0:toff + t0 + tw])
                        xg, xgk = xgr.next()
                        xr0 = (1 + o) * 1024 + cb_ * 128
                        p.dma("sp", ["HY"], [xgk], xg[:, :tw], HY[xr0:xr0 + 128, toff + t0:toff + t0 + tw])
                        tm, tmk = tr_.next()
                        p.stt([zpk, "hbias", pkk[b]], [tmk], tm[:, :tw], zp[:, :tw], hbias[:, o, cb_:cb_ + 1], psb[b][:, :tw], ALU.mult, ALU.add)
                        if o == 0:
                            zn, znk = znr.next()
                            p.tt([tmk, xgk], [znk], zn[:, :tw], tm[:, :tw], xg[:, :tw], ALU.mult)
                            p.dma("sp", [znk], ["Z2"], Z2[cb_ * 128:(cb_ + 1) * 128, toff + t0:toff + t0 + tw], zn[:, :tw])
                        else:
                            yn, ynk = ynr.next()
                            p.tt([tmk, xgk], [ynk], yn[:, :tw], tm[:, :tw], xg[:, :tw], ALU.mult)
                            p.dma("sp", [ynk], ["YH"], YH[cb_ * 128:(cb_ + 1) * 128, toff + t0:toff + t0 + tw], yn[:, :tw])
        p.release(m)


    def hy4(l, hbias):
        m = p.mark()
        n = NLAT
        toff = NCTX
        psb, pkk = psr.tiles, psr.keys
        identb = p.sb("identb", [128, 128], BF16)
        p.cp(["ident"], ["identb"], identb[:], ident[:])
        S1 = p.sb("S1", [32, 128], BF16)
        S2 = p.sb("S2", [128, 32], BF16)
        p.dma("sp", [], ["S1"], S1[:], S1_d[:, :])
        p.dma("sp", [], ["S2"], S2[:], S2_d[:, :])
        rot = Rot([0, 1, 2, 3, 4, 5, 6, 7])
        evc = [0]

        def evac(R, W, out, in_):
            evc[0] ^= 1
            if evc[0]:
                p.act(R, W, out, in_, AF.Identity)
            else:
                p.cp(R, W, out, in_)

        CB = 64

        def v_zin(X):
            return X[:32, :].rearrange("p (c t) -> p c t", t=128)

        def v_A(X):
            return X[:, :].rearrange("p (r f c) -> p r f c", r=2, f=64)

        def v_Y(X):
            return X[:64, :].rearrange("p (r f c) -> p r f c", r=2, f=64)

        def v_B(X):
            return X[:, :].rearrange("p (c k) -> p c k", k=128)

        def load_zin(X, xk, src_rows, skey):
            zv = v_zin(X)
            for c4 in range(CB // 32):
                p.dma("sp", [skey], [xk], zv[:, c4 * 32:(c4 + 1) * 32, :],
                      src_rows[c4 * 32:(c4 + 1) * 32, :].rearrange("c (a b) -> a c b", b=128))

        def stage1(Xi, xik, Xo, xok):
            zin = v_zin(Xi)
            Ac = v_B(Xo)
            for c0 in range(0, CB, 4):
                pst, pk = rot.next()
                for q in range(4):
                    p.mm([xik, "S1"], [pk], pst[:, q * 128:(q + 1) * 128], zin[:, c0 + q, :], S1[:, :])
                evac([pk], [xok], Ac[:, c0:c0 + 4, :], pst[:, :].rearrange("p (c a) -> p c a", a=128))

        def stage2(XA_, xak, g8, XA2=None, xak2=None):
            A = v_B(XA_)
            A2 = v_B(XA2) if XA2 is not None else A
            k2 = xak2 if XA2 is not None else xak
            zr, zrk = rot.next()
            zi, zik = rot.next()
            for q in range(8):
                f1 = g8 * 8 + q
                o_ = slice(q * CB, (q + 1) * CB)
                p.mm(["WF", xak], [zrk], zr[:64, o_], WF[:, f1, 0, :], A[:, :, f1], start=True, stop=False)
                p.mm(["WF", xak], [zrk], zr[:64, o_], WF[:, f1, 2, :], A[:, :, 64 + f1], start=False, stop=True)
            for q in range(8):
                f1 = g8 * 8 + q
                o_ = slice(q * CB, (q + 1) * CB)
                p.mm(["WF", k2], [zik], zi[:64, o_], WF[:, f1, 1, :], A2[:, :, f1], start=True, stop=False)
                p.mm(["WF", k2], [zik], zi[:64, o_], WF[:, f1, 0, :], A2[:, :, 64 + f1], start=False, stop=True)
            return zr, zrk, zi, zik

        m0 = p.mark()
        w1 = p.sb("hw1", [33, 64], F32)
        w2 = p.sb("hw2", [64, 64], F32)
        w3 = p.sb("hw3", [64, 4096], F32)
        fr = p.sb("hfr", [64, 1], F32)
        fb1 = p.sb("hfb1", [64, 1], F32)
        fb2 = p.sb("hfb2", [64, 1], F32)
        h2 = p.sb("hh2", [64, n], F32)
        ndec = p.sb("hndec", [128, 32], F32)
        tbc = p.sb("htbc", [128, n], F32)
        p.dma("sp", [], ["hw1"], w1[:], hy_w1[l])
        p.dma("sp", [], ["hw2"], w2[:], hy_w2[l])
        p.dma("sp", [], ["hw3"], w3[:], hy_w3[l])
        p.dma("sp", [], ["hfr"], fr[:], hy_freqT[l])
        p.dma("sp", [], ["hfb1"], fb1[:], hy_b1T[l])
        p.dma("sp", [], ["hfb2"], fb2[:], hy_b2T[l])
        p.dma("sp", [], ["hndec"], ndec[:], hy_ndecT[l])
        p.dma("sp", [], ["htbc"], tbc[:], tvec[0:1, :].to_broadcast([128, n]))
        p.ts(["hfb1", "hfr"], ["hfb1"], fb1[:], fb1[:], fr[:, 0:1], None, ALU.mult)
        p.ts(["hfb2", "hfr"], ["hfb2"], fb2[:], fb2[:], fr[:, 0:1], None, ALU.mult)
        mm_ = p.mark()
        emb = p.sb("hemb", [33, n], F32)
        h1 = p.sb("hh1", [64, n], F32)
        p.dma("sp", [], ["hemb"], emb[:], embT_l[:, :])
        sinr = Ring(p, "hsin", [64, 512], F32, 4)
        for (src, sk_, wgt, wk, fb, fbk, dst, dk) in ((emb, "hemb", w1, "hw1", fb1, "hfb1", h1, "hh1"),
                                                       (h1, "hh1", w2, "hw2", fb2, "hfb2", h2, "hh2")):
            for c0 in range(0, n, 512):
                pst, pk = rot.next()
                p.mm([sk_, wk], [pk], pst[:64, :], wgt[:, :], src[:, c0:c0 + 512])
                dv = dst[:, c0:c0 + 512]
                p.ts([pk, "hfr", fbk], [dk], dv, pst[:64, :], fr[:, 0:1], fb[:, 0:1], ALU.mult, ALU.add)
                sa, sak = sinr.next()
                sb_, sbk = sinr.next()
                p.act([dk], [sak], sa[:, :], dv, AF.Sin, scale=0.25)
                p.act([dk], [sbk], sb_[:, :], dv, AF.Sin, scale=0.125)
                p.tt([sbk], [sbk], sb_[:, :], sb_[:, :], sb_[:, :], ALU.mult)
                p.ts([sbk], [sbk], sb_[:, :], sb_[:, :], -2.0, 1.0, ALU.mult, ALU.add)
                p.tt([sak, sbk], [sbk], sb_[:, :], sa[:, :], sb_[:, :], ALU.mult)
                p.tt([sak], [sak], sa[:, :], sa[:, :], sa[:, :], ALU.mult)
                p.ts([sak], [sak], sa[:, :], sa[:, :], -2.0, 1.0, ALU.mult, ALU.add)
                p.stt([sak, sbk], [dk], dv, sb_[:, :], 4.0, sa[:, :], ALU.mult, ALU.mult)
        p.release(mm_)
        hrow = [p.sb("hrow0", [128, n], F32), p.sb("hrow1", [128, n], F32)]
        hrk = ["hrow0", "hrow1"]
        junk = p.sb("hjunk", [128, n], F32)
        upr = Ring(p, "hup", [128, n], F32, 1)
        ubr = Ring(p, "hub", [128, n], BF16, 2)
        er = Ring(p, "hE", [128, 512], F32, 3)
        ssr = Ring(p, "hss", [128, 4], F32, 2)
        for o in range(2):
            for cb_ in range(8):
                ss, ssk = ssr.next()
                for dirn in range(2):
                    colblk = dirn * 16 + o * 8 + cb_
                    col = colblk * 128
                    for tt_ in range(8):
                        pst, pk = rot.next()
                        p.mm(["hh2", "hw3"], [pk], pst[:, :], w3[:, col:col + 128], h2[:, tt_ * 512:(tt_ + 1) * 512])
                        E, ek = er.next()
                        p.act(["htbc", "hndec"], [ek], E[:], tbc[:, tt_ * 512:(tt_ + 1) * 512], AF.Exp, scale=ndec[:, colblk:colblk + 1])
                        p.tt([pk, ek], [hrk[dirn]], hrow[dirn][:, tt_ * 512:(tt_ + 1) * 512], pst[:, :], E[:], ALU.mult)
                    if dirn == 1:
                        p.op("dve", [hrk[1]], [hrk[1]], lambda e: e.memset(hrow[1][:, 0:1], 0.0))
                    p.act([hrk[dirn]], ["hjunk", ssk], junk[:], hrow[dirn][:], AF.Abs, accum_out=ss[:, dirn:dirn + 1])
                p.tt([ssk], [ssk], ss[:, 2:3], ss[:, 0:1], ss[:, 1:2], ALU.add)
                p.op("dve", [ssk], [ssk], lambda e: e.reciprocal(out=ss[:, 3:4], in_=ss[:, 2:3]))
                for sgn, opx in ((0, ALU.add), (1, ALU.subtract)):
                    up, upk = upr.next()
                    ub, ubk = ubr.next()
                    p.tt([hrk[0], hrk[1]], [upk], up[:], hrow[0][:], hrow[1][:], opx)
                    p.ts([upk, ssk], [ubk], ub[:], up[:], ss[:, 3:4], None, ALU.mult)
                    r0 = o * 1024 + cb_ * 128
                    p.dma("sp", [ubk], ["HFB"], HFB[sgn, r0:r0 + 128, :], ub[:])
        p.release(m0)
        if HY4_STOP == "F0":
            p.release(m)
            return
        WF = p.sb("WF", [128, 64, 3, 64], BF16)
        p.dma("sp", [], ["WF"], WF[:], WF_d[:, :, :, :])
        NBLK = D // CB
        m1 = p.mark()
        XE = CB * 128
        fb = {}
        for nm in ("Zp", "Zm", "Ap", "Am"):
            for i in range(2):
                fb[(nm, i)] = (p.sb("X%s%d" % (nm, i), [128, XE], BF16), "X%s%d" % (nm, i))
        ksr = Ring(p, "hks", [64, 2, 16, CB], F32, 2)

        def f1_load(b):
            o, blk = divmod(b, NBLK)
            r0 = o * 1024 + blk * CB
            i = b % 2
            load_zin(fb[("Zp", i)][0], fb[("Zp", i)][1], HFB[0, r0:r0 + CB, :], "HFB")
            load_zin(fb[("Zm", i)][0], fb[("Zm", i)][1], HFB[1, r0:r0 + CB, :], "HFB")

        f1_load(0)
        deferred = []
        for b in range(2 * NBLK):
            o, blk = divmod(b, NBLK)
            i = b % 2
            if b + 1 < 2 * NBLK:
                f1_load(b + 1)
            for fn in deferred:
                fn()
            deferred = []
            stage1(fb[("Zp", i)][0], fb[("Zp", i)][1], fb[("Ap", i)][0], fb[("Ap", i)][1])
            stage1(fb[("Zm", i)][0], fb[("Zm", i)][1], fb[("Am", i)][0], fb[("Am", i)][1])
            ks, ksk = None, None
            for g8 in range(8):
                if g8 % 2 == 0:
                    ks, ksk = ksr.next()
                zr, zrk, zi, zik = stage2(fb[("Ap", i)][0], fb[("Ap", i)][1], g8, fb[("Am", i)][0], fb[("Am", i)][1])
                fo = (g8 % 2) * 8
                evac([zrk], [ksk], ks[:, 0, fo:fo + 8, :], zr[:64, :].rearrange("p (f c) -> p f c", c=CB))
                evac([zik], [ksk], ks[:, 1, fo:fo + 8, :], zi[:64, :].rearrange("p (f c) -> p f c", c=CB))
                if g8 % 2 == 1:
                    f0 = (g8 // 2) * 16
                    p.dma("sp", [ksk], ["KF2"], KF2[o, blk, :, :, f0:f0 + 16, :], ks[:])
        for fn in deferred:
            fn()
        deferred = []
        p.release(m1)
        if HY4_STOP == "F1":
            p.release(m)
            return
        WI = p.sb("WI", [64, 64, 3, 128], BF16)
        for f4 in range(4):
            p.dma("sp", [], ["WI"], WI[:, f4 * 16:(f4 + 1) * 16, :, :], WI_d[:, f4 * 16:(f4 + 1) * 16, :, :])
        XA = [(p.sb("XAa", [128, XE], BF16), "XAa"), (p.sb("XAb", [128, XE], BF16), "XAb")]
        XB = [(p.sb("XBa", [128, XE], BF16), "XBa"), (p.sb("XBb", [128, XE], BF16), "XBb")]
        ktr = Ring(p, "hkt", [64, 2, 16, CB], F32, 3)
        tr_ = Ring(p, "htm", [64, 512], F32, 4)
        zsr = Ring(p, "hzs", [64, 512], F32, 4)
        ytr = Ring(p, "hyt", [32, 8, 128], F32, 2)
        for o in range(2):
            src = HY if o == 0 else Z2
            srcb = HYB if o == 0 else Z2B
            skey = "HY" if o == 0 else "Z2"
            md = p.mark()

            def d_load(b):
                load_zin(XA[b % 2][0], XA[b % 2][1], srcb[b * CB:(b + 1) * CB, toff:toff + n], skey)

            def k_load(b, gg):
                kt, kk = ktr.next()
                p.dma("sp", ["KF2"], [kk], kt[:], KF2[o, b, :, :, gg * 16:(gg + 1) * 16, :])
                return kt, kk

            d_load(0)
            deferred = []
            for b in range(NBLK):
                i = b % 2
                X1, x1k = XA[i]
                X2, x2k = XB[i]
                if b + 1 < NBLK:
                    d_load(b + 1)
                for fn in deferred:
                    fn()
                deferred = []
                stage1(X1, x1k, X2, x2k)
                Y = v_Y(X1)
                knext = k_load(b, 0)
                for g8 in range(8):
                    if g8 % 2 == 0:
                        kt, kk = knext
                        if g8 + 2 < 8:
                            knext = k_load(b, g8 // 2 + 1)
                    zr, zrk, zi, zik = stage2(X2, x2k, g8)
                    fo = (g8 % 2) * 8
                    kr_ = kt[:, 0, fo:fo + 8, :].rearrange("p f c -> p (f c)")
                    ki_ = kt[:, 1, fo:fo + 8, :].rearrange("p f c -> p (f c)")
                    szr, szrk = zsr.next()
                    szi, szik = zsr.next()
                    p.act([zrk], [szrk], szr[:], zr[:64, :], AF.Identity)
                    p.act([zik], [szik], szi[:], zi[:64, :], AF.Identity)
                    t1, k1 = tr_.next()
                    t2, k2 = tr_.next()
                    p.tt([szrk, kk], [k1], t1[:], szr[:], kr_, ALU.mult)
                    p.tt([szik, kk], [k2], t2[:], szi[:], ki_, ALU.mult)
                    p.tt([k1, k2], [x1k], Y[:, 0, g8 * 8:g8 * 8 + 8, :], t1[:].rearrange("p (f c) -> p f c", c=CB),
                         t2[:].rearrange("p (f c) -> p f c", c=CB), ALU.subtract)
                    t3, k3 = tr_.next()
                    t4, k4 = tr_.next()
                    p.tt([szrk, kk], [k3], t3[:], szr[:], ki_, ALU.mult, eng="pool")
                    p.tt([szik, kk], [k4], t4[:], szi[:], kr_, ALU.mult, eng="pool")
                    p.tt([k3, k4], [x1k], Y[:, 1, g8 * 8:g8 * 8 + 8, :], t3[:].rearrange("p (f c) -> p f c", c=CB),
                         t4[:].rearrange("p (f c) -> p f c", c=CB), ALU.add, eng="pool")
                Bt = X2[:, :].rearrange("p (k c) -> p k c", c=CB)
                for g8 in range(8):
                    br, brk = rot.next()
                    bi, bik = rot.next()
                    for q in range(8):
                        f1 = g8 * 8 + q
                        o_ = slice(q * CB, (q + 1) * CB)
                        p.mm(["WI", x1k], [brk], br[:, o_], WI[:, f1, 0, :], Y[:, 0, f1, :], start=True, stop=False)
                        p.mm(["WI", x1k], [brk], br[:, o_], WI[:, f1, 1, :], Y[:, 1, f1, :], start=False, stop=True)
                    for q in range(8):
                        f1 = g8 * 8 + q
                        o_ = slice(q * CB, (q + 1) * CB)
                        p.mm(["WI", x1k], [bik], bi[:, o_], WI[:, f1, 0, :], Y[:, 1, f1, :], start=True, stop=False)
                        p.mm(["WI", x1k], [bik], bi[:, o_], WI[:, f1, 2, :], Y[:, 0, f1, :], start=False, stop=True)
                    for ri, (bb, bbk) in enumerate(((br, brk), (bi, bik))):
                        k0 = ri * 64 + g8 * 8
                        evac([bbk], [x2k], Bt[:, k0:k0 + 8, :], bb[:, :].rearrange("p (f c) -> p f c", c=CB))
                B2 = v_B(X1)
                for c0 in range(0, CB, 8):
                    pst, pk = rot.next()
                    pv = pst[:, :].bitcast(BF16)
                    for q in range(8):
                        p.tr([x2k, "identb"], [pk], pv[:, q * 128:(q + 1) * 128], Bt[:, :, c0 + q], identb[:])
                    evac([pk], [x1k], B2[:, c0:c0 + 8, :], pv[:, :].rearrange("p (c k) -> p c k", k=128))
                yt, ytk = None, None
                for c0 in range(0, CB, 4):
                    if c0 % 8 == 0:
                        yt, ytk = ytr.next()
                    pst, pk = rot.next()
                    p.mm(["S2", x1k], [pk], pst[:32, :], S2[:, :], B2[:, c0:c0 + 4, :])
                    evac([pk], [ytk], yt[:, c0 % 8:c0 % 8 + 4, :], pst[:32, :].rearrange("p (c k) -> p c k", k=128))
                    if c0 % 8 == 4:
                        cr = b * CB + c0 - 4
                        p.dma("sp", [ytk], ["CONV"], CONV[cr:cr + 8, :].rearrange("c (a b) -> a c b", b=128), yt[:])
                if b % 2 == 1 or True:
                    for fn in deferred:
                        fn()
                    deferred = []
            if HY4_STOP == "D4":
                p.release(m)
                return
            GW_ = 512
            cvr = Ring(p, "hcv", [128, GW_], F32, 2)
            zpr = Ring(p, "hzp", [128, GW_], F32, 2)
            xgr = Ring(p, "hxg", [128, GW_], F32, 2)
            ybr = Ring(p, "hyb", [128, GW_], BF16, 2)
            for cb_ in range(8):
                for g0 in range(0, n, GW_):
                    cv, cvk = cvr.next()
                    p.dma("sp", ["CONV"], [cvk], cv[:], CONV[cb_ * 128:(cb_ + 1) * 128, g0:g0 + GW_])
                    zp, zpk = zpr.next()
                    p.dma("sp", [skey], [zpk], zp[:], src[cb_ * 128:(cb_ + 1) * 128, toff + g0:toff + g0 + GW_])
                    xg, xgk = xgr.next()
                    xr0 = (1 + o) * 1024 + cb_ * 128
                    p.dma("sp", ["HY"], [xgk], xg[:], HY[xr0:xr0 + 128, toff + g0:toff + g0 + GW_])
                    p.stt([zpk, "hbias", cvk], [cvk], cv[:], zp[:], hbias[:, o, cb_:cb_ + 1], cv[:], ALU.mult, ALU.add)
                    if o == 0:
                        p.tt([cvk, xgk], [cvk], cv[:], cv[:], xg[:], ALU.mult)
                        p.dma("sp", [cvk], ["Z2"], Z2[cb_ * 128:(cb_ + 1) * 128, toff + g0:toff + g0 + GW_], cv[:])
                        yb, ybk = ybr.next()
                        p.act([cvk], [ybk], yb[:], cv[:], AF.Identity)
                        p.dma("sp", [ybk], ["Z2"], Z2B[cb_ * 128:(cb_ + 1) * 128, toff + g0:toff + g0 + GW_], yb[:])
                    else:
                        yb, ybk = ybr.next()
                        p.tt([cvk, xgk], [ybk], yb[:], cv[:], xg[:], ALU.mult)
                        p.dma("sp", [ybk], ["YH"], YH[cb_ * 128:(cb_ + 1) * 128, toff + g0:toff + g0 + GW_], yb[:])
            p.release(md)
        p.release(m)

    def phase_hyena(l):
        m = p.mark()
        cw = p.sb("hcw", [128, 24, 3], F32)
        cb = p.sb("hcb", [128, 24], F32)
        hbias = p.sb("hbias", [128, 2, 8], F32)
        p.dma("sp", [], ["hcw"], cw[:], hy_cwT[l])
        p.dma("sp", [], ["hcb"], cb[:], hy_cbT[l])
        p.dma("sp", [], ["hbias"], hbias[:], hy_biasT[l])
        segs = [(0, NCTX), (NCTX, T)]
        m1 = p.mark()
        T1r = Ring(p, "hT1", [128, T], F32, 2)
        XSr = Ring(p, "hXS", [128, T], F32, 2)
        XBr = Ring(p, "hXB", [128, T], BF16, 2)
        for blk in range(24):
            T1, k1 = T1r.next()
            XS, k2 = XSr.next()
            p.dma("sp", ["PROJ"], [k1], T1[:], PROJ[2048 + blk * 128:2048 + (blk + 1) * 128, :])
            seg_conv(p, XS, T1, lambda j: cw[:, blk, j:j + 1], cb[:, blk:blk + 1], 3, 1, segs, [k1, "hcw", "hcb"], [k2])
            p.dma("sp", [k2], ["HY"], HY[blk * 128:(blk + 1) * 128, :], XS[:])
            if blk < 8:
                xb_, xbk = XBr.next()
                p.act([k2], [xbk], xb_[:], XS[:, :], AF.Identity)
                p.dma("sp", [xbk], ["HY"], HYB[blk * 128:(blk + 1) * 128, :], xb_[:])
        p.release(m1)
        hy_filter(l, NCTX, embT_c, tv_c, FW_c, KF_c)
        hy_data(l, NCTX, 0, FW_c, GW_c, KF_c, hbias)
        if HY_DENSE:
            hy_filter(l, NLAT, embT_l, tv_l, FW_l, KF_l)
            hy_data(l, NLAT, NCTX, FW_l, GW_l, KF_l, hbias)
        else:
            hy4(l, hbias)
        p.release(m)

    def phase_merge(l, xsrc):
        m = p.mark()
        wb = p.sb("wb", [128, 3, 8, D], BF16)
        wo = p.sb("wo", [128, 8, D], BF16)
        for br in range(3):
            p.dma("pool", [], ["wb"], wb[:, br, :, :], fm(w_branch[l, br]))
        p.dma("pool", [], ["wo"], wo[:], fm(w_out[l]))
        ytr = Ring(p, "mytok", [128, 4, D], F32, 1)
        ysr = Ring(p, "mys", [128, 8, 512], BF16, 1)
        yrr = Ring(p, "myr", [128, 8, 512], BF16, 1)
        yhr = Ring(p, "myh", [128, 8, 512], BF16, 1)
        gr = Ring(p, "mg", [128, 24, 512], BF16, 1)
        mtr = Ring(p, "mmt", [128, 8, 512], BF16, 1)
        xr = Ring(p, "mx", [128, 8, 512], F32, 1)
        xnr = Ring(p, "mxn", [128, 8, 512], F32, 1)
        tr_ = Ring(p, "mtm", [128, 512], F32, 4)
        for ti, (t0, tw) in enumerate(TT):
            s_ = 0 if ti == 0 else 1
            nsub = tw // 128
            yt, ytk = ytr.next()
            p.dma("sp", ["YSTOK"], [ytk], yt[:, :nsub, :], YSTOK[t0:t0 + tw, :].rearrange("(a p) d -> p a d", p=128))
            ys, ysk = ysr.next()
            for kc in range(8):
                pst, pk = psr.next()
                for a in range(nsub):
                    p.tr([ytk, "ident"], [pk], pst[:, a * 128:(a + 1) * 128], yt[:, a, kc * 128:(kc + 1) * 128], ident[:])
                p.cp([pk], [ysk], ys[:, kc, :tw], pst[:, :tw], eng=("dve" if kc % 2 else "act")) if False else (
                    p.act([pk], [ysk], ys[:, kc, :tw], pst[:, :tw], AF.Identity) if kc % 2 == 0 else p.cp([pk], [ysk], ys[:, kc, :tw], pst[:, :tw]))
            yr, yrk = yrr.next()
            p.dma("sp", ["YR"], [yrk], yr[:, :, :tw], fm(YR)[:, :, t0:t0 + tw])
            yh, yhk = yhr.next()
            p.dma("sp", ["YH"], [yhk], yh[:, :, :tw], fm(YH)[:, :, t0:t0 + tw])
            g, gk = gr.next()
            p.dma("sp", ["GT"], [gk], g[:, :, :tw], fm(GT)[:, :, t0:t0 + tw])
            xt, xk = xr.next()
            p.dma("sp", ["XT"], [xk], xt[:, :, :tw], xsrc[:, :, t0:t0 + tw])
            mt, mtk = mtr.next()
            for cb_ in range(8):
                pbs = []
                for br, (yb, ybk) in enumerate(((yr, yrk), (yh, yhk), (ys, ysk))):
                    pst, pk = psr.next()
                    for kc in range(8):
                        p.mm(["wb", ybk], [pk], pst[:, :tw], wb[:, br, kc, cb_ * 128:(cb_ + 1) * 128], yb[:, kc, :tw],
                             start=(kc == 0), stop=(kc == 7))
                    pbs.append((pst, pk))
                t1, k1 = tr_.next()
                t2, k2 = tr_.next()
                p.tt([pbs[0][1], gk], [k1], t1[:, :tw], pbs[0][0][:, :tw], g[:, cb_, :tw], ALU.mult)
                p.tt([pbs[1][1], gk], [k2], t2[:, :tw], pbs[1][0][:, :tw], g[:, 8 + cb_, :tw], ALU.mult)
                p.tt([k1, k2], [k1], t1[:, :tw], t1[:, :tw], t2[:, :tw], ALU.add)
                t3, k3 = tr_.next()
                p.tt([pbs[2][1], gk], [k3], t3[:, :tw], pbs[2][0][:, :tw], g[:, 16 + cb_, :tw], ALU.mult)
                p.tt([k1, k3], [mtk], mt[:, cb_, :tw], t1[:, :tw], t3[:, :tw], ALU.add)
            xn, xnk = xnr.next()
            for co in range(8):
                pst, pk = psr.next()
                for cb_ in range(8):
                    p.mm(["wo", mtk], [pk], pst[:, :tw], wo[:, cb_, co * 128:(co + 1) * 128], mt[:, cb_, :tw],
                         start=(cb_ == 0), stop=(cb_ == 7))
                p.stt([pk, "modT", xk], [xnk], xn[:, co, :tw], pst[:, :tw], modT[:, 16 + co, s_:s_ + 1], xt[:, co, :tw], ALU.mult, ALU.add)
            p.dma("sp", [xnk], ["XT"], fm(XT)[:, :, t0:t0 + tw], xn[:, :, :tw])
        p.release(m)

    def phase_ffn(l, HT):
        m = p.mark()
        wv = fm(w_up[l])
        wr = Ring(p, "fwu", [128, 2, 8, 128], BF16, 3)
        str_ = Ring(p, "fst", [128, T], BF16, 2)
        sgr = Ring(p, "fsg", [128, 512], F32, 3)
        for j in range(22):
            wt, wk = wr.next()
            p.dma("pool", [], [wk], wt[:, 0, :, :], wv[:, :, j * 128:(j + 1) * 128])
            p.dma("pool", [], [wk], wt[:, 1, :, :], wv[:, :, D_FF + j * 128:D_FF + (j + 1) * 128])
            st, stk = str_.next()
            for ti, (t0, tw) in enumerate(TT):
                pg, pgk = psr.next()
                pu, puk = psr.next()
                for kc in range(8):
                    p.mm([wk, "HT"], [pgk], pg[:, :tw], wt[:, 0, kc, :], HT[:, kc, t0:t0 + tw], start=(kc == 0), stop=(kc == 7))
                for kc in range(8):
                    p.mm([wk, "HT"], [puk], pu[:, :tw], wt[:, 1, kc, :], HT[:, kc, t0:t0 + tw], start=(kc == 0), stop=(kc == 7))
                sg, sgk = sgr.next()
                p.act([pgk], [sgk], sg[:, :tw], pg[:, :tw], AF.Silu)
                p.tt([sgk, puk], [stk], st[:, t0:t0 + tw], sg[:, :tw], pu[:, :tw], ALU.mult)
            p.dma("sp", [stk], ["AFF"], AFF[j * 128:(j + 1) * 128, :], st[:])
        p.release(m)

    def phase_ffn2(l):
        m = p.mark()
        wd = p.sb("fwd", [128, 22, D], BF16)
        p.dma("pool", [], ["fwd"], wd[:], w_down[l].rearrange("(j p) c -> p j c", p=128))
        ar = Ring(p, "fa", [128, 22, 512], BF16, 2)
        xr = Ring(p, "fx", [128, 8, 512], F32, 2)
        xnr = Ring(p, "fxn", [128, 8, 512], F32, 2)
        av = AFF.rearrange("(j p) t -> p j t", p=128)
        for ti, (t0, tw) in enumerate(TT):
            s_ = 0 if ti == 0 else 1
            at, ak = ar.next()
            p.dma("sp", ["AFF"], [ak], at[:, :, :tw], av[:, :, t0:t0 + tw])
            xt, xk = xr.next()
            p.dma("sp", ["XT"], [xk], xt[:, :, :tw], fm(XT)[:, :, t0:t0 + tw])
            xn, xnk = xnr.next()
            for co in range(8):
                pst, pk = psr.next()
                for j in range(22):
                    p.mm(["fwd", ak], [pk], pst[:, :tw], wd[:, j, co * 128:(co + 1) * 128], at[:, j, :tw], start=(j == 0), stop=(j == 21))
                p.stt([pk, "modT", xk], [xnk], xn[:, co, :tw], pst[:, :tw], modT[:, 40 + co, s_:s_ + 1], xt[:, co, :tw], ALU.mult, ALU.add)
            p.dma("sp", [xnk], ["XT"], fm(XT)[:, :, t0:t0 + tw], xn[:, :, :tw])
        p.release(m)

    def phase_final():
        m = p.mark()
        fn = p.sb("fnw", [128, 8], F32)
        p.dma("sp", [], ["fnw"], fn[:], final_normT[:, :])
        xr = Ring(p, "ox", [128, 8, 512], F32, 2)
        sqr = Ring(p, "osq", [128, 8, 512], BF16, 2)
        rr = Ring(p, "orstd", [128, 512], F32, 2)
        outr = Ring(p, "oo", [128, 8, 512], F32, 2)
        src = fm(XT)
        for ti, (t0, tw) in enumerate(TT):
            if ti == 0:
                continue
            xt, xk = xr.next()
            p.dma("sp", ["XT"], [xk], xt[:], src[:, :, t0:t0 + tw])
            sq, sqk = sqr.next()
            p.act([xk], [sqk], sq[:], xt[:], AF.Square)
            pst, pk = psr.next()
            for kc in range(8):
                p.mm([sqk, "onesb"], [pk], pst[:, :], onesb[:], sq[:, kc, :], start=(kc == 0), stop=(kc == 7))
            rs, rk = rr.next()
            p.ts([pk], [rk], rs[:], pst[:, :], 1.0 / D, EPS, ALU.mult, ALU.add)
            p.act([rk], [rk], rs[:], rs[:], AF.Sqrt)
            p.op("dve", [rk], [rk], lambda e: e.reciprocal(out=rs[:], in_=rs[:]))
            ot, ok_ = outr.next()
            for kc in range(8):
                p.stt([xk, "fnw", rk], [ok_], ot[:, kc, :], xt[:, kc, :], fn[:, kc:kc + 1], rs[:], ALU.mult, ALU.mult)
            p.dma("sp", [ok_], ["OUT"], fm(out)[:, :, t0 - NCTX:t0 - NCTX + tw], ot[:])
        p.wait_all("sp", ["OUT"])
        p.release(m)

    for l in range(nlayers):
        phase_mod(l)
        mk = p.mark()
        HT = p.sb("HT", [128, 8, T], BF16)
        phase_norm(fm(xin if l == 0 else XT), A1, "A1", 0, HT)
        if "HTD" in dbg:
            p.dma("sp", ["HT"], ["HTD"], fm(HTD), HT[:])
        if stop_after == "norm":
            p.release(mk)
            break
        phase_proj(l, HT)
        p.release(mk)
        if stop_after == "proj":
            break
        if stop_after not in ("ssd", "hyena"):
            phase_rglru(l)
        if stop_after == "rglru":
            break
        if stop_after != "hyena":
            phase_ssd(l)
        if stop_after == "ssd":
            break
        phase_hyena(l)
        if stop_after == "hyena":
            break
        phase_merge(l, fm(xin if l == 0 else XT))
        if stop_after == "merge":
            break
        mk = p.mark()
        HT = p.sb("HT", [128, 8, T], BF16)
        phase_norm(fm(XT), A2, "A2", 24, HT)
        phase_ffn(l, HT)
        p.release(mk)
        phase_ffn2(l)
    if stop_after is None:
        phase_final()
    p.barrier()
    p.close()
    print("instructions:", p.ninst)
    return nc


def fmT(v, nchunk):
    return np.ascontiguousarray(np.swapaxes(v.reshape(v.shape[:-1] + (nchunk, 128)), -1, -2))

def hyena_emb(n):
    f = np.float32
    t = np.linspace(0.0, 1.0, n, dtype=f)
    bands = np.linspace(1e-4, 15.0, 16, dtype=f)
    ang = (f(2.0 * np.pi / n) * np.arange(n, dtype=f)[:, None]) * bands[None]
    emb = np.concatenate([t[:, None], np.cos(ang), np.sin(ang)], axis=-1).astype(f)
    return np.ascontiguousarray(emb.T)

def hyena_consts(n):
    import ml_dtypes
    f = np.float32
    t = np.linspace(0.0, 1.0, n, dtype=f)
    bands = np.linspace(1e-4, 15.0, 16, dtype=f)
    ang = (f(2.0 * np.pi / n) * np.arange(n, dtype=f)[:, None]) * bands[None]
    emb = np.concatenate([t[:, None], np.cos(ang), np.sin(ang)], axis=-1).astype(f)
    embT = np.ascontiguousarray(emb.T)
    nt = n // 128
    tv = np.ascontiguousarray(t.reshape(nt, 128).T)
    N = 2 * n
    tt = np.arange(n, dtype=np.int64)
    ff = np.arange(n, dtype=np.int64)
    ph = ((2 * ff[None, :] + 1) * tt[:, None]) % (2 * N)
    angm = np.pi * ph.astype(np.float64) / N
    C = np.cos(angm); S = np.sin(angm)
    def tile_f(M):
        return M.reshape(nt, 128, nt, 128).transpose(2, 1, 0, 3)
    FW = np.concatenate([tile_f(C), tile_f(S)], axis=0).astype(ml_dtypes.bfloat16)
    tw = 512 if n >= 512 else n
    def tile_g(M):
        return (M.T * (2.0 / N)).reshape(nt, 128, n // tw, tw).transpose(2, 1, 0, 3)
    GW = np.concatenate([tile_g(C), tile_g(S)], axis=2).astype(ml_dtypes.bfloat16)
    return embT, tv, np.ascontiguousarray(FW), np.ascontiguousarray(GW)

def hyena_consts4():
    import ml_dtypes
    bf = ml_dtypes.bfloat16
    n, N = 4096, 8192
    t1 = np.arange(32, dtype=np.int64)[:, None]
    f1 = np.arange(64, dtype=np.int64)[None, :]
    g = 2.0 * np.pi * (((2 * f1 + 1) * t1) % 128).astype(np.float64) / 128.0
    S1 = np.concatenate([np.cos(g), -np.sin(g)], axis=1).astype(bf)
    S2 = (np.concatenate([np.cos(g).T, -np.sin(g).T], axis=0) * (2.0 / N)).astype(bf)
    f1v = np.arange(64, dtype=np.int64)[:, None, None]
    t2v = np.arange(128, dtype=np.int64)[None, :, None]
    f2v = np.arange(64, dtype=np.int64)[None, None, :]
    ph = ((2 * f1v + 1) * t2v + 128 * f2v * t2v) % 16384
    phi = 2.0 * np.pi * ph.astype(np.float64) / 16384.0
    Wr = np.cos(phi); Wi = -np.sin(phi)
    WF = np.stack([Wr, Wi, -Wi], axis=0).transpose(2, 1, 0, 3)
    WI = np.stack([Wr, Wi, -Wi], axis=0).transpose(3, 1, 0, 2)
    tvec = np.linspace(0.0, 1.0, n, dtype=np.float32)[None, :]
    return S1, S2, np.ascontiguousarray(WF.astype(bf)), np.ascontiguousarray(WI.astype(bf)), np.ascontiguousarray(tvec)

def prep_shared(inp):
    f = np.float32
    sh = {}
    sh["w_mod"] = inp["w_mod"]
    sh["b_modT"] = fmT(inp["b_mod"], 48)
    sh["norm_mixT"] = fmT(inp["norm_mix"], 8)
    sh["norm_ffnT"] = fmT(inp["norm_ffn"], 8)
    sh["final_normT"] = fmT(inp["final_norm"], 8)
    sh["w_in"] = inp["w_in"]
    sh["rnn_cwT"] = np.ascontiguousarray(inp["rnn_conv_w"].reshape(4, 4, 8, 128).transpose(0, 3, 2, 1))
    sh["rnn_cbT"] = fmT(inp["rnn_conv_b"], 8)
    sh["rnn_aw"] = inp["rnn_gate_a_w"]
    sh["rnn_xw"] = inp["rnn_gate_x_w"]
    sh["rnn_abT"] = np.ascontiguousarray(fmT(inp["rnn_gate_a_b"], 8).transpose(0, 2, 1, 3))
    sh["rnn_xbT"] = np.ascontiguousarray(fmT(inp["rnn_gate_x_b"], 8).transpose(0, 2, 1, 3))
    sh["rnn_lamT"] = np.ascontiguousarray(fmT(inp["rnn_lambda"], 8).transpose(0, 2, 1, 3))
    sh["ident"] = np.eye(128, dtype=f)
    j = np.arange(128)[:, None]; ll = np.arange(128)[None, :]
    sh["masks"] = np.stack([(j <= ll), (j > ll), (j >= ll), (j < ll), np.ones((128, 128), bool)]).astype(f)
    sh["ssm_cwT"] = np.ascontiguousarray(inp["ssm_conv_w"].reshape(4, 4, 12, 128).transpose(0, 3, 2, 1))
    sh["ssm_cbT"] = fmT(inp["ssm_conv_b"], 12)
    sh["ssm_alogT"] = np.ascontiguousarray(inp["ssm_a_log"].reshape(4, 32, 1))
    sh["ssm_dtbT"] = np.ascontiguousarray(inp["ssm_dt_bias"].reshape(4, 32, 1))
    sh["ssm_d"] = inp["ssm_d"]
    sh["ssm_norm"] = inp["ssm_norm"]
    sh["hy_cwT"] = np.ascontiguousarray(inp["hy_short_w"].reshape(4, 3, 24, 128).transpose(0, 3, 2, 1))
    sh["hy_cbT"] = fmT(inp["hy_short_b"], 24)
    sh["hy_biasT"] = np.ascontiguousarray(fmT(inp["hy_bias"], 8).transpose(0, 2, 1, 3))
    sh["hy_w1"] = inp["hy_w1"]; sh["hy_w2"] = inp["hy_w2"]; sh["hy_w3"] = inp["hy_w3"]
    sh["hy_b1T"] = np.ascontiguousarray(inp["hy_b1"].reshape(4, 64, 1))
    sh["hy_b2T"] = np.ascontiguousarray(inp["hy_b2"].reshape(4, 64, 1))
    sh["hy_freqT"] = np.ascontiguousarray(inp["hy_freq"].reshape(4, 64, 1))
    sh["hy_decay"] = inp["hy_decay"]
    emb, tv, FW, GW = hyena_consts(256)
    sh["embT_c"] = emb; sh["tv_c"] = tv; sh["FW_c"] = FW; sh["GW_c"] = GW
    sh["embT_l"] = hyena_emb(4096)
    S1, S2, WF, WI, tvec = hyena_consts4()
    sh["S1"] = S1; sh["S2"] = S2; sh["WF"] = WF; sh["WI"] = WI; sh["tvec"] = tvec
    sh["hy_ndecT"] = np.ascontiguousarray(-fmT(inp["hy_decay"], 32))
    sh["w_branch"] = inp["w_branch"]; sh["w_out"] = inp["w_out"]; sh["w_up"] = inp["w_up"]; sh["w_down"] = inp["w_down"]
    return sh

def prep_core(inp, b):
    xin = np.ascontiguousarray(np.concatenate([inp["ctx"][b].T, inp["x"][b].T], axis=1))
    cc = np.stack([inp["c_ctx"], inp["c"][b]], axis=-1)
    cc = np.ascontiguousarray(cc.reshape(8, 128, 2).transpose(1, 0, 2))
    return {"xin": xin, "cc": cc}


def kernel(**inputs):
    inp = {k: np.asarray(v) for k, v in inputs.items()}
    sh = prep_shared(inp)
    nc = build()
    in_maps = []
    for b in range(8):
        im = dict(sh)
        im.update(prep_core(inp, b))
        in_maps.append(im)
    res = run_bass_kernel_spmd(nc, in_maps, core_ids=list(range(8)))
    out = np.stack([np.ascontiguousarray(np.asarray(r["out"]).T) for r in res.results], axis=0)
    return out.astype(np.float32)
```

```python
import numpy as np
import concourse.bass as bass
import concourse.mybir as mybir
from concourse.bass_utils import run_bass_kernel_spmd

F32 = mybir.dt.float32
BF16 = mybir.dt.bfloat16
ALU = mybir.AluOpType
AF = mybir.ActivationFunctionType
AX = mybir.AxisListType

D = 1024
NCTX = 256
NLAT = 4096
T = NCTX + NLAT
DEPTH = 4
D_IN = 10784
D_FF = 2816
EPS = 1e-6
HY_DENSE = False
HY4_STOP = None
TT = [(0, 256)] + [(256 + 512 * i, 512) for i in range(8)]


class Prog:
    NDMA = 24

    def __init__(self, nc):
        self.nc = nc
        self.engs = {"pe": nc.tensor, "dve": nc.vector, "act": nc.scalar, "pool": nc.gpsimd, "sp": nc.sync}
        self._ctx = []
        self.sem = {}
        self.cnt = {}
        for e in ("pe", "dve", "act", "pool"):
            self.sem[e] = self._enter(nc.semaphore("s_" + e))
            self.cnt[e] = 0
        self.dsem = [self._enter(nc.semaphore("d%d" % i)) for i in range(self.NDMA)]
        self.dcnt = [0] * self.NDMA
        self.dnext = 0
        self.semobj = {}
        for e in ("pe", "dve", "act", "pool"):
            self.semobj[("c", e)] = self.sem[e]
        for i in range(self.NDMA):
            self.semobj[("d", i)] = self.dsem[i]
        self.waited = {e: {} for e in self.engs}
        self.lastw = {}
        self.reads = {}
        self.ninst = 0
        self.uid = 0

    def _enter(self, cm):
        v = cm.__enter__()
        self._ctx.append(cm)
        return v

    def sb(self, name, shape, dt):
        self.uid += 1
        return self._enter(self.nc.sbuf_tensor("%s_%d" % (name, self.uid), list(shape), dt))

    def ps(self, name, shape, dt=F32):
        return self._enter(self.nc.psum_tensor(name, list(shape), dt))

    def mark(self):
        return len(self._ctx)

    def release(self, mark):
        self.barrier()
        while len(self._ctx) > mark:
            cm = self._ctx.pop()
            cm.__exit__(None, None, None)

    def close(self):
        while self._ctx:
            cm = self._ctx.pop()
            cm.__exit__(None, None, None)

    def barrier(self):
        targets = []
        for e in ("pe", "dve", "act", "pool"):
            if self.cnt[e]:
                targets.append((("c", e), self.cnt[e]))
        for i in range(self.NDMA):
            if self.dcnt[i]:
                targets.append((("d", i), self.dcnt[i] * 16))
        for q in ("pe", "dve", "act", "pool", "sp"):
            e = self.engs[q]
            for sk, val in targets:
                if sk == ("c", q):
                    continue
                if self.waited[q].get(sk, 0) < val:
                    e.wait_ge(self.semobj[sk], val)
                    self.waited[q][sk] = val
        self.lastw = {}
        self.reads = {}

    def _deps(self, eng, R, W):
        deps = []
        for r in R:
            lw = self.lastw.get(r)
            if lw is not None:
                deps.append((lw, "raw"))
        for w in W:
            lw = self.lastw.get(w)
            if lw is not None:
                deps.append((lw, "waw"))
            for rd in self.reads.get(w, ()):
                deps.append((rd, "war"))
        own = ("c", eng)
        e = self.engs[eng]
        wt = self.waited[eng]
        need = {}
        for (sk, val), kind in deps:
            if sk == own:
                if eng == "pe":
                    continue
                if kind != "raw":
                    continue
            if wt.get(sk, 0) >= val:
                continue
            if need.get(sk, 0) < val:
                need[sk] = val
        for sk, val in need.items():
            e.wait_ge(self.semobj[sk], val)
            wt[sk] = val

    def _commit(self, tick, R, W):
        for w in W:
            self.lastw[w] = tick
            self.reads[w] = []
        for r in R:
            if r in W:
                continue
            lst = self.reads.setdefault(r, [])
            lst.append(tick)
            if len(lst) > 48:
                best = {}
                for sk, v in lst:
                    if best.get(sk, 0) < v:
                        best[sk] = v
                self.reads[r] = list(best.items())

    def op(self, eng, R, W, fn):
        self._deps(eng, R, W)
        ins = fn(self.engs[eng])
        self.cnt[eng] += 1
        ins.then_inc(self.sem[eng], 1)
        tick = (("c", eng), self.cnt[eng])
        self._commit(tick, R, W)
        self.ninst += 1
        return tick

    def dma(self, q, R, W, out, in_, **kw):
        i = self.dnext
        self.dnext = (self.dnext + 1) % self.NDMA
        sk = ("d", i)
        e = self.engs[q]
        prev = self.dcnt[i] * 16
        if prev and self.waited[q].get(sk, 0) < prev:
            e.wait_ge(self.dsem[i], prev)
            self.waited[q][sk] = prev
        self._deps(q, R, W)
        ins = e.dma_start(out=out, in_=in_, **kw)
        self.dcnt[i] += 1
        ins.then_inc(self.dsem[i], 16)
        tick = (sk, self.dcnt[i] * 16)
        self._commit(tick, R, W)
        self.ninst += 1
        return tick

    def wait_all(self, eng, keys):
        e = self.engs[eng]
        for k in keys:
            lw = self.lastw.get(k)
            if lw is None:
                continue
            sk, val = lw
            if self.waited[eng].get(sk, 0) < val:
                e.wait_ge(self.semobj[sk], val)
                self.waited[eng][sk] = val

    def mm(self, R, W, out, lhsT, rhs, start=True, stop=True):
        return self.op("pe", R, W, lambda e: e.matmul(out, lhsT, rhs, start=start, stop=stop))

    def tr(self, R, W, out, in_, ident):
        return self.op("pe", R, W, lambda e: e.transpose(out, in_, ident))

    def act(self, R, W, out, in_, func, **kw):
        return self.op("act", R, W, lambda e: e.activation(out=out, in_=in_, func=func, **kw))

    def tt(self, R, W, out, in0, in1, op, eng="dve"):
        return self.op(eng, R, W, lambda e: e.tensor_tensor(out=out, in0=in0, in1=in1, op=op))

    def ts(self, R, W, out, in0, s1, s2, op0, op1=None, eng="dve"):
        if op1 is None:
            return self.op(eng, R, W, lambda e: e.tensor_scalar(out=out, in0=in0, scalar1=s1, scalar2=None, op0=op0))
        return self.op(eng, R, W, lambda e: e.tensor_scalar(out=out, in0=in0, scalar1=s1, scalar2=s2, op0=op0, op1=op1))

    def stt(self, R, W, out, in0, scalar, in1, op0, op1, eng="dve"):
        return self.op(eng, R, W, lambda e: e.scalar_tensor_tensor(out=out, in0=in0, scalar=scalar, in1=in1, op0=op0, op1=op1))

    def cp(self, R, W, out, in_, eng="dve"):
        return self.op(eng, R, W, lambda e: e.tensor_copy(out=out, in_=in_))


class Ring:
    def __init__(self, p, name, shape, dt, n):
        self.tiles = [p.sb("%s%d" % (name, i), shape, dt) for i in range(n)]
        self.keys = ["%s#%d_%d" % (name, p.uid, i) for i in range(n)]
        self.i = 0

    def next(self):
        t, k = self.tiles[self.i], self.keys[self.i]
        self.i = (self.i + 1) % len(self.tiles)
        return t, k


class PsRing:
    def __init__(self, p, n=8):
        self.tiles = [p.ps("psb%d" % i, [128, 512]) for i in range(n)]
        self.keys = ["psb%d" % i for i in range(n)]
        self.i = 0

    def next(self):
        t, k = self.tiles[self.i], self.keys[self.i]
        self.i = (self.i + 1) % len(self.tiles)
        return t, k


def fm(ap2d):
    return ap2d.rearrange("(kc p) t -> p kc t", p=128)


def seg_conv(p, out, in_, wv, bv, ntap, left, segs, Rk, Wk):
    p.ts(Rk, Wk, out[:, :], in_[:, :], wv(left), bv, ALU.mult, ALU.add)
    for j in range(ntap):
        d = j - left
        if d == 0:
            continue
        for (s0, s1) in segs:
            lo = max(s0, s0 - d)
            hi = min(s1, s1 - d)
            p.stt(Rk + Wk, Wk, out[:, lo:hi], in_[:, lo + d:hi + d], wv(j), out[:, lo:hi], ALU.mult, ALU.add)


def build(nlayers=DEPTH, stop_after=None, dbg=()):
    nc = bass.Bass("TRN2", target_bir_lowering=False)

    def din(name, shape, dt=F32):
        return nc.dram_tensor(name, list(shape), dt, kind="ExternalInput").ap()

    def dscr(name, shape, dt=F32):
        kind = "ExternalOutput" if name in dbg else "Internal"
        return nc.dram_tensor(name, list(shape), dt, kind=kind).ap()

    xin = din("xin", [D, T])
    cc = din("cc", [128, 8, 2])
    w_mod = din("w_mod", [DEPTH, D, 6 * D])
    b_modT = din("b_modT", [DEPTH, 128, 48])
    norm_mixT = din("norm_mixT", [DEPTH, 128, 8])
    norm_ffnT = din("norm_ffnT", [DEPTH, 128, 8])
    final_normT = din("final_normT", [128, 8])
    w_in = din("w_in", [DEPTH, D, D_IN])
    rnn_cwT = din("rnn_cwT", [DEPTH, 128, 8, 4])
    rnn_cbT = din("rnn_cbT", [DEPTH, 128, 8])
    rnn_aw = din("rnn_aw", [DEPTH, 2, 8, 128, 128])
    rnn_xw = din("rnn_xw", [DEPTH, 2, 8, 128, 128])
    rnn_abT = din("rnn_abT", [DEPTH, 128, 2, 8])
    rnn_xbT = din("rnn_xbT", [DEPTH, 128, 2, 8])
    rnn_lamT = din("rnn_lamT", [DEPTH, 128, 2, 8])
    ident_d = din("ident", [128, 128])
    masks_d = din("masks", [5, 128, 128])
    ssm_cwT = din("ssm_cwT", [DEPTH, 128, 12, 4])
    ssm_cbT = din("ssm_cbT", [DEPTH, 128, 12])
    ssm_alogT = din("ssm_alogT", [DEPTH, 32, 1])
    ssm_dtbT = din("ssm_dtbT", [DEPTH, 32, 1])
    ssm_d = din("ssm_d", [DEPTH, 16])
    ssm_norm = din("ssm_norm", [DEPTH, 1024])
    hy_cwT = din("hy_cwT", [DEPTH, 128, 24, 3])
    hy_cbT = din("hy_cbT", [DEPTH, 128, 24])
    hy_biasT = din("hy_biasT", [DEPTH, 128, 2, 8])
    hy_w1 = din("hy_w1", [DEPTH, 33, 64])
    hy_w2 = din("hy_w2", [DEPTH, 64, 64])
    hy_w3 = din("hy_w3", [DEPTH, 64, 4096])
    hy_b1T = din("hy_b1T", [DEPTH, 64, 1])
    hy_b2T = din("hy_b2T", [DEPTH, 64, 1])
    hy_freqT = din("hy_freqT", [DEPTH, 64, 1])
    hy_decay = din("hy_decay", [DEPTH, 4096])
    embT_l = din("embT_l", [33, NLAT])
    embT_c = din("embT_c", [33, NCTX])
    tv_l = din("tv_l", [128, NLAT // 128]) if HY_DENSE else None
    tv_c = din("tv_c", [128, NCTX // 128])
    FW_l = din("FW_l", [64, 128, 32, 128], BF16) if HY_DENSE else None
    GW_l = din("GW_l", [8, 128, 64, 512], BF16) if HY_DENSE else None
    FW_c = din("FW_c", [4, 128, 2, 128], BF16)
    GW_c = din("GW_c", [1, 128, 4, 256], BF16)
    S1_d = din("S1", [32, 128], BF16)
    S2_d = din("S2", [128, 32], BF16)
    WF_d = din("WF", [128, 64, 3, 64], BF16)
    WI_d = din("WI", [64, 64, 3, 128], BF16)
    tvec = din("tvec", [1, NLAT])
    hy_ndecT = din("hy_ndecT", [DEPTH, 128, 32])
    w_branch = din("w_branch", [DEPTH, 3, D, D])
    w_out = din("w_out", [DEPTH, D, D])
    w_up = din("w_up", [DEPTH, D, 2 * D_FF])
    w_down = din("w_down", [DEPTH, D_FF, D])
    out = nc.dram_tensor("out", [D, NLAT], F32, kind="ExternalOutput").ap()

    XT = dscr("XT", [D, T])
    PROJ = dscr("PROJ", [5120, T])
    SSMP = dscr("SSMP", [2592, T])
    GT = dscr("GT", [3072, T], BF16)
    YR = dscr("YR", [D, T], BF16)
    HTD = dscr("HTD", [D, T], BF16)
    HP = dscr("HP", [2, 34, 128, 1024], BF16)
    YSTOK = dscr("YSTOK", [T, 1024])
    HY = dscr("HY", [3072, T])
    Z2 = dscr("Z2", [D, T])
    YH = dscr("YH", [D, T], BF16)
    KF_l = dscr("KF_l", [2, 32, 128, 2, 1024]) if HY_DENSE else None
    KF_c = dscr("KF_c", [2, 2, 128, 2, 1024])
    AFF = dscr("AFF", [D_FF, T], BF16)
    HFB = dscr("HFB", [2, 2048, NLAT], BF16)
    HYB = dscr("HYB", [D, T], BF16)
    Z2B = dscr("Z2B", [D, T], BF16)
    KF2 = dscr("KF2", [2, 16, 64, 2, 64, 64])
    CONV = dscr("CONV", [D, NLAT])

    p = Prog(nc)
    psr = PsRing(p)

    ident = p.sb("ident", [128, 128], F32)
    onesb = p.sb("onesb", [128, 128], BF16)
    modT = p.sb("modT", [128, 48, 2], F32)
    A1 = p.sb("A1", [128, 8, 2], F32)
    A2 = p.sb("A2", [128, 8, 2], F32)
    p.dma("sp", [], ["ident"], ident[:], ident_d[:, :])
    masks = p.sb("masks", [128, 5, 128], F32)
    p.dma("sp", [], ["masks"], masks[:], masks_d.rearrange("m p l -> p m l"))
    LE, GT_, GE, LT, ONES = 0, 1, 2, 3, 4
    p.op("dve", [], ["onesb"], lambda e: e.memset(onesb[:], 1.0))

    def phase_mod(l):
        m = p.mark()
        cs = p.sb("cs", [128, 8, 2], F32)
        bm = p.sb("bm", [128, 48], F32)
        nm = p.sb("nm", [128, 8], F32)
        nf = p.sb("nf", [128, 8], F32)
        p.dma("sp", [], ["cs"], cs[:], cc[:, :, :])
        p.dma("sp", [], ["bm"], bm[:], b_modT[l])
        p.dma("sp", [], ["nm"], nm[:], norm_mixT[l])
        p.dma("sp", [], ["nf"], nf[:], norm_ffnT[l])
        p.act(["cs"], ["cs"], cs[:], cs[:], AF.Silu)
        wring = Ring(p, "wmod", [128, 8, 512], F32, 2)
        pst, pk = psr.next()
        wv = fm(w_mod[l])
        for cg in range(12):
            wt, wk = wring.next()
            p.dma("sp", [], [wk], wt[:], wv[:, :, cg * 512:(cg + 1) * 512])
            for j4 in range(4):
                j = cg * 4 + j4
                for kc in range(8):
                    p.mm([wk, "cs"], [pk], pst[:, j * 2:(j + 1) * 2], wt[:, kc, j4 * 128:(j4 + 1) * 128], cs[:, kc, :],
                         start=(kc == 0), stop=(kc == 7))
        p.tt([pk, "bm"], ["modT"], modT[:], pst[:, 0:96].rearrange("p (j s) -> p j s", s=2),
             bm[:].unsqueeze(2).to_broadcast([128, 48, 2]), ALU.add)
        for (Aq, key, nrm, j0) in ((A1, "A1", nm, 8), (A2, "A2", nf, 32)):
            p.ts(["modT"], [key], Aq[:], modT[:, j0:j0 + 8, :], 1.0, None, ALU.add)
            p.tt([key, "nm", "nf"], [key], Aq[:], Aq[:], nrm[:].unsqueeze(2).to_broadcast([128, 8, 2]), ALU.mult)
        p.release(m)

    def phase_norm(src, A, Akey, bj0, HT):
        m = p.mark()
        xr = Ring(p, "xn", [128, 8, 512], F32, 2)
        sqr = Ring(p, "sq", [128, 8, 512], BF16, 2)
        rr = Ring(p, "rstd", [128, 512], F32, 2)
        tr_ = Ring(p, "tmpn", [128, 512], F32, 3)
        for ti, (t0, tw) in enumerate(TT):
            s = 0 if ti == 0 else 1
            xt, xk = xr.next()
            p.dma("sp", ["XT"], [xk], xt[:, :, :tw], src[:, :, t0:t0 + tw])
            sq, sqk = sqr.next()
            p.act([xk], [sqk], sq[:, :, :tw], xt[:, :, :tw], AF.Square)
            pst, pk = psr.next()
            for kc in range(8):
                p.mm([sqk, "onesb"], [pk], pst[:, :tw], onesb[:], sq[:, kc, :tw], start=(kc == 0), stop=(kc == 7))
            rs, rk = rr.next()
            p.ts([pk], [rk], rs[:, :tw], pst[:, :tw], 1.0 / D, EPS, ALU.mult, ALU.add)
            p.act([rk], [rk], rs[:, :tw], rs[:, :tw], AF.Sqrt)
            p.op("dve", [rk], [rk], lambda e: e.reciprocal(out=rs[:, :tw], in_=rs[:, :tw]))
            for kc in range(8):
                tm, tk = tr_.next()
                p.tt([xk, rk], [tk], tm[:, :tw], xt[:, kc, :tw], rs[:, :tw], ALU.mult)
                p.act([tk, Akey, "modT"], ["HT"], HT[:, kc, t0:t0 + tw], tm[:, :tw], AF.Identity,
                      scale=A[:, kc, s:s + 1], bias=modT[:, bj0 + kc, s:s + 1])
        p.release(m)

    def ht_rhs(HT, kc, ti, ssd):
        t0, tw = TT[ti]
        if ti == 0 or not ssd:
            return HT[:, kc, t0:t0 + tw]
        i = ti - 1
        return HT[:, kc, NCTX:].rearrange("p (r c) -> p c r", c=64)[:, 8 * i:8 * i + 8, :]

    def ht_lhs(HT, kc, q):
        if q < 2:
            return HT[:, kc, q * 128:(q + 1) * 128]
        c2 = q - 2
        return HT[:, kc, NCTX:].rearrange("p (r c) -> p c r", c=64)[:, 2 * c2:2 * c2 + 2, :]

    def phase_proj(l, HT):
        m = p.mark()
        wring = Ring(p, "win", [128, 8, 512], BF16, 3)
        stg = Ring(p, "pstg", [128, T], F32, 2)
        stgb = Ring(p, "pstgb", [128, T], BF16, 2)
        wv = fm(w_in[l])
        ev = [0]

        def load_w(c0, cw):
            wt, wk = wring.next()
            p.dma("pool", [], [wk], wt[:, :, :cw], wv[:, :, c0:c0 + cw])
            return wt, wk

        def fm_group(c0, dst, dst_row0, dkey, ssd=False, gate=False, ncols=512):
            wt, wk = load_w(c0, ncols)
            for j in range((ncols + 127) // 128):
                cw_ = min(128, ncols - j * 128)
                st, stkey = (stgb if gate else stg).next()
                for ti, (t0, tw) in enumerate(TT):
                    pst, pk = psr.next()
                    for kc in range(8):
                        p.mm([wk, "HT"], [pk], pst[:cw_, :tw], wt[:, kc, j * 128:j * 128 + cw_], ht_rhs(HT, kc, ti, ssd),
                             start=(kc == 0), stop=(kc == 7))
                    if gate:
                        p.act([pk], [stkey], st[:cw_, t0:t0 + tw], pst[:cw_, :tw], AF.Sigmoid)
                    else:
                        ev[0] ^= 1
                        if ev[0]:
                            p.act([pk], [stkey], st[:cw_, t0:t0 + tw], pst[:cw_, :tw], AF.Identity)
                        else:
                            p.cp([pk], [stkey], st[:cw_, t0:t0 + tw], pst[:cw_, :tw])
                r0 = dst_row0 + j * 128
                p.dma("sp", [stkey], [dkey], dst[r0:r0 + cw_, :], st[:cw_, :])

        for g in range(10):
            fm_group(g * 512, PROJ, g * 512, "PROJ")
        for g in range(5):
            fm_group(5120 + g * 512, SSMP, g * 512, "SSMP", ssd=True)
        fm_group(7680, SSMP, 2560, "SSMP", ssd=True, ncols=32)
        for g in range(6):
            fm_group(7712 + g * 512, GT, g * 512, "GT", gate=True)
        p.release(m)

    def phase_rglru(l):
        m = p.mark()
        cw = p.sb("rcw", [128, 8, 4], F32)
        cb = p.sb("rcb", [128, 8], F32)
        ab = p.sb("rab", [128, 2, 8], F32)
        xb = p.sb("rxb", [128, 2, 8], F32)
        cA = p.sb("rcA", [128, 2, 8], F32)
        c2A = p.sb("rc2A", [128, 2, 8], F32)
        p.dma("sp", [], ["rcw"], cw[:], rnn_cwT[l])
        p.dma("sp", [], ["rcb"], cb[:], rnn_cbT[l])
        p.dma("sp", [], ["rab"], ab[:], rnn_abT[l])
        p.dma("sp", [], ["rxb"], xb[:], rnn_xbT[l])
        p.dma("sp", [], ["rcA"], cA[:], rnn_lamT[l])
        p.act(["rcA"], ["rcA"], cA[:], cA[:], AF.Exp, scale=-1.0)
        p.act(["rcA"], ["rcA"], cA[:], cA[:], AF.Ln, bias=1.0)
        p.ts(["rcA"], ["rc2A"], c2A[:], cA[:], -16.0, None, ALU.mult)
        p.ts(["rcA"], ["rcA"], cA[:], cA[:], -8.0, None, ALU.mult)
        T1 = p.sb("rT1", [128, T], F32)
        U = p.sb("rU", [128, T], F32)
        Af = p.sb("rA", [128, T], F32)
        Gf = p.sb("rG", [128, T], F32)
        HS = p.sb("rHS", [128, T], F32)
        Y = p.sb("rY", [128, T], BF16)
        gw = Ring(p, "rgw", [128, 4, 128], F32, 2)
        rr = Ring(p, "rr", [128, 512], F32, 2)
        ir = Ring(p, "ri", [128, 512], F32, 2)
        sr = Ring(p, "rs", [128, 512], F32, 2)
        segs = [(0, NCTX), (NCTX, T)]

        def rev(t, lo, hi):
            a = t[:, lo:hi]
            return bass.AP(a.tensor, a.offset + (hi - lo - 1), [list(a.ap[0]), [-1, hi - lo]])

        for hb in range(8):
            p.dma("sp", ["PROJ"], ["rT1"], T1[:], PROJ[hb * 128:(hb + 1) * 128, :])
            seg_conv(p, U, T1, lambda j: cw[:, hb, j:j + 1], cb[:, hb:hb + 1], 4, 2, segs, ["rT1", "rcw", "rcb"], ["rU"])
            g4, gk = gw.next()
            for d in range(2):
                p.dma("sp", [], [gk], g4[:, d, :], rnn_aw[l, d, hb])
                p.dma("sp", [], [gk], g4[:, 2 + d, :], rnn_xw[l, d, hb])
            for d in range(2):
                for ti, (t0, tw) in enumerate(TT):
                    pa, pak = psr.next()
                    px, pxk = psr.next()
                    p.mm([gk, "rU"], [pak], pa[:, :tw], g4[:, d, :], U[:, t0:t0 + tw])
                    p.mm([gk, "rU"], [pxk], px[:, :tw], g4[:, 2 + d, :], U[:, t0:t0 + tw])
                    r_, rk = rr.next()
                    i_, ik = ir.next()
                    s_, sk_ = sr.next()
                    p.act([pak, "rab"], [rk], r_[:, :tw], pa[:, :tw], AF.Sigmoid, bias=ab[:, d, hb:hb + 1])
                    p.act([pxk, "rxb"], [ik], i_[:, :tw], px[:, :tw], AF.Sigmoid, bias=xb[:, d, hb:hb + 1])
                    p.act([rk, "rc2A"], [sk_], s_[:, :tw], r_[:, :tw], AF.Exp, scale=c2A[:, d, hb:hb + 1])
                    p.act([rk, "rcA"], ["rA"], Af[:, t0:t0 + tw], r_[:, :tw], AF.Exp, scale=cA[:, d, hb:hb + 1])
                    p.act([sk_], [sk_], s_[:, :tw], s_[:, :tw], AF.Sqrt, scale=-1.0, bias=1.0)
                    p.tt([ik, sk_], [ik], i_[:, :tw], i_[:, :tw], s_[:, :tw], ALU.mult)
                    p.tt([ik, "rU"], ["rG"], Gf[:, t0:t0 + tw], i_[:, :tw], U[:, t0:t0 + tw], ALU.mult)
                if d == 0:
                    p.op("dve", ["rA", "rG"], ["rHS"], lambda e: e.tensor_tensor_scan(
                        out=HS[:, :], data0=Af[:, :], data1=Gf[:, :], initial=0.0, op0=ALU.mult, op1=ALU.add))
                else:
                    p.op("dve", ["rA", "rG"], ["rT1"], lambda e: e.tensor_tensor_scan(
                        out=rev(T1, 0, NCTX), data0=rev(Af, 0, NCTX), data1=rev(Gf, 0, NCTX), initial=0.0,
                        op0=ALU.mult, op1=ALU.add))
                    p.op("dve", ["rA", "rG", "rT1"], ["rT1"], lambda e: e.tensor_tensor_scan(
                        out=rev(T1, NCTX, T), data0=rev(Af, NCTX, T), data1=rev(Gf, NCTX, T), initial=T1[:, 0:1],
                        op0=ALU.mult, op1=ALU.add))
                    p.tt(["rHS", "rT1"], ["rHS"], HS[:, :], HS[:, :], T1[:, :], ALU.add)
            p.dma("sp", ["PROJ"], ["rT1"], T1[:], PROJ[1024 + hb * 128:1024 + (hb + 1) * 128, :])
            p.tt(["rT1"], ["rG"], Gf[:, :], T1[:, :], T1[:, :], ALU.mult)
            p.ts(["rG"], ["rG"], Gf[:, :], Gf[:, :], 0.044715, 1.0, ALU.mult, ALU.add)
            p.tt(["rG", "rT1"], ["rG"], Gf[:, :], Gf[:, :], T1[:, :], ALU.mult)
            p.act(["rG"], ["rG"], Gf[:, :], Gf[:, :], AF.Sigmoid, scale=1.5957691216057308)
            p.tt(["rG", "rT1"], ["rG"], Gf[:, :], Gf[:, :], T1[:, :], ALU.mult)
            p.tt(["rG", "rHS"], ["rY"], Y[:, :], Gf[:, :], HS[:, :], ALU.mult)
            p.dma("sp", ["rY"], ["YR"], YR[hb * 128:(hb + 1) * 128, :], Y[:])
        p.release(m)

    def phase_ssd(l):
        m = p.mark()
        psb = psr.tiles
        pkk = psr.keys
        cw = p.sb("scw", [128, 12, 4], F32)
        cb = p.sb("scb", [128, 12], F32)
        p.dma("sp", [], ["scw"], cw[:], ssm_cwT[l])
        p.dma("sp", [], ["scb"], cb[:], ssm_cbT[l])
        XTOK = p.sb("XTOK", [128, 34, 1024], BF16)
        BT = p.sb("BT", [128, 2, T], BF16)
        CT = p.sb("CT", [128, 2, T], BF16)
        DT_tok = p.sb("DT_tok", [128, 34, 32], F32)
        ADT_tok = p.sb("ADT_tok", [128, 34, 32], F32)
        EA = p.sb("EA", [128, 34, 32], F32)
        DTE = p.sb("DTE", [128, 34, 32], F32)
        DEC = p.sb("DEC", [128, 34, 32], F32)
        mark_a = p.mark()
        BTOK = p.sb("BTOK", [128, 34, 256], BF16)
        DTD = p.sb("DTD", [128, 34, 32], F32)
        segs = [(0, NCTX), (NCTX, T)]
        m1 = p.mark()
        T1r = Ring(p, "sT1", [128, T], F32, 2)
        XSr = Ring(p, "sXS", [128, T], F32, 1)
        for blk in range(12):
            T1, t1k = T1r.next()
            XS, xsk = XSr.next()
            p.dma("sp", ["SSMP"], [t1k], T1[:], SSMP[1024 + blk * 128:1024 + (blk + 1) * 128, :])
            seg_conv(p, XS, T1, lambda j: cw[:, blk, j:j + 1], cb[:, blk:blk + 1], 4, 2, segs, [t1k, "scw", "scb"], [xsk])
            p.act([xsk], [xsk], XS[:, :], XS[:, :], AF.Silu)
            if blk >= 8:
                g = (blk - 8) % 2
                dstT, dk = (BT, "BT") if blk < 10 else (CT, "CT")
                p.cp([xsk], [dk], dstT[:, g, :], XS[:, :])
            if blk < 10:
                for q0 in range(0, 34, 4):
                    nq = min(4, 34 - q0)
                    pst, pk = psr.next()
                    for qi in range(nq):
                        q = q0 + qi
                        p.tr([xsk, "ident"], [pk], pst[:, qi * 128:(qi + 1) * 128], XS[:, q * 128:(q + 1) * 128], ident[:])
                    if blk < 8:
                        p.cp([pk], ["XTOK"], XTOK[:, q0:q0 + nq, blk * 128:(blk + 1) * 128],
                             pst[:, :nq * 128].rearrange("p (q c) -> p q c", c=128), eng=("dve" if (q0 // 4) % 2 else "act") if False else "dve")
                    else:
                        g = blk - 8
                        p.cp([pk], ["BTOK"], BTOK[:, q0:q0 + nq, g * 128:(g + 1) * 128],
                             pst[:, :nq * 128].rearrange("p (q c) -> p q c", c=128))
        p.release(m1)
        m2 = p.mark()
        DTF = p.sb("DTF", [32, T], F32)
        ADF = p.sb("ADF", [32, T], F32)
        dtb = p.sb("dtb", [32, 1], F32)
        aneg = p.sb("aneg", [32, 1], F32)
        p.dma("sp", ["SSMP"], ["DTF"], DTF[:], SSMP[2560:2592, :])
        p.dma("sp", [], ["dtb"], dtb[:], ssm_dtbT[l])
        p.dma("sp", [], ["aneg"], aneg[:], ssm_alogT[l])
        p.act(["aneg"], ["aneg"], aneg[:], aneg[:], AF.Exp)
        p.ts(["aneg"], ["aneg"], aneg[:], aneg[:], -1.0, None, ALU.mult)
        p.act(["DTF", "dtb"], ["DTF"], DTF[:, :], DTF[:, :], AF.Exp, bias=dtb[:, 0:1])
        p.act(["DTF"], ["DTF"], DTF[:, :], DTF[:, :], AF.Ln, bias=1.0)
        p.ts(["DTF", "aneg"], ["ADF"], ADF[:, :], DTF[:, :], aneg[:, 0:1], None, ALU.mult)
        for (src, sk_, dst, dk) in ((DTF, "DTF", DT_tok, "DT_tok"), (ADF, "ADF", ADT_tok, "ADT_tok")):
            for q0 in range(0, 34, 16):
                nq = min(16, 34 - q0)
                pst, pk = psr.next()
                for qi in range(nq):
                    q = q0 + qi
                    p.tr([sk_, "ident"], [pk], pst[:, qi * 32:(qi + 1) * 32], src[:, q * 128:(q + 1) * 128], ident[:32, :32])
                p.cp([pk], [dk], dst[:, q0:q0 + nq, :], pst[:, :nq * 32].rearrange("p (q c) -> p q c", c=32))
        for d in range(2):
            for cg in range(2):
                rhs = ADT_tok[:, cg * 17:(cg + 1) * 17, d * 16:(d + 1) * 16]
                for (mk_, dst, dk) in (((LE, GE)[d], EA, "EA"), ((GT_, LT)[d], DTE, "DTE"), (ONES, DEC, "DEC")):
                    pst, pk = psr.next()
                    p.mm(["masks", "ADT_tok"], [pk], pst[:, :272], masks[:, mk_, :], rhs)
                    p.act([pk], [dk], dst[:, cg * 17:(cg + 1) * 17, d * 16:(d + 1) * 16],
                          pst[:, :272].rearrange("p (q c) -> p q c", c=16), AF.Exp)
        p.tt(["DT_tok", "DTE"], ["DTD"], DTD[:], DT_tok[:], DTE[:], ALU.mult)
        p.release(m2)
        H = p.sb("Hst", [128, 1024], F32)
        hbr = Ring(p, "Hb", [128, 1024], BF16, 3)
        xsr = Ring(p, "xsd", [128, 1024], BF16, 3)
        for d in range(2):
            order = list(range(34)) if d == 0 else [1, 0] + list(range(33, 1, -1))
            p.op("dve", [], ["Hst"], lambda e: e.memset(H[:], 0.0))
            for q in order:
                hb_, hbk = hbr.next()
                p.act(["Hst"], [hbk], hb_[:], H[:], AF.Identity)
                p.dma("sp", [hbk], ["HP"], HP[d, q], hb_[:])
                xs, xk = xsr.next()
                p.tt(["XTOK", "DTD"], [xk], xs[:].rearrange("p (h c) -> p h c", c=64),
                     XTOK[:, q, :].rearrange("p (h c) -> p h c", c=64),
                     DTD[:, q, d * 16:(d + 1) * 16].unsqueeze(2).to_broadcast([128, 16, 64]), ALU.mult)
                pss = []
                for g in range(2):
                    pst, pk = psr.next()
                    p.mm(["BTOK", xk], [pk], pst[:, :], BTOK[:, q, g * 128:(g + 1) * 128], xs[:, g * 512:(g + 1) * 512])
                    pss.append((pst, pk))
                p.tt(["Hst", "DEC"], ["Hst"], H[:].rearrange("p (h c) -> p h c", c=64), H[:].rearrange("p (h c) -> p h c", c=64),
                     DEC[:, q, d * 16:(d + 1) * 16].unsqueeze(2).to_broadcast([128, 16, 64]), ALU.mult)
                for g in range(2):
                    p.tt(["Hst", pss[g][1]], ["Hst"], H[:, g * 512:(g + 1) * 512], H[:, g * 512:(g + 1) * 512], pss[g][0][:, :], ALU.add)
        p.release(mark_a)
        dsk = p.sb("dsk", [128, 16], F32)
        nw = p.sb("snw", [128, 1024], F32)
        p.dma("sp", [], ["dsk"], dsk[:], ssm_d[l:l + 1, :].to_broadcast([128, 16]))
        p.dma("sp", [], ["snw"], nw[:], ssm_norm[l:l + 1, :].to_broadcast([128, 1024]))
        hpr = Ring(p, "hp", [128, 2, 1024], BF16, 2)
        cbmr = Ring(p, "cbm", [128, 2, 256], F32, 2)
        rsr = Ring(p, "rseg", [128, 16, 128], F32, 1)
        er = Ring(p, "eseg", [128, 16, 128], F32, 1)
        mr = Ring(p, "mseg", [128, 16, 128], BF16, 2)
        xdr = Ring(p, "xdt", [128, 1024], BF16, 2)
        accr = Ring(p, "acc", [128, 1024], F32, 2)
        tmpr = Ring(p, "stmp", [128, 1024], F32, 1)
        zfr = Ring(p, "zf", [128, 8, 128], F32, 1)
        szr = Ring(p, "sz", [128, 1024], F32, 2)
        ysr = Ring(p, "ys", [128, 1024], F32, 2)
        ssr = Ring(p, "ssq", [128, 2], F32, 2)
        for q in range(34):
            hp, hpk = hpr.next()
            for d in range(2):
                p.dma("sp", ["HP"], [hpk], hp[:, d, :], HP[d, q])
            zf, zfk = zfr.next()
            p.dma("sp", ["SSMP"], [zfk], zf[:], SSMP[0:1024, q * 128:(q + 1) * 128].rearrange("(b c) t -> c b t", c=128))
            for g in range(2):
                p.mm(["BT", "CT"], [pkk[0]], psb[0][:, g * 128:(g + 1) * 128], BT[:, g, q * 128:(q + 1) * 128], CT[:, g, q * 128:(q + 1) * 128])
            cbm, cbk = cbmr.next()
            for d in range(2):
                p.tt([pkk[0], "masks"], [cbk], cbm[:, d, :].rearrange("p (g c) -> p g c", c=128),
                     psb[0][:, 0:256].rearrange("p (g c) -> p g c", c=128),
                     masks[:, (LE, GE)[d], :].unsqueeze(1).to_broadcast([128, 2, 128]), ALU.mult)
            acc, acck = accr.next()
            for d in range(2):
                rs, rsk = rsr.next()
                p.tt(["ADT_tok", "masks"], [rsk], rs[:], ADT_tok[:, q, d * 16:(d + 1) * 16].unsqueeze(2).to_broadcast([128, 16, 128]),
                     masks[:, (LE, GE)[d], :].unsqueeze(1).to_broadcast([128, 16, 128]), ALU.mult, eng="pool")
                es, esk = er.next()
                for i in range(4):
                    p.mm(["masks", rsk], [pkk[1 + i]], psb[1 + i][:, :], masks[:, (GT_, LT)[d], :],
                         rs[:, 4 * i:4 * i + 4, :])
                    p.act([pkk[1 + i]], [esk], es[:, 4 * i:4 * i + 4, :], psb[1 + i][:, :].rearrange("p (h c) -> p h c", c=128), AF.Exp)
                ms, msk = mr.next()
                p.tt([esk, cbk], [msk], ms[:].rearrange("p (g e) c -> p g e c", g=2), es[:].rearrange("p (g e) c -> p g e c", g=2),
                     cbm[:, d, :].rearrange("p (g c) -> p g c", c=128).unsqueeze(2).to_broadcast([128, 2, 8, 128]), ALU.mult)
                xd, xdk = xdr.next()
                p.tt(["XTOK", "DT_tok"], [xdk], xd[:].rearrange("p (h c) -> p h c", c=64),
                     XTOK[:, q, :].rearrange("p (h c) -> p h c", c=64),
                     DT_tok[:, q, d * 16:(d + 1) * 16].unsqueeze(2).to_broadcast([128, 16, 64]), ALU.mult, eng="pool")
                for h in range(16):
                    bk = 5 + h // 8
                    p.mm([msk, xdk], [pkk[bk]], psb[bk][:, (h % 8) * 64:(h % 8 + 1) * 64], ms[:, h, :], xd[:, h * 64:(h + 1) * 64],
                         start=(d == 0 and h % 8 == 0), stop=(d == 1 and h % 8 == 7))
                for g in range(2):
                    bk = 7 if g == 0 else 0
                    p.mm(["CT", hpk], [pkk[bk]], psb[bk][:, :], CT[:, g, q * 128:(q + 1) * 128], hp[:, d, g * 512:(g + 1) * 512])
                    eab = EA[:, q, d * 16 + g * 8:d * 16 + (g + 1) * 8].unsqueeze(2).to_broadcast([128, 8, 64])
                    if d == 0:
                        p.tt([pkk[bk], "EA"], [acck], acc[:, g * 512:(g + 1) * 512].rearrange("p (h c) -> p h c", c=64),
                             psb[bk][:, :].rearrange("p (h c) -> p h c", c=64), eab, ALU.mult)
                    else:
                        tm, tmk = tmpr.next()
                        p.tt([pkk[bk], "EA"], [tmk], tm[:, :512].rearrange("p (h c) -> p h c", c=64),
                             psb[bk][:, :].rearrange("p (h c) -> p h c", c=64), eab, ALU.mult)
                        p.tt([acck, tmk], [acck], acc[:, g * 512:(g + 1) * 512], acc[:, g * 512:(g + 1) * 512], tm[:, :512], ALU.add)
            for g in range(2):
                p.tt([acck, pkk[5 + g]], [acck], acc[:, g * 512:(g + 1) * 512], acc[:, g * 512:(g + 1) * 512], psb[5 + g][:, :], ALU.add)
            tm, tmk = tmpr.next()
            p.tt(["XTOK", "dsk"], [tmk], tm[:].rearrange("p (h c) -> p h c", c=64), XTOK[:, q, :].rearrange("p (h c) -> p h c", c=64),
                 dsk[:].unsqueeze(2).to_broadcast([128, 16, 64]), ALU.mult, eng="pool")
            p.tt([acck, tmk], [acck], acc[:], acc[:], tm[:], ALU.add)
            sz, szk = szr.next()
            for g in range(2):
                for b4 in range(4):
                    p.tr([zfk, "ident"], [pkk[1 + g]], psb[1 + g][:, b4 * 128:(b4 + 1) * 128], zf[:, g * 4 + b4, :], ident[:])
                p.act([pkk[1 + g]], [szk], sz[:, g * 512:(g + 1) * 512], psb[1 + g][:, :], AF.Silu)
            p.tt([acck, szk], [acck], acc[:], acc[:], sz[:], ALU.mult)
            ss, ssk = ssr.next()
            p.act([acck], [szk, ssk], sz[:], acc[:], AF.Square, accum_out=ss[:, 0:1])
            p.ts([ssk], [ssk], ss[:, 1:2], ss[:, 0:1], 1.0 / 1024, EPS, ALU.mult, ALU.add)
            p.act([ssk], [ssk], ss[:, 1:2], ss[:, 1:2], AF.Sqrt)
            p.op("dve", [ssk], [ssk], lambda e: e.reciprocal(out=ss[:, 1:2], in_=ss[:, 1:2]))
            ys, ysk = ysr.next()
            p.stt([acck, ssk, "snw"], [ysk], ys[:], acc[:], ss[:, 1:2], nw[:], ALU.mult, ALU.mult)
            if q < 2:
                p.dma("sp", [ysk], ["YSTOK"], YSTOK[q * 128:(q + 1) * 128, :], ys[:])
            else:
                c2 = q - 2
                yv = YSTOK[NCTX:, :].rearrange("(r c) d -> c r d", c=64)
                for cl in range(2):
                    p.dma("sp", [ysk], ["YSTOK"], yv[2 * c2 + cl], ys[cl * 64:(cl + 1) * 64, :])
        p.release(m)

    class Rot:
        def __init__(self, idxs):
            self.idxs = idxs
            self.i = 0

        def next(self):
            k = self.idxs[self.i]
            self.i = (self.i + 1) % len(self.idxs)
            return psr.tiles[k], psr.keys[k]

    PI = float(np.pi)

    def hy_filter(l, n, embT_d, tv_d, FW, KF):
        m = p.mark()
        nt = n // 128
        psb, pkk = psr.tiles, psr.keys
        w1 = p.sb("hw1", [33, 64], F32)
        w2 = p.sb("hw2", [64, 64], F32)
        w3 = p.sb("hw3", [64, 4096], F32)
        fr = p.sb("hfr", [64, 1], F32)
        fb1 = p.sb("hfb1", [64, 1], F32)
        fb2 = p.sb("hfb2", [64, 1], F32)
        emb = p.sb("hemb", [33, n], F32)
        h1 = p.sb("hh1", [64, n], F32)
        h2 = p.sb("hh2", [64, n], F32)
        negpi = p.sb("hnegpi", [128, 1], F32)
        nz0 = p.sb("hnz0", [128, 1], F32)
        dec = p.sb("hdec", [128, 4096], F32)
        negt = p.sb("hnegt", [128, nt], F32)
        p.dma("sp", [], ["hw1"], w1[:], hy_w1[l])
        p.dma("sp", [], ["hw2"], w2[:], hy_w2[l])
        p.dma("sp", [], ["hw3"], w3[:], hy_w3[l])
        p.dma("sp", [], ["hfr"], fr[:], hy_freqT[l])
        p.dma("sp", [], ["hfb1"], fb1[:], hy_b1T[l])
        p.dma("sp", [], ["hfb2"], fb2[:], hy_b2T[l])
        p.dma("sp", [], ["hemb"], emb[:], embT_d[:, :])
        p.dma("sp", [], ["hdec"], dec[:], hy_decay[l:l + 1, :].to_broadcast([128, 4096]))
        p.dma("sp", [], ["hnegt"], negt[:], tv_d[:, :])
        p.ts(["hnegt"], ["hnegt"], negt[:], negt[:], -1.0, None, ALU.mult)
        p.op("dve", [], ["hnegpi"], lambda e: e.memset(negpi[:], -PI))
        p.op("dve", [], ["hnz0"], lambda e: e.memset(nz0[:], 1.0))
        p.op("dve", ["hnz0"], ["hnz0"], lambda e: e.memset(nz0[0:1, :], 0.0))
        p.ts(["hfb1", "hfr"], ["hfb1"], fb1[:], fb1[:], fr[:, 0:1], None, ALU.mult)
        p.ts(["hfb2", "hfr"], ["hfb2"], fb2[:], fb2[:], fr[:, 0:1], None, ALU.mult)
        rot = Rot([0, 1, 2, 3, 4, 5, 6])
        sinr = Ring(p, "hsin", [64, 512], F32, 4)
        for (src, sk_, wgt, wk, fb, fbk, dst, dk) in ((emb, "hemb", w1, "hw1", fb1, "hfb1", h1, "hh1"),
                                                       (h1, "hh1", w2, "hw2", fb2, "hfb2", h2, "hh2")):
            for c0 in range(0, n, 512):
                cwid = min(512, n - c0)
                pst, pk = rot.next()
                p.mm([sk_, wk], [pk], pst[:64, :cwid], wgt[:, :], src[:, c0:c0 + cwid])
                p.ts([pk, "hfr", fbk], [dk], dst[:, c0:c0 + cwid], pst[:64, :cwid], fr[:, 0:1], fb[:, 0:1], ALU.mult, ALU.add)
                sa, sak = sinr.next()
                sb_, sbk = sinr.next()
                dv = dst[:, c0:c0 + cwid]
                p.act([dk], [sak], sa[:, :cwid], dv, AF.Sin, scale=0.25)
                p.act([dk], [sbk], sb_[:, :cwid], dv, AF.Sin, scale=0.125)
                p.tt([sbk], [sbk], sb_[:, :cwid], sb_[:, :cwid], sb_[:, :cwid], ALU.mult)
                p.ts([sbk], [sbk], sb_[:, :cwid], sb_[:, :cwid], -2.0, 1.0, ALU.mult, ALU.add)
                p.tt([sak, sbk], [sbk], sb_[:, :cwid], sa[:, :cwid], sb_[:, :cwid], ALU.mult)
                p.tt([sak], [sak], sa[:, :cwid], sa[:, :cwid], sa[:, :cwid], ALU.mult)
                p.ts([sak], [sak], sa[:, :cwid], sa[:, :cwid], -2.0, 1.0, ALU.mult, ALU.add)
                p.stt([sak, sbk], [dk], dv, sb_[:, :cwid], 4.0, sa[:, :cwid], ALU.mult, ALU.mult)
        UP = p.sb("hUP", [128, nt, 512], BF16)
        UM = p.sb("hUM", [128, nt, 512], BF16)
        rinv = p.sb("hrinv", [128, 512], F32)
        er = Ring(p, "hE", [128, 512], F32, 2)
        hfr_ = Ring(p, "hhf", [128, 512], F32, 2)
        hbr_ = Ring(p, "hhb", [128, 512], F32, 2)
        abr = Ring(p, "hab", [128, 512], F32, 2)
        fring = Ring(p, "hF", [128, nt, 128], BF16, 3)
        kr = Ring(p, "hkt", [128, 512], F32, 3)
        for o in range(2):
            for cg in range(2):
                colf = o * 1024 + cg * 512
                colb = 2048 + colf
                for tc in range(nt):
                    hh = []
                    for dirn, col in ((0, colf), (1, colb)):
                        pst, pk = rot.next()
                        p.mm(["hh2", "hw3"], [pk], pst[:, :], h2[:, tc * 128:(tc + 1) * 128], w3[:, col:col + 512])
                        E, ek = er.next()
                        p.act(["hdec", "hnegt"], [ek], E[:], dec[:, col:col + 512], AF.Exp, scale=negt[:, tc:tc + 1])
                        ht_, hk = (hfr_ if dirn == 0 else hbr_).next()
                        p.tt([pk, ek], [hk], ht_[:], pst[:, :], E[:], ALU.mult)
                        if dirn == 1 and tc == 0:
                            p.ts([hk, "hnz0"], [hk], ht_[:], ht_[:], nz0[:, 0:1], None, ALU.mult)
                        ab, abk = abr.next()
                        p.act([hk], [abk], ab[:], ht_[:], AF.Abs)
                        p.mm(["masks", abk], [pkk[7]], psb[7][:, :], masks[:, ONES, :], ab[:],
                             start=(tc == 0 and dirn == 0), stop=(tc == nt - 1 and dirn == 1))
                        hh.append((ht_, hk))
                    p.tt([hh[0][1], hh[1][1]], ["hUP"], UP[:, tc, :], hh[0][0][:], hh[1][0][:], ALU.add)
                    p.tt([hh[0][1], hh[1][1]], ["hUM"], UM[:, tc, :], hh[0][0][:], hh[1][0][:], ALU.subtract)
                p.op("dve", [pkk[7]], ["hrinv"], lambda e: e.reciprocal(out=rinv[:], in_=psb[7][:, :]))
                for j in range(nt):
                    for (pq, U, uk) in ((0, UP, "hUP"), (1, UM, "hUM")):
                        Ft, fk = fring.next()
                        p.dma("sp", [], [fk], Ft[:], FW[pq * nt + j])
                        pst, pk = rot.next()
                        for tc in range(nt):
                            p.mm([fk, uk], [pk], pst[:, :], Ft[:, tc, :], U[:, tc, :], start=(tc == 0), stop=(tc == nt - 1))
                        kt, kk = kr.next()
                        p.tt([pk, "hrinv"], [kk], kt[:], pst[:, :], rinv[:], ALU.mult)
                        p.dma("sp", [kk], ["KF"], KF[o, j, :, pq, cg * 512:(cg + 1) * 512], kt[:])
        p.release(m)

    def hy_data(l, n, toff, FW, GW, KF, hbias):
        m = p.mark()
        nt = n // 128
        psb, pkk = psr.tiles, psr.keys
        tiles = [(i * 512, 512) for i in range(n // 512)] if n >= 512 else [(0, n)]
        ZT = p.sb("hZT", [128, nt, 512], BF16)
        YSs = p.sb("hYS", [128, nt, 2, 512], BF16)
        zfr = Ring(p, "hzf", [128, n], F32, 1)
        fring = Ring(p, "hF2", [128, nt, 128], BF16, 3)
        kr = Ring(p, "hkt2", [128, 2, 512], F32, 2)
        tr_ = Ring(p, "htm", [128, 512], F32, 4)
        gring = Ring(p, "hG", [128, 8, 512], BF16, 3)
        zpr = Ring(p, "hzp", [128, 512], F32, 2)
        xgr = Ring(p, "hxg", [128, 512], F32, 2)
        znr = Ring(p, "hzn", [128, 512], F32, 2)
        ynr = Ring(p, "hyn", [128, 512], BF16, 2)
        rot = Rot([4, 5, 6, 7])
        for cg in range(2):
            for o in range(2):
                src = HY if o == 0 else Z2
                skey = "HY" if o == 0 else "Z2"
                for b in range(4):
                    zf, zk = zfr.next()
                    r0 = cg * 512 + b * 128
                    p.dma("sp", [skey], [zk], zf[:], src[r0:r0 + 128, toff:toff + n])
                    for tc0 in range(0, nt, 4):
                        nq = min(4, nt - tc0)
                        pst, pk = rot.next()
                        for qi in range(nq):
                            p.tr([zk, "ident"], [pk], pst[:, qi * 128:(qi + 1) * 128], zf[:, (tc0 + qi) * 128:(tc0 + qi + 1) * 128], ident[:])
                        p.cp([pk], ["hZT"], ZT[:, tc0:tc0 + nq, b * 128:(b + 1) * 128], pst[:, :nq * 128].rearrange("p (q c) -> p q c", c=128))
                for j in range(nt):
                    Fc, fck = fring.next()
                    p.dma("sp", [], [fck], Fc[:], FW[j])
                    Fs, fsk = fring.next()
                    p.dma("sp", [], [fsk], Fs[:], FW[nt + j])
                    kt, kk = kr.next()
                    p.dma("sp", ["KF"], [kk], kt[:], KF[o, j, :, :, cg * 512:(cg + 1) * 512])
                    pA, pAk = rot.next()
                    pB, pBk = rot.next()
                    for tc in range(nt):
                        p.mm([fck, "hZT"], [pAk], pA[:, :], Fc[:, tc, :], ZT[:, tc, :], start=(tc == 0), stop=(tc == nt - 1))
                    for tc in range(nt):
                        p.mm([fsk, "hZT"], [pBk], pB[:, :], Fs[:, tc, :], ZT[:, tc, :], start=(tc == 0), stop=(tc == nt - 1))
                    t1, k1 = tr_.next()
                    t2, k2 = tr_.next()
                    p.tt([pAk, kk], [k1], t1[:], pA[:, :], kt[:, 0, :], ALU.mult)
                    p.tt([pBk, kk], [k2], t2[:], pB[:, :], kt[:, 1, :], ALU.mult)
                    p.tt([k1, k2], ["hYS"], YSs[:, j, 0, :], t1[:], t2[:], ALU.subtract)
                    t3, k3 = tr_.next()
                    t4, k4 = tr_.next()
                    p.tt([pAk, kk], [k3], t3[:], pA[:, :], kt[:, 1, :], ALU.mult)
                    p.tt([pBk, kk], [k4], t4[:], pB[:, :], kt[:, 0, :], ALU.mult)
                    p.tt([k3, k4], ["hYS"], YSs[:, j, 1, :], t3[:], t4[:], ALU.add)
                for ti, (t0, tw) in enumerate(tiles):
                    Gt, gk = None, None
                    for j2 in range(2 * nt):
                        if j2 % 8 == 0:
                            ng = min(8, 2 * nt - j2)
                            Gt, gk = gring.next()
                            p.dma("sp", [], [gk], Gt[:, :ng, :tw], GW[ti, :, j2:j2 + ng, :])
                        part, j = j2 // nt, j2 % nt
                        for b in range(4):
                            p.mm(["hYS", gk], [pkk[b]], psb[b][:, :tw], YSs[:, j, part, b * 128:(b + 1) * 128], Gt[:, j2 % 8, :tw],
                                 start=(j2 == 0), stop=(j2 == 2 * nt - 1))
                    for b in range(4):
                        cb_ = cg * 4 + b
                        zp, zpk = zpr.next()
                        p.dma("sp", [skey], [zpk], zp[:, :tw], src[cb_ * 128:(cb_ + 1) * 128, toff + t0:toff + t0 + tw])
                        xg, xgk = xgr.next()
                        xr0 = (1 + o) * 1024 + cb_ * 128
                        p.dma("sp", ["HY"], [xgk], xg[:, :tw], HY[xr0:xr0 + 128, toff + t0:toff + t0 + tw])
                        tm, tmk = tr_.next()
                        p.stt([zpk, "hbias", pkk[b]], [tmk], tm[:, :tw], zp[:, :tw], hbias[:, o, cb_:cb_ + 1], psb[b][:, :tw], ALU.mult, ALU.add)
                        if o == 0:
                            zn, znk = znr.next()
                            p.tt([tmk, xgk], [znk], zn[:, :tw], tm[:, :tw], xg[:, :tw], ALU.mult)
                            p.dma("sp", [znk], ["Z2"], Z2[cb_ * 128:(cb_ + 1) * 128, toff + t0:toff + t0 + tw], zn[:, :tw])
                        else:
                            yn, ynk = ynr.next()
                            p.tt([tmk, xgk], [ynk], yn[:, :tw], tm[:, :tw], xg[:, :tw], ALU.mult)
                            p.dma("sp", [ynk], ["YH"], YH[cb_ * 128:(cb_ + 1) * 128, toff + t0:toff + t0 + tw], yn[:, :tw])
        p.release(m)


    def hy4(l, hbias):
        m = p.mark()
        n = NLAT
        toff = NCTX
        psb, pkk = psr.tiles, psr.keys
        identb = p.sb("identb", [128, 128], BF16)
        p.cp(["ident"], ["identb"], identb[:], ident[:])
        S1 = p.sb("S1", [32, 128], BF16)
        S2 = p.sb("S2", [128, 32], BF16)
        p.dma("sp", [], ["S1"], S1[:], S1_d[:, :])
        p.dma("sp", [], ["S2"], S2[:], S2_d[:, :])
        rot = Rot([0, 1, 2, 3, 4, 5, 6, 7])
        evc = [0]

        def evac(R, W, out, in_):
            evc[0] ^= 1
            if evc[0]:
                p.act(R, W, out, in_, AF.Identity)
            else:
                p.cp(R, W, out, in_)

        CB = 64

        def v_zin(X):
            return X[:32, :].rearrange("p (c t) -> p c t", t=128)

        def v_A(X):
            return X[:, :].rearrange("p (r f c) -> p r f c", r=2, f=64)

        def v_Y(X):
            return X[:64, :].rearrange("p (r f c) -> p r f c", r=2, f=64)

        def v_B(X):
            return X[:, :].rearrange("p (c k) -> p c k", k=128)

        def load_zin(X, xk, src_rows, skey):
            zv = v_zin(X)
            for c4 in range(CB // 32):
                p.dma("sp", [skey], [xk], zv[:, c4 * 32:(c4 + 1) * 32, :],
                      src_rows[c4 * 32:(c4 + 1) * 32, :].rearrange("c (a b) -> a c b", b=128))

        def stage1(Xi, xik, Xo, xok):
            zin = v_zin(Xi)
            Ac = v_B(Xo)
            for c0 in range(0, CB, 4):
                pst, pk = rot.next()
                for q in range(4):
                    p.mm([xik, "S1"], [pk], pst[:, q * 128:(q + 1) * 128], zin[:, c0 + q, :], S1[:, :])
                evac([pk], [xok], Ac[:, c0:c0 + 4, :], pst[:, :].rearrange("p (c a) -> p c a", a=128))

        def stage2(XA_, xak, g8, XA2=None, xak2=None):
            A = v_B(XA_)
            A2 = v_B(XA2) if XA2 is not None else A
            k2 = xak2 if XA2 is not None else xak
            zr, zrk = rot.next()
            zi, zik = rot.next()
            for q in range(8):
                f1 = g8 * 8 + q
                o_ = slice(q * CB, (q + 1) * CB)
                p.mm(["WF", xak], [zrk], zr[:64, o_], WF[:, f1, 0, :], A[:, :, f1], start=True, stop=False)
                p.mm(["WF", xak], [zrk], zr[:64, o_], WF[:, f1, 2, :], A[:, :, 64 + f1], start=False, stop=True)
            for q in range(8):
                f1 = g8 * 8 + q
                o_ = slice(q * CB, (q + 1) * CB)
                p.mm(["WF", k2], [zik], zi[:64, o_], WF[:, f1, 1, :], A2[:, :, f1], start=True, stop=False)
                p.mm(["WF", k2], [zik], zi[:64, o_], WF[:, f1, 0, :], A2[:, :, 64 + f1], start=False, stop=True)
            return zr, zrk, zi, zik

        m0 = p.mark()
        w1 = p.sb("hw1", [33, 64], F32)
        w2 = p.sb("hw2", [64, 64], F32)
        w3 = p.sb("hw3", [64, 4096], F32)
        fr = p.sb("hfr", [64, 1], F32)
        fb1 = p.sb("hfb1", [64, 1], F32)
        fb2 = p.sb("hfb2", [64, 1], F32)
        h2 = p.sb("hh2", [64, n], F32)
        ndec = p.sb("hndec", [128, 32], F32)
        tbc = p.sb("htbc", [128, n], F32)
        p.dma("sp", [], ["hw1"], w1[:], hy_w1[l])
        p.dma("sp", [], ["hw2"], w2[:], hy_w2[l])
        p.dma("sp", [], ["hw3"], w3[:], hy_w3[l])
        p.dma("sp", [], ["hfr"], fr[:], hy_freqT[l])
        p.dma("sp", [], ["hfb1"], fb1[:], hy_b1T[l])
        p.dma("sp", [], ["hfb2"], fb2[:], hy_b2T[l])
        p.dma("sp", [], ["hndec"], ndec[:], hy_ndecT[l])
        p.dma("sp", [], ["htbc"], tbc[:], tvec[0:1, :].to_broadcast([128, n]))
        p.ts(["hfb1", "hfr"], ["hfb1"], fb1[:], fb1[:], fr[:, 0:1], None, ALU.mult)
        p.ts(["hfb2", "hfr"], ["hfb2"], fb2[:], fb2[:], fr[:, 0:1], None, ALU.mult)
        mm_ = p.mark()
        emb = p.sb("hemb", [33, n], F32)
        h1 = p.sb("hh1", [64, n], F32)
        p.dma("sp", [], ["hemb"], emb[:], embT_l[:, :])
        sinr = Ring(p, "hsin", [64, 512], F32, 4)
        for (src, sk_, wgt, wk, fb, fbk, dst, dk) in ((emb, "hemb", w1, "hw1", fb1, "hfb1", h1, "hh1"),
                                                       (h1, "hh1", w2, "hw2", fb2, "hfb2", h2, "hh2")):
            for c0 in range(0, n, 512):
                pst, pk = rot.next()
                p.mm([sk_, wk], [pk], pst[:64, :], wgt[:, :], src[:, c0:c0 + 512])
                dv = dst[:, c0:c0 + 512]
                p.ts([pk, "hfr", fbk], [dk], dv, pst[:64, :], fr[:, 0:1], fb[:, 0:1], ALU.mult, ALU.add)
                sa, sak = sinr.next()
                sb_, sbk = sinr.next()
                p.act([dk], [sak], sa[:, :], dv, AF.Sin, scale=0.25)
                p.act([dk], [sbk], sb_[:, :], dv, AF.Sin, scale=0.125)
                p.tt([sbk], [sbk], sb_[:, :], sb_[:, :], sb_[:, :], ALU.mult)
                p.ts([sbk], [sbk], sb_[:, :], sb_[:, :], -2.0, 1.0, ALU.mult, ALU.add)
                p.tt([sak, sbk], [sbk], sb_[:, :], sa[:, :], sb_[:, :], ALU.mult)
                p.tt([sak], [sak], sa[:, :], sa[:, :], sa[:, :], ALU.mult)
                p.ts([sak], [sak], sa[:, :], sa[:, :], -2.0, 1.0, ALU.mult, ALU.add)
                p.stt([sak, sbk], [dk], dv, sb_[:, :], 4.0, sa[:, :], ALU.mult, ALU.mult)
        p.release(mm_)
        hrow = [p.sb("hrow0", [128, n], F32), p.sb("hrow1", [128, n], F32)]
        hrk = ["hrow0", "hrow1"]
        junk = p.sb("hjunk", [128, n], F32)
        upr = Ring(p, "hup", [128, n], F32, 1)
        ubr = Ring(p, "hub", [128, n], BF16, 2)
        er = Ring(p, "hE", [128, 512], F32, 3)
        ssr = Ring(p, "hss", [128, 4], F32, 2)
        for o in range(2):
            for cb_ in range(8):
                ss, ssk = ssr.next()
                for dirn in range(2):
                    colblk = dirn * 16 + o * 8 + cb_
                    col = colblk * 128
                    for tt_ in range(8):
                        pst, pk = rot.next()
                        p.mm(["hh2", "hw3"], [pk], pst[:, :], w3[:, col:col + 128], h2[:, tt_ * 512:(tt_ + 1) * 512])
                        E, ek = er.next()
                        p.act(["htbc", "hndec"], [ek], E[:], tbc[:, tt_ * 512:(tt_ + 1) * 512], AF.Exp, scale=ndec[:, colblk:colblk + 1])
                        p.tt([pk, ek], [hrk[dirn]], hrow[dirn][:, tt_ * 512:(tt_ + 1) * 512], pst[:, :], E[:], ALU.mult)
                    if dirn == 1:
                        p.op("dve", [hrk[1]], [hrk[1]], lambda e: e.memset(hrow[1][:, 0:1], 0.0))
                    p.act([hrk[dirn]], ["hjunk", ssk], junk[:], hrow[dirn][:], AF.Abs, accum_out=ss[:, dirn:dirn + 1])
                p.tt([ssk], [ssk], ss[:, 2:3], ss[:, 0:1], ss[:, 1:2], ALU.add)
                p.op("dve", [ssk], [ssk], lambda e: e.reciprocal(out=ss[:, 3:4], in_=ss[:, 2:3]))
                for sgn, opx in ((0, ALU.add), (1, ALU.subtract)):
                    up, upk = upr.next()
                    ub, ubk = ubr.next()
                    p.tt([hrk[0], hrk[1]], [upk], up[:], hrow[0][:], hrow[1][:], opx)
                    p.ts([upk, ssk], [ubk], ub[:], up[:], ss[:, 3:4], None, ALU.mult)
                    r0 = o * 1024 + cb_ * 128
                    p.dma("sp", [ubk], ["HFB"], HFB[sgn, r0:r0 + 128, :], ub[:])
        p.release(m0)
        if HY4_STOP == "F0":
            p.release(m)
            return
        WF = p.sb("WF", [128, 64, 3, 64], BF16)
        p.dma("sp", [], ["WF"], WF[:], WF_d[:, :, :, :])
        NBLK = D // CB
        m1 = p.mark()
        XE = CB * 128
        fb = {}
        for nm in ("Zp", "Zm", "Ap", "Am"):
            for i in range(2):
                fb[(nm, i)] = (p.sb("X%s%d" % (nm, i), [128, XE], BF16), "X%s%d" % (nm, i))
        ksr = Ring(p, "hks", [64, 2, 16, CB], F32, 2)

        def f1_load(b):
            o, blk = divmod(b, NBLK)
            r0 = o * 1024 + blk * CB
            i = b % 2
            load_zin(fb[("Zp", i)][0], fb[("Zp", i)][1], HFB[0, r0:r0 + CB, :], "HFB")
            load_zin(fb[("Zm", i)][0], fb[("Zm", i)][1], HFB[1, r0:r0 + CB, :], "HFB")

        f1_load(0)
        deferred = []
        for b in range(2 * NBLK):
            o, blk = divmod(b, NBLK)
            i = b % 2
            if b + 1 < 2 * NBLK:
                f1_load(b + 1)
            for fn in deferred:
                fn()
            deferred = []
            stage1(fb[("Zp", i)][0], fb[("Zp", i)][1], fb[("Ap", i)][0], fb[("Ap", i)][1])
            stage1(fb[("Zm", i)][0], fb[("Zm", i)][1], fb[("Am", i)][0], fb[("Am", i)][1])
            ks, ksk = None, None
            for g8 in range(8):
                if g8 % 2 == 0:
                    ks, ksk = ksr.next()
                zr, zrk, zi, zik = stage2(fb[("Ap", i)][0], fb[("Ap", i)][1], g8, fb[("Am", i)][0], fb[("Am", i)][1])
                fo = (g8 % 2) * 8
                evac([zrk], [ksk], ks[:, 0, fo:fo + 8, :], zr[:64, :].rearrange("p (f c) -> p f c", c=CB))
                evac([zik], [ksk], ks[:, 1, fo:fo + 8, :], zi[:64, :].rearrange("p (f c) -> p f c", c=CB))
                if g8 % 2 == 1:
                    f0 = (g8 // 2) * 16
                    p.dma("sp", [ksk], ["KF2"], KF2[o, blk, :, :, f0:f0 + 16, :], ks[:])
        for fn in deferred:
            fn()
        deferred = []
        p.release(m1)
        if HY4_STOP == "F1":
            p.release(m)
            return
        WI = p.sb("WI", [64, 64, 3, 128], BF16)
        for f4 in range(4):
            p.dma("sp", [], ["WI"], WI[:, f4 * 16:(f4 + 1) * 16, :, :], WI_d[:, f4 * 16:(f4 + 1) * 16, :, :])
        XA = [(p.sb("XAa", [128, XE], BF16), "XAa"), (p.sb("XAb", [128, XE], BF16), "XAb")]
        XB = [(p.sb("XBa", [128, XE], BF16), "XBa"), (p.sb("XBb", [128, XE], BF16), "XBb")]
        ktr = Ring(p, "hkt", [64, 2, 16, CB], F32, 3)
        tr_ = Ring(p, "htm", [64, 512], F32, 4)
        zsr = Ring(p, "hzs", [64, 512], F32, 4)
        ytr = Ring(p, "hyt", [32, 8, 128], F32, 2)
        for o in range(2):
            src = HY if o == 0 else Z2
            srcb = HYB if o == 0 else Z2B
            skey = "HY" if o == 0 else "Z2"
            md = p.mark()

            def d_load(b):
                load_zin(XA[b % 2][0], XA[b % 2][1], srcb[b * CB:(b + 1) * CB, toff:toff + n], skey)

            def k_load(b, gg):
                kt, kk = ktr.next()
                p.dma("sp", ["KF2"], [kk], kt[:], KF2[o, b, :, :, gg * 16:(gg + 1) * 16, :])
                return kt, kk

            d_load(0)
            deferred = []
            for b in range(NBLK):
                i = b % 2
                X1, x1k = XA[i]
                X2, x2k = XB[i]
                if b + 1 < NBLK:
                    d_load(b + 1)
                for fn in deferred:
                    fn()
                deferred = []
                stage1(X1, x1k, X2, x2k)
                Y = v_Y(X1)
                knext = k_load(b, 0)
                for g8 in range(8):
                    if g8 % 2 == 0:
                        kt, kk = knext
                        if g8 + 2 < 8:
                            knext = k_load(b, g8 // 2 + 1)
                    zr, zrk, zi, zik = stage2(X2, x2k, g8)
                    fo = (g8 % 2) * 8
                    kr_ = kt[:, 0, fo:fo + 8, :].rearrange("p f c -> p (f c)")
                    ki_ = kt[:, 1, fo:fo + 8, :].rearrange("p f c -> p (f c)")
                    szr, szrk = zsr.next()
                    szi, szik = zsr.next()
                    p.act([zrk], [szrk], szr[:], zr[:64, :], AF.Identity)
                    p.act([zik], [szik], szi[:], zi[:64, :], AF.Identity)
                    t1, k1 = tr_.next()
                    t2, k2 = tr_.next()
                    p.tt([szrk, kk], [k1], t1[:], szr[:], kr_, ALU.mult)
                    p.tt([szik, kk], [k2], t2[:], szi[:], ki_, ALU.mult)
                    p.tt([k1, k2], [x1k], Y[:, 0, g8 * 8:g8 * 8 + 8, :], t1[:].rearrange("p (f c) -> p f c", c=CB),
                         t2[:].rearrange("p (f c) -> p f c", c=CB), ALU.subtract)
                    t3, k3 = tr_.next()
                    t4, k4 = tr_.next()
                    p.tt([szrk, kk], [k3], t3[:], szr[:], ki_, ALU.mult, eng="pool")
                    p.tt([szik, kk], [k4], t4[:], szi[:], kr_, ALU.mult, eng="pool")
                    p.tt([k3, k4], [x1k], Y[:, 1, g8 * 8:g8 * 8 + 8, :], t3[:].rearrange("p (f c) -> p f c", c=CB),
                         t4[:].rearrange("p (f c) -> p f c", c=CB), ALU.add, eng="pool")
                Bt = X2[:, :].rearrange("p (k c) -> p k c", c=CB)
                for g8 in range(8):
                    br, brk = rot.next()
                    bi, bik = rot.next()
                    for q in range(8):
                        f1 = g8 * 8 + q
                        o_ = slice(q * CB, (q + 1) * CB)
                        p.mm(["WI", x1k], [brk], br[:, o_], WI[:, f1, 0, :], Y[:, 0, f1, :], start=True, stop=False)
                        p.mm(["WI", x1k], [brk], br[:, o_], WI[:, f1, 1, :], Y[:, 1, f1, :], start=False, stop=True)
                    for q in range(8):
                        f1 = g8 * 8 + q
                        o_ = slice(q * CB, (q + 1) * CB)
                        p.mm(["WI", x1k], [bik], bi[:, o_], WI[:, f1, 0, :], Y[:, 1, f1, :], start=True, stop=False)
                        p.mm(["WI", x1k], [bik], bi[:, o_], WI[:, f1, 2, :], Y[:, 0, f1, :], start=False, stop=True)
                    for ri, (bb, bbk) in enumerate(((br, brk), (bi, bik))):
                        k0 = ri * 64 + g8 * 8
                        evac([bbk], [x2k], Bt[:, k0:k0 + 8, :], bb[:, :].rearrange("p (f c) -> p f c", c=CB))
                B2 = v_B(X1)
                for c0 in range(0, CB, 8):
                    pst, pk = rot.next()
                    pv = pst[:, :].bitcast(BF16)
                    for q in range(8):
                        p.tr([x2k, "identb"], [pk], pv[:, q * 128:(q + 1) * 128], Bt[:, :, c0 + q], identb[:])
                    evac([pk], [x1k], B2[:, c0:c0 + 8, :], pv[:, :].rearrange("p (c k) -> p c k", k=128))
                yt, ytk = None, None
                for c0 in range(0, CB, 4):
                    if c0 % 8 == 0:
                        yt, ytk = ytr.next()
                    pst, pk = rot.next()
                    p.mm(["S2", x1k], [pk], pst[:32, :], S2[:, :], B2[:, c0:c0 + 4, :])
                    evac([pk], [ytk], yt[:, c0 % 8:c0 % 8 + 4, :], pst[:32, :].rearrange("p (c k) -> p c k", k=128))
                    if c0 % 8 == 4:
                        cr = b * CB + c0 - 4
                        p.dma("sp", [ytk], ["CONV"], CONV[cr:cr + 8, :].rearrange("c (a b) -> a c b", b=128), yt[:])
                if b % 2 == 1 or True:
                    for fn in deferred:
                        fn()
                    deferred = []
            if HY4_STOP == "D4":
                p.release(m)
                return
            GW_ = 512
            cvr = Ring(p, "hcv", [128, GW_], F32, 2)
            zpr = Ring(p, "hzp", [128, GW_], F32, 2)
            xgr = Ring(p, "hxg", [128, GW_], F32, 2)
            ybr = Ring(p, "hyb", [128, GW_], BF16, 2)
            for cb_ in range(8):
                for g0 in range(0, n, GW_):
                    cv, cvk = cvr.next()
                    p.dma("sp", ["CONV"], [cvk], cv[:], CONV[cb_ * 128:(cb_ + 1) * 128, g0:g0 + GW_])
                    zp, zpk = zpr.next()
                    p.dma("sp", [skey], [zpk], zp[:], src[cb_ * 128:(cb_ + 1) * 128, toff + g0:toff + g0 + GW_])
                    xg, xgk = xgr.next()
                    xr0 = (1 + o) * 1024 + cb_ * 128
                    p.dma("sp", ["HY"], [xgk], xg[:], HY[xr0:xr0 + 128, toff + g0:toff + g0 + GW_])
                    p.stt([zpk, "hbias", cvk], [cvk], cv[:], zp[:], hbias[:, o, cb_:cb_ + 1], cv[:], ALU.mult, ALU.add)
                    if o == 0:
                        p.tt([cvk, xgk], [cvk], cv[:], cv[:], xg[:], ALU.mult)
                        p.dma("sp", [cvk], ["Z2"], Z2[cb_ * 128:(cb_ + 1) * 128, toff + g0:toff + g0 + GW_], cv[:])
                        yb, ybk = ybr.next()
                        p.act([cvk], [ybk], yb[:], cv[:], AF.Identity)
                        p.dma("sp", [ybk], ["Z2"], Z2B[cb_ * 128:(cb_ + 1) * 128, toff + g0:toff + g0 + GW_], yb[:])
                    else:
                        yb, ybk = ybr.next()
                        p.tt([cvk, xgk], [ybk], yb[:], cv[:], xg[:], ALU.mult)
                        p.dma("sp", [ybk], ["YH"], YH[cb_ * 128:(cb_ + 1) * 128, toff + g0:toff + g0 + GW_], yb[:])
            p.release(md)
        p.release(m)

    def phase_hyena(l):
        m = p.mark()
        cw = p.sb("hcw", [128, 24, 3], F32)
        cb = p.sb("hcb", [128, 24], F32)
        hbias = p.sb("hbias", [128, 2, 8], F32)
        p.dma("sp", [], ["hcw"], cw[:], hy_cwT[l])
        p.dma("sp", [], ["hcb"], cb[:], hy_cbT[l])
        p.dma("sp", [], ["hbias"], hbias[:], hy_biasT[l])
        segs = [(0, NCTX), (NCTX, T)]
        m1 = p.mark()
        T1r = Ring(p, "hT1", [128, T], F32, 2)
        XSr = Ring(p, "hXS", [128, T], F32, 2)
        XBr = Ring(p, "hXB", [128, T], BF16, 2)
        for blk in range(24):
            T1, k1 = T1r.next()
            XS, k2 = XSr.next()
            p.dma("sp", ["PROJ"], [k1], T1[:], PROJ[2048 + blk * 128:2048 + (blk + 1) * 128, :])
            seg_conv(p, XS, T1, lambda j: cw[:, blk, j:j + 1], cb[:, blk:blk + 1], 3, 1, segs, [k1, "hcw", "hcb"], [k2])
            p.dma("sp", [k2], ["HY"], HY[blk * 128:(blk + 1) * 128, :], XS[:])
            if blk < 8:
                xb_, xbk = XBr.next()
                p.act([k2], [xbk], xb_[:], XS[:, :], AF.Identity)
                p.dma("sp", [xbk], ["HY"], HYB[blk * 128:(blk + 1) * 128, :], xb_[:])
        p.release(m1)
        hy_filter(l, NCTX, embT_c, tv_c, FW_c, KF_c)
        hy_data(l, NCTX, 0, FW_c, GW_c, KF_c, hbias)
        if HY_DENSE:
            hy_filter(l, NLAT, embT_l, tv_l, FW_l, KF_l)
            hy_data(l, NLAT, NCTX, FW_l, GW_l, KF_l, hbias)
        else:
            hy4(l, hbias)
        p.release(m)

    def phase_merge(l, xsrc):
        m = p.mark()
        wb = p.sb("wb", [128, 3, 8, D], BF16)
        wo = p.sb("wo", [128, 8, D], BF16)
        for br in range(3):
            p.dma("pool", [], ["wb"], wb[:, br, :, :], fm(w_branch[l, br]))
        p.dma("pool", [], ["wo"], wo[:], fm(w_out[l]))
        ytr = Ring(p, "mytok", [128, 4, D], F32, 1)
        ysr = Ring(p, "mys", [128, 8, 512], BF16, 1)
        yrr = Ring(p, "myr", [128, 8, 512], BF16, 1)
        yhr = Ring(p, "myh", [128, 8, 512], BF16, 1)
        gr = Ring(p, "mg", [128, 24, 512], BF16, 1)
        mtr = Ring(p, "mmt", [128, 8, 512], BF16, 1)
        xr = Ring(p, "mx", [128, 8, 512], F32, 1)
        xnr = Ring(p, "mxn", [128, 8, 512], F32, 1)
        tr_ = Ring(p, "mtm", [128, 512], F32, 4)
        for ti, (t0, tw) in enumerate(TT):
            s_ = 0 if ti == 0 else 1
            nsub = tw // 128
            yt, ytk = ytr.next()
            p.dma("sp", ["YSTOK"], [ytk], yt[:, :nsub, :], YSTOK[t0:t0 + tw, :].rearrange("(a p) d -> p a d", p=128))
            ys, ysk = ysr.next()
            for kc in range(8):
                pst, pk = psr.next()
                for a in range(nsub):
                    p.tr([ytk, "ident"], [pk], pst[:, a * 128:(a + 1) * 128], yt[:, a, kc * 128:(kc + 1) * 128], ident[:])
                p.cp([pk], [ysk], ys[:, kc, :tw], pst[:, :tw], eng=("dve" if kc % 2 else "act")) if False else (
                    p.act([pk], [ysk], ys[:, kc, :tw], pst[:, :tw], AF.Identity) if kc % 2 == 0 else p.cp([pk], [ysk], ys[:, kc, :tw], pst[:, :tw]))
            yr, yrk = yrr.next()
            p.dma("sp", ["YR"], [yrk], yr[:, :, :tw], fm(YR)[:, :, t0:t0 + tw])
            yh, yhk = yhr.next()
            p.dma("sp", ["YH"], [yhk], yh[:, :, :tw], fm(YH)[:, :, t0:t0 + tw])
            g, gk = gr.next()
            p.dma("sp", ["GT"], [gk], g[:, :, :tw], fm(GT)[:, :, t0:t0 + tw])
            xt, xk = xr.next()
            p.dma("sp", ["XT"], [xk], xt[:, :, :tw], xsrc[:, :, t0:t0 + tw])
            mt, mtk = mtr.next()
            for cb_ in range(8):
                pbs = []
                for br, (yb, ybk) in enumerate(((yr, yrk), (yh, yhk), (ys, ysk))):
                    pst, pk = psr.next()
                    for kc in range(8):
                        p.mm(["wb", ybk], [pk], pst[:, :tw], wb[:, br, kc, cb_ * 128:(cb_ + 1) * 128], yb[:, kc, :tw],
                             start=(kc == 0), stop=(kc == 7))
                    pbs.append((pst, pk))
                t1, k1 = tr_.next()
                t2, k2 = tr_.next()
                p.tt([pbs[0][1], gk], [k1], t1[:, :tw], pbs[0][0][:, :tw], g[:, cb_, :tw], ALU.mult)
                p.tt([pbs[1][1], gk], [k2], t2[:, :tw], pbs[1][0][:, :tw], g[:, 8 + cb_, :tw], ALU.mult)
                p.tt([k1, k2], [k1], t1[:, :tw], t1[:, :tw], t2[:, :tw], ALU.add)
                t3, k3 = tr_.next()
                p.tt([pbs[2][1], gk], [k3], t3[:, :tw], pbs[2][0][:, :tw], g[:, 16 + cb_, :tw], ALU.mult)
                p.tt([k1, k3], [mtk], mt[:, cb_, :tw], t1[:, :tw], t3[:, :tw], ALU.add)
            xn, xnk = xnr.next()
            for co in range(8):
                pst, pk = psr.next()
                for cb_ in range(8):
                    p.mm(["wo", mtk], [pk], pst[:, :tw], wo[:, cb_, co * 128:(co + 1) * 128], mt[:, cb_, :tw],
                         start=(cb_ == 0), stop=(cb_ == 7))
                p.stt([pk, "modT", xk], [xnk], xn[:, co, :tw], pst[:, :tw], modT[:, 16 + co, s_:s_ + 1], xt[:, co, :tw], ALU.mult, ALU.add)
            p.dma("sp", [xnk], ["XT"], fm(XT)[:, :, t0:t0 + tw], xn[:, :, :tw])
        p.release(m)

    def phase_ffn(l, HT):
        m = p.mark()
        wv = fm(w_up[l])
        wr = Ring(p, "fwu", [128, 2, 8, 128], BF16, 3)
        str_ = Ring(p, "fst", [128, T], BF16, 2)
        sgr = Ring(p, "fsg", [128, 512], F32, 3)
        for j in range(22):
            wt, wk = wr.next()
            p.dma("pool", [], [wk], wt[:, 0, :, :], wv[:, :, j * 128:(j + 1) * 128])
            p.dma("pool", [], [wk], wt[:, 1, :, :], wv[:, :, D_FF + j * 128:D_FF + (j + 1) * 128])
            st, stk = str_.next()
            for ti, (t0, tw) in enumerate(TT):
                pg, pgk = psr.next()
                pu, puk = psr.next()
                for kc in range(8):
                    p.mm([wk, "HT"], [pgk], pg[:, :tw], wt[:, 0, kc, :], HT[:, kc, t0:t0 + tw], start=(kc == 0), stop=(kc == 7))
                for kc in range(8):
                    p.mm([wk, "HT"], [puk], pu[:, :tw], wt[:, 1, kc, :], HT[:, kc, t0:t0 + tw], start=(kc == 0), stop=(kc == 7))
                sg, sgk = sgr.next()
                p.act([pgk], [sgk], sg[:, :tw], pg[:, :tw], AF.Silu)
                p.tt([sgk, puk], [stk], st[:, t0:t0 + tw], sg[:, :tw], pu[:, :tw], ALU.mult)
            p.dma("sp", [stk], ["AFF"], AFF[j * 128:(j + 1) * 128, :], st[:])
        p.release(m)

    def phase_ffn2(l):
        m = p.mark()
        wd = p.sb("fwd", [128, 22, D], BF16)
        p.dma("pool", [], ["fwd"], wd[:], w_down[l].rearrange("(j p) c -> p j c", p=128))
        ar = Ring(p, "fa", [128, 22, 512], BF16, 2)
        xr = Ring(p, "fx", [128, 8, 512], F32, 2)
        xnr = Ring(p, "fxn", [128, 8, 512], F32, 2)
        av = AFF.rearrange("(j p) t -> p j t", p=128)
        for ti, (t0, tw) in enumerate(TT):
            s_ = 0 if ti == 0 else 1
            at, ak = ar.next()
            p.dma("sp", ["AFF"], [ak], at[:, :, :tw], av[:, :, t0:t0 + tw])
            xt, xk = xr.next()
            p.dma("sp", ["XT"], [xk], xt[:, :, :tw], fm(XT)[:, :, t0:t0 + tw])
            xn, xnk = xnr.next()
            for co in range(8):
                pst, pk = psr.next()
                for j in range(22):
                    p.mm(["fwd", ak], [pk], pst[:, :tw], wd[:, j, co * 128:(co + 1) * 128], at[:, j, :tw], start=(j == 0), stop=(j == 21))
                p.stt([pk, "modT", xk], [xnk], xn[:, co, :tw], pst[:, :tw], modT[:, 40 + co, s_:s_ + 1], xt[:, co, :tw], ALU.mult, ALU.add)
            p.dma("sp", [xnk], ["XT"], fm(XT)[:, :, t0:t0 + tw], xn[:, :, :tw])
        p.release(m)

    def phase_final():
        m = p.mark()
        fn = p.sb("fnw", [128, 8], F32)
        p.dma("sp", [], ["fnw"], fn[:], final_normT[:, :])
        xr = Ring(p, "ox", [128, 8, 512], F32, 2)
        sqr = Ring(p, "osq", [128, 8, 512], BF16, 2)
        rr = Ring(p, "orstd", [128, 512], F32, 2)
        outr = Ring(p, "oo", [128, 8, 512], F32, 2)
        src = fm(XT)
        for ti, (t0, tw) in enumerate(TT):
            if ti == 0:
                continue
            xt, xk = xr.next()
            p.dma("sp", ["XT"], [xk], xt[:], src[:, :, t0:t0 + tw])
            sq, sqk = sqr.next()
            p.act([xk], [sqk], sq[:], xt[:], AF.Square)
            pst, pk = psr.next()
            for kc in range(8):
                p.mm([sqk, "onesb"], [pk], pst[:, :], onesb[:], sq[:, kc, :], start=(kc == 0), stop=(kc == 7))
            rs, rk = rr.next()
            p.ts([pk], [rk], rs[:], pst[:, :], 1.0 / D, EPS, ALU.mult, ALU.add)
            p.act([rk], [rk], rs[:], rs[:], AF.Sqrt)
            p.op("dve", [rk], [rk], lambda e: e.reciprocal(out=rs[:], in_=rs[:]))
            ot, ok_ = outr.next()
            for kc in range(8):
                p.stt([xk, "fnw", rk], [ok_], ot[:, kc, :], xt[:, kc, :], fn[:, kc:kc + 1], rs[:], ALU.mult, ALU.mult)
            p.dma("sp", [ok_], ["OUT"], fm(out)[:, :, t0 - NCTX:t0 - NCTX + tw], ot[:])
        p.wait_all("sp", ["OUT"])
        p.release(m)

    for l in range(nlayers):
        phase_mod(l)
        mk = p.mark()
        HT = p.sb("HT", [128, 8, T], BF16)
        phase_norm(fm(xin if l == 0 else XT), A1, "A1", 0, HT)
        if "HTD" in dbg:
            p.dma("sp", ["HT"], ["HTD"], fm(HTD), HT[:])
        if stop_after == "norm":
            p.release(mk)
            break
        phase_proj(l, HT)
        p.release(mk)
        if stop_after == "proj":
            break
        if stop_after not in ("ssd", "hyena"):
            phase_rglru(l)
        if stop_after == "rglru":
            break
        if stop_after != "hyena":
            phase_ssd(l)
        if stop_after == "ssd":
            break
        phase_hyena(l)
        if stop_after == "hyena":
            break
        phase_merge(l, fm(xin if l == 0 else XT))
        if stop_after == "merge":
            break
        mk = p.mark()
        HT = p.sb("HT", [128, 8, T], BF16)
        phase_norm(fm(XT), A2, "A2", 24, HT)
        phase_ffn(l, HT)
        p.release(mk)
        phase_ffn2(l)
    if stop_after is None:
        phase_final()
    p.barrier()
    p.close()
    print("instructions:", p.ninst)
    return nc


def fmT(v, nchunk):
    return np.ascontiguousarray(np.swapaxes(v.reshape(v.shape[:-1] + (nchunk, 128)), -1, -2))

def hyena_emb(n):
    f = np.float32
    t = np.linspace(0.0, 1.0, n, dtype=f)
    bands = np.linspace(1e-4, 15.0, 16, dtype=f)
    ang = (f(2.0 * np.pi / n) * np.arange(n, dtype=f)[:, None]) * bands[None]
    emb = np.concatenate([t[:, None], np.cos(ang), np.sin(ang)], axis=-1).astype(f)
    return np.ascontiguousarray(emb.T)

def hyena_consts(n):
    import ml_dtypes
    f = np.float32
    t = np.linspace(0.0, 1.0, n, dtype=f)
    bands = np.linspace(1e-4, 15.0, 16, dtype=f)
    ang = (f(2.0 * np.pi / n) * np.arange(n, dtype=f)[:, None]) * bands[None]
    emb = np.concatenate([t[:, None], np.cos(ang), np.sin(ang)], axis=-1).astype(f)
    embT = np.ascontiguousarray(emb.T)
    nt = n // 128
    tv = np.ascontiguousarray(t.reshape(nt, 128).T)
    N = 2 * n
    tt = np.arange(n, dtype=np.int64)
    ff = np.arange(n, dtype=np.int64)
    ph = ((2 * ff[None, :] + 1) * tt[:, None]) % (2 * N)
    angm = np.pi * ph.astype(np.float64) / N
    C = np.cos(angm); S = np.sin(angm)
    def tile_f(M):
        return M.reshape(nt, 128, nt, 128).transpose(2, 1, 0, 3)
    FW = np.concatenate([tile_f(C), tile_f(S)], axis=0).astype(ml_dtypes.bfloat16)
    tw = 512 if n >= 512 else n
    def tile_g(M):
        return (M.T * (2.0 / N)).reshape(nt, 128, n // tw, tw).transpose(2, 1, 0, 3)
    GW = np.concatenate([tile_g(C), tile_g(S)], axis=2).astype(ml_dtypes.bfloat16)
    return embT, tv, np.ascontiguousarray(FW), np.ascontiguousarray(GW)

def hyena_consts4():
    import ml_dtypes
    bf = ml_dtypes.bfloat16
    n, N = 4096, 8192
    t1 = np.arange(32, dtype=np.int64)[:, None]
    f1 = np.arange(64, dtype=np.int64)[None, :]
    g = 2.0 * np.pi * (((2 * f1 + 1) * t1) % 128).astype(np.float64) / 128.0
    S1 = np.concatenate([np.cos(g), -np.sin(g)], axis=1).astype(bf)
    S2 = (np.concatenate([np.cos(g).T, -np.sin(g).T], axis=0) * (2.0 / N)).astype(bf)
    f1v = np.arange(64, dtype=np.int64)[:, None, None]
    t2v = np.arange(128, dtype=np.int64)[None, :, None]
    f2v = np.arange(64, dtype=np.int64)[None, None, :]
    ph = ((2 * f1v + 1) * t2v + 128 * f2v * t2v) % 16384
    phi = 2.0 * np.pi * ph.astype(np.float64) / 16384.0
    Wr = np.cos(phi); Wi = -np.sin(phi)
    WF = np.stack([Wr, Wi, -Wi], axis=0).transpose(2, 1, 0, 3)
    WI = np.stack([Wr, Wi, -Wi], axis=0).transpose(3, 1, 0, 2)
    tvec = np.linspace(0.0, 1.0, n, dtype=np.float32)[None, :]
    return S1, S2, np.ascontiguousarray(WF.astype(bf)), np.ascontiguousarray(WI.astype(bf)), np.ascontiguousarray(tvec)

def prep_shared(inp):
    f = np.float32
    sh = {}
    sh["w_mod"] = inp["w_mod"]
    sh["b_modT"] = fmT(inp["b_mod"], 48)
    sh["norm_mixT"] = fmT(inp["norm_mix"], 8)
    sh["norm_ffnT"] = fmT(inp["norm_ffn"], 8)
    sh["final_normT"] = fmT(inp["final_norm"], 8)
    sh["w_in"] = inp["w_in"]
    sh["rnn_cwT"] = np.ascontiguousarray(inp["rnn_conv_w"].reshape(4, 4, 8, 128).transpose(0, 3, 2, 1))
    sh["rnn_cbT"] = fmT(inp["rnn_conv_b"], 8)
    sh["rnn_aw"] = inp["rnn_gate_a_w"]
    sh["rnn_xw"] = inp["rnn_gate_x_w"]
    sh["rnn_abT"] = np.ascontiguousarray(fmT(inp["rnn_gate_a_b"], 8).transpose(0, 2, 1, 3))
    sh["rnn_xbT"] = np.ascontiguousarray(fmT(inp["rnn_gate_x_b"], 8).transpose(0, 2, 1, 3))
    sh["rnn_lamT"] = np.ascontiguousarray(fmT(inp["rnn_lambda"], 8).transpose(0, 2, 1, 3))
    sh["ident"] = np.eye(128, dtype=f)
    j = np.arange(128)[:, None]; ll = np.arange(128)[None, :]
    sh["masks"] = np.stack([(j <= ll), (j > ll), (j >= ll), (j < ll), np.ones((128, 128), bool)]).astype(f)
    sh["ssm_cwT"] = np.ascontiguousarray(inp["ssm_conv_w"].reshape(4, 4, 12, 128).transpose(0, 3, 2, 1))
    sh["ssm_cbT"] = fmT(inp["ssm_conv_b"], 12)
    sh["ssm_alogT"] = np.ascontiguousarray(inp["ssm_a_log"].reshape(4, 32, 1))
    sh["ssm_dtbT"] = np.ascontiguousarray(inp["ssm_dt_bias"].reshape(4, 32, 1))
    sh["ssm_d"] = inp["ssm_d"]
    sh["ssm_norm"] = inp["ssm_norm"]
    sh["hy_cwT"] = np.ascontiguousarray(inp["hy_short_w"].reshape(4, 3, 24, 128).transpose(0, 3, 2, 1))
    sh["hy_cbT"] = fmT(inp["hy_short_b"], 24)
    sh["hy_biasT"] = np.ascontiguousarray(fmT(inp["hy_bias"], 8).transpose(0, 2, 1, 3))
    sh["hy_w1"] = inp["hy_w1"]; sh["hy_w2"] = inp["hy_w2"]; sh["hy_w3"] = inp["hy_w3"]
    sh["hy_b1T"] = np.ascontiguousarray(inp["hy_b1"].reshape(4, 64, 1))
    sh["hy_b2T"] = np.ascontiguousarray(inp["hy_b2"].reshape(4, 64, 1))
    sh["hy_freqT"] = np.ascontiguousarray(inp["hy_freq"].reshape(4, 64, 1))
    sh["hy_decay"] = inp["hy_decay"]
    emb, tv, FW, GW = hyena_consts(256)
    sh["embT_c"] = emb; sh["tv_c"] = tv; sh["FW_c"] = FW; sh["GW_c"] = GW
    sh["embT_l"] = hyena_emb(4096)
    S1, S2, WF, WI, tvec = hyena_consts4()
    sh["S1"] = S1; sh["S2"] = S2; sh["WF"] = WF; sh["WI"] = WI; sh["tvec"] = tvec
    sh["hy_ndecT"] = np.ascontiguousarray(-fmT(inp["hy_decay"], 32))
    sh["w_branch"] = inp["w_branch"]; sh["w_out"] = inp["w_out"]; sh["w_up"] = inp["w_up"]; sh["w_down"] = inp["w_down"]
    return sh

def prep_core(inp, b):
    xin = np.ascontiguousarray(np.concatenate([inp["ctx"][b].T, inp["x"][b].T], axis=1))
    cc = np.stack([inp["c_ctx"], inp["c"][b]], axis=-1)
    cc = np.ascontiguousarray(cc.reshape(8, 128, 2).transpose(1, 0, 2))
    return {"xin": xin, "cc": cc}


def kernel(**inputs):
    inp = {k: np.asarray(v) for k, v in inputs.items()}
    sh = prep_shared(inp)
    nc = build()
    in_maps = []
    for b in range(8):
        im = dict(sh)
        im.update(prep_core(inp, b))
        in_maps.append(im)
    res = run_bass_kernel_spmd(nc, in_maps, core_ids=list(range(8)))
    out = np.stack([np.ascontiguousarray(np.asarray(r["out"]).T) for r in res.results], axis=0)
    return out.astype(np.float32)
```
